# Optimizing a Trainium2 kernel written in Bass

```python
import math
import jax, jax.numpy as jnp
from jax import lax
import numpy as np

D_MODEL = 1024
BATCH = 16
SEQ = 2048
DEPTH = 1
DEC_BATCH = 2
DEC_SEQ = 8192
PAST_LEN = 128

MIX_WIDTH = D_MODEL
MLA_WIDTH = MIX_WIDTH // 2
NA_WIDTH = MIX_WIDTH - MLA_WIDTH

MLA_HEADS = 8
MLA_V_DIM = MLA_WIDTH // MLA_HEADS
MLA_NOPE_DIM = 64
MLA_ROPE_DIM = 32
MLA_QK_DIM = MLA_NOPE_DIM + MLA_ROPE_DIM
Q_LORA_RANK = 384
KV_LORA_RANK = 256
ROPE_THETA = 10000.0
Q_BLOCK = 128

NA_HEADS = 8
NA_HEAD_DIM = NA_WIDTH // NA_HEADS
GRID_W = 64
NA_KH = 8
NA_KW = 16

IN_COLS = Q_LORA_RANK + KV_LORA_RANK + MLA_ROPE_DIM + 3 * NA_WIDTH
D_FF = int(math.ceil(8 * D_MODEL / 3 / 256)) * 256
EPS = 1e-6
NEG_INF = -1e30

kernel_name = "hymba_mla_natten_encoder"


def rmsnorm(x, g):
    xf = x.astype(jnp.float32)
    y = xf * lax.rsqrt(jnp.mean(xf * xf, axis=-1, keepdims=True) + EPS)
    return (y * g.astype(jnp.float32)).astype(x.dtype)


def rope_tables(seq_len):
    inv = ROPE_THETA ** (-jnp.arange(0, MLA_ROPE_DIM, 2, dtype=jnp.float32) / MLA_ROPE_DIM)
    ang = jnp.arange(seq_len, dtype=jnp.float32)[:, None] * inv[None, :]
    return jnp.cos(ang), jnp.sin(ang)


def apply_rope(x, cos, sin):
    shape = (cos.shape[0],) + (1,) * (x.ndim - 3) + (cos.shape[1],)
    c, s = cos.reshape(shape), sin.reshape(shape)
    xf = x.astype(jnp.float32)
    x1, x2 = jnp.split(xf, 2, axis=-1)
    return jnp.concatenate([x1 * c - x2 * s, x1 * s + x2 * c], axis=-1).astype(x.dtype)


def mla_group(c_q, c_kv, k_r, q_norm_g, kv_norm_g, w_uq, w_ukv):
    B, S, _ = c_q.shape
    q = (rmsnorm(c_q, q_norm_g) @ w_uq).reshape(B, S, MLA_HEADS, MLA_QK_DIM)
    kv = (rmsnorm(c_kv, kv_norm_g) @ w_ukv).reshape(B, S, MLA_HEADS, MLA_NOPE_DIM + MLA_V_DIM)
    k_nope, v = kv[..., :MLA_NOPE_DIM], kv[..., MLA_NOPE_DIM:]
    cos, sin = rope_tables(S)
    q = jnp.concatenate([q[..., :MLA_NOPE_DIM], apply_rope(q[..., MLA_NOPE_DIM:], cos, sin)], axis=-1)
    q = q * (MLA_QK_DIM ** -0.5)
    k_rope = apply_rope(k_r, cos, sin)
    n_blk = S // Q_BLOCK
    qb = q.reshape(B, n_blk, Q_BLOCK, MLA_HEADS, MLA_QK_DIM).transpose(1, 0, 2, 3, 4)

    def block(qi):
        s = (jnp.einsum('bqhd,bkhd->bhqk', qi[..., :MLA_NOPE_DIM], k_nope,
                        preferred_element_type=jnp.float32)
             + jnp.einsum('bqhr,bkr->bhqk', qi[..., MLA_NOPE_DIM:], k_rope,
                          preferred_element_type=jnp.float32))
        p = jax.nn.softmax(s, axis=-1).astype(v.dtype)
        return jnp.einsum('bhqk,bkhd->bqhd', p, v)

    o = lax.map(block, qb)
    return o.transpose(1, 0, 2, 3, 4).reshape(B, S, MLA_WIDTH)


def na_group(qkv, rpb):
    B, S, _ = qkv.shape
    rows = S // GRID_W
    kh = min(NA_KH, rows)
    qkv = qkv.reshape(B, rows, GRID_W, 3, NA_HEADS, NA_HEAD_DIM)
    q = qkv[:, :, :, 0] * (NA_HEAD_DIM ** -0.5)
    k = qkv[:, :, :, 1]
    v = qkv[:, :, :, 2]
    qc = np.arange(GRID_W)[:, None]
    kc = np.arange(GRID_W)[None, :]
    col_start = np.clip(qc - NA_KW // 2, 0, GRID_W - NA_KW)
    col_mask = (kc >= col_start) & (kc < col_start + NA_KW)
    dj_idx = (np.clip(kc - qc, -(NA_KW - 1), NA_KW - 1) + NA_KW - 1).astype(np.int32)

    def row_block(r):
        start = jnp.clip(r - kh // 2, 0, rows - kh)
        q_r = lax.dynamic_index_in_dim(q, r, axis=1, keepdims=False)
        k_b = lax.dynamic_slice_in_dim(k, start, kh, axis=1)
        v_b = lax.dynamic_slice_in_dim(v, start, kh, axis=1)
        s = jnp.einsum('bqhd,bkwhd->bhqkw', q_r, k_b, preferred_element_type=jnp.float32)
        di = start + jnp.arange(kh) - r
        bias = rpb[:, di + NA_KH - 1][:, :, dj_idx]
        bias = bias.transpose(0, 2, 1, 3).astype(jnp.float32)
        s = jnp.where(col_mask[None, None, :, None, :], s + bias[None], NEG_INF)
        p = jax.nn.softmax(s.reshape(B, NA_HEADS, GRID_W, kh * GRID_W), axis=-1)
        p = p.reshape(B, NA_HEADS, GRID_W, kh, GRID_W).astype(v.dtype)
        return jnp.einsum('bhqkw,bkwhd->bqhd', p, v_b)

    o = lax.map(row_block, jnp.arange(rows))
    return o.transpose(1, 0, 2, 3, 4).reshape(B, S, NA_WIDTH)


def encoder_layer(x, attn_norm_g, w_in, q_norm_g, kv_norm_g, w_uq, w_ukv, na_rpb,
                  mla_out_g, na_out_g, w_o, ffn_norm_g, w_gate, w_up, w_down):
    h = rmsnorm(x, attn_norm_g)
    proj = h @ w_in
    o1 = Q_LORA_RANK
    o2 = o1 + KV_LORA_RANK
    o3 = o2 + MLA_ROPE_DIM
    a = mla_group(proj[..., :o1], proj[..., o1:o2], proj[..., o2:o3],
                  q_norm_g, kv_norm_g, w_uq, w_ukv)
    b = na_group(proj[..., o3:], na_rpb)
    mix = jnp.concatenate([rmsnorm(a, mla_out_g), rmsnorm(b, na_out_g)], axis=-1)
    x = x + mix @ w_o
    h2 = rmsnorm(x, ffn_norm_g)
    x = x + (jax.nn.silu(h2 @ w_gate) * (h2 @ w_up)) @ w_down
    return x


def run_trunk(x, attn_norm_g, w_in, q_norm_g, kv_norm_g, w_uq, w_ukv, na_rpb,
              mla_out_g, na_out_g, w_o, ffn_norm_g, w_gate, w_up, w_down, final_norm_g):
    for l in range(DEPTH):
        x = encoder_layer(x, attn_norm_g[l], w_in[l], q_norm_g[l], kv_norm_g[l], w_uq[l],
                          w_ukv[l], na_rpb[l], mla_out_g[l], na_out_g[l], w_o[l],
                          ffn_norm_g[l], w_gate[l], w_up[l], w_down[l])
    return rmsnorm(x, final_norm_g)


def setup_inputs(seed: int = 0) -> dict:
    key = jax.random.key(seed)
    ks = jax.random.split(key, 20)
    f32 = jnp.float32

    def w(k, shape, fan_in):
        return jax.random.normal(k, shape, f32) * (fan_in ** -0.5)

    def gain(k, shape):
        return 1.0 + 0.05 * jax.random.normal(k, shape, f32)

    L = DEPTH
    return {
        "x_prompt": jax.random.normal(ks[0], (BATCH, SEQ, D_MODEL), f32),
        "x_sample": jax.random.normal(ks[1], (DEC_BATCH, DEC_SEQ, D_MODEL), f32),
        "attn_norm_g": gain(ks[2], (L, D_MODEL)),
        "w_in": w(ks[3], (L, D_MODEL, IN_COLS), D_MODEL),
        "q_norm_g": gain(ks[4], (L, Q_LORA_RANK)),
        "kv_norm_g": gain(ks[5], (L, KV_LORA_RANK)),
        "w_uq": w(ks[6], (L, Q_LORA_RANK, MLA_HEADS * MLA_QK_DIM), Q_LORA_RANK),
        "w_ukv": w(ks[7], (L, KV_LORA_RANK, MLA_HEADS * (MLA_NOPE_DIM + MLA_V_DIM)), KV_LORA_RANK),
        "na_rpb": 0.1 * jax.random.normal(ks[8], (L, NA_HEADS, 2 * NA_KH - 1, 2 * NA_KW - 1), f32),
        "mla_out_g": gain(ks[9], (L, MLA_WIDTH)),
        "na_out_g": gain(ks[10], (L, NA_WIDTH)),
        "w_o": w(ks[11], (L, MIX_WIDTH, D_MODEL), MIX_WIDTH),
        "ffn_norm_g": gain(ks[12], (L, D_MODEL)),
        "w_gate": w(ks[13], (L, D_MODEL, D_FF), D_MODEL),
        "w_up": w(ks[14], (L, D_MODEL, D_FF), D_MODEL),
        "w_down": w(ks[15], (L, D_FF, D_MODEL), D_FF),
        "final_norm_g": gain(ks[16], (D_MODEL,)),
    }


def reference(x_prompt, x_sample, attn_norm_g, w_in, q_norm_g, kv_norm_g, w_uq, w_ukv,
              na_rpb, mla_out_g, na_out_g, w_o, ffn_norm_g, w_gate, w_up, w_down,
              final_norm_g):
    y_prompt = run_trunk(x_prompt, attn_norm_g, w_in, q_norm_g, kv_norm_g, w_uq, w_ukv,
                         na_rpb, mla_out_g, na_out_g, w_o, ffn_norm_g, w_gate, w_up,
                         w_down, final_norm_g)
    y_sample = run_trunk(x_sample, attn_norm_g, w_in, q_norm_g, kv_norm_g, w_uq, w_ukv,
                         na_rpb, mla_out_g, na_out_g, w_o, ffn_norm_g, w_gate, w_up,
                         w_down, final_norm_g)
    return (y_prompt, y_sample)
```

```python
import numpy as np
import concourse.bass as bass
import concourse.mybir as mybir
from concourse.bass_utils import run_bass_kernel_spmd

F32 = mybir.dt.float32
BF16 = mybir.dt.bfloat16
AF = mybir.ActivationFunctionType
ALU = mybir.AluOpType

QUEUES = ("sync", "scalar", "vector", "gpsimd", "tensor")
NDSEM = {"sync": 16, "gpsimd": 12, "scalar": 4}


class Res:
    __slots__ = ("name", "w", "rc", "rd", "psum")

    def __init__(self, name="", psum=False):
        self.name = name
        self.psum = psum
        self.w = None
        self.rc = {}
        self.rd = []


class Op:
    __slots__ = ("q", "dma", "fn", "deps", "signal", "cidx", "didx", "pos")


class Sched:
    def __init__(self):
        self.ops = {q: [] for q in QUEUES}
        self.ncomp = {q: 0 for q in QUEUES}
        self.ndma = {q: 0 for q in QUEUES}
        self.dma_ops = {q: [] for q in QUEUES}
        self.all_dma = []

    def op(self, q, fn, reads=(), writes=(), dma=False):
        o = Op()
        o.q, o.dma, o.fn, o.signal = q, dma, fn, dma
        deps = set()
        for r in reads:
            if r.w is not None:
                deps.add(r.w)
            if r.psum:
                deps.update(o2 for q2, o2 in r.rc.items() if q2 != q)
        for r in writes:
            if r.w is not None:
                deps.add(r.w)
            deps.update(r.rc.values())
            deps.update(r.rd)
        if dma:
            o.didx = self.ndma[q]
            self.ndma[q] += 1
            R = NDSEM[q]
            if o.didx >= R:
                deps.add(self.dma_ops[q][o.didx - R])
            self.dma_ops[q].append(o)
            self.all_dma.append(o)
            o.cidx = None
        else:
            o.cidx = self.ncomp[q]
            self.ncomp[q] += 1
            o.didx = None
        deps.discard(o)
        if q == "tensor" and not dma:
            deps = {d for d in deps if not (d.q == "tensor" and not d.dma)}
        o.deps = deps
        for d in deps:
            d.signal = True
        for r in reads:
            if dma:
                r.rd.append(o)
            else:
                r.rc[q] = o
        for r in writes:
            r.w = o
            r.rc = {}
            r.rd = []
        o.pos = len(self.ops[q])
        self.ops[q].append(o)
        return o

    def emit(self, nc, block, csem, dsem):
        sched = self
        val = {}
        for q in QUEUES:
            n = 0
            for o in self.ops[q]:
                if o.dma:
                    val[o] = (dsem[q][o.didx % NDSEM[q]], 16 * (o.didx // NDSEM[q] + 1))
                elif o.signal:
                    n += 1
                    val[o] = (csem[q], n)

        def run(q, eng):
            waited = {}
            last_dma = []
            for o in sched.ops[q]:
                need = {}
                for d in o.deps:
                    s, v = val[d]
                    k = id(s)
                    if v > need.get(k, (None, 0))[1]:
                        need[k] = (s, v)
                for k, (s, v) in need.items():
                    if waited.get(k, 0) < v:
                        eng.wait_ge(s, v)
                        waited[k] = v
                ins = o.fn(eng)
                if o.dma:
                    s, v = val[o]
                    ins.then_inc(s, 16)
                elif o.signal:
                    ins.then_inc(csem[q], 1)
            for o in sched.dma_ops[q][-NDSEM.get(q, 0):] if sched.dma_ops[q] else []:
                s, v = val[o]
                if waited.get(id(s), 0) < v:
                    eng.wait_ge(s, v)
                    waited[id(s)] = v

        @block.sync
        def _(e):
            run("sync", e)

        @block.scalar
        def _(e):
            run("scalar", e)

        @block.vector
        def _(e):
            run("vector", e)

        @block.gpsimd
        def _(e):
            run("gpsimd", e)

        @block.tensor
        def _(e):
            run("tensor", e)


D = 1024
NCH = 8
QL, KVL, RD = 384, 256, 32
H = 8
QKD = 96
INC = 2208
C_CQ, C_CKV, C_KR, C_NQ, C_NK, C_NV = 0, 384, 640, 672, 1184, 1696
DFF = 2816
NFF = 22
EPS = 1e-6
NEG = -30000.0
GRID_W = 64
G_ATTN, G_Q, G_KV, G_MIX, G_FFN = 0, 8, 11, 13, 21


class Cfg:
    def __init__(self, NP, SP, SC, NQ):
        self.NP, self.SP, self.SC, self.NQ = NP, SP, SC, NQ
        self.NNA = NQ + 512
        self.QOFF = 256


def na_plan(kind, nrows_q):
    plan = []
    if kind == "P":
        R = nrows_q
        kh = min(8, R)

        def start(r):
            return min(max(r - kh // 2, 0), R - kh)
        for b in range(R // 8):
            items = []
            for kr in range(R):
                rows = [i for i in range(8 * b, 8 * b + 8) if start(i) <= kr <= start(i) + kh - 1]
                if not rows:
                    continue
                i0, i1 = rows[0], rows[-1]
                assert rows == list(range(i0, i1 + 1))
                items.append((kr, i0, i1, "T", 7 - (kr - i0)))
            plan.append(items)
    else:
        R = nrows_q
        for b in range(R // 8):
            items = []
            for ks in range(R + 7):
                rows = [i for i in range(8 * b, 8 * b + 8) if i <= ks <= i + 7]
                if not rows:
                    continue
                i0, i1 = rows[0], rows[-1]
                assert rows == list(range(i0, i1 + 1))
                if ks < 4:
                    hb0 = [0, 1, 3, 6][ks]
                    assert i0 == 0
                    items.append((ks, i0, i1, "H", hb0))
                elif ks >= R + 4:
                    j = ks - (R + 4)
                    hb0 = 10 + [0, 3, 5][j]
                    assert i0 == R - 3 + j and i1 == R - 1
                    items.append((ks, i0, i1, "H", hb0))
                else:
                    kr = ks - 4
                    items.append((ks, i0, i1, "T", 7 - (kr - i0)))
            plan.append(items)
    return plan


def build_program(cfg):
    import contextlib
    nc = bass.Bass("TRN2", target_bir_lowering=False)
    NP, SP, SC, NQ, NNA, QOFF = cfg.NP, cfg.SP, cfg.SC, cfg.NQ, cfg.NNA, cfg.QOFF
    PHASES = getattr(cfg, 'phases', (1, 2, 3))

    def din(name, shape, dt=F32):
        return nc.dram_tensor(name, list(shape), dt, kind="ExternalInput").ap()

    def dscr(name, shape, dt):
        return nc.dram_tensor(name, list(shape), dt, kind="Internal").ap()

    xp = din("xp", [NP * SP, D])
    xc = din("xc", [SC, D])
    xn = din("xn", [NNA, D])
    w_in = din("w_in", [D, INC])
    w_uq = din("w_uq", [QL, H * QKD])
    w_ukv = din("w_ukv", [KVL, H * 128])
    w_o = din("w_o", [D, D])
    w_gate = din("w_gate", [D, DFF])
    w_up = din("w_up", [D, DFF])
    w_down = din("w_down", [DFF, D])
    gcol_d = din("gcol", [128, 32])
    gfin_d = din("gfin", [1, D])
    ident_d = din("ident", [128, 128])
    ropeP_d = din("ropeP", [128, SP // 128, 32])
    ropeC_d = din("ropeC", [128, SC // 128, 32])
    ropeN_d = din("ropeN", [128, NNA // 128, 32])
    natab_d = din("natab", [128, 4, 15 * 64])
    nahalo_d = din("nahalo", [128, 4, 16 * 64])
    namask_d = din("namask", [128, 2, 64])
    yp = nc.dram_tensor("yp", [NP * SP, D], F32, kind="ExternalOutput").ap()
    yn = nc.dram_tensor("yn", [NQ, D], F32, kind="ExternalOutput").ap()

    units = []
    for i in range(NP):
        units.append(dict(kind="P", S=SP, NT=SP, NQ=SP, q0=0, name=f"p{i}",
                          xres=xp[i * SP:(i + 1) * SP, :], yout=yp[i * SP:(i + 1) * SP, :]))
    units.append(dict(kind="S", S=SC, NT=NNA, NQ=NQ, q0=QOFF, name="s",
                      xres=xn[QOFF:QOFF + NQ, :], yout=yn))
    for u in units:
        n = u["name"]
        u["kt"] = dscr("s_kt_" + n, [H, QKD, u["S"]], BF16)
        u["v"] = dscr("s_v_" + n, [H, 128, u["S"] // 128, 65], BF16)
        u["qt"] = dscr("s_qt_" + n, [H, QKD, u["NT"]], BF16)
        u["naq"] = dscr("s_naq_" + n, [128, 4, u["NT"]], BF16)
        u["nak"] = dscr("s_nak_" + n, [128, 4, u["NT"]], BF16)
        u["nav"] = dscr("s_nav_" + n, [64, u["NT"] // 128, 2, 8 * 65], BF16)
        u["ot"] = dscr("s_ot_" + n, [16, 65, u["NQ"]], F32)

    S = Sched()
    top = contextlib.ExitStack()
    with top:
        csem = {q: top.enter_context(nc.semaphore("c_" + q)) for q in QUEUES}
        dsem = {q: [top.enter_context(nc.semaphore(f"d_{q}{i}")) for i in range(n)]
                for q, n in NDSEM.items()}
        block = top.enter_context(nc.Block())

        pending = {q: set() for q in QUEUES}

        def OP(q, fn, reads=(), writes=(), dma=False):
            o = S.op(q, fn, reads, writes, dma)
            if pending[q]:
                extra = {d for d in pending[q] if d is not o}
                if q == "tensor" and not dma:
                    extra = {d for d in extra if not (d.q == "tensor" and not d.dma)}
                for d in extra:
                    d.signal = True
                o.deps |= extra
                pending[q] = set()
            return o

        def barrier():
            B = set()
            for q in QUEUES:
                comp = [o for o in S.ops[q] if not o.dma]
                if comp:
                    B.add(comp[-1])
                if S.dma_ops[q]:
                    B.update(S.dma_ops[q][-NDSEM[q]:])
            for q in QUEUES:
                pending[q] |= B

        def DMA(q, out, in_, reads=(), writes=()):
            return OP(q, lambda e, o=out, i=in_: e.dma_start(out=o, in_=i), reads, writes, dma=True)

        def ACT(out, in_, func, reads=(), writes=(), **kw):
            return OP("scalar", lambda e, o=out, i=in_, f=func, kw=kw: e.activation(out=o, in_=i, func=f, **kw),
                      reads, writes)

        def TS(q, out, in0, s1, s2, op0, op1, reads=(), writes=()):
            if s2 is None:
                return OP(q, lambda e, o=out, i=in0, a=s1, p0=op0:
                          e.tensor_scalar(out=o, in0=i, scalar1=a, scalar2=0.0, op0=p0, op1=ALU.add), reads, writes)
            return OP(q, lambda e, o=out, i=in0, a=s1, b=s2, p0=op0, p1=op1:
                      e.tensor_scalar(out=o, in0=i, scalar1=a, scalar2=b, op0=p0, op1=p1), reads, writes)

        def TT(q, out, in0, in1, op, reads=(), writes=()):
            return OP(q, lambda e, o=out, a=in0, b=in1, p=op: e.tensor_tensor(out=o, in0=a, in1=b, op=p),
                      reads, writes)

        def STT(q, out, in0, scalar, in1, op0, op1, reads=(), writes=()):
            return OP(q, lambda e, o=out, a=in0, s=scalar, b=in1, p0=op0, p1=op1:
                      e.scalar_tensor_tensor(out=o, in0=a, scalar=s, in1=b, op0=p0, op1=p1), reads, writes)

        def CP(q, out, in_, reads=(), writes=()):
            if q == "scalar":
                return ACT(out, in_, AF.Copy, reads, writes)
            return OP(q, lambda e, o=out, i=in_: e.tensor_copy(out=o, in_=i), reads, writes)

        def MM(out, lhsT, rhs, start, stop, reads=(), writes=(), **kw):
            return OP("tensor", lambda e, o=out, l=lhsT, r=rhs, s=start, t=stop, kw=kw:
                      e.matmul(out=o, lhsT=l, rhs=r, start=s, stop=t, **kw), reads, writes)

        def TR(out, in_, idn, reads=(), writes=()):
            return OP("tensor", lambda e, o=out, i=in_, d=idn: e.transpose(out=o, in_=i, identity=d),
                      reads, writes)

        def MEMSET(q, ap, val, writes=()):
            return OP(q, lambda e, a=ap, v=val: e.memset(a, v), (), writes)

        def sb(stack, name, shape, dt):
            return stack.enter_context(nc.sbuf_tensor("sb_" + name, list(shape), dt))

        def psum(stack, name, shape, dt=F32):
            return stack.enter_context(nc.psum_tensor("ps_" + name, list(shape), dt))

        identf = sb(top, "identf", [128, 128], F32)
        identb = sb(top, "identb", [128, 128], BF16)
        gcol = sb(top, "gcol", [128, 32], F32)
        epsc = sb(top, "epsc", [128, 1], F32)
        r_identf, r_identb, r_gcol, r_eps = Res(), Res(), Res(), Res()
        DMA("sync", identf[:], ident_d[:, :], (), [r_identf])
        DMA("sync", gcol[:], gcol_d[:, :], (), [r_gcol])
        CP("vector", identb[:], identf[:], [r_identf], [r_identb])
        MEMSET("vector", epsc[:], EPS, [r_eps])

        def rstd(st, c0, n, R):
            ACT(st[:, c0 + 1:c0 + 2], st[:, c0:c0 + 1], AF.Ln, [R, r_eps], [R], bias=epsc[:, 0:1], scale=1.0 / n)
            ACT(st[:, c0 + 2:c0 + 3], st[:, c0 + 1:c0 + 2], AF.Exp, [R], [R], scale=-0.5)

        def load_weight(wst, r_wst, WST, rot, dst, r_dst, src, nchunk, ncols, gbase, col_scales):
            for c in range(nchunk):
                for c0 in range(0, ncols, WST):
                    c1 = min(ncols, c0 + WST)
                    k = rot[0] % 2
                    rot[0] += 1
                    DMA("sync", wst[k][:, 0:c1 - c0], src[c * 128:(c + 1) * 128, c0:c1], (), [r_wst[k]])
                    for (a, b, const) in col_scales:
                        lo, hi = max(a, c0), min(b, c1)
                        if lo >= hi:
                            continue
                        eng = "gpsimd" if (rot[1] % 2) else "vector"
                        rot[1] += 1
                        if gbase is None:
                            CP(eng, dst[:, c, lo:hi], wst[k][:, lo - c0:hi - c0], [r_wst[k]], [r_dst[c]])
                        else:
                            TS(eng, dst[:, c, lo:hi], wst[k][:, lo - c0:hi - c0], gcol[:, gbase + c:gbase + c + 1],
                               const, ALU.mult, ALU.mult, [r_wst[k], r_gcol], [r_dst[c]])

        WST = 1408
        ph1 = contextlib.ExitStack()
        with ph1 if 1 in PHASES else contextlib.nullcontext():
          if 1 in PHASES:
              wst = [sb(ph1, f"wst{i}", [128, WST], F32) for i in range(2)]
              r_wst = [Res(), Res()]
              rot = [0, 0]
              win = sb(ph1, "win", [128, NCH, INC], BF16)
              wuq = sb(ph1, "wuq", [128, 3, H * QKD], BF16)
              wukv = sb(ph1, "wukv", [128, 2, H * 128], BF16)
              r_win = [Res() for _ in range(NCH)]
              r_wuq = [Res() for _ in range(3)]
              r_wukv = [Res() for _ in range(2)]
              load_weight(wst, r_wst, WST, rot, win, r_win, w_in, NCH, INC, G_ATTN,
                          [(0, C_NQ, 1.0), (C_NQ, C_NK, 0.125), (C_NK, INC, 1.0)])
              load_weight(wst, r_wst, WST, rot, wuq, r_wuq, w_uq, 3, H * QKD, G_Q, [(0, H * QKD, QKD ** -0.5)])
              load_weight(wst, r_wst, WST, rot, wukv, r_wukv, w_ukv, 2, H * 128, G_KV, [(0, H * 128, 1.0)])

              ropeP = sb(ph1, "ropeP", [128, SP // 128, 32], F32)
              ropeC = sb(ph1, "ropeC", [128, SC // 128, 32], F32)
              ropeN = sb(ph1, "ropeN", [128, NNA // 128, 32], F32)
              r_rope = Res()
              DMA("sync", ropeP[:], ropeP_d[:, :, :], (), [r_rope])
              DMA("sync", ropeC[:], ropeC_d[:, :, :], (), [r_rope])
              DMA("sync", ropeN[:], ropeN_d[:, :, :], (), [r_rope])

              NX = 3
              xt = [sb(ph1, f"xt{i}", [128, D], F32) for i in range(NX)]
              r_xt = [Res() for _ in range(NX)]
              junk = [sb(ph1, f"junk{i}", [128, D], BF16) for i in range(2)]
              r_junk = [Res(), Res()]
              jctr = [0]
              hb = [sb(ph1, f"hb{i}", [128, D], BF16) for i in range(2)]
              r_hb = [Res(), Res()]
              hT = [sb(ph1, f"hT{i}", [128, NCH, 512], BF16) for i in range(2)]
              r_hT = [[Res() for _ in range(4)] for _ in range(2)]
              NST = 4
              stt = [sb(ph1, f"stt{i}", [128, 16], F32) for i in range(NST)]
              r_stx = [Res() for _ in range(NST)]
              r_stq = [Res() for _ in range(NST)]
              r_stkv = [Res() for _ in range(NST)]
              cqn = [sb(ph1, f"cqn{i}", [128, QL], BF16) for i in range(2)]
              ckvn = [sb(ph1, f"ckvn{i}", [128, KVL], BF16) for i in range(2)]
              r_cqn = [Res(), Res()]
              r_ckvn = [Res(), Res()]
              cqnT = [sb(ph1, f"cqnT{i}", [128, 3, 128], BF16) for i in range(2)]
              ckvnT = [sb(ph1, f"ckvnT{i}", [128, 2, 128], BF16) for i in range(2)]
              r_cqnT = [Res(), Res()]
              r_ckvnT = [Res(), Res()]
              rtmp = [sb(ph1, f"rtmp{i}", [128, 96 + 512], F32) for i in range(2)]
              r_tc = [Res(), Res()]
              r_ts = [Res(), Res()]
              r_kro = [Res(), Res()]
              r_tcq = [[Res(), Res()], [Res(), Res()]]
              r_tsq = [[Res(), Res()], [Res(), Res()]]
              Ktok = [sb(ph1, f"Ktok{i}", [128, H, QKD], BF16) for i in range(2)]
              Qtok = [sb(ph1, f"Qtok{i}", [128, H, QKD], BF16) for i in range(2)]
              r_Ktok = [Res(), Res()]
              r_Qtok = [Res(), Res()]
              KTst = [sb(ph1, f"KTst{i}", [128, H, 512], BF16) for i in range(2)]
              QTst = [sb(ph1, f"QTst{i}", [128, H, 512], BF16) for i in range(2)]
              Vst = [sb(ph1, f"Vst{i}", [128, H, 4, 65], BF16) for i in range(2)]
              naqst = [sb(ph1, f"naqst{i}", [128, 4, 512], BF16) for i in range(2)]
              nakst = [sb(ph1, f"nakst{i}", [128, 4, 512], BF16) for i in range(2)]
              navst = [sb(ph1, f"navst{i}", [128, 4, H, 65], BF16) for i in range(2)]
              r_KTst = [[Res() for _ in range(4)] for _ in range(2)]
              r_QTst = [[Res() for _ in range(4)] for _ in range(2)]
              r_Vst = [[Res() for _ in range(4)] for _ in range(2)]
              r_naqst = [[Res() for _ in range(4)] for _ in range(2)]
              r_nakst = [[Res() for _ in range(4)] for _ in range(2)]
              r_navst = [[Res() for _ in range(4)] for _ in range(2)]
              for i in range(2):
                  for t in range(4):
                      MEMSET("gpsimd", Vst[i][:, :, t, 64:65], 1.0, [r_Vst[i][t]])
                      MEMSET("gpsimd", navst[i][:, t, :, 64:65], 1.0, [r_navst[i][t]])

              PB = [psum(ph1, f"pb{i}", [128, 512], F32) for i in range(4)]
              PKVU = psum(ph1, "pkvu", [128, 1024], F32)
              PQU = psum(ph1, "pqu", [128, 1024], F32)
              r_PB = [Res(psum=True) for _ in range(4)]
              r_PKVU, r_PQU = Res(psum=True), Res(psum=True)
              ptr_bf = PB[0][:].bitcast(BF16)
              pkt_bf = PB[1][:].bitcast(BF16)
              pqt_bf = PB[2][:].bitcast(BF16)

              tilectr = [0]
              grpctr = [0]

              def square_acc(in_ap, n, acc, reads, writes):
                  j = jctr[0] % 2
                  jctr[0] += 1
                  ACT(junk[j][:, 0:n], in_ap, AF.Square, reads, [r_junk[j]] + writes, accum_out=acc)

              def token_pass(u, src, ntile, rope_sb, do_q, do_kv, do_na):
                  dbg = getattr(cfg, 'dbg', (1, 1, 1))
                  do_q, do_kv, do_na = do_q and dbg[0], do_kv and dbg[1], do_na and dbg[2]
                  ngrp = ntile // 4
                  for g in range(ngrp):
                      gi = grpctr[0] % 2
                      grpctr[0] += 1
                      for t in range(4):
                          tg = g * 4 + t
                          tc_ = tilectr[0]
                          tilectr[0] += 1
                          xi = tc_ % NX
                          k2 = tc_ % 2
                          si = tc_ % NST
                          st = stt[si]
                          tsl = slice(t * 128, (t + 1) * 128)
                          DMA("sync", xt[xi][:], src[tg * 128:(tg + 1) * 128, :], (), [r_xt[xi]])
                          square_acc(xt[xi][:], D, st[:, 0:1], [r_xt[xi]], [r_stx[si]])
                          rstd(st, 0, D, r_stx[si])
                          TS("vector", hb[k2][:], xt[xi][:], st[:, 2:3], None, ALU.mult, None,
                             [r_xt[xi], r_stx[si]], [r_hb[k2]])
                          for c in range(NCH):
                              TR(ptr_bf[:, c * 128:(c + 1) * 128], hb[k2][:, c * 128:(c + 1) * 128], identb[:],
                                 [r_hb[k2], r_identb], [r_PB[0]])
                          CP("scalar", hT[gi][:, :, tsl],
                             ptr_bf[:, 0:1024].rearrange("p (c n) -> p c n", c=NCH), [r_PB[0]], [r_hT[gi][t]])
                          if do_q:
                              for c in range(NCH):
                                  MM(PB[1][:, 0:QL], hT[gi][:, c, tsl], win[:, c, C_CQ:C_CQ + QL],
                                     c == 0, c == NCH - 1, [r_hT[gi][t], r_win[c]], [r_PB[1]])
                          if do_kv:
                              for c in range(NCH):
                                  MM(PB[2][:, 0:KVL + RD], hT[gi][:, c, tsl], win[:, c, C_CKV:C_CKV + KVL + RD],
                                     c == 0, c == NCH - 1, [r_hT[gi][t], r_win[c]], [r_PB[2]])
                          if do_na:
                              for c in range(NCH):
                                  MM(PB[3][:, :], hT[gi][:, c, tsl], win[:, c, C_NV:C_NV + 512],
                                     c == 0, c == NCH - 1, [r_hT[gi][t], r_win[c]], [r_PB[3]])
                              CP("vector", navst[gi][:, t, :, 0:64], PB[3][:, :].rearrange("p (h d) -> p h d", h=H),
                                 [r_PB[3]], [r_navst[gi][t]])
                          if do_q:
                              square_acc(PB[1][:, 0:QL], QL, st[:, 3:4], [r_PB[1]], [r_stq[si]])
                              rstd(st, 3, QL, r_stq[si])
                              TS("vector", cqn[k2][:], PB[1][:, 0:QL], st[:, 5:6], None, ALU.mult, None,
                                 [r_PB[1], r_stq[si]], [r_cqn[k2]])
                          if do_kv:
                              square_acc(PB[2][:, 0:KVL], KVL, st[:, 6:7], [r_PB[2]], [r_stkv[si]])
                              rstd(st, 6, KVL, r_stkv[si])
                              TS("vector", ckvn[k2][:], PB[2][:, 0:KVL], st[:, 8:9], None, ALU.mult, None,
                                 [r_PB[2], r_stkv[si]], [r_ckvn[k2]])
                              rt = rtmp[k2]
                              xk = PB[2][:, KVL:KVL + RD].rearrange("p (a j) -> p a j", a=2)
                              cosb = rope_sb[:, tg, 0:16].unsqueeze(1).broadcast_to([128, 2, 16])
                              sinb = rope_sb[:, tg, 16:32].unsqueeze(1).broadcast_to([128, 2, 16])
                              tcv = rt[:, 0:32].rearrange("p (a j) -> p a j", a=2)
                              tsv = rt[:, 32:64].rearrange("p (a j) -> p a j", a=2)
                              TT("vector", tcv, xk, cosb, ALU.mult, [r_PB[2], r_rope, r_stkv[si]], [r_tc[k2]])
                              TT("vector", tsv, xk, sinb, ALU.mult, [r_PB[2], r_rope, r_stkv[si]], [r_ts[k2]])
                              kro = rt[:, 64:96]
                              TT("vector", kro[:, 0:16], rt[:, 0:16], rt[:, 48:64], ALU.subtract,
                                 [r_tc[k2], r_ts[k2]], [r_kro[k2]])
                              TT("vector", kro[:, 16:32], rt[:, 32:48], rt[:, 16:32], ALU.add,
                                 [r_tc[k2], r_ts[k2]], [r_kro[k2]])
                          if do_q:
                              for c in range(3):
                                  TR(ptr_bf[:, c * 128:(c + 1) * 128], cqn[k2][:, c * 128:(c + 1) * 128], identb[:],
                                     [r_cqn[k2], r_identb], [r_PB[0]])
                              CP("scalar", cqnT[k2][:], ptr_bf[:, 0:384].rearrange("p (c n) -> p c n", c=3),
                                 [r_PB[0]], [r_cqnT[k2]])
                          if do_kv:
                              for c in range(2):
                                  TR(ptr_bf[:, 384 + c * 128:384 + (c + 1) * 128], ckvn[k2][:, c * 128:(c + 1) * 128],
                                     identb[:], [r_ckvn[k2], r_identb], [r_PB[0]])
                              CP("scalar", ckvnT[k2][:], ptr_bf[:, 384:640].rearrange("p (c n) -> p c n", c=2),
                                 [r_PB[0]], [r_ckvnT[k2]])
                          if do_kv:
                              for hh in range(2):
                                  for c in range(2):
                                      MM(PKVU[:, hh * 512:(hh + 1) * 512], ckvnT[k2][:, c, :],
                                         wukv[:, c, hh * 512:(hh + 1) * 512], c == 0, c == 1,
                                         [r_ckvnT[k2], r_wukv[c]], [r_PKVU])
                              kvv = PKVU[:, :].rearrange("p (h d) -> p h d", h=H)
                              CP("vector", Ktok[k2][:, :, 0:64], kvv[:, :, 0:64], [r_PKVU], [r_Ktok[k2]])
                              CP("vector", Ktok[k2][:, :, 64:96], kro.unsqueeze(1).broadcast_to([128, H, RD]),
                                 [r_kro[k2]], [r_Ktok[k2]])
                              CP("scalar", Vst[gi][:, :, t, 0:64], kvv[:, :, 64:128], [r_PKVU], [r_Vst[gi][t]])
                              for h in range(H):
                                  TR(pkt_bf[0:QKD, h * 128:(h + 1) * 128], Ktok[k2][:, h, :], identb[:],
                                     [r_Ktok[k2], r_identb], [r_PB[1]])
                              CP("scalar", KTst[gi][0:QKD, :, tsl],
                                 pkt_bf[0:QKD, :].rearrange("p (h n) -> p h n", h=H), [r_PB[1]], [r_KTst[gi][t]])
                          if do_q:
                              for hh in range(2):
                                  for c in range(3):
                                      MM(PQU[:, hh * 512:hh * 512 + 384], cqnT[k2][:, c, :],
                                         wuq[:, c, hh * 384:(hh + 1) * 384], c == 0, c == 2,
                                         [r_cqnT[k2], r_wuq[c]], [r_PQU])
                              rt = rtmp[k2]
                              for hh in range(2):
                                  qv = PQU[:, hh * 512:hh * 512 + 384].rearrange("p (h d) -> p h d", h=4)
                                  CP("vector", Qtok[k2][:, hh * 4:(hh + 1) * 4, 0:64], qv[:, :, 0:64],
                                     [r_PQU], [r_Qtok[k2]])
                                  xq = qv[:, :, 64:96].rearrange("p h (a j) -> p h a j", a=2)
                                  cosb = rope_sb[:, tg, 0:16].unsqueeze(1).unsqueeze(1).broadcast_to([128, 4, 2, 16])
                                  sinb = rope_sb[:, tg, 16:32].unsqueeze(1).unsqueeze(1).broadcast_to([128, 4, 2, 16])
                                  o0 = 96 + hh * 256
                                  tcq = rt[:, o0:o0 + 128].rearrange("p (h a j) -> p h a j", h=4, a=2)
                                  tsq = rt[:, o0 + 128:o0 + 256].rearrange("p (h a j) -> p h a j", h=4, a=2)
                                  TT("vector", tcq, xq, cosb, ALU.mult, [r_PQU, r_rope], [r_tcq[k2][hh]])
                                  TT("vector", tsq, xq, sinb, ALU.mult, [r_PQU, r_rope], [r_tsq[k2][hh]])
                                  TT("gpsimd", Qtok[k2][:, hh * 4:(hh + 1) * 4, 64:80], tcq[:, :, 0, :], tsq[:, :, 1, :],
                                     ALU.subtract, [r_tcq[k2][hh], r_tsq[k2][hh]], [r_Qtok[k2]])
                                  TT("gpsimd", Qtok[k2][:, hh * 4:(hh + 1) * 4, 80:96], tsq[:, :, 0, :], tcq[:, :, 1, :],
                                     ALU.add, [r_tcq[k2][hh], r_tsq[k2][hh]], [r_Qtok[k2]])
                              for h in range(H):
                                  TR(pqt_bf[0:QKD, h * 128:(h + 1) * 128], Qtok[k2][:, h, :], identb[:],
                                     [r_Qtok[k2], r_identb], [r_PB[2]])
                              CP("vector", QTst[gi][0:QKD, :, tsl],
                                 pqt_bf[0:QKD, :].rearrange("p (h n) -> p h n", h=H), [r_PB[2]], [r_QTst[gi][t]])
                      if do_na:
                          for blk in range(8):
                              col0 = C_NQ + blk * 128
                              for c in range(NCH):
                                  MM(PB[3][:, :], win[:, c, col0:col0 + 128], hT[gi][:, c, :], c == 0, c == NCH - 1,
                                     r_hT[gi] + [r_win[c]], [r_PB[3]])
                              dst = naqst[gi] if blk < 4 else nakst[gi]
                              rr = r_naqst[gi] if blk < 4 else r_nakst[gi]
                              CP("scalar" if blk % 2 else "vector", dst[:, blk % 4, :], PB[3][:, :], [r_PB[3]],
                                 [rr[blk % 4]])
                          DMA("gpsimd", u["naq"][:, :, g * 512:(g + 1) * 512], naqst[gi][:], r_naqst[gi], ())
                          DMA("gpsimd", u["nak"][:, :, g * 512:(g + 1) * 512], nakst[gi][:], r_nakst[gi], ())
                          for half in range(2):
                              DMA("gpsimd", u["nav"][:, g * 4:(g + 1) * 4, half, :],
                                  navst[gi][half * 64:(half + 1) * 64, :, :, :].rearrange("p t h d -> p t (h d)"),
                                  r_navst[gi], ())
                      if do_kv:
                          DMA("gpsimd", u["kt"][:, :, g * 512:(g + 1) * 512].rearrange("h d n -> d h n"),
                              KTst[gi][0:QKD, :, :], r_KTst[gi], ())
                          DMA("gpsimd", u["v"][:, :, g * 4:(g + 1) * 4, :].rearrange("h p t d -> p h t d"),
                              Vst[gi][:], r_Vst[gi], ())
                      if do_q:
                          DMA("gpsimd", u["qt"][:, :, g * 512:(g + 1) * 512].rearrange("h d n -> d h n"),
                              QTst[gi][0:QKD, :, :], r_QTst[gi], ())

              for ui, u in enumerate(units):
                  if u["kind"] == "P":
                      token_pass(u, xp[ui * SP:(ui + 1) * SP, :], SP // 128, ropeP, True, True, True)
                  else:
                      token_pass(u, xn, NNA // 128, ropeN, True, False, True)
                      token_pass(u, xc, SC // 128, ropeC, False, True, False)
        barrier()

        ph2 = contextlib.ExitStack()
        with ph2 if 2 in PHASES else contextlib.nullcontext():
          if 2 in PHASES:
              SMAX = max(u["S"] for u in units)
              NQMAX = max(u["NQ"] for u in units)
              NTMAX = max(u["NT"] for u in units)
              KTb = [sb(ph2, f"KTb{i}", [128, SMAX], BF16) for i in range(2)]
              Vb = [sb(ph2, f"Vb{i}", [128, SMAX // 128, 65], BF16) for i in range(2)]
              QTb = [sb(ph2, f"QTb{i}", [128, NQMAX], BF16) for i in range(2)]
              r_KTb, r_Vb, r_QTb = [Res(), Res()], [Res(), Res()], [Res(), Res()]
              PT = [sb(ph2, f"PT{i}", [128, 1024], BF16) for i in range(3)]
              r_PT = [Res() for _ in range(3)]
              oTs = [sb(ph2, f"oTs{i}", [128, 512], F32) for i in range(2)]
              r_oTs = [Res(), Res()]
              naq = sb(ph2, "naq", [128, 4, NTMAX], BF16)
              nak = sb(ph2, "nak", [128, 4, NTMAX], BF16)
              nav = sb(ph2, "nav", [64, NTMAX // 64, 8 * 65], BF16)
              r_naq, r_nak, r_nav = Res(), Res(), Res()
              tabf = sb(ph2, "tabf", [128, 4, 15 * 64], F32)
              half_ = sb(ph2, "halof", [128, 4, 16 * 64], F32)
              maskf = sb(ph2, "maskf", [128, 2, 64], F32)
              tabb = sb(ph2, "tabb", [128, 4, 15 * 64], BF16)
              halb = sb(ph2, "halob", [128, 4, 16 * 64], BF16)
              r_tabf, r_half, r_maskf, r_tabb, r_halb = Res(), Res(), Res(), Res(), Res()
              PTn = [sb(ph2, f"PTn{i}", [64, 512], BF16) for i in range(3)]
              r_PTn = [Res() for _ in range(3)]

              PS = [psum(ph2, f"ps{i}", [128, 1024], F32) for i in range(2)]
              PO = [psum(ph2, f"po{i}", [128, 512], F32) for i in range(2)]
              PSn = [psum(ph2, f"psn{i}", [128, 512], F32) for i in range(1)]
              PA = [psum(ph2, f"pa{i}", [128, 512], F32) for i in range(1)]
              r_PS, r_PO = [Res(psum=True), Res(psum=True)], [Res(psum=True), Res(psum=True)]
              r_PSn, r_PA = [Res(psum=True)], [Res(psum=True)]

              DMA("sync", tabf[:], natab_d[:, :, :], (), [r_tabf])
              DMA("sync", half_[:], nahalo_d[:, :, :], (), [r_half])
              DMA("sync", maskf[:], namask_d[:, :, :], (), [r_maskf])
              for (srcf, dstb, nb, rs, rd) in ((tabf, tabb, 15, r_tabf, r_tabb), (half_, halb, 16, r_half, r_halb)):
                  v = srcf[:].rearrange("p a (b q) -> p (a b) q", q=64)
                  vb = dstb[:].rearrange("p a (b q) -> p (a b) q", q=64)
                  m01 = maskf[:, 0, :].unsqueeze(1).broadcast_to([128, 4 * nb, 64])
                  mng = maskf[:, 1, :].unsqueeze(1).broadcast_to([128, 4 * nb, 64])
                  TT("vector", v, v, m01, ALU.mult, [rs, r_maskf], [rs])
                  TT("vector", vb, v, mng, ALU.add, [rs, r_maskf], [rd])

              ctr = dict(hb=0, pt=0, ps=0, po=0, ot=0, ptn=0)

              def mla_attention(u):
                  Sx, NQu, q0 = u["S"], u["NQ"], u["q0"]
                  nkt = Sx // 128
                  for h in range(H):
                      bi = ctr["hb"] % 2
                      ctr["hb"] += 1
                      DMA("sync", KTb[bi][0:QKD, 0:Sx], u["kt"][h, :, :], (), [r_KTb[bi]])
                      DMA("sync", Vb[bi][:, 0:nkt, :], u["v"][h, :, :, :], (), [r_Vb[bi]])
                      DMA("sync", QTb[bi][0:QKD, 0:NQu], u["qt"][h, :, q0:q0 + NQu], (), [r_QTb[bi]])
                      for qb in range(NQu // 512):
                          oi = ctr["po"] % 2
                          ctr["po"] += 1
                          for k2 in range(nkt // 2):
                              si = ctr["ps"] % 2
                              ctr["ps"] += 1
                              pi = ctr["pt"] % 3
                              ctr["pt"] += 1
                              for j in range(2):
                                  kt = 2 * k2 + j
                                  MM(PS[si][:, j * 512:(j + 1) * 512], KTb[bi][0:QKD, kt * 128:(kt + 1) * 128],
                                     QTb[bi][0:QKD, qb * 512:(qb + 1) * 512], True, True,
                                     [r_KTb[bi], r_QTb[bi]], [r_PS[si]])
                              ACT(PT[pi][:, :], PS[si][:, :], AF.Exp, [r_PS[si]], [r_PT[pi]])
                              for j in range(2):
                                  kt = 2 * k2 + j
                                  MM(PO[oi][0:65, :], Vb[bi][:, kt, :], PT[pi][:, j * 512:(j + 1) * 512],
                                     kt == 0, kt == nkt - 1, [r_Vb[bi], r_PT[pi]], [r_PO[oi]])
                          ti = ctr["ot"] % 2
                          ctr["ot"] += 1
                          CP("vector", oTs[ti][0:65, :], PO[oi][0:65, :], [r_PO[oi]], [r_oTs[ti]])
                          DMA("gpsimd", u["ot"][h, :, qb * 512:(qb + 1) * 512], oTs[ti][0:65, :], [r_oTs[ti]], ())

              def na_attention(u):
                  NT, NQu = u["NT"], u["NQ"]
                  nrq = NQu // 64
                  plan = na_plan(u["kind"], nrq)
                  qrow0 = 0 if u["kind"] == "P" else 4
                  DMA("sync", naq[:, :, 0:NT], u["naq"][:, :, :], (), [r_naq])
                  DMA("sync", nak[:, :, 0:NT], u["nak"][:, :, :], (), [r_nak])
                  DMA("sync", nav[:, 0:NT // 64, :], u["nav"][:, :, :, :].rearrange("p t a d -> p (t a) d"), (), [r_nav])
                  for h in range(H):
                      pr, pb = h // 2, (h % 2) * 64
                      for b, items in enumerate(plan):
                          first = True
                          for (ks, i0, i1, src, blk0) in items:
                              N = 64 * (i1 - i0 + 1)
                              qtok0 = (qrow0 + i0) * 64
                              c0 = (i0 - 8 * b) * 64
                              pi = ctr["ptn"] % 3
                              ctr["ptn"] += 1
                              MM(PSn[0][0:64, 0:N], nak[pb:pb + 64, pr, ks * 64:(ks + 1) * 64],
                                 naq[pb:pb + 64, pr, qtok0:qtok0 + N], True, False, [r_nak, r_naq], [r_PSn[0]])
                              tb = tabb if src == "T" else halb
                              rtb = r_tabb if src == "T" else r_halb
                              MM(PSn[0][0:64, 0:N], identb[pb:pb + 64, pb:pb + 64],
                                 tb[pb:pb + 64, pr, blk0 * 64:blk0 * 64 + N], False, True, [rtb, r_identb], [r_PSn[0]])
                              ACT(PTn[pi][:, 0:N], PSn[0][0:64, 0:N], AF.Exp, [r_PSn[0]], [r_PTn[pi]])
                              last = (ks, i0, i1, src, blk0) == items[-1]
                              MM(PA[0][0:65, c0:c0 + N], nav[:, ks, h * 65:(h + 1) * 65], PTn[pi][:, 0:N],
                                 first, last, [r_nav, r_PTn[pi]], [r_PA[0]], skip_group_check=True)
                              first = False
                          ti = ctr["ot"] % 2
                          ctr["ot"] += 1
                          CP("vector", oTs[ti][0:65, :], PA[0][0:65, :], [r_PA[0]], [r_oTs[ti]])
                          DMA("gpsimd", u["ot"][8 + h, :, b * 512:(b + 1) * 512], oTs[ti][0:65, :], [r_oTs[ti]], ())

              for u in units:
                  na_attention(u)
                  mla_attention(u)
        barrier()

        ph3 = contextlib.ExitStack()
        with ph3 if 3 in PHASES else contextlib.nullcontext():
          if 3 in PHASES:
              WST = 704
              wst = [sb(ph3, f"wst3_{i}", [128, WST], F32) for i in range(2)]
              r_wst = [Res(), Res()]
              rot = [0, 0]
              wo = sb(ph3, "wo", [128, NCH, D], BF16)
              wg = sb(ph3, "wg", [128, NCH, DFF], BF16)
              wu = sb(ph3, "wu", [128, NCH, DFF], BF16)
              wd = sb(ph3, "wd", [128, NFF, D], BF16)
              r_wo = [Res() for _ in range(NCH)]
              r_wg = [Res() for _ in range(NCH)]
              r_wu = [Res() for _ in range(NCH)]
              r_wd = [Res() for _ in range(NFF)]
              load_weight(wst, r_wst, WST, rot, wo, r_wo, w_o, NCH, D, G_MIX, [(0, D, 1.0)])
              load_weight(wst, r_wst, WST, rot, wg, r_wg, w_gate, NCH, DFF, G_FFN, [(0, DFF, 1.0)])
              load_weight(wst, r_wst, WST, rot, wu, r_wu, w_up, NCH, DFF, G_FFN, [(0, DFF, 1.0)])
              load_weight(wst, r_wst, WST, rot, wd, r_wd, w_down, NFF, D, None, [(0, D, 1.0)])
              gfin = sb(ph3, "gfin", [128, D], F32)
              r_gfin = Res()
              DMA("sync", gfin[:], gfin_d[0:1, :].partition_broadcast(128), (), [r_gfin])

              TG = 2
              NTOK = TG * 128
              oTin = sb(ph3, "oTin", [128, 8, 128], F32)
              r_oTin = Res()
              atok = sb(ph3, "atok", [128, D], F32)
              r_atok = Res()
              x1 = [sb(ph3, f"x1_{i}", [128, D], F32) for i in range(TG)]
              r_x1 = [Res() for _ in range(TG)]
              mixb = sb(ph3, "mixb", [128, D], BF16)
              r_mixb = Res()
              mixT = sb(ph3, "mixT", [128, NCH, 128], BF16)
              r_mixT = Res()
              h2b = sb(ph3, "h2b", [128, D], BF16)
              r_h2b = Res()
              h2T = sb(ph3, "h2T", [128, NCH, NTOK], BF16)
              r_h2T = [Res() for _ in range(TG)]
              actT = sb(ph3, "actT", [128, NFF, NTOK], BF16)
              r_actT = [Res() for _ in range(NFF)]
              sg = [sb(ph3, f"sg{i}", [128, NTOK], F32) for i in range(2)]
              r_sg = [Res(), Res()]
              junk3 = [sb(ph3, "junk3_0", [128, D], BF16)] * 2
              r_junk3 = [Res()] * 2
              st3 = [sb(ph3, f"st3_{i}", [128, 32], F32) for i in range(2)]
              r_fin = [Res(), Res()]
              r_st3 = [[Res() for _ in range(6)] for _ in range(2)]
              j3 = [0]

              def sq3(in_ap, n, acc, reads, writes):
                  j = j3[0] % 2
                  j3[0] += 1
                  ACT(junk3[j][:, 0:n], in_ap, AF.Square, reads, [r_junk3[j]] + writes, accum_out=acc)

              POh = psum(ph3, "poh", [128, 512], F32)
              r_POh = Res(psum=True)
              PTR = psum(ph3, "ptr3", [128, 512], F32)
              r_PTR = Res(psum=True)
              ptr3_bf = PTR[:].bitcast(BF16)
              PY = psum(ph3, "py", [128, 1024], F32)
              r_PY = Res(psum=True)
              PG = [psum(ph3, f"pg{i}", [128, 512], F32) for i in range(2)]
              PU = [psum(ph3, f"pu{i}", [128, 512], F32) for i in range(2)]
              r_PG, r_PU = [Res(psum=True), Res(psum=True)], [Res(psum=True), Res(psum=True)]
              tctr = [0]

              for u in units:
                  NQu = u["NQ"]
                  for g in range(NQu // NTOK):
                      for t in range(TG):
                          tok0 = g * NTOK + t * 128
                          si = tctr[0] % 2
                          tctr[0] += 1
                          st = st3[si]
                          rs_ = r_st3[si]
                          DMA("sync", x1[t][:], u["xres"][tok0:tok0 + 128, :], (), [r_x1[t]])
                          for half in range(2):
                              DMA("sync", oTin[0:65, :, :],
                                  u["ot"][half * 8:(half + 1) * 8, :, tok0:tok0 + 128].rearrange("h d n -> d h n"),
                                  (), [r_oTin])
                              for qd in range(2):
                                  for hh in range(4):
                                      TR(POh[:, hh * 128:hh * 128 + 65], oTin[0:65, qd * 4 + hh, :], identf[0:65, 0:65],
                                         [r_oTin, r_identf], [r_POh])
                                  pov = POh[:, :].rearrange("p (h d) -> p h d", h=4)
                                  rc = st[:, 8:12].unsqueeze(2)
                                  OP("vector", lambda e, o=rc, i=pov[:, :, 64:65]: e.reciprocal(out=o, in_=i),
                                     [r_POh], [rs_[4]])
                                  a0 = half * 512 + qd * 256
                                  TT("vector", atok[:, a0:a0 + 256].rearrange("p (h d) -> p h d", h=4),
                                     pov[:, :, 0:64], rc.broadcast_to([128, 4, 64]), ALU.mult,
                                     [r_POh, rs_[4]], [r_atok])
                          for half in range(2):
                              sq3(atok[:, half * 512:(half + 1) * 512], 512, st[:, 3 * half:3 * half + 1],
                                  [r_atok], [rs_[half]])
                              rstd(st, 3 * half, 512, rs_[half])
                              TS("vector", mixb[:, half * 512:(half + 1) * 512], atok[:, half * 512:(half + 1) * 512],
                                 st[:, 3 * half + 2:3 * half + 3], None, ALU.mult, None,
                                 [r_atok, rs_[half]], [r_mixb])
                          for c in range(NCH):
                              TR(ptr3_bf[:, c * 128:(c + 1) * 128], mixb[:, c * 128:(c + 1) * 128], identb[:],
                                 [r_mixb, r_identb], [r_PTR])
                          CP("scalar", mixT[:], ptr3_bf[:, 0:1024].rearrange("p (c n) -> p c n", c=NCH),
                             [r_PTR], [r_mixT])
                          for hh in range(2):
                              for c in range(NCH):
                                  MM(PY[:, hh * 512:(hh + 1) * 512], mixT[:, c, :], wo[:, c, hh * 512:(hh + 1) * 512],
                                     c == 0, c == NCH - 1, [r_mixT, r_wo[c]], [r_PY])
                          TT("vector", x1[t][:], x1[t][:], PY[:, :], ALU.add, [r_x1[t], r_PY], [r_x1[t]])
                          sq3(x1[t][:], D, st[:, 16:17], [r_x1[t]], [rs_[2]])
                          rstd(st, 16, D, rs_[2])
                          TS("vector", h2b[:], x1[t][:], st[:, 18:19], None, ALU.mult, None,
                             [r_x1[t], rs_[2]], [r_h2b])
                          for c in range(NCH):
                              TR(ptr3_bf[:, c * 128:(c + 1) * 128], h2b[:, c * 128:(c + 1) * 128], identb[:],
                                 [r_h2b, r_identb], [r_PTR])
                          CP("scalar", h2T[:, :, t * 128:(t + 1) * 128],
                             ptr3_bf[:, 0:1024].rearrange("p (c n) -> p c n", c=NCH), [r_PTR], [r_h2T[t]])
                      for f in range(NFF):
                          pi = f % 2
                          for c in range(NCH):
                              MM(PG[pi][:, 0:NTOK], wg[:, c, f * 128:(f + 1) * 128], h2T[:, c, :], c == 0, c == NCH - 1,
                                 r_h2T + [r_wg[c]], [r_PG[pi]])
                          for c in range(NCH):
                              MM(PU[pi][:, 0:NTOK], wu[:, c, f * 128:(f + 1) * 128], h2T[:, c, :], c == 0, c == NCH - 1,
                                 r_h2T + [r_wu[c]], [r_PU[pi]])
                          ACT(sg[pi][:], PG[pi][:, 0:NTOK], AF.Silu, [r_PG[pi]], [r_sg[pi]])
                          TT("vector", actT[:, f, :], sg[pi][:], PU[pi][:, 0:NTOK], ALU.mult,
                             [r_sg[pi], r_PU[pi]], [r_actT[f]])
                      for t in range(TG):
                          tok0 = g * NTOK + t * 128
                          for hh in range(2):
                              for f in range(NFF):
                                  MM(PY[:, hh * 512:(hh + 1) * 512], actT[:, f, t * 128:(t + 1) * 128],
                                     wd[:, f, hh * 512:(hh + 1) * 512], f == 0, f == NFF - 1,
                                     [r_actT[f], r_wd[f]], [r_PY])
                          TT("vector", x1[t][:], x1[t][:], PY[:, :], ALU.add, [r_x1[t], r_PY], [r_x1[t]])
                          st = st3[t % 2]
                          rr = r_fin[t % 2]
                          sq3(x1[t][:], D, st[:, 24:25], [r_x1[t]], [rr])
                          rstd(st, 24, D, rr)
                          STT("vector", atok[:], x1[t][:], st[:, 26:27], gfin[:], ALU.mult, ALU.mult,
                              [r_x1[t], rr, r_gfin], [r_atok])
                          DMA("gpsimd", u["yout"][tok0:tok0 + 128, :], atok[:], [r_atok], ())
        S.emit(nc, block, csem, dsem)
    return nc


def _rope_table(pos):
    inv = (10000.0 ** (-np.arange(0, RD, 2, dtype=np.float32) / np.float32(RD))).astype(np.float32)
    ang = pos.astype(np.float32)[:, None] * inv[None, :]
    tab = np.concatenate([np.cos(ang), np.sin(ang)], axis=1).astype(np.float32)
    nt = pos.shape[0] // 128
    return np.ascontiguousarray(tab.reshape(nt, 128, 32).transpose(1, 0, 2))


def _col(g):
    g = np.asarray(g, np.float32).reshape(-1, 128)
    return g.T


def make_in_maps(inputs, cfg):
    NP, SP, SC, NQ, NNA = cfg.NP, cfg.SP, cfg.SC, cfg.NQ, cfg.NNA
    xpr = np.asarray(inputs["x_prompt"], np.float32)
    xsm = np.asarray(inputs["x_sample"], np.float32)
    nquart = SC // NQ
    ncores = xsm.shape[0] * nquart
    assert xpr.shape[0] == NP * ncores
    Rq = NQ // 64
    Rtot = SC // 64
    f = lambda k: np.ascontiguousarray(np.asarray(inputs[k], np.float32)[0])
    gcol = np.zeros((128, 32), np.float32)
    gcol[:, G_ATTN:G_ATTN + 8] = _col(inputs["attn_norm_g"][0])
    gcol[:, G_Q:G_Q + 3] = _col(inputs["q_norm_g"][0])
    gcol[:, G_KV:G_KV + 2] = _col(inputs["kv_norm_g"][0])
    gcol[:, G_MIX:G_MIX + 4] = _col(inputs["mla_out_g"][0])
    gcol[:, G_MIX + 4:G_MIX + 8] = _col(inputs["na_out_g"][0])
    gcol[:, G_FFN:G_FFN + 8] = _col(inputs["ffn_norm_g"][0])
    rpb = np.asarray(inputs["na_rpb"], np.float32)[0]
    kc = np.arange(64)[:, None]
    qc = np.arange(64)[None, :]
    dj = np.clip(kc - qc, -15, 15) + 15
    cs = np.clip(qc - 8, 0, 48)
    m01 = ((kc >= cs) & (kc < cs + 16)).astype(np.float32)
    namask = np.zeros((128, 2, 64), np.float32)
    namask[:, 0, :] = np.tile(m01, (2, 1))
    namask[:, 1, :] = np.tile((1.0 - m01) * NEG, (2, 1))

    def blockT(h, di):
        return rpb[h, di + 7][dj]

    natab = np.zeros((128, 4, 15, 64), np.float32)
    for h in range(H):
        pb, pr = (h % 2) * 64, h // 2
        for j in range(15):
            natab[pb:pb + 64, pr, j, :] = blockT(h, 7 - j)
    natab = natab.reshape(128, 4, 15 * 64)
    shared = dict(w_in=f("w_in"), w_uq=f("w_uq"), w_ukv=f("w_ukv"), w_o=f("w_o"), w_gate=f("w_gate"),
                  w_up=f("w_up"), w_down=f("w_down"), gcol=gcol,
                  gfin=np.asarray(inputs["final_norm_g"], np.float32).reshape(1, D),
                  ident=np.eye(128, dtype=np.float32), natab=natab, namask=namask,
                  ropeP=_rope_table(np.arange(SP)), ropeC=_rope_table(np.arange(SC)))
    maps = []
    for c in range(ncores):
        sq, qt = c // nquart, c % nquart
        m = dict(shared)
        m["xp"] = np.ascontiguousarray(xpr[c * NP:(c + 1) * NP].reshape(NP * SP, D))
        seq = xsm[sq]
        m["xc"] = np.ascontiguousarray(seq)
        r0 = qt * Rq
        rows_b = [4, 5, 6, 7] if qt == 0 else [r0 - 4, r0 - 3, r0 - 2, r0 - 1]
        last = (qt == nquart - 1)
        rows_a = [Rtot - 8, Rtot - 7, Rtot - 6] if last else [r0 + Rq, r0 + Rq + 1, r0 + Rq + 2]
        rows = rows_b + list(range(r0, r0 + Rq)) + rows_a
        xn = np.zeros((NNA, D), np.float32)
        posn = np.zeros((NNA,), np.float32)
        for i, r in enumerate(rows):
            xn[i * 64:(i + 1) * 64] = seq[r * 64:(r + 1) * 64]
            posn[i * 64:(i + 1) * 64] = np.arange(r * 64, (r + 1) * 64)
        m["xn"] = xn
        m["ropeN"] = _rope_table(posn)
        hal = np.zeros((128, 4, 16, 64), np.float32)
        for h in range(H):
            pb, pr = (h % 2) * 64, h // 2
            for ks in range(4):
                for i in range(ks + 1):
                    di = (rows_b[ks] - r0) - i
                    hal[pb:pb + 64, pr, [0, 1, 3, 6][ks] + i, :] = blockT(h, di)
            for j in range(3):
                for n_, i in enumerate(range(Rq - 3 + j, Rq)):
                    di = (rows_a[j] - r0) - i
                    hal[pb:pb + 64, pr, 10 + [0, 3, 5][j] + n_, :] = blockT(h, di)
        m["nahalo"] = hal.reshape(128, 4, 16 * 64)
        maps.append(m)
    return maps


def assemble(results, cfg, nsample):
    NP, SP, SC, NQ = cfg.NP, cfg.SP, cfg.SC, cfg.NQ
    nquart = SC // NQ
    ncores = nsample * nquart
    yp = np.concatenate([np.asarray(results[c]["yp"], np.float32).reshape(NP, SP, D) for c in range(ncores)], axis=0)
    ys = np.zeros((nsample, SC, D), np.float32)
    for c in range(ncores):
        sq, qt = c // nquart, c % nquart
        ys[sq, qt * NQ:(qt + 1) * NQ] = np.asarray(results[c]["yn"], np.float32)
    return yp, ys


_NC_CACHE = {}


def kernel(**inputs):
    cfg = Cfg(NP=2, SP=2048, SC=8192, NQ=2048)
    maps = make_in_maps(inputs, cfg)
    key = (cfg.NP, cfg.SP, cfg.SC, cfg.NQ)
    if key not in _NC_CACHE:
        _NC_CACHE[key] = build_program(cfg)
    nc = _NC_CACHE[key]
    res = run_bass_kernel_spmd(nc, maps, core_ids=list(range(len(maps))))
    return assemble(res.results, cfg, np.asarray(inputs["x_sample"]).shape[0])
```

```python
import numpy as np
import concourse.bass as bass
import concourse.mybir as mybir
from concourse.bass_utils import run_bass_kernel_spmd

F32 = mybir.dt.float32
BF16 = mybir.dt.bfloat16
AF = mybir.ActivationFunctionType
ALU = mybir.AluOpType

QUEUES = ("sync", "scalar", "vector", "gpsimd", "tensor")
NDSEM = {"sync": 16, "gpsimd": 12, "scalar": 4}


class Res:
    __slots__ = ("name", "w", "rc", "rd", "psum")

    def __init__(self, name="", psum=False):
        self.name = name
        self.psum = psum
        self.w = None
        self.rc = {}
        self.rd = []


class Op:
    __slots__ = ("q", "dma", "fn", "deps", "signal", "cidx", "didx", "pos")


class Sched:
    def __init__(self):
        self.ops = {q: [] for q in QUEUES}
        self.ncomp = {q: 0 for q in QUEUES}
        self.ndma = {q: 0 for q in QUEUES}
        self.dma_ops = {q: [] for q in QUEUES}
        self.all_dma = []

    def op(self, q, fn, reads=(), writes=(), dma=False):
        o = Op()
        o.q, o.dma, o.fn, o.signal = q, dma, fn, dma
        deps = set()
        for r in reads:
            if r.w is not None:
                deps.add(r.w)
            if r.psum:
                deps.update(o2 for q2, o2 in r.rc.items() if q2 != q)
        for r in writes:
            if r.w is not None:
                deps.add(r.w)
            deps.update(r.rc.values())
            deps.update(r.rd)
        if dma:
            o.didx = self.ndma[q]
            self.ndma[q] += 1
            R = NDSEM[q]
            if o.didx >= R:
                deps.add(self.dma_ops[q][o.didx - R])
            self.dma_ops[q].append(o)
            self.all_dma.append(o)
            o.cidx = None
        else:
            o.cidx = self.ncomp[q]
            self.ncomp[q] += 1
            o.didx = None
        deps.discard(o)
        if q == "tensor" and not dma:
            deps = {d for d in deps if not (d.q == "tensor" and not d.dma)}
        o.deps = deps
        for d in deps:
            d.signal = True
        for r in reads:
            if dma:
                r.rd.append(o)
            else:
                r.rc[q] = o
        for r in writes:
            r.w = o
            r.rc = {}
            r.rd = []
        o.pos = len(self.ops[q])
        self.ops[q].append(o)
        return o

    def emit(self, nc, block, csem, dsem):
        sched = self
        val = {}
        for q in QUEUES:
            n = 0
            for o in self.ops[q]:
                if o.dma:
                    val[o] = (dsem[q][o.didx % NDSEM[q]], 16 * (o.didx // NDSEM[q] + 1))
                elif o.signal:
                    n += 1
                    val[o] = (csem[q], n)

        def run(q, eng):
            waited = {}
            last_dma = []
            for o in sched.ops[q]:
                need = {}
                for d in o.deps:
                    s, v = val[d]
                    k = id(s)
                    if v > need.get(k, (None, 0))[1]:
                        need[k] = (s, v)
                for k, (s, v) in need.items():
                    if waited.get(k, 0) < v:
                        eng.wait_ge(s, v)
                        waited[k] = v
                ins = o.fn(eng)
                if o.dma:
                    s, v = val[o]
                    ins.then_inc(s, 16)
                elif o.signal:
                    ins.then_inc(csem[q], 1)
            for o in sched.dma_ops[q][-NDSEM.get(q, 0):] if sched.dma_ops[q] else []:
                s, v = val[o]
                if waited.get(id(s), 0) < v:
                    eng.wait_ge(s, v)
                    waited[id(s)] = v

        @block.sync
        def _(e):
            run("sync", e)

        @block.scalar
        def _(e):
            run("scalar", e)

        @block.vector
        def _(e):
            run("vector", e)

        @block.gpsimd
        def _(e):
            run("gpsimd", e)

        @block.tensor
        def _(e):
            run("tensor", e)


D = 1024
NCH = 8
QL, KVL, RD = 384, 256, 32
H = 8
QKD = 96
INC = 2208
C_CQ, C_CKV, C_KR, C_NQ, C_NK, C_NV = 0, 384, 640, 672, 1184, 1696
DFF = 2816
NFF = 22
EPS = 1e-6
NEG = -30000.0
GRID_W = 64
G_ATTN, G_Q, G_KV, G_MIX, G_FFN = 0, 8, 11, 13, 21


class Cfg:
    def __init__(self, NP, SP, SC, NQ):
        self.NP, self.SP, self.SC, self.NQ = NP, SP, SC, NQ
        self.NNA = NQ + 512
        self.QOFF = 256


def na_plan(kind, nrows_q):
    plan = []
    if kind == "P":
        R = nrows_q
        kh = min(8, R)

        def start(r):
            return min(max(r - kh // 2, 0), R - kh)
        for b in range(R // 8):
            items = []
            for kr in range(R):
                rows = [i for i in range(8 * b, 8 * b + 8) if start(i) <= kr <= start(i) + kh - 1]
                if not rows:
                    continue
                i0, i1 = rows[0], rows[-1]
                assert rows == list(range(i0, i1 + 1))
                items.append((kr, i0, i1, "T", 7 - (kr - i0)))
            plan.append(items)
    else:
        R = nrows_q
        for b in range(R // 8):
            items = []
            for ks in range(R + 7):
                rows = [i for i in range(8 * b, 8 * b + 8) if i <= ks <= i + 7]
                if not rows:
                    continue
                i0, i1 = rows[0], rows[-1]
                assert rows == list(range(i0, i1 + 1))
                if ks < 4:
                    hb0 = [0, 1, 3, 6][ks]
                    assert i0 == 0
                    items.append((ks, i0, i1, "H", hb0))
                elif ks >= R + 4:
                    j = ks - (R + 4)
                    hb0 = 10 + [0, 3, 5][j]
                    assert i0 == R - 3 + j and i1 == R - 1
                    items.append((ks, i0, i1, "H", hb0))
                else:
                    kr = ks - 4
                    items.append((ks, i0, i1, "T", 7 - (kr - i0)))
            plan.append(items)
    return plan


def build_program(cfg):
    import contextlib
    nc = bass.Bass("TRN2", target_bir_lowering=False)
    NP, SP, SC, NQ, NNA, QOFF = cfg.NP, cfg.SP, cfg.SC, cfg.NQ, cfg.NNA, cfg.QOFF
    PHASES = getattr(cfg, 'phases', (1, 2, 3))

    def din(name, shape, dt=F32):
        return nc.dram_tensor(name, list(shape), dt, kind="ExternalInput").ap()

    def dscr(name, shape, dt):
        return nc.dram_tensor(name, list(shape), dt, kind="Internal").ap()

    xp = din("xp", [NP * SP, D])
    xc = din("xc", [SC, D])
    xn = din("xn", [NNA, D])
    w_in = din("w_in", [D, INC])
    w_uq = din("w_uq", [QL, H * QKD])
    w_ukv = din("w_ukv", [KVL, H * 128])
    w_o = din("w_o", [D, D])
    w_gate = din("w_gate", [D, DFF])
    w_up = din("w_up", [D, DFF])
    w_down = din("w_down", [DFF, D])
    gcol_d = din("gcol", [128, 32])
    gfin_d = din("gfin", [1, D])
    ident_d = din("ident", [128, 128])
    ropeP_d = din("ropeP", [128, SP // 128, 32])
    ropeC_d = din("ropeC", [128, SC // 128, 32])
    ropeN_d = din("ropeN", [128, NNA // 128, 32])
    natab_d = din("natab", [128, 4, 15 * 64])
    nahalo_d = din("nahalo", [128, 4, 16 * 64])
    namask_d = din("namask", [128, 2, 64])
    yp = nc.dram_tensor("yp", [NP * SP, D], F32, kind="ExternalOutput").ap()
    yn = nc.dram_tensor("yn", [NQ, D], F32, kind="ExternalOutput").ap()

    units = []
    for i in range(NP):
        units.append(dict(kind="P", S=SP, NT=SP, NQ=SP, q0=0, name=f"p{i}",
                          xres=xp[i * SP:(i + 1) * SP, :], yout=yp[i * SP:(i + 1) * SP, :]))
    units.append(dict(kind="S", S=SC, NT=NNA, NQ=NQ, q0=QOFF, name="s",
                      xres=xn[QOFF:QOFF + NQ, :], yout=yn))
    for u in units:
        n = u["name"]
        u["kt"] = dscr("s_kt_" + n, [H, QKD, u["S"]], BF16)
        u["v"] = dscr("s_v_" + n, [H, 128, u["S"] // 128, 65], BF16)
        u["qt"] = dscr("s_qt_" + n, [H, QKD, u["NT"]], BF16)
        u["naq"] = dscr("s_naq_" + n, [128, 4, u["NT"]], BF16)
        u["nak"] = dscr("s_nak_" + n, [128, 4, u["NT"]], BF16)
        u["nav"] = dscr("s_nav_" + n, [64, u["NT"] // 128, 2, 8 * 65], BF16)
        u["ot"] = dscr("s_ot_" + n, [16, 65, u["NQ"]], F32)

    S = Sched()
    top = contextlib.ExitStack()
    with top:
        csem = {q: top.enter_context(nc.semaphore("c_" + q)) for q in QUEUES}
        dsem = {q: [top.enter_context(nc.semaphore(f"d_{q}{i}")) for i in range(n)]
                for q, n in NDSEM.items()}
        block = top.enter_context(nc.Block())

        pending = {q: set() for q in QUEUES}

        def OP(q, fn, reads=(), writes=(), dma=False):
            o = S.op(q, fn, reads, writes, dma)
            if pending[q]:
                extra = {d for d in pending[q] if d is not o}
                if q == "tensor" and not dma:
                    extra = {d for d in extra if not (d.q == "tensor" and not d.dma)}
                for d in extra:
                    d.signal = True
                o.deps |= extra
                pending[q] = set()
            return o

        def barrier():
            B = set()
            for q in QUEUES:
                comp = [o for o in S.ops[q] if not o.dma]
                if comp:
                    B.add(comp[-1])
                if S.dma_ops[q]:
                    B.update(S.dma_ops[q][-NDSEM[q]:])
            for q in QUEUES:
                pending[q] |= B

        def DMA(q, out, in_, reads=(), writes=()):
            return OP(q, lambda e, o=out, i=in_: e.dma_start(out=o, in_=i), reads, writes, dma=True)

        def ACT(out, in_, func, reads=(), writes=(), **kw):
            return OP("scalar", lambda e, o=out, i=in_, f=func, kw=kw: e.activation(out=o, in_=i, func=f, **kw),
                      reads, writes)

        def TS(q, out, in0, s1, s2, op0, op1, reads=(), writes=()):
            if s2 is None:
                return OP(q, lambda e, o=out, i=in0, a=s1, p0=op0:
                          e.tensor_scalar(out=o, in0=i, scalar1=a, scalar2=0.0, op0=p0, op1=ALU.add), reads, writes)
            return OP(q, lambda e, o=out, i=in0, a=s1, b=s2, p0=op0, p1=op1:
                      e.tensor_scalar(out=o, in0=i, scalar1=a, scalar2=b, op0=p0, op1=p1), reads, writes)

        def TT(q, out, in0, in1, op, reads=(), writes=()):
            return OP(q, lambda e, o=out, a=in0, b=in1, p=op: e.tensor_tensor(out=o, in0=a, in1=b, op=p),
                      reads, writes)

        def STT(q, out, in0, scalar, in1, op0, op1, reads=(), writes=()):
            return OP(q, lambda e, o=out, a=in0, s=scalar, b=in1, p0=op0, p1=op1:
                      e.scalar_tensor_tensor(out=o, in0=a, scalar=s, in1=b, op0=p0, op1=p1), reads, writes)

        def CP(q, out, in_, reads=(), writes=()):
            if q == "scalar":
                return ACT(out, in_, AF.Copy, reads, writes)
            return OP(q, lambda e, o=out, i=in_: e.tensor_copy(out=o, in_=i), reads, writes)

        def MM(out, lhsT, rhs, start, stop, reads=(), writes=(), **kw):
            return OP("tensor", lambda e, o=out, l=lhsT, r=rhs, s=start, t=stop, kw=kw:
                      e.matmul(out=o, lhsT=l, rhs=r, start=s, stop=t, **kw), reads, writes)

        def TR(out, in_, idn, reads=(), writes=()):
            return OP("tensor", lambda e, o=out, i=in_, d=idn: e.transpose(out=o, in_=i, identity=d),
                      reads, writes)

        def MEMSET(q, ap, val, writes=()):
            return OP(q, lambda e, a=ap, v=val: e.memset(a, v), (), writes)

        def sb(stack, name, shape, dt):
            return stack.enter_context(nc.sbuf_tensor("sb_" + name, list(shape), dt))

        def psum(stack, name, shape, dt=F32):
            return stack.enter_context(nc.psum_tensor("ps_" + name, list(shape), dt))

        identf = sb(top, "identf", [128, 128], F32)
        identb = sb(top, "identb", [128, 128], BF16)
        gcol = sb(top, "gcol", [128, 32], F32)
        epsc = sb(top, "epsc", [128, 1], F32)
        r_identf, r_identb, r_gcol, r_eps = Res(), Res(), Res(), Res()
        DMA("sync", identf[:], ident_d[:, :], (), [r_identf])
        DMA("sync", gcol[:], gcol_d[:, :], (), [r_gcol])
        CP("vector", identb[:], identf[:], [r_identf], [r_identb])
        MEMSET("vector", epsc[:], EPS, [r_eps])

        def rstd(st, c0, n, R):
            ACT(st[:, c0 + 1:c0 + 2], st[:, c0:c0 + 1], AF.Ln, [R, r_eps], [R], bias=epsc[:, 0:1], scale=1.0 / n)
            ACT(st[:, c0 + 2:c0 + 3], st[:, c0 + 1:c0 + 2], AF.Exp, [R], [R], scale=-0.5)

        def load_weight(wst, r_wst, WST, rot, dst, r_dst, src, nchunk, ncols, gbase, col_scales):
            for c in range(nchunk):
                for c0 in range(0, ncols, WST):
                    c1 = min(ncols, c0 + WST)
                    k = rot[0] % 2
                    rot[0] += 1
                    DMA("sync", wst[k][:, 0:c1 - c0], src[c * 128:(c + 1) * 128, c0:c1], (), [r_wst[k]])
                    for (a, b, const) in col_scales:
                        lo, hi = max(a, c0), min(b, c1)
                        if lo >= hi:
                            continue
                        eng = "gpsimd" if (rot[1] % 2) else "vector"
                        rot[1] += 1
                        if gbase is None:
                            CP(eng, dst[:, c, lo:hi], wst[k][:, lo - c0:hi - c0], [r_wst[k]], [r_dst[c]])
                        else:
                            TS(eng, dst[:, c, lo:hi], wst[k][:, lo - c0:hi - c0], gcol[:, gbase + c:gbase + c + 1],
                               const, ALU.mult, ALU.mult, [r_wst[k], r_gcol], [r_dst[c]])

        WST = 1408
        ph1 = contextlib.ExitStack()
        with ph1 if 1 in PHASES else contextlib.nullcontext():
          if 1 in PHASES:
              wst = [sb(ph1, f"wst{i}", [128, WST], F32) for i in range(2)]
              r_wst = [Res(), Res()]
              rot = [0, 0]
              win = sb(ph1, "win", [128, NCH, INC], BF16)
              wuq = sb(ph1, "wuq", [128, 3, H * QKD], BF16)
              wukv = sb(ph1, "wukv", [128, 2, H * 128], BF16)
              r_win = [Res() for _ in range(NCH)]
              r_wuq = [Res() for _ in range(3)]
              r_wukv = [Res() for _ in range(2)]
              load_weight(wst, r_wst, WST, rot, win, r_win, w_in, NCH, INC, G_ATTN,
                          [(0, C_NQ, 1.0), (C_NQ, C_NK, 0.125), (C_NK, INC, 1.0)])
              load_weight(wst, r_wst, WST, rot, wuq, r_wuq, w_uq, 3, H * QKD, G_Q, [(0, H * QKD, QKD ** -0.5)])
              load_weight(wst, r_wst, WST, rot, wukv, r_wukv, w_ukv, 2, H * 128, G_KV, [(0, H * 128, 1.0)])

              ropeP = sb(ph1, "ropeP", [128, SP // 128, 32], F32)
              ropeC = sb(ph1, "ropeC", [128, SC // 128, 32], F32)
              ropeN = sb(ph1, "ropeN", [128, NNA // 128, 32], F32)
              r_rope = Res()
              DMA("sync", ropeP[:], ropeP_d[:, :, :], (), [r_rope])
              DMA("sync", ropeC[:], ropeC_d[:, :, :], (), [r_rope])
              DMA("sync", ropeN[:], ropeN_d[:, :, :], (), [r_rope])

              NX = 3
              xt = [sb(ph1, f"xt{i}", [128, D], F32) for i in range(NX)]
              r_xt = [Res() for _ in range(NX)]
              junk = [sb(ph1, f"junk{i}", [128, D], BF16) for i in range(2)]
              r_junk = [Res(), Res()]
              jctr = [0]
              hb = [sb(ph1, f"hb{i}", [128, D], BF16) for i in range(2)]
              r_hb = [Res(), Res()]
              hT = [sb(ph1, f"hT{i}", [128, NCH, 512], BF16) for i in range(2)]
              r_hT = [[Res() for _ in range(4)] for _ in range(2)]
              NST = 4
              stt = [sb(ph1, f"stt{i}", [128, 16], F32) for i in range(NST)]
              r_stx = [Res() for _ in range(NST)]
              r_stq = [Res() for _ in range(NST)]
              r_stkv = [Res() for _ in range(NST)]
              cqn = [sb(ph1, f"cqn{i}", [128, QL], BF16) for i in range(2)]
              ckvn = [sb(ph1, f"ckvn{i}", [128, KVL], BF16) for i in range(2)]
              r_cqn = [Res(), Res()]
              r_ckvn = [Res(), Res()]
              cqnT = [sb(ph1, f"cqnT{i}", [128, 3, 128], BF16) for i in range(2)]
              ckvnT = [sb(ph1, f"ckvnT{i}", [128, 2, 128], BF16) for i in range(2)]
              r_cqnT = [Res(), Res()]
              r_ckvnT = [Res(), Res()]
              rtmp = [sb(ph1, f"rtmp{i}", [128, 96 + 512], F32) for i in range(2)]
              r_tc = [Res(), Res()]
              r_ts = [Res(), Res()]
              r_kro = [Res(), Res()]
              r_tcq = [[Res(), Res()], [Res(), Res()]]
              r_tsq = [[Res(), Res()], [Res(), Res()]]
              Ktok = [sb(ph1, f"Ktok{i}", [128, H, QKD], BF16) for i in range(2)]
              Qtok = [sb(ph1, f"Qtok{i}", [128, H, QKD], BF16) for i in range(2)]
              r_Ktok = [Res(), Res()]
              r_Qtok = [Res(), Res()]
              KTst = [sb(ph1, f"KTst{i}", [128, H, 512], BF16) for i in range(2)]
              QTst = [sb(ph1, f"QTst{i}", [128, H, 512], BF16) for i in range(2)]
              Vst = [sb(ph1, f"Vst{i}", [128, H, 4, 65], BF16) for i in range(2)]
              naqst = [sb(ph1, f"naqst{i}", [128, 4, 512], BF16) for i in range(2)]
              nakst = [sb(ph1, f"nakst{i}", [128, 4, 512], BF16) for i in range(2)]
              navst = [sb(ph1, f"navst{i}", [128, 4, H, 65], BF16) for i in range(2)]
              r_KTst = [[Res() for _ in range(4)] for _ in range(2)]
              r_QTst = [[Res() for _ in range(4)] for _ in range(2)]
              r_Vst = [[Res() for _ in range(4)] for _ in range(2)]
              r_naqst = [[Res() for _ in range(4)] for _ in range(2)]
              r_nakst = [[Res() for _ in range(4)] for _ in range(2)]
              r_navst = [[Res() for _ in range(4)] for _ in range(2)]
              for i in range(2):
                  for t in range(4):
                      MEMSET("gpsimd", Vst[i][:, :, t, 64:65], 1.0, [r_Vst[i][t]])
                      MEMSET("gpsimd", navst[i][:, t, :, 64:65], 1.0, [r_navst[i][t]])

              PB = [psum(ph1, f"pb{i}", [128, 512], F32) for i in range(4)]
              PKVU = psum(ph1, "pkvu", [128, 1024], F32)
              PQU = psum(ph1, "pqu", [128, 1024], F32)
              r_PB = [Res(psum=True) for _ in range(4)]
              r_PKVU, r_PQU = Res(psum=True), Res(psum=True)
              ptr_bf = PB[0][:].bitcast(BF16)
              pkt_bf = PB[1][:].bitcast(BF16)
              pqt_bf = PB[2][:].bitcast(BF16)

              tilectr = [0]
              grpctr = [0]

              def square_acc(in_ap, n, acc, reads, writes):
                  j = jctr[0] % 2
                  jctr[0] += 1
                  ACT(junk[j][:, 0:n], in_ap, AF.Square, reads, [r_junk[j]] + writes, accum_out=acc)

              def token_pass(u, src, ntile, rope_sb, do_q, do_kv, do_na):
                  dbg = getattr(cfg, 'dbg', (1, 1, 1))
                  do_q, do_kv, do_na = do_q and dbg[0], do_kv and dbg[1], do_na and dbg[2]
                  ngrp = ntile // 4
                  for g in range(ngrp):
                      gi = grpctr[0] % 2
                      grpctr[0] += 1
                      for t in range(4):
                          tg = g * 4 + t
                          tc_ = tilectr[0]
                          tilectr[0] += 1
                          xi = tc_ % NX
                          k2 = tc_ % 2
                          si = tc_ % NST
                          st = stt[si]
                          tsl = slice(t * 128, (t + 1) * 128)
                          DMA("sync", xt[xi][:], src[tg * 128:(tg + 1) * 128, :], (), [r_xt[xi]])
                          square_acc(xt[xi][:], D, st[:, 0:1], [r_xt[xi]], [r_stx[si]])
                          rstd(st, 0, D, r_stx[si])
                          TS("vector", hb[k2][:], xt[xi][:], st[:, 2:3], None, ALU.mult, None,
                             [r_xt[xi], r_stx[si]], [r_hb[k2]])
                          for c in range(NCH):
                              TR(ptr_bf[:, c * 128:(c + 1) * 128], hb[k2][:, c * 128:(c + 1) * 128], identb[:],
                                 [r_hb[k2], r_identb], [r_PB[0]])
                          CP("scalar", hT[gi][:, :, tsl],
                             ptr_bf[:, 0:1024].rearrange("p (c n) -> p c n", c=NCH), [r_PB[0]], [r_hT[gi][t]])
                          if do_q:
                              for c in range(NCH):
                                  MM(PB[1][:, 0:QL], hT[gi][:, c, tsl], win[:, c, C_CQ:C_CQ + QL],
                                     c == 0, c == NCH - 1, [r_hT[gi][t], r_win[c]], [r_PB[1]])
                          if do_kv:
                              for c in range(NCH):
                                  MM(PB[2][:, 0:KVL + RD], hT[gi][:, c, tsl], win[:, c, C_CKV:C_CKV + KVL + RD],
                                     c == 0, c == NCH - 1, [r_hT[gi][t], r_win[c]], [r_PB[2]])
                          if do_na:
                              for c in range(NCH):
                                  MM(PB[3][:, :], hT[gi][:, c, tsl], win[:, c, C_NV:C_NV + 512],
                                     c == 0, c == NCH - 1, [r_hT[gi][t], r_win[c]], [r_PB[3]])
                              CP("vector", navst[gi][:, t, :, 0:64], PB[3][:, :].rearrange("p (h d) -> p h d", h=H),
                                 [r_PB[3]], [r_navst[gi][t]])
                          if do_q:
                              square_acc(PB[1][:, 0:QL], QL, st[:, 3:4], [r_PB[1]], [r_stq[si]])
                              rstd(st, 3, QL, r_stq[si])
                              TS("vector", cqn[k2][:], PB[1][:, 0:QL], st[:, 5:6], None, ALU.mult, None,
                                 [r_PB[1], r_stq[si]], [r_cqn[k2]])
                          if do_kv:
                              square_acc(PB[2][:, 0:KVL], KVL, st[:, 6:7], [r_PB[2]], [r_stkv[si]])
                              rstd(st, 6, KVL, r_stkv[si])
                              TS("vector", ckvn[k2][:], PB[2][:, 0:KVL], st[:, 8:9], None, ALU.mult, None,
                                 [r_PB[2], r_stkv[si]], [r_ckvn[k2]])
                              rt = rtmp[k2]
                              xk = PB[2][:, KVL:KVL + RD].rearrange("p (a j) -> p a j", a=2)
                              cosb = rope_sb[:, tg, 0:16].unsqueeze(1).broadcast_to([128, 2, 16])
                              sinb = rope_sb[:, tg, 16:32].unsqueeze(1).broadcast_to([128, 2, 16])
                              tcv = rt[:, 0:32].rearrange("p (a j) -> p a j", a=2)
                              tsv = rt[:, 32:64].rearrange("p (a j) -> p a j", a=2)
                              TT("vector", tcv, xk, cosb, ALU.mult, [r_PB[2], r_rope, r_stkv[si]], [r_tc[k2]])
                              TT("vector", tsv, xk, sinb, ALU.mult, [r_PB[2], r_rope, r_stkv[si]], [r_ts[k2]])
                              kro = rt[:, 64:96]
                              TT("vector", kro[:, 0:16], rt[:, 0:16], rt[:, 48:64], ALU.subtract,
                                 [r_tc[k2], r_ts[k2]], [r_kro[k2]])
                              TT("vector", kro[:, 16:32], rt[:, 32:48], rt[:, 16:32], ALU.add,
                                 [r_tc[k2], r_ts[k2]], [r_kro[k2]])
                          if do_q:
                              for c in range(3):
                                  TR(ptr_bf[:, c * 128:(c + 1) * 128], cqn[k2][:, c * 128:(c + 1) * 128], identb[:],
                                     [r_cqn[k2], r_identb], [r_PB[0]])
                              CP("scalar", cqnT[k2][:], ptr_bf[:, 0:384].rearrange("p (c n) -> p c n", c=3),
                                 [r_PB[0]], [r_cqnT[k2]])
                          if do_kv:
                              for c in range(2):
                                  TR(ptr_bf[:, 384 + c * 128:384 + (c + 1) * 128], ckvn[k2][:, c * 128:(c + 1) * 128],
                                     identb[:], [r_ckvn[k2], r_identb], [r_PB[0]])
                              CP("scalar", ckvnT[k2][:], ptr_bf[:, 384:640].rearrange("p (c n) -> p c n", c=2),
                                 [r_PB[0]], [r_ckvnT[k2]])
                          if do_kv:
                              for hh in range(2):
                                  for c in range(2):
                                      MM(PKVU[:, hh * 512:(hh + 1) * 512], ckvnT[k2][:, c, :],
                                         wukv[:, c, hh * 512:(hh + 1) * 512], c == 0, c == 1,
                                         [r_ckvnT[k2], r_wukv[c]], [r_PKVU])
                              kvv = PKVU[:, :].rearrange("p (h d) -> p h d", h=H)
                              CP("vector", Ktok[k2][:, :, 0:64], kvv[:, :, 0:64], [r_PKVU], [r_Ktok[k2]])
                              CP("vector", Ktok[k2][:, :, 64:96], kro.unsqueeze(1).broadcast_to([128, H, RD]),
                                 [r_kro[k2]], [r_Ktok[k2]])
                              CP("scalar", Vst[gi][:, :, t, 0:64], kvv[:, :, 64:128], [r_PKVU], [r_Vst[gi][t]])
                              for h in range(H):
                                  TR(pkt_bf[0:QKD, h * 128:(h + 1) * 128], Ktok[k2][:, h, :], identb[:],
                                     [r_Ktok[k2], r_identb], [r_PB[1]])
                              CP("scalar", KTst[gi][0:QKD, :, tsl],
                                 pkt_bf[0:QKD, :].rearrange("p (h n) -> p h n", h=H), [r_PB[1]], [r_KTst[gi][t]])
                          if do_q:
                              for hh in range(2):
                                  for c in range(3):
                                      MM(PQU[:, hh * 512:hh * 512 + 384], cqnT[k2][:, c, :],
                                         wuq[:, c, hh * 384:(hh + 1) * 384], c == 0, c == 2,
                                         [r_cqnT[k2], r_wuq[c]], [r_PQU])
                              rt = rtmp[k2]
                              for hh in range(2):
                                  qv = PQU[:, hh * 512:hh * 512 + 384].rearrange("p (h d) -> p h d", h=4)
                                  CP("vector", Qtok[k2][:, hh * 4:(hh + 1) * 4, 0:64], qv[:, :, 0:64],
                                     [r_PQU], [r_Qtok[k2]])
                                  xq = qv[:, :, 64:96].rearrange("p h (a j) -> p h a j", a=2)
                                  cosb = rope_sb[:, tg, 0:16].unsqueeze(1).unsqueeze(1).broadcast_to([128, 4, 2, 16])
                                  sinb = rope_sb[:, tg, 16:32].unsqueeze(1).unsqueeze(1).broadcast_to([128, 4, 2, 16])
                                  o0 = 96 + hh * 256
                                  tcq = rt[:, o0:o0 + 128].rearrange("p (h a j) -> p h a j", h=4, a=2)
                                  tsq = rt[:, o0 + 128:o0 + 256].rearrange("p (h a j) -> p h a j", h=4, a=2)
                                  TT("vector", tcq, xq, cosb, ALU.mult, [r_PQU, r_rope], [r_tcq[k2][hh]])
                                  TT("vector", tsq, xq, sinb, ALU.mult, [r_PQU, r_rope], [r_tsq[k2][hh]])
                                  TT("gpsimd", Qtok[k2][:, hh * 4:(hh + 1) * 4, 64:80], tcq[:, :, 0, :], tsq[:, :, 1, :],
                                     ALU.subtract, [r_tcq[k2][hh], r_tsq[k2][hh]], [r_Qtok[k2]])
                                  TT("gpsimd", Qtok[k2][:, hh * 4:(hh + 1) * 4, 80:96], tsq[:, :, 0, :], tcq[:, :, 1, :],
                                     ALU.add, [r_tcq[k2][hh], r_tsq[k2][hh]], [r_Qtok[k2]])
                              for h in range(H):
                                  TR(pqt_bf[0:QKD, h * 128:(h + 1) * 128], Qtok[k2][:, h, :], identb[:],
                                     [r_Qtok[k2], r_identb], [r_PB[2]])
                              CP("vector", QTst[gi][0:QKD, :, tsl],
                                 pqt_bf[0:QKD, :].rearrange("p (h n) -> p h n", h=H), [r_PB[2]], [r_QTst[gi][t]])
                      if do_na:
                          for blk in range(8):
                              col0 = C_NQ + blk * 128
                              for c in range(NCH):
                                  MM(PB[3][:, :], win[:, c, col0:col0 + 128], hT[gi][:, c, :], c == 0, c == NCH - 1,
                                     r_hT[gi] + [r_win[c]], [r_PB[3]])
                              dst = naqst[gi] if blk < 4 else nakst[gi]
                              rr = r_naqst[gi] if blk < 4 else r_nakst[gi]
                              CP("scalar" if blk % 2 else "vector", dst[:, blk % 4, :], PB[3][:, :], [r_PB[3]],
                                 [rr[blk % 4]])
                          DMA("gpsimd", u["naq"][:, :, g * 512:(g + 1) * 512], naqst[gi][:], r_naqst[gi], ())
                          DMA("gpsimd", u["nak"][:, :, g * 512:(g + 1) * 512], nakst[gi][:], r_nakst[gi], ())
                          for half in range(2):
                              DMA("gpsimd", u["nav"][:, g * 4:(g + 1) * 4, half, :],
                                  navst[gi][half * 64:(half + 1) * 64, :, :, :].rearrange("p t h d -> p t (h d)"),
                                  r_navst[gi], ())
                      if do_kv:
                          DMA("gpsimd", u["kt"][:, :, g * 512:(g + 1) * 512].rearrange("h d n -> d h n"),
                              KTst[gi][0:QKD, :, :], r_KTst[gi], ())
                          DMA("gpsimd", u["v"][:, :, g * 4:(g + 1) * 4, :].rearrange("h p t d -> p h t d"),
                              Vst[gi][:], r_Vst[gi], ())
                      if do_q:
                          DMA("gpsimd", u["qt"][:, :, g * 512:(g + 1) * 512].rearrange("h d n -> d h n"),
                              QTst[gi][0:QKD, :, :], r_QTst[gi], ())

              for ui, u in enumerate(units):
                  if u["kind"] == "P":
                      token_pass(u, xp[ui * SP:(ui + 1) * SP, :], SP // 128, ropeP, True, True, True)
                  else:
                      token_pass(u, xn, NNA // 128, ropeN, True, False, True)
                      token_pass(u, xc, SC // 128, ropeC, False, True, False)
        barrier()

        ph2 = contextlib.ExitStack()
        with ph2 if 2 in PHASES else contextlib.nullcontext():
          if 2 in PHASES:
              SMAX = max(u["S"] for u in units)
              NQMAX = max(u["NQ"] for u in units)
              NTMAX = max(u["NT"] for u in units)
              KTb = [sb(ph2, f"KTb{i}", [128, SMAX], BF16) for i in range(2)]
              Vb = [sb(ph2, f"Vb{i}", [128, SMAX // 128, 65], BF16) for i in range(2)]
              QTb = [sb(ph2, f"QTb{i}", [128, NQMAX], BF16) for i in range(2)]
              r_KTb, r_Vb, r_QTb = [Res(), Res()], [Res(), Res()], [Res(), Res()]
              PT = [sb(ph2, f"PT{i}", [128, 1024], BF16) for i in range(3)]
              r_PT = [Res() for _ in range(3)]
              oTs = [sb(ph2, f"oTs{i}", [128, 512], F32) for i in range(2)]
              r_oTs = [Res(), Res()]
              naq = sb(ph2, "naq", [128, 4, NTMAX], BF16)
              nak = sb(ph2, "nak", [128, 4, NTMAX], BF16)
              nav = sb(ph2, "nav", [64, NTMAX // 64, 8 * 65], BF16)
              r_naq, r_nak, r_nav = Res(), Res(), Res()
              tabf = sb(ph2, "tabf", [128, 4, 15 * 64], F32)
              half_ = sb(ph2, "halof", [128, 4, 16 * 64], F32)
              maskf = sb(ph2, "maskf", [128, 2, 64], F32)
              tabb = sb(ph2, "tabb", [128, 4, 15 * 64], BF16)
              halb = sb(ph2, "halob", [128, 4, 16 * 64], BF16)
              r_tabf, r_half, r_maskf, r_tabb, r_halb = Res(), Res(), Res(), Res(), Res()
              NCHAIN = 4
              PTn = [sb(ph2, f"PTn{i}", [64, 512], BF16) for i in range(2 * NCHAIN)]
              r_PTn = [Res() for _ in range(2 * NCHAIN)]

              PS = [psum(ph2, f"ps{i}", [128, 1024], F32) for i in range(2)]
              PO = [psum(ph2, f"po{i}", [128, 512], F32) for i in range(4)]
              r_PSh = [[Res(psum=True), Res(psum=True)] for _ in range(2)]
              r_PO = [Res(psum=True) for _ in range(4)]
              PSn = [PS[0][:, 0:512], PS[0][:, 512:1024], PS[1][:, 0:512], PS[1][:, 512:1024]]
              r_PSn = [r_PSh[0][0], r_PSh[0][1], r_PSh[1][0], r_PSh[1][1]]
              PA = PO
              r_PA = r_PO

              DMA("sync", tabf[:], natab_d[:, :, :], (), [r_tabf])
              DMA("sync", half_[:], nahalo_d[:, :, :], (), [r_half])
              DMA("sync", maskf[:], namask_d[:, :, :], (), [r_maskf])
              for (srcf, dstb, nb, rs, rd) in ((tabf, tabb, 15, r_tabf, r_tabb), (half_, halb, 16, r_half, r_halb)):
                  v = srcf[:].rearrange("p a (b q) -> p (a b) q", q=64)
                  vb = dstb[:].rearrange("p a (b q) -> p (a b) q", q=64)
                  m01 = maskf[:, 0, :].unsqueeze(1).broadcast_to([128, 4 * nb, 64])
                  mng = maskf[:, 1, :].unsqueeze(1).broadcast_to([128, 4 * nb, 64])
                  TT("vector", v, v, m01, ALU.mult, [rs, r_maskf], [rs])
                  TT("vector", vb, v, mng, ALU.add, [rs, r_maskf], [rd])

              ctr = dict(hb=0, pt=0, ps=0, po=0, ot=0, ptn=0)

              def mla_attention(u):
                  Sx, NQu, q0 = u["S"], u["NQ"], u["q0"]
                  nkt = Sx // 128
                  nk2 = nkt // 2
                  nqb = NQu // 512
                  steps = [(h, qb, k2) for h in range(H) for qb in range(nqb) for k2 in range(nk2)]
                  hbase = ctr["hb"]
                  ctr["hb"] += H
                  pobase = ctr["po"]
                  ctr["po"] += H * nqb

                  def load(h):
                      bi = (hbase + h) % 2
                      DMA("sync", KTb[bi][0:QKD, 0:Sx], u["kt"][h, :, :], (), [r_KTb[bi]])
                      DMA("sync", Vb[bi][:, 0:nkt, :], u["v"][h, :, :, :], (), [r_Vb[bi]])
                      DMA("sync", QTb[bi][0:QKD, 0:NQu], u["qt"][h, :, q0:q0 + NQu], (), [r_QTb[bi]])

                  def QK(i):
                      h, qb, k2 = steps[i]
                      bi = (hbase + h) % 2
                      si = i % 2
                      for j in range(2):
                          kt = 2 * k2 + j
                          MM(PS[si][:, j * 512:(j + 1) * 512], KTb[bi][0:QKD, kt * 128:(kt + 1) * 128],
                             QTb[bi][0:QKD, qb * 512:(qb + 1) * 512], True, True,
                             [r_KTb[bi], r_QTb[bi]], [r_PSh[si][j]])

                  def EXPPV(i):
                      h, qb, k2 = steps[i]
                      bi = (hbase + h) % 2
                      si = i % 2
                      pi = i % 3
                      oi = (pobase + h * nqb + qb) % 4
                      ACT(PT[pi][:, :], PS[si][:, :], AF.Exp, r_PSh[si], [r_PT[pi]])
                      for j in range(2):
                          kt = 2 * k2 + j
                          MM(PO[oi][0:65, :], Vb[bi][:, kt, :], PT[pi][:, j * 512:(j + 1) * 512],
                             kt == 0, kt == nkt - 1, [r_Vb[bi], r_PT[pi]], [r_PO[oi]])
                      if k2 == nk2 - 1:
                          ti = ctr["ot"] % 2
                          ctr["ot"] += 1
                          CP("vector", oTs[ti][0:65, :], PO[oi][0:65, :], [r_PO[oi]], [r_oTs[ti]])
                          DMA("gpsimd", u["ot"][h, :, qb * 512:(qb + 1) * 512], oTs[ti][0:65, :], [r_oTs[ti]], ())

                  load(0)
                  QK(0)
                  for i in range(len(steps)):
                      h, qb, k2 = steps[i]
                      if qb == 0 and k2 == 0 and h + 1 < H:
                          load(h + 1)
                      if i + 1 < len(steps):
                          QK(i + 1)
                      EXPPV(i)

              def na_attention(u):
                  NT, NQu = u["NT"], u["NQ"]
                  nrq = NQu // 64
                  plan = na_plan(u["kind"], nrq)
                  qrow0 = 0 if u["kind"] == "P" else 4
                  DMA("sync", naq[:, :, 0:NT], u["naq"][:, :, :], (), [r_naq])
                  DMA("sync", nak[:, :, 0:NT], u["nak"][:, :, :], (), [r_nak])
                  DMA("sync", nav[:, 0:NT // 64, :], u["nav"][:, :, :, :].rearrange("p t a d -> p (t a) d"), (), [r_nav])
                  chains = [(h, b) for b in range(len(plan)) for h in range(H)]
                  for c0_ in range(0, len(chains), NCHAIN):
                      grp = chains[c0_:c0_ + NCHAIN]
                      nst = max(len(plan[b]) for (h, b) in grp)

                      def QKB(j, s):
                          h, b = grp[j]
                          items = plan[b]
                          if s >= len(items):
                              return
                          ks, i0, i1, src, blk0 = items[s]
                          pr, pb = h // 2, (h % 2) * 64
                          N = 64 * (i1 - i0 + 1)
                          qtok0 = (qrow0 + i0) * 64
                          MM(PSn[j][0:64, 0:N], nak[pb:pb + 64, pr, ks * 64:(ks + 1) * 64],
                             naq[pb:pb + 64, pr, qtok0:qtok0 + N], True, False, [r_nak, r_naq], [r_PSn[j]])
                          tb = tabb if src == "T" else halb
                          rtb = r_tabb if src == "T" else r_halb
                          MM(PSn[j][0:64, 0:N], identb[pb:pb + 64, pb:pb + 64],
                             tb[pb:pb + 64, pr, blk0 * 64:blk0 * 64 + N], False, True, [rtb, r_identb], [r_PSn[j]])

                      def EXPN(j, s):
                          h, b = grp[j]
                          items = plan[b]
                          if s >= len(items):
                              return
                          ks, i0, i1, src, blk0 = items[s]
                          N = 64 * (i1 - i0 + 1)
                          pi = 2 * j + (s % 2)
                          ACT(PTn[pi][:, 0:N], PSn[j][0:64, 0:N], AF.Exp, [r_PSn[j]], [r_PTn[pi]])

                      def PVN(j, s):
                          h, b = grp[j]
                          items = plan[b]
                          if s >= len(items):
                              return
                          ks, i0, i1, src, blk0 = items[s]
                          N = 64 * (i1 - i0 + 1)
                          c0 = (i0 - 8 * b) * 64
                          pi = 2 * j + (s % 2)
                          MM(PA[j][0:65, c0:c0 + N], nav[:, ks, h * 65:(h + 1) * 65], PTn[pi][:, 0:N],
                             s == 0, s == len(items) - 1, [r_nav, r_PTn[pi]], [r_PA[j]], skip_group_check=True)
                          if s == len(items) - 1:
                              ti = ctr["ot"] % 2
                              ctr["ot"] += 1
                              CP("vector", oTs[ti][0:65, :], PA[j][0:65, :], [r_PA[j]], [r_oTs[ti]])
                              DMA("gpsimd", u["ot"][8 + h, :, b * 512:(b + 1) * 512], oTs[ti][0:65, :],
                                  [r_oTs[ti]], ())

                      for j in range(len(grp)):
                          QKB(j, 0)
                      for s in range(nst):
                          for j in range(len(grp)):
                              EXPN(j, s)
                          for j in range(len(grp)):
                              QKB(j, s + 1)
                          for j in range(len(grp)):
                              PVN(j, s)

              for u in units:
                  na_attention(u)
                  mla_attention(u)
        barrier()

        ph3 = contextlib.ExitStack()
        with ph3 if 3 in PHASES else contextlib.nullcontext():
          if 3 in PHASES:
              WST = 704
              wst = [sb(ph3, f"wst3_{i}", [128, WST], F32) for i in range(2)]
              r_wst = [Res(), Res()]
              rot = [0, 0]
              wo = sb(ph3, "wo", [128, NCH, D], BF16)
              wg = sb(ph3, "wg", [128, NCH, DFF], BF16)
              wu = sb(ph3, "wu", [128, NCH, DFF], BF16)
              wd = sb(ph3, "wd", [128, NFF, D], BF16)
              r_wo = [Res() for _ in range(NCH)]
              r_wg = [Res() for _ in range(NCH)]
              r_wu = [Res() for _ in range(NCH)]
              r_wd = [Res() for _ in range(NFF)]
              load_weight(wst, r_wst, WST, rot, wo, r_wo, w_o, NCH, D, G_MIX, [(0, D, 1.0)])
              load_weight(wst, r_wst, WST, rot, wg, r_wg, w_gate, NCH, DFF, G_FFN, [(0, DFF, 1.0)])
              load_weight(wst, r_wst, WST, rot, wu, r_wu, w_up, NCH, DFF, G_FFN, [(0, DFF, 1.0)])
              load_weight(wst, r_wst, WST, rot, wd, r_wd, w_down, NFF, D, None, [(0, D, 1.0)])
              gfin = sb(ph3, "gfin", [128, D], F32)
              r_gfin = Res()
              DMA("sync", gfin[:], gfin_d[0:1, :].partition_broadcast(128), (), [r_gfin])

              TG = 2
              NTOK = TG * 128
              oTin = sb(ph3, "oTin", [128, 8, 128], F32)
              r_oTin = Res()
              atok = sb(ph3, "atok", [128, D], F32)
              r_atok = Res()
              x1 = [sb(ph3, f"x1_{i}", [128, D], F32) for i in range(TG)]
              r_x1 = [Res() for _ in range(TG)]
              mixb = sb(ph3, "mixb", [128, D], BF16)
              r_mixb = Res()
              mixT = sb(ph3, "mixT", [128, NCH, 128], BF16)
              r_mixT = Res()
              h2b = sb(ph3, "h2b", [128, D], BF16)
              r_h2b = Res()
              h2T = sb(ph3, "h2T", [128, NCH, NTOK], BF16)
              r_h2T = [Res() for _ in range(TG)]
              actT = sb(ph3, "actT", [128, NFF, NTOK], BF16)
              r_actT = [Res() for _ in range(NFF)]
              sg = [sb(ph3, f"sg{i}", [128, NTOK], F32) for i in range(2)]
              r_sg = [Res(), Res()]
              junk3 = [sb(ph3, "junk3_0", [128, D], BF16)] * 2
              r_junk3 = [Res()] * 2
              st3 = [sb(ph3, f"st3_{i}", [128, 32], F32) for i in range(2)]
              r_fin = [Res(), Res()]
              r_st3 = [[Res() for _ in range(6)] for _ in range(2)]
              j3 = [0]

              def sq3(in_ap, n, acc, reads, writes):
                  j = j3[0] % 2
                  j3[0] += 1
                  ACT(junk3[j][:, 0:n], in_ap, AF.Square, reads, [r_junk3[j]] + writes, accum_out=acc)

              POh = psum(ph3, "poh", [128, 512], F32)
              r_POh = Res(psum=True)
              PTR = psum(ph3, "ptr3", [128, 512], F32)
              r_PTR = Res(psum=True)
              ptr3_bf = PTR[:].bitcast(BF16)
              PY = psum(ph3, "py", [128, 1024], F32)
              r_PY = Res(psum=True)
              PG = [psum(ph3, f"pg{i}", [128, 512], F32) for i in range(2)]
              PU = [psum(ph3, f"pu{i}", [128, 512], F32) for i in range(2)]
              r_PG, r_PU = [Res(psum=True), Res(psum=True)], [Res(psum=True), Res(psum=True)]
              tctr = [0]

              for u in units:
                  NQu = u["NQ"]
                  for g in range(NQu // NTOK):
                      for t in range(TG):
                          tok0 = g * NTOK + t * 128
                          si = tctr[0] % 2
                          tctr[0] += 1
                          st = st3[si]
                          rs_ = r_st3[si]
                          DMA("sync", x1[t][:], u["xres"][tok0:tok0 + 128, :], (), [r_x1[t]])
                          for half in range(2):
                              DMA("sync", oTin[0:65, :, :],
                                  u["ot"][half * 8:(half + 1) * 8, :, tok0:tok0 + 128].rearrange("h d n -> d h n"),
                                  (), [r_oTin])
                              for qd in range(2):
                                  for hh in range(4):
                                      TR(POh[:, hh * 128:hh * 128 + 65], oTin[0:65, qd * 4 + hh, :], identf[0:65, 0:65],
                                         [r_oTin, r_identf], [r_POh])
                                  pov = POh[:, :].rearrange("p (h d) -> p h d", h=4)
                                  rc = st[:, 8:12].unsqueeze(2)
                                  OP("vector", lambda e, o=rc, i=pov[:, :, 64:65]: e.reciprocal(out=o, in_=i),
                                     [r_POh], [rs_[4]])
                                  a0 = half * 512 + qd * 256
                                  TT("vector", atok[:, a0:a0 + 256].rearrange("p (h d) -> p h d", h=4),
                                     pov[:, :, 0:64], rc.broadcast_to([128, 4, 64]), ALU.mult,
                                     [r_POh, rs_[4]], [r_atok])
                          for half in range(2):
                              sq3(atok[:, half * 512:(half + 1) * 512], 512, st[:, 3 * half:3 * half + 1],
                                  [r_atok], [rs_[half]])
                              rstd(st, 3 * half, 512, rs_[half])
                              TS("vector", mixb[:, half * 512:(half + 1) * 512], atok[:, half * 512:(half + 1) * 512],
                                 st[:, 3 * half + 2:3 * half + 3], None, ALU.mult, None,
                                 [r_atok, rs_[half]], [r_mixb])
                          for c in range(NCH):
                              TR(ptr3_bf[:, c * 128:(c + 1) * 128], mixb[:, c * 128:(c + 1) * 128], identb[:],
                                 [r_mixb, r_identb], [r_PTR])
                          CP("scalar", mixT[:], ptr3_bf[:, 0:1024].rearrange("p (c n) -> p c n", c=NCH),
                             [r_PTR], [r_mixT])
                          for hh in range(2):
                              for c in range(NCH):
                                  MM(PY[:, hh * 512:(hh + 1) * 512], mixT[:, c, :], wo[:, c, hh * 512:(hh + 1) * 512],
                                     c == 0, c == NCH - 1, [r_mixT, r_wo[c]], [r_PY])
                          TT("vector", x1[t][:], x1[t][:], PY[:, :], ALU.add, [r_x1[t], r_PY], [r_x1[t]])
                          sq3(x1[t][:], D, st[:, 16:17], [r_x1[t]], [rs_[2]])
                          rstd(st, 16, D, rs_[2])
                          TS("vector", h2b[:], x1[t][:], st[:, 18:19], None, ALU.mult, None,
                             [r_x1[t], rs_[2]], [r_h2b])
                          for c in range(NCH):
                              TR(ptr3_bf[:, c * 128:(c + 1) * 128], h2b[:, c * 128:(c + 1) * 128], identb[:],
                                 [r_h2b, r_identb], [r_PTR])
                          CP("scalar", h2T[:, :, t * 128:(t + 1) * 128],
                             ptr3_bf[:, 0:1024].rearrange("p (c n) -> p c n", c=NCH), [r_PTR], [r_h2T[t]])
                      for f in range(NFF):
                          pi = f % 2
                          for c in range(NCH):
                              MM(PG[pi][:, 0:NTOK], wg[:, c, f * 128:(f + 1) * 128], h2T[:, c, :], c == 0, c == NCH - 1,
                                 r_h2T + [r_wg[c]], [r_PG[pi]])
                          for c in range(NCH):
                              MM(PU[pi][:, 0:NTOK], wu[:, c, f * 128:(f + 1) * 128], h2T[:, c, :], c == 0, c == NCH - 1,
                                 r_h2T + [r_wu[c]], [r_PU[pi]])
                          ACT(sg[pi][:], PG[pi][:, 0:NTOK], AF.Silu, [r_PG[pi]], [r_sg[pi]])
                          TT("vector", actT[:, f, :], sg[pi][:], PU[pi][:, 0:NTOK], ALU.mult,
                             [r_sg[pi], r_PU[pi]], [r_actT[f]])
                      for t in range(TG):
                          tok0 = g * NTOK + t * 128
                          for hh in range(2):
                              for f in range(NFF):
                                  MM(PY[:, hh * 512:(hh + 1) * 512], actT[:, f, t * 128:(t + 1) * 128],
                                     wd[:, f, hh * 512:(hh + 1) * 512], f == 0, f == NFF - 1,
                                     [r_actT[f], r_wd[f]], [r_PY])
                          TT("vector", x1[t][:], x1[t][:], PY[:, :], ALU.add, [r_x1[t], r_PY], [r_x1[t]])
                          st = st3[t % 2]
                          rr = r_fin[t % 2]
                          sq3(x1[t][:], D, st[:, 24:25], [r_x1[t]], [rr])
                          rstd(st, 24, D, rr)
                          STT("vector", atok[:], x1[t][:], st[:, 26:27], gfin[:], ALU.mult, ALU.mult,
                              [r_x1[t], rr, r_gfin], [r_atok])
                          DMA("gpsimd", u["yout"][tok0:tok0 + 128, :], atok[:], [r_atok], ())
        S.emit(nc, block, csem, dsem)
    return nc


def _rope_table(pos):
    inv = (10000.0 ** (-np.arange(0, RD, 2, dtype=np.float32) / np.float32(RD))).astype(np.float32)
    ang = pos.astype(np.float32)[:, None] * inv[None, :]
    tab = np.concatenate([np.cos(ang), np.sin(ang)], axis=1).astype(np.float32)
    nt = pos.shape[0] // 128
    return np.ascontiguousarray(tab.reshape(nt, 128, 32).transpose(1, 0, 2))


def _col(g):
    g = np.asarray(g, np.float32).reshape(-1, 128)
    return g.T


def make_in_maps(inputs, cfg):
    NP, SP, SC, NQ, NNA = cfg.NP, cfg.SP, cfg.SC, cfg.NQ, cfg.NNA
    xpr = np.asarray(inputs["x_prompt"], np.float32)
    xsm = np.asarray(inputs["x_sample"], np.float32)
    nquart = SC // NQ
    ncores = xsm.shape[0] * nquart
    assert xpr.shape[0] == NP * ncores
    Rq = NQ // 64
    Rtot = SC // 64
    f = lambda k: np.ascontiguousarray(np.asarray(inputs[k], np.float32)[0])
    gcol = np.zeros((128, 32), np.float32)
    gcol[:, G_ATTN:G_ATTN + 8] = _col(inputs["attn_norm_g"][0])
    gcol[:, G_Q:G_Q + 3] = _col(inputs["q_norm_g"][0])
    gcol[:, G_KV:G_KV + 2] = _col(inputs["kv_norm_g"][0])
    gcol[:, G_MIX:G_MIX + 4] = _col(inputs["mla_out_g"][0])
    gcol[:, G_MIX + 4:G_MIX + 8] = _col(inputs["na_out_g"][0])
    gcol[:, G_FFN:G_FFN + 8] = _col(inputs["ffn_norm_g"][0])
    rpb = np.asarray(inputs["na_rpb"], np.float32)[0]
    kc = np.arange(64)[:, None]
    qc = np.arange(64)[None, :]
    dj = np.clip(kc - qc, -15, 15) + 15
    cs = np.clip(qc - 8, 0, 48)
    m01 = ((kc >= cs) & (kc < cs + 16)).astype(np.float32)
    namask = np.zeros((128, 2, 64), np.float32)
    namask[:, 0, :] = np.tile(m01, (2, 1))
    namask[:, 1, :] = np.tile((1.0 - m01) * NEG, (2, 1))

    def blockT(h, di):
        return rpb[h, di + 7][dj]

    natab = np.zeros((128, 4, 15, 64), np.float32)
    for h in range(H):
        pb, pr = (h % 2) * 64, h // 2
        for j in range(15):
            natab[pb:pb + 64, pr, j, :] = blockT(h, 7 - j)
    natab = natab.reshape(128, 4, 15 * 64)
    shared = dict(w_in=f("w_in"), w_uq=f("w_uq"), w_ukv=f("w_ukv"), w_o=f("w_o"), w_gate=f("w_gate"),
                  w_up=f("w_up"), w_down=f("w_down"), gcol=gcol,
                  gfin=np.asarray(inputs["final_norm_g"], np.float32).reshape(1, D),
                  ident=np.eye(128, dtype=np.float32), natab=natab, namask=namask,
                  ropeP=_rope_table(np.arange(SP)), ropeC=_rope_table(np.arange(SC)))
    maps = []
    for c in range(ncores):
        sq, qt = c // nquart, c % nquart
        m = dict(shared)
        m["xp"] = np.ascontiguousarray(xpr[c * NP:(c + 1) * NP].reshape(NP * SP, D))
        seq = xsm[sq]
        m["xc"] = np.ascontiguousarray(seq)
        r0 = qt * Rq
        rows_b = [4, 5, 6, 7] if qt == 0 else [r0 - 4, r0 - 3, r0 - 2, r0 - 1]
        last = (qt == nquart - 1)
        rows_a = [Rtot - 8, Rtot - 7, Rtot - 6] if last else [r0 + Rq, r0 + Rq + 1, r0 + Rq + 2]
        rows = rows_b + list(range(r0, r0 + Rq)) + rows_a
        xn = np.zeros((NNA, D), np.float32)
        posn = np.zeros((NNA,), np.float32)
        for i, r in enumerate(rows):
            xn[i * 64:(i + 1) * 64] = seq[r * 64:(r + 1) * 64]
            posn[i * 64:(i + 1) * 64] = np.arange(r * 64, (r + 1) * 64)
        m["xn"] = xn
        m["ropeN"] = _rope_table(posn)
        hal = np.zeros((128, 4, 16, 64), np.float32)
        for h in range(H):
            pb, pr = (h % 2) * 64, h // 2
            for ks in range(4):
                for i in range(ks + 1):
                    di = (rows_b[ks] - r0) - i
                    hal[pb:pb + 64, pr, [0, 1, 3, 6][ks] + i, :] = blockT(h, di)
            for j in range(3):
                for n_, i in enumerate(range(Rq - 3 + j, Rq)):
                    di = (rows_a[j] - r0) - i
                    hal[pb:pb + 64, pr, 10 + [0, 3, 5][j] + n_, :] = blockT(h, di)
        m["nahalo"] = hal.reshape(128, 4, 16 * 64)
        maps.append(m)
    return maps


def assemble(results, cfg, nsample):
    NP, SP, SC, NQ = cfg.NP, cfg.SP, cfg.SC, cfg.NQ
    nquart = SC // NQ
    ncores = nsample * nquart
    yp = np.concatenate([np.asarray(results[c]["yp"], np.float32).reshape(NP, SP, D) for c in range(ncores)], axis=0)
    ys = np.zeros((nsample, SC, D), np.float32)
    for c in range(ncores):
        sq, qt = c // nquart, c % nquart
        ys[sq, qt * NQ:(qt + 1) * NQ] = np.asarray(results[c]["yn"], np.float32)
    return yp, ys


_NC_CACHE = {}


def kernel(**inputs):
    cfg = Cfg(NP=2, SP=2048, SC=8192, NQ=2048)
    maps = make_in_maps(inputs, cfg)
    key = (cfg.NP, cfg.SP, cfg.SC, cfg.NQ)
    if key not in _NC_CACHE:
        _NC_CACHE[key] = build_program(cfg)
    nc = _NC_CACHE[key]
    res = run_bass_kernel_spmd(nc, maps, core_ids=list(range(len(maps))))
    return assemble(res.results, cfg, np.asarray(inputs["x_sample"]).shape[0])
```

```python
import numpy as np
import concourse.bass as bass
import concourse.mybir as mybir
from concourse.bass_utils import run_bass_kernel_spmd

F32 = mybir.dt.float32
BF16 = mybir.dt.bfloat16
AF = mybir.ActivationFunctionType
ALU = mybir.AluOpType

QUEUES = ("sync", "scalar", "vector", "gpsimd", "tensor")
NDSEM = {"sync": 16, "gpsimd": 12, "scalar": 4}


class Res:
    __slots__ = ("name", "w", "rc", "rd", "psum")

    def __init__(self, name="", psum=False):
        self.name = name
        self.psum = psum
        self.w = None
        self.rc = {}
        self.rd = []


class Op:
    __slots__ = ("q", "dma", "fn", "deps", "signal", "cidx", "didx", "pos")


class Sched:
    def __init__(self):
        self.ops = {q: [] for q in QUEUES}
        self.ncomp = {q: 0 for q in QUEUES}
        self.ndma = {q: 0 for q in QUEUES}
        self.dma_ops = {q: [] for q in QUEUES}
        self.all_dma = []

    def op(self, q, fn, reads=(), writes=(), dma=False):
        o = Op()
        o.q, o.dma, o.fn, o.signal = q, dma, fn, dma
        deps = set()
        for r in reads:
            if r.w is not None:
                deps.add(r.w)
            if r.psum:
                deps.update(o2 for q2, o2 in r.rc.items() if q2 != q)
        for r in writes:
            if r.w is not None:
                deps.add(r.w)
            deps.update(r.rc.values())
            deps.update(r.rd)
        if dma:
            o.didx = self.ndma[q]
            self.ndma[q] += 1
            R = NDSEM[q]
            if o.didx >= R:
                deps.add(self.dma_ops[q][o.didx - R])
            self.dma_ops[q].append(o)
            self.all_dma.append(o)
            o.cidx = None
        else:
            o.cidx = self.ncomp[q]
            self.ncomp[q] += 1
            o.didx = None
        deps.discard(o)
        if q == "tensor" and not dma:
            deps = {d for d in deps if not (d.q == "tensor" and not d.dma)}
        o.deps = deps
        for d in deps:
            d.signal = True
        for r in reads:
            if dma:
                r.rd.append(o)
            else:
                r.rc[q] = o
        for r in writes:
            r.w = o
            r.rc = {}
            r.rd = []
        o.pos = len(self.ops[q])
        self.ops[q].append(o)
        return o

    def emit(self, nc, block, csem, dsem):
        sched = self
        val = {}
        for q in QUEUES:
            n = 0
            for o in self.ops[q]:
                if o.dma:
                    val[o] = (dsem[q][o.didx % NDSEM[q]], 16 * (o.didx // NDSEM[q] + 1))
                elif o.signal:
                    n += 1
                    val[o] = (csem[q], n)

        def run(q, eng):
            waited = {}
            last_dma = []
            for o in sched.ops[q]:
                need = {}
                for d in o.deps:
                    s, v = val[d]
                    k = id(s)
                    if v > need.get(k, (None, 0))[1]:
                        need[k] = (s, v)
                for k, (s, v) in need.items():
                    if waited.get(k, 0) < v:
                        eng.wait_ge(s, v)
                        waited[k] = v
                ins = o.fn(eng)
                if o.dma:
                    s, v = val[o]
                    ins.then_inc(s, 16)
                elif o.signal:
                    ins.then_inc(csem[q], 1)
            for o in sched.dma_ops[q][-NDSEM.get(q, 0):] if sched.dma_ops[q] else []:
                s, v = val[o]
                if waited.get(id(s), 0) < v:
                    eng.wait_ge(s, v)
                    waited[id(s)] = v

        @block.sync
        def _(e):
            run("sync", e)

        @block.scalar
        def _(e):
            run("scalar", e)

        @block.vector
        def _(e):
            run("vector", e)

        @block.gpsimd
        def _(e):
            run("gpsimd", e)

        @block.tensor
        def _(e):
            run("tensor", e)


D = 1024
NCH = 8
QL, KVL, RD = 384, 256, 32
H = 8
QKD = 96
INC = 2208
C_CQ, C_CKV, C_KR, C_NQ, C_NK, C_NV = 0, 384, 640, 672, 1184, 1696
DFF = 2816
NFF = 22
EPS = 1e-6
NEG = -30000.0
GRID_W = 64
G_ATTN, G_Q, G_KV, G_MIX, G_FFN = 0, 8, 11, 13, 21


class Cfg:
    def __init__(self, NP, SP, SC, NQ):
        self.NP, self.SP, self.SC, self.NQ = NP, SP, SC, NQ
        self.NNA = NQ + 512
        self.QOFF = 256


def na_plan(kind, nrows_q):
    plan = []
    if kind == "P":
        R = nrows_q
        kh = min(8, R)

        def start(r):
            return min(max(r - kh // 2, 0), R - kh)
        for b in range(R // 8):
            items = []
            for kr in range(R):
                rows = [i for i in range(8 * b, 8 * b + 8) if start(i) <= kr <= start(i) + kh - 1]
                if not rows:
                    continue
                i0, i1 = rows[0], rows[-1]
                assert rows == list(range(i0, i1 + 1))
                items.append((kr, i0, i1, "T", 7 - (kr - i0)))
            plan.append(items)
    else:
        R = nrows_q
        for b in range(R // 8):
            items = []
            for ks in range(R + 7):
                rows = [i for i in range(8 * b, 8 * b + 8) if i <= ks <= i + 7]
                if not rows:
                    continue
                i0, i1 = rows[0], rows[-1]
                assert rows == list(range(i0, i1 + 1))
                if ks < 4:
                    hb0 = [0, 1, 3, 6][ks]
                    assert i0 == 0
                    items.append((ks, i0, i1, "H", hb0))
                elif ks >= R + 4:
                    j = ks - (R + 4)
                    hb0 = 10 + [0, 3, 5][j]
                    assert i0 == R - 3 + j and i1 == R - 1
                    items.append((ks, i0, i1, "H", hb0))
                else:
                    kr = ks - 4
                    items.append((ks, i0, i1, "T", 7 - (kr - i0)))
            plan.append(items)
    return plan


def build_program(cfg):
    import contextlib
    nc = bass.Bass("TRN2", target_bir_lowering=False)
    NP, SP, SC, NQ, NNA, QOFF = cfg.NP, cfg.SP, cfg.SC, cfg.NQ, cfg.NNA, cfg.QOFF
    PHASES = getattr(cfg, 'phases', (1, 2, 3))

    def din(name, shape, dt=F32):
        return nc.dram_tensor(name, list(shape), dt, kind="ExternalInput").ap()

    def dscr(name, shape, dt):
        return nc.dram_tensor(name, list(shape), dt, kind="Internal").ap()

    xp = din("xp", [NP * SP, D])
    xc = din("xc", [SC, D])
    xn = din("xn", [NNA, D])
    w_in = din("w_in", [D, INC])
    w_uq = din("w_uq", [QL, H * QKD])
    w_ukv = din("w_ukv", [KVL, H * 128])
    w_o = din("w_o", [D, D])
    w_gate = din("w_gate", [D, DFF])
    w_up = din("w_up", [D, DFF])
    w_down = din("w_down", [DFF, D])
    gcol_d = din("gcol", [128, 32])
    gfin_d = din("gfin", [1, D])
    ident_d = din("ident", [128, 128])
    ropeP_d = din("ropeP", [128, SP // 128, 32])
    ropeC_d = din("ropeC", [128, SC // 128, 32])
    ropeN_d = din("ropeN", [128, NNA // 128, 32])
    natab_d = din("natab", [128, 4, 15 * 64])
    nahalo_d = din("nahalo", [128, 4, 16 * 64])
    namask_d = din("namask", [128, 2, 64])
    yp = nc.dram_tensor("yp", [NP * SP, D], F32, kind="ExternalOutput").ap()
    yn = nc.dram_tensor("yn", [NQ, D], F32, kind="ExternalOutput").ap()

    units = []
    for i in range(NP):
        units.append(dict(kind="P", S=SP, NT=SP, NQ=SP, q0=0, name=f"p{i}",
                          xres=xp[i * SP:(i + 1) * SP, :], yout=yp[i * SP:(i + 1) * SP, :]))
    units.append(dict(kind="S", S=SC, NT=NNA, NQ=NQ, q0=QOFF, name="s",
                      xres=xn[QOFF:QOFF + NQ, :], yout=yn))
    for u in units:
        n = u["name"]
        u["kt"] = dscr("s_kt_" + n, [H, QKD, u["S"]], BF16)
        u["v"] = dscr("s_v_" + n, [H, 128, u["S"] // 128, 65], BF16)
        u["qt"] = dscr("s_qt_" + n, [H, QKD, u["NT"]], BF16)
        u["naq"] = dscr("s_naq_" + n, [128, 4, u["NT"]], BF16)
        u["nak"] = dscr("s_nak_" + n, [128, 4, u["NT"]], BF16)
        u["nav"] = dscr("s_nav_" + n, [64, u["NT"] // 128, 2, 8 * 65], BF16)
        u["ot"] = dscr("s_ot_" + n, [16, 65, u["NQ"]], F32)

    S = Sched()
    top = contextlib.ExitStack()
    with top:
        csem = {q: top.enter_context(nc.semaphore("c_" + q)) for q in QUEUES}
        dsem = {q: [top.enter_context(nc.semaphore(f"d_{q}{i}")) for i in range(n)]
                for q, n in NDSEM.items()}
        block = top.enter_context(nc.Block())

        pending = {q: set() for q in QUEUES}

        def OP(q, fn, reads=(), writes=(), dma=False):
            o = S.op(q, fn, reads, writes, dma)
            if pending[q]:
                extra = {d for d in pending[q] if d is not o}
                if q == "tensor" and not dma:
                    extra = {d for d in extra if not (d.q == "tensor" and not d.dma)}
                for d in extra:
                    d.signal = True
                o.deps |= extra
                pending[q] = set()
            return o

        def barrier():
            B = set()
            for q in QUEUES:
                comp = [o for o in S.ops[q] if not o.dma]
                if comp:
                    B.add(comp[-1])
                if S.dma_ops[q]:
                    B.update(S.dma_ops[q][-NDSEM[q]:])
            for q in QUEUES:
                pending[q] |= B

        def DMA(q, out, in_, reads=(), writes=()):
            return OP(q, lambda e, o=out, i=in_: e.dma_start(out=o, in_=i), reads, writes, dma=True)

        def ACT(out, in_, func, reads=(), writes=(), **kw):
            return OP("scalar", lambda e, o=out, i=in_, f=func, kw=kw: e.activation(out=o, in_=i, func=f, **kw),
                      reads, writes)

        def TS(q, out, in0, s1, s2, op0, op1, reads=(), writes=()):
            if s2 is None:
                return OP(q, lambda e, o=out, i=in0, a=s1, p0=op0:
                          e.tensor_scalar(out=o, in0=i, scalar1=a, scalar2=0.0, op0=p0, op1=ALU.add), reads, writes)
            return OP(q, lambda e, o=out, i=in0, a=s1, b=s2, p0=op0, p1=op1:
                      e.tensor_scalar(out=o, in0=i, scalar1=a, scalar2=b, op0=p0, op1=p1), reads, writes)

        def TT(q, out, in0, in1, op, reads=(), writes=()):
            return OP(q, lambda e, o=out, a=in0, b=in1, p=op: e.tensor_tensor(out=o, in0=a, in1=b, op=p),
                      reads, writes)

        def STT(q, out, in0, scalar, in1, op0, op1, reads=(), writes=()):
            return OP(q, lambda e, o=out, a=in0, s=scalar, b=in1, p0=op0, p1=op1:
                      e.scalar_tensor_tensor(out=o, in0=a, scalar=s, in1=b, op0=p0, op1=p1), reads, writes)

        def CP(q, out, in_, reads=(), writes=()):
            if q == "scalar":
                return ACT(out, in_, AF.Copy, reads, writes)
            return OP(q, lambda e, o=out, i=in_: e.tensor_copy(out=o, in_=i), reads, writes)

        def MM(out, lhsT, rhs, start, stop, reads=(), writes=(), **kw):
            return OP("tensor", lambda e, o=out, l=lhsT, r=rhs, s=start, t=stop, kw=kw:
                      e.matmul(out=o, lhsT=l, rhs=r, start=s, stop=t, **kw), reads, writes)

        def TR(out, in_, idn, reads=(), writes=()):
            return OP("tensor", lambda e, o=out, i=in_, d=idn: e.transpose(out=o, in_=i, identity=d),
                      reads, writes)

        def MEMSET(q, ap, val, writes=()):
            return OP(q, lambda e, a=ap, v=val: e.memset(a, v), (), writes)

        def sb(stack, name, shape, dt):
            return stack.enter_context(nc.sbuf_tensor("sb_" + name, list(shape), dt))

        def psum(stack, name, shape, dt=F32):
            return stack.enter_context(nc.psum_tensor("ps_" + name, list(shape), dt))

        identf = sb(top, "identf", [128, 128], F32)
        identb = sb(top, "identb", [128, 128], BF16)
        gcol = sb(top, "gcol", [128, 32], F32)
        epsc = sb(top, "epsc", [128, 1], F32)
        r_identf, r_identb, r_gcol, r_eps = Res(), Res(), Res(), Res()
        DMA("sync", identf[:], ident_d[:, :], (), [r_identf])
        DMA("sync", gcol[:], gcol_d[:, :], (), [r_gcol])
        CP("vector", identb[:], identf[:], [r_identf], [r_identb])
        MEMSET("vector", epsc[:], EPS, [r_eps])

        def rstd(st, c0, n, R):
            ACT(st[:, c0 + 1:c0 + 2], st[:, c0:c0 + 1], AF.Ln, [R, r_eps], [R], bias=epsc[:, 0:1], scale=1.0 / n)
            ACT(st[:, c0 + 2:c0 + 3], st[:, c0 + 1:c0 + 2], AF.Exp, [R], [R], scale=-0.5)

        def load_weight(wst, r_wst, WST, rot, dst, r_dst, src, nchunk, ncols, gbase, col_scales):
            for c in range(nchunk):
                for c0 in range(0, ncols, WST):
                    c1 = min(ncols, c0 + WST)
                    k = rot[0] % 2
                    rot[0] += 1
                    DMA("sync", wst[k][:, 0:c1 - c0], src[c * 128:(c + 1) * 128, c0:c1], (), [r_wst[k]])
                    for (a, b, const) in col_scales:
                        lo, hi = max(a, c0), min(b, c1)
                        if lo >= hi:
                            continue
                        eng = "gpsimd" if (rot[1] % 2) else "vector"
                        rot[1] += 1
                        if gbase is None:
                            CP(eng, dst[:, c, lo:hi], wst[k][:, lo - c0:hi - c0], [r_wst[k]], [r_dst[c]])
                        else:
                            TS(eng, dst[:, c, lo:hi], wst[k][:, lo - c0:hi - c0], gcol[:, gbase + c:gbase + c + 1],
                               const, ALU.mult, ALU.mult, [r_wst[k], r_gcol], [r_dst[c]])

        WST = 1408
        ph1 = contextlib.ExitStack()
        with ph1 if 1 in PHASES else contextlib.nullcontext():
          if 1 in PHASES:
              wst = [sb(ph1, f"wst{i}", [128, WST], F32) for i in range(2)]
              r_wst = [Res(), Res()]
              rot = [0, 0]
              win = sb(ph1, "win", [128, NCH, INC], BF16)
              wuq = sb(ph1, "wuq", [128, 3, H * QKD], BF16)
              wukv = sb(ph1, "wukv", [128, 2, H * 128], BF16)
              r_win = [Res() for _ in range(NCH)]
              r_wuq = [Res() for _ in range(3)]
              r_wukv = [Res() for _ in range(2)]
              load_weight(wst, r_wst, WST, rot, win, r_win, w_in, NCH, INC, G_ATTN,
                          [(0, C_NQ, 1.0), (C_NQ, C_NK, 0.125), (C_NK, INC, 1.0)])
              load_weight(wst, r_wst, WST, rot, wuq, r_wuq, w_uq, 3, H * QKD, G_Q, [(0, H * QKD, QKD ** -0.5)])
              load_weight(wst, r_wst, WST, rot, wukv, r_wukv, w_ukv, 2, H * 128, G_KV, [(0, H * 128, 1.0)])

              ropeP = sb(ph1, "ropeP", [128, SP // 128, 32], F32)
              ropeC = sb(ph1, "ropeC", [128, SC // 128, 32], F32)
              ropeN = sb(ph1, "ropeN", [128, NNA // 128, 32], F32)
              r_rope = Res()
              DMA("sync", ropeP[:], ropeP_d[:, :, :], (), [r_rope])
              DMA("sync", ropeC[:], ropeC_d[:, :, :], (), [r_rope])
              DMA("sync", ropeN[:], ropeN_d[:, :, :], (), [r_rope])

              NX = 3
              xt = [sb(ph1, f"xt{i}", [128, D], F32) for i in range(NX)]
              r_xt = [Res() for _ in range(NX)]
              junk = [sb(ph1, f"junk{i}", [128, D], BF16) for i in range(2)]
              r_junk = [Res(), Res()]
              jctr = [0]
              hb = [sb(ph1, f"hb{i}", [128, D], BF16) for i in range(2)]
              r_hb = [Res(), Res()]
              hT = [sb(ph1, f"hT{i}", [128, NCH, 512], BF16) for i in range(2)]
              r_hT = [[Res() for _ in range(4)] for _ in range(2)]
              NST = 4
              stt = [sb(ph1, f"stt{i}", [128, 16], F32) for i in range(NST)]
              r_stx = [Res() for _ in range(NST)]
              r_stq = [Res() for _ in range(NST)]
              r_stkv = [Res() for _ in range(NST)]
              cqn = [sb(ph1, f"cqn{i}", [128, QL], BF16) for i in range(2)]
              ckvn = [sb(ph1, f"ckvn{i}", [128, KVL], BF16) for i in range(2)]
              r_cqn = [Res(), Res()]
              r_ckvn = [Res(), Res()]
              cqnT = [sb(ph1, f"cqnT{i}", [128, 3, 128], BF16) for i in range(2)]
              ckvnT = [sb(ph1, f"ckvnT{i}", [128, 2, 128], BF16) for i in range(2)]
              r_cqnT = [Res(), Res()]
              r_ckvnT = [Res(), Res()]
              rtmp = [sb(ph1, f"rtmp{i}", [128, 96 + 512], F32) for i in range(2)]
              r_tc = [Res(), Res()]
              r_ts = [Res(), Res()]
              r_kro = [Res(), Res()]
              r_tcq = [[Res(), Res()], [Res(), Res()]]
              r_tsq = [[Res(), Res()], [Res(), Res()]]
              Ktok = [sb(ph1, f"Ktok{i}", [128, H, QKD], BF16) for i in range(2)]
              Qtok = [sb(ph1, f"Qtok{i}", [128, H, QKD], BF16) for i in range(2)]
              r_Ktok = [Res(), Res()]
              r_Qtok = [Res(), Res()]
              KTst = [sb(ph1, f"KTst{i}", [128, H, 512], BF16) for i in range(2)]
              QTst = [sb(ph1, f"QTst{i}", [128, H, 512], BF16) for i in range(2)]
              Vst = [sb(ph1, f"Vst{i}", [128, H, 4, 65], BF16) for i in range(2)]
              naqst = [sb(ph1, f"naqst{i}", [128, 4, 512], BF16) for i in range(2)]
              nakst = [sb(ph1, f"nakst{i}", [128, 4, 512], BF16) for i in range(2)]
              navst = [sb(ph1, f"navst{i}", [128, 4, H, 65], BF16) for i in range(2)]
              r_KTst = [[Res() for _ in range(4)] for _ in range(2)]
              r_QTst = [[Res() for _ in range(4)] for _ in range(2)]
              r_Vst = [[Res() for _ in range(4)] for _ in range(2)]
              r_naqst = [[Res() for _ in range(4)] for _ in range(2)]
              r_nakst = [[Res() for _ in range(4)] for _ in range(2)]
              r_navst = [[Res() for _ in range(4)] for _ in range(2)]
              for i in range(2):
                  for t in range(4):
                      MEMSET("gpsimd", Vst[i][:, :, t, 64:65], 1.0, [r_Vst[i][t]])
                      MEMSET("gpsimd", navst[i][:, t, :, 64:65], 1.0, [r_navst[i][t]])

              PB = [psum(ph1, f"pb{i}", [128, 512], F32) for i in range(4)]
              PKVU = psum(ph1, "pkvu", [128, 1024], F32)
              PQU = psum(ph1, "pqu", [128, 1024], F32)
              r_PB = [Res(psum=True) for _ in range(4)]
              r_PKVU, r_PQU = Res(psum=True), Res(psum=True)
              ptr_bf = PB[0][:].bitcast(BF16)
              pkt_bf = PB[1][:].bitcast(BF16)
              pqt_bf = PB[2][:].bitcast(BF16)

              tilectr = [0]
              grpctr = [0]
              if getattr(cfg, "verbose", False):
                  print("ph1 sbuf remaining", nc.sbuf_bytes_remaining)

              def square_acc(in_ap, n, acc, reads, writes):
                  j = jctr[0] % 2
                  jctr[0] += 1
                  ACT(junk[j][:, 0:n], in_ap, AF.Square, reads, [r_junk[j]] + writes, accum_out=acc)

              def token_pass(u, src, ntile, rope_sb, do_q, do_kv, do_na):
                  dbg = getattr(cfg, 'dbg', (1, 1, 1))
                  do_q, do_kv, do_na = do_q and dbg[0], do_kv and dbg[1], do_na and dbg[2]
                  ngrp = ntile // 4
                  for g in range(ngrp):
                      gi = grpctr[0] % 2
                      grpctr[0] += 1
                      for t in range(4):
                          tg = g * 4 + t
                          tc_ = tilectr[0]
                          tilectr[0] += 1
                          xi = tc_ % NX
                          k2 = tc_ % 2
                          si = tc_ % NST
                          st = stt[si]
                          tsl = slice(t * 128, (t + 1) * 128)
                          DMA("sync", xt[xi][:], src[tg * 128:(tg + 1) * 128, :], (), [r_xt[xi]])
                          square_acc(xt[xi][:], D, st[:, 0:1], [r_xt[xi]], [r_stx[si]])
                          rstd(st, 0, D, r_stx[si])
                          TS("vector", hb[k2][:], xt[xi][:], st[:, 2:3], None, ALU.mult, None,
                             [r_xt[xi], r_stx[si]], [r_hb[k2]])
                          for c in range(NCH):
                              TR(ptr_bf[:, c * 128:(c + 1) * 128], hb[k2][:, c * 128:(c + 1) * 128], identb[:],
                                 [r_hb[k2], r_identb], [r_PB[0]])
                          CP("scalar", hT[gi][:, :, tsl],
                             ptr_bf[:, 0:1024].rearrange("p (c n) -> p c n", c=NCH), [r_PB[0]], [r_hT[gi][t]])
                          if do_q:
                              for c in range(NCH):
                                  MM(PB[1][:, 0:QL], hT[gi][:, c, tsl], win[:, c, C_CQ:C_CQ + QL],
                                     c == 0, c == NCH - 1, [r_hT[gi][t], r_win[c]], [r_PB[1]])
                          if do_kv:
                              for c in range(NCH):
                                  MM(PB[2][:, 0:KVL + RD], hT[gi][:, c, tsl], win[:, c, C_CKV:C_CKV + KVL + RD],
                                     c == 0, c == NCH - 1, [r_hT[gi][t], r_win[c]], [r_PB[2]])
                          if do_na:
                              for c in range(NCH):
                                  MM(PB[3][:, :], hT[gi][:, c, tsl], win[:, c, C_NV:C_NV + 512],
                                     c == 0, c == NCH - 1, [r_hT[gi][t], r_win[c]], [r_PB[3]])
                              CP("vector", navst[gi][:, t, :, 0:64], PB[3][:, :].rearrange("p (h d) -> p h d", h=H),
                                 [r_PB[3]], [r_navst[gi][t]])
                          if do_q:
                              square_acc(PB[1][:, 0:QL], QL, st[:, 3:4], [r_PB[1]], [r_stq[si]])
                              rstd(st, 3, QL, r_stq[si])
                              TS("vector", cqn[k2][:], PB[1][:, 0:QL], st[:, 5:6], None, ALU.mult, None,
                                 [r_PB[1], r_stq[si]], [r_cqn[k2]])
                          if do_kv:
                              square_acc(PB[2][:, 0:KVL], KVL, st[:, 6:7], [r_PB[2]], [r_stkv[si]])
                              rstd(st, 6, KVL, r_stkv[si])
                              TS("vector", ckvn[k2][:], PB[2][:, 0:KVL], st[:, 8:9], None, ALU.mult, None,
                                 [r_PB[2], r_stkv[si]], [r_ckvn[k2]])
                              rt = rtmp[k2]
                              xk = PB[2][:, KVL:KVL + RD].rearrange("p (a j) -> p a j", a=2)
                              cosb = rope_sb[:, tg, 0:16].unsqueeze(1).broadcast_to([128, 2, 16])
                              sinb = rope_sb[:, tg, 16:32].unsqueeze(1).broadcast_to([128, 2, 16])
                              tcv = rt[:, 0:32].rearrange("p (a j) -> p a j", a=2)
                              tsv = rt[:, 32:64].rearrange("p (a j) -> p a j", a=2)
                              TT("vector", tcv, xk, cosb, ALU.mult, [r_PB[2], r_rope, r_stkv[si]], [r_tc[k2]])
                              TT("vector", tsv, xk, sinb, ALU.mult, [r_PB[2], r_rope, r_stkv[si]], [r_ts[k2]])
                              kro = rt[:, 64:96]
                              TT("vector", kro[:, 0:16], rt[:, 0:16], rt[:, 48:64], ALU.subtract,
                                 [r_tc[k2], r_ts[k2]], [r_kro[k2]])
                              TT("vector", kro[:, 16:32], rt[:, 32:48], rt[:, 16:32], ALU.add,
                                 [r_tc[k2], r_ts[k2]], [r_kro[k2]])
                          if do_q:
                              for c in range(3):
                                  TR(ptr_bf[:, c * 128:(c + 1) * 128], cqn[k2][:, c * 128:(c + 1) * 128], identb[:],
                                     [r_cqn[k2], r_identb], [r_PB[0]])
                              CP("scalar", cqnT[k2][:], ptr_bf[:, 0:384].rearrange("p (c n) -> p c n", c=3),
                                 [r_PB[0]], [r_cqnT[k2]])
                          if do_kv:
                              for c in range(2):
                                  TR(ptr_bf[:, 384 + c * 128:384 + (c + 1) * 128], ckvn[k2][:, c * 128:(c + 1) * 128],
                                     identb[:], [r_ckvn[k2], r_identb], [r_PB[0]])
                              CP("scalar", ckvnT[k2][:], ptr_bf[:, 384:640].rearrange("p (c n) -> p c n", c=2),
                                 [r_PB[0]], [r_ckvnT[k2]])
                          if do_kv:
                              for hh in range(2):
                                  for c in range(2):
                                      MM(PKVU[:, hh * 512:(hh + 1) * 512], ckvnT[k2][:, c, :],
                                         wukv[:, c, hh * 512:(hh + 1) * 512], c == 0, c == 1,
                                         [r_ckvnT[k2], r_wukv[c]], [r_PKVU])
                              kvv = PKVU[:, :].rearrange("p (h d) -> p h d", h=H)
                              CP("vector", Ktok[k2][:, :, 0:64], kvv[:, :, 0:64], [r_PKVU], [r_Ktok[k2]])
                              CP("vector", Ktok[k2][:, :, 64:96], kro.unsqueeze(1).broadcast_to([128, H, RD]),
                                 [r_kro[k2]], [r_Ktok[k2]])
                              CP("scalar", Vst[gi][:, :, t, 0:64], kvv[:, :, 64:128], [r_PKVU], [r_Vst[gi][t]])
                              for h in range(H):
                                  TR(pkt_bf[0:QKD, h * 128:(h + 1) * 128], Ktok[k2][:, h, :], identb[:],
                                     [r_Ktok[k2], r_identb], [r_PB[1]])
                              CP("scalar", KTst[gi][0:QKD, :, tsl],
                                 pkt_bf[0:QKD, :].rearrange("p (h n) -> p h n", h=H), [r_PB[1]], [r_KTst[gi][t]])
                          if do_q:
                              for hh in range(2):
                                  for c in range(3):
                                      MM(PQU[:, hh * 512:hh * 512 + 384], cqnT[k2][:, c, :],
                                         wuq[:, c, hh * 384:(hh + 1) * 384], c == 0, c == 2,
                                         [r_cqnT[k2], r_wuq[c]], [r_PQU])
                              rt = rtmp[k2]
                              for hh in range(2):
                                  qv = PQU[:, hh * 512:hh * 512 + 384].rearrange("p (h d) -> p h d", h=4)
                                  CP("vector", Qtok[k2][:, hh * 4:(hh + 1) * 4, 0:64], qv[:, :, 0:64],
                                     [r_PQU], [r_Qtok[k2]])
                                  xq = qv[:, :, 64:96].rearrange("p h (a j) -> p h a j", a=2)
                                  cosb = rope_sb[:, tg, 0:16].unsqueeze(1).unsqueeze(1).broadcast_to([128, 4, 2, 16])
                                  sinb = rope_sb[:, tg, 16:32].unsqueeze(1).unsqueeze(1).broadcast_to([128, 4, 2, 16])
                                  o0 = 96 + hh * 256
                                  tcq = rt[:, o0:o0 + 128].rearrange("p (h a j) -> p h a j", h=4, a=2)
                                  tsq = rt[:, o0 + 128:o0 + 256].rearrange("p (h a j) -> p h a j", h=4, a=2)
                                  TT("vector", tcq, xq, cosb, ALU.mult, [r_PQU, r_rope], [r_tcq[k2][hh]])
                                  TT("vector", tsq, xq, sinb, ALU.mult, [r_PQU, r_rope], [r_tsq[k2][hh]])
                                  TT("gpsimd", Qtok[k2][:, hh * 4:(hh + 1) * 4, 64:80], tcq[:, :, 0, :], tsq[:, :, 1, :],
                                     ALU.subtract, [r_tcq[k2][hh], r_tsq[k2][hh]], [r_Qtok[k2]])
                                  TT("gpsimd", Qtok[k2][:, hh * 4:(hh + 1) * 4, 80:96], tsq[:, :, 0, :], tcq[:, :, 1, :],
                                     ALU.add, [r_tcq[k2][hh], r_tsq[k2][hh]], [r_Qtok[k2]])
                              for h in range(H):
                                  TR(pqt_bf[0:QKD, h * 128:(h + 1) * 128], Qtok[k2][:, h, :], identb[:],
                                     [r_Qtok[k2], r_identb], [r_PB[2]])
                              CP("vector", QTst[gi][0:QKD, :, tsl],
                                 pqt_bf[0:QKD, :].rearrange("p (h n) -> p h n", h=H), [r_PB[2]], [r_QTst[gi][t]])
                      if do_na:
                          for blk in range(8):
                              col0 = C_NQ + blk * 128
                              for c in range(NCH):
                                  MM(PB[3][:, :], win[:, c, col0:col0 + 128], hT[gi][:, c, :], c == 0, c == NCH - 1,
                                     r_hT[gi] + [r_win[c]], [r_PB[3]])
                              dst = naqst[gi] if blk < 4 else nakst[gi]
                              rr = r_naqst[gi] if blk < 4 else r_nakst[gi]
                              CP("scalar" if blk % 2 else "vector", dst[:, blk % 4, :], PB[3][:, :], [r_PB[3]],
                                 [rr[blk % 4]])
                          DMA("gpsimd", u["naq"][:, :, g * 512:(g + 1) * 512], naqst[gi][:], r_naqst[gi], ())
                          DMA("gpsimd", u["nak"][:, :, g * 512:(g + 1) * 512], nakst[gi][:], r_nakst[gi], ())
                          for half in range(2):
                              DMA("gpsimd", u["nav"][:, g * 4:(g + 1) * 4, half, :],
                                  navst[gi][half * 64:(half + 1) * 64, :, :, :].rearrange("p t h d -> p t (h d)"),
                                  r_navst[gi], ())
                      if do_kv:
                          DMA("gpsimd", u["kt"][:, :, g * 512:(g + 1) * 512].rearrange("h d n -> d h n"),
                              KTst[gi][0:QKD, :, :], r_KTst[gi], ())
                          DMA("gpsimd", u["v"][:, :, g * 4:(g + 1) * 4, :].rearrange("h p t d -> p h t d"),
                              Vst[gi][:], r_Vst[gi], ())
                      if do_q:
                          DMA("gpsimd", u["qt"][:, :, g * 512:(g + 1) * 512].rearrange("h d n -> d h n"),
                              QTst[gi][0:QKD, :, :], r_QTst[gi], ())

              for ui, u in enumerate(units):
                  if u["kind"] == "P":
                      token_pass(u, xp[ui * SP:(ui + 1) * SP, :], SP // 128, ropeP, True, True, True)
                  else:
                      token_pass(u, xn, NNA // 128, ropeN, True, False, True)
                      token_pass(u, xc, SC // 128, ropeC, False, True, False)
        barrier()

        ph2 = contextlib.ExitStack()
        with ph2 if 2 in PHASES else contextlib.nullcontext():
          if 2 in PHASES:
              SMAX = max(u["S"] for u in units)
              NQMAX = max(u["NQ"] for u in units)
              NTMAX = max(u["NT"] for u in units)
              KTb = [sb(ph2, f"KTb{i}", [128, SMAX], BF16) for i in range(2)]
              Vb = [sb(ph2, f"Vb{i}", [128, SMAX // 128, 65], BF16) for i in range(2)]
              QTb = [sb(ph2, f"QTb{i}", [128, NQMAX], BF16) for i in range(2)]
              r_KTb, r_Vb, r_QTb = [Res(), Res()], [Res(), Res()], [Res(), Res()]
              PT = [sb(ph2, f"PT{i}", [128, 1024], BF16) for i in range(3)]
              r_PT = [Res() for _ in range(3)]
              oTs = [sb(ph2, f"oTs{i}", [128, 512], F32) for i in range(2)]
              r_oTs = [Res(), Res()]
              naq = sb(ph2, "naq", [128, 4, NTMAX], BF16)
              nak = sb(ph2, "nak", [128, 4, NTMAX], BF16)
              nav = sb(ph2, "nav", [64, NTMAX // 64, 8 * 65], BF16)
              r_naq, r_nak, r_nav = Res(), Res(), Res()
              tabf = sb(ph2, "tabf", [128, 4, 15 * 64], F32)
              half_ = sb(ph2, "halof", [128, 4, 16 * 64], F32)
              maskf = sb(ph2, "maskf", [128, 2, 64], F32)
              tabb = sb(ph2, "tabb", [128, 4, 15 * 64], BF16)
              halb = sb(ph2, "halob", [128, 4, 16 * 64], BF16)
              r_tabf, r_half, r_maskf, r_tabb, r_halb = Res(), Res(), Res(), Res(), Res()
              NCHAIN = 4
              PTn = [sb(ph2, f"PTn{i}", [64, 512], BF16) for i in range(2 * NCHAIN)]
              r_PTn = [Res() for _ in range(2 * NCHAIN)]

              PS = [psum(ph2, f"ps{i}", [128, 1024], F32) for i in range(2)]
              PO = [psum(ph2, f"po{i}", [128, 512], F32) for i in range(4)]
              r_PSh = [[Res(psum=True), Res(psum=True)] for _ in range(2)]
              r_PO = [Res(psum=True) for _ in range(4)]
              PSn = [PS[0][:, 0:512], PS[0][:, 512:1024], PS[1][:, 0:512], PS[1][:, 512:1024]]
              r_PSn = [r_PSh[0][0], r_PSh[0][1], r_PSh[1][0], r_PSh[1][1]]
              PA = PO
              r_PA = r_PO

              DMA("sync", tabf[:], natab_d[:, :, :], (), [r_tabf])
              DMA("sync", half_[:], nahalo_d[:, :, :], (), [r_half])
              DMA("sync", maskf[:], namask_d[:, :, :], (), [r_maskf])
              for (srcf, dstb, nb, rs, rd) in ((tabf, tabb, 15, r_tabf, r_tabb), (half_, halb, 16, r_half, r_halb)):
                  v = srcf[:].rearrange("p a (b q) -> p (a b) q", q=64)
                  vb = dstb[:].rearrange("p a (b q) -> p (a b) q", q=64)
                  m01 = maskf[:, 0, :].unsqueeze(1).broadcast_to([128, 4 * nb, 64])
                  mng = maskf[:, 1, :].unsqueeze(1).broadcast_to([128, 4 * nb, 64])
                  TT("vector", v, v, m01, ALU.mult, [rs, r_maskf], [rs])
                  TT("vector", vb, v, mng, ALU.add, [rs, r_maskf], [rd])

              ctr = dict(hb=0, pt=0, ps=0, po=0, ot=0, ptn=0)
              if getattr(cfg, "verbose", False):
                  print("ph2 sbuf remaining", nc.sbuf_bytes_remaining)

              def mla_attention(u):
                  Sx, NQu, q0 = u["S"], u["NQ"], u["q0"]
                  nkt = Sx // 128
                  nk2 = nkt // 2
                  nqb = NQu // 512
                  steps = [(h, qb, k2) for h in range(H) for qb in range(nqb) for k2 in range(nk2)]
                  hbase = ctr["hb"]
                  ctr["hb"] += H
                  pobase = ctr["po"]
                  ctr["po"] += H * nqb

                  def load(h):
                      bi = (hbase + h) % 2
                      DMA("sync", KTb[bi][0:QKD, 0:Sx], u["kt"][h, :, :], (), [r_KTb[bi]])
                      DMA("sync", Vb[bi][:, 0:nkt, :], u["v"][h, :, :, :], (), [r_Vb[bi]])
                      DMA("sync", QTb[bi][0:QKD, 0:NQu], u["qt"][h, :, q0:q0 + NQu], (), [r_QTb[bi]])

                  def QK(i):
                      h, qb, k2 = steps[i]
                      bi = (hbase + h) % 2
                      si = i % 2
                      for j in range(2):
                          kt = 2 * k2 + j
                          MM(PS[si][:, j * 512:(j + 1) * 512], KTb[bi][0:QKD, kt * 128:(kt + 1) * 128],
                             QTb[bi][0:QKD, qb * 512:(qb + 1) * 512], True, True,
                             [r_KTb[bi], r_QTb[bi]], [r_PSh[si][j]])

                  def EXPPV(i):
                      h, qb, k2 = steps[i]
                      bi = (hbase + h) % 2
                      si = i % 2
                      pi = i % 3
                      oi = (pobase + h * nqb + qb) % 4
                      ACT(PT[pi][:, :], PS[si][:, :], AF.Exp, r_PSh[si], [r_PT[pi]])
                      for j in range(2):
                          kt = 2 * k2 + j
                          MM(PO[oi][0:65, :], Vb[bi][:, kt, :], PT[pi][:, j * 512:(j + 1) * 512],
                             kt == 0, kt == nkt - 1, [r_Vb[bi], r_PT[pi]], [r_PO[oi]])
                      if k2 == nk2 - 1:
                          ti = ctr["ot"] % 2
                          ctr["ot"] += 1
                          CP("vector", oTs[ti][0:65, :], PO[oi][0:65, :], [r_PO[oi]], [r_oTs[ti]])
                          DMA("gpsimd", u["ot"][h, :, qb * 512:(qb + 1) * 512], oTs[ti][0:65, :], [r_oTs[ti]], ())

                  load(0)
                  QK(0)
                  for i in range(len(steps)):
                      h, qb, k2 = steps[i]
                      if qb == 0 and k2 == 0 and h + 1 < H:
                          load(h + 1)
                      if i + 1 < len(steps):
                          QK(i + 1)
                      EXPPV(i)

              def na_attention(u):
                  NT, NQu = u["NT"], u["NQ"]
                  nrq = NQu // 64
                  plan = na_plan(u["kind"], nrq)
                  qrow0 = 0 if u["kind"] == "P" else 4
                  DMA("sync", naq[:, :, 0:NT], u["naq"][:, :, :], (), [r_naq])
                  DMA("sync", nak[:, :, 0:NT], u["nak"][:, :, :], (), [r_nak])
                  DMA("sync", nav[:, 0:NT // 64, :], u["nav"][:, :, :, :].rearrange("p t a d -> p (t a) d"), (), [r_nav])
                  chains = [(h, b) for b in range(len(plan)) for h in range(H)]
                  for c0_ in range(0, len(chains), NCHAIN):
                      grp = chains[c0_:c0_ + NCHAIN]
                      nst = max(len(plan[b]) for (h, b) in grp)

                      def QKB(j, s):
                          h, b = grp[j]
                          items = plan[b]
                          if s >= len(items):
                              return
                          ks, i0, i1, src, blk0 = items[s]
                          pr, pb = h // 2, (h % 2) * 64
                          N = 64 * (i1 - i0 + 1)
                          qtok0 = (qrow0 + i0) * 64
                          MM(PSn[j][0:64, 0:N], nak[pb:pb + 64, pr, ks * 64:(ks + 1) * 64],
                             naq[pb:pb + 64, pr, qtok0:qtok0 + N], True, False, [r_nak, r_naq], [r_PSn[j]])
                          tb = tabb if src == "T" else halb
                          rtb = r_tabb if src == "T" else r_halb
                          MM(PSn[j][0:64, 0:N], identb[pb:pb + 64, pb:pb + 64],
                             tb[pb:pb + 64, pr, blk0 * 64:blk0 * 64 + N], False, True, [rtb, r_identb], [r_PSn[j]])

                      def EXPN(j, s):
                          h, b = grp[j]
                          items = plan[b]
                          if s >= len(items):
                              return
                          ks, i0, i1, src, blk0 = items[s]
                          N = 64 * (i1 - i0 + 1)
                          pi = 2 * j + (s % 2)
                          ACT(PTn[pi][:, 0:N], PSn[j][0:64, 0:N], AF.Exp, [r_PSn[j]], [r_PTn[pi]])

                      def PVN(j, s):
                          h, b = grp[j]
                          items = plan[b]
                          if s >= len(items):
                              return
                          ks, i0, i1, src, blk0 = items[s]
                          N = 64 * (i1 - i0 + 1)
                          c0 = (i0 - 8 * b) * 64
                          pi = 2 * j + (s % 2)
                          MM(PA[j][0:65, c0:c0 + N], nav[:, ks, h * 65:(h + 1) * 65], PTn[pi][:, 0:N],
                             s == 0, s == len(items) - 1, [r_nav, r_PTn[pi]], [r_PA[j]], skip_group_check=True)
                          if s == len(items) - 1:
                              ti = ctr["ot"] % 2
                              ctr["ot"] += 1
                              CP("vector", oTs[ti][0:65, :], PA[j][0:65, :], [r_PA[j]], [r_oTs[ti]])
                              DMA("gpsimd", u["ot"][8 + h, :, b * 512:(b + 1) * 512], oTs[ti][0:65, :],
                                  [r_oTs[ti]], ())

                      for j in range(len(grp)):
                          QKB(j, 0)
                      for s in range(nst):
                          for j in range(len(grp)):
                              EXPN(j, s)
                          for j in range(len(grp)):
                              QKB(j, s + 1)
                          for j in range(len(grp)):
                              PVN(j, s)

              for u in units:
                  na_attention(u)
                  mla_attention(u)
        barrier()

        ph3 = contextlib.ExitStack()
        with ph3 if 3 in PHASES else contextlib.nullcontext():
          if 3 in PHASES:
              WST = 352
              wst = [sb(ph3, f"wst3_{i}", [128, WST], F32) for i in range(2)]
              r_wst = [Res(), Res()]
              rot = [0, 0]
              wo = sb(ph3, "wo", [128, NCH, D], BF16)
              wg = sb(ph3, "wg", [128, NCH, DFF], BF16)
              wu = sb(ph3, "wu", [128, NCH, DFF], BF16)
              wd = sb(ph3, "wd", [128, NFF, D], BF16)
              r_wo = [Res() for _ in range(NCH)]
              r_wg = [Res() for _ in range(NCH)]
              r_wu = [Res() for _ in range(NCH)]
              r_wd = [Res() for _ in range(NFF)]
              load_weight(wst, r_wst, WST, rot, wo, r_wo, w_o, NCH, D, G_MIX, [(0, D, 1.0)])
              load_weight(wst, r_wst, WST, rot, wg, r_wg, w_gate, NCH, DFF, G_FFN, [(0, DFF, 1.0)])
              load_weight(wst, r_wst, WST, rot, wu, r_wu, w_up, NCH, DFF, G_FFN, [(0, DFF, 1.0)])
              load_weight(wst, r_wst, WST, rot, wd, r_wd, w_down, NFF, D, None, [(0, D, 1.0)])
              gfin = sb(ph3, "gfin", [128, D], F32)
              r_gfin = Res()
              DMA("sync", gfin[:], gfin_d[0:1, :].partition_broadcast(128), (), [r_gfin])
              onec = sb(ph3, "onec", [128, 1], F32)
              r_one = Res()
              MEMSET("vector", onec[:], 1.0, [r_one])

              TG = 2
              NTOK = TG * 128
              oTin = sb(ph3, "oTin", [128, 4, 128], F32)
              r_oTin = Res()
              atok = sb(ph3, "atok", [128, D], F32)
              r_atok = Res()
              x1 = [[sb(ph3, f"x1_{s_}_{i}", [128, D], F32) for i in range(TG)] for s_ in range(2)]
              r_x1 = [[Res() for _ in range(TG)] for _ in range(2)]
              mixb = sb(ph3, "mixb", [128, D], BF16)
              r_mixb = Res()
              mixT = sb(ph3, "mixT", [128, NCH, 128], BF16)
              r_mixT = Res()
              h2b = sb(ph3, "h2b", [128, D], BF16)
              r_h2b = Res()
              h2T = [sb(ph3, f"h2T{s_}", [128, NCH, NTOK], BF16) for s_ in range(2)]
              r_h2T = [[Res() for _ in range(TG)] for _ in range(2)]
              actT = sb(ph3, "actT", [128, NFF, NTOK], BF16)
              r_actT = [Res() for _ in range(NFF)]
              sg = [sb(ph3, f"sg{i}", [128, NTOK], F32) for i in range(2)]
              r_sg = [Res(), Res()]
              st3 = [sb(ph3, f"st3_{i}", [128, 32], F32) for i in range(4)]
              r_st3 = [[Res() for _ in range(6)] for _ in range(4)]

              POh = psum(ph3, "poh", [128, 512], F32)
              r_POh = Res(psum=True)
              PTR = psum(ph3, "ptr3", [128, 512], F32)
              r_PTR = Res(psum=True)
              ptr3_bf = PTR[:].bitcast(BF16)
              PY = psum(ph3, "py", [128, 1024], F32)
              r_PY = Res(psum=True)
              PG = [psum(ph3, f"pg{i}", [128, 512], F32) for i in range(2)]
              PU = [psum(ph3, f"pu{i}", [128, 512], F32) for i in range(2)]
              r_PG, r_PU = [Res(psum=True), Res(psum=True)], [Res(psum=True), Res(psum=True)]
              if getattr(cfg, "verbose", False):
                  print("ph3 sbuf remaining", nc.sbuf_bytes_remaining)

              def prologue(u, g, slot):
                  for t in range(TG):
                      tok0 = g * NTOK + t * 128
                      X, rX = x1[slot][t], r_x1[slot][t]
                      st = st3[slot * 2 + t]
                      rs_ = r_st3[slot * 2 + t]
                      DMA("sync", X[:], u["xres"][tok0:tok0 + 128, :], (), [rX])
                      for hq in range(4):
                          DMA("sync", oTin[0:65, :, :],
                              u["ot"][hq * 4:(hq + 1) * 4, :, tok0:tok0 + 128].rearrange("h d n -> d h n"),
                              (), [r_oTin])
                          yield
                          for hh in range(4):
                              TR(POh[:, hh * 128:hh * 128 + 65], oTin[0:65, hh, :], identf[0:65, 0:65],
                                 [r_oTin, r_identf], [r_POh])
                          pov = POh[:, :].rearrange("p (h d) -> p h d", h=4)
                          rc = st[:, 8:12].unsqueeze(2)
                          OP("vector", lambda e, o=rc, i=pov[:, :, 64:65]: e.reciprocal(out=o, in_=i),
                             [r_POh], [rs_[4]])
                          TT("vector", atok[:, hq * 256:(hq + 1) * 256].rearrange("p (h d) -> p h d", h=4),
                             pov[:, :, 0:64], rc.broadcast_to([128, 4, 64]), ALU.mult,
                             [r_POh, rs_[4]], [r_atok])
                      for half in range(2):
                          hs = slice(half * 512, (half + 1) * 512)
                          ACT(mixb[:, hs], atok[:, hs], AF.Square, [r_atok], [r_mixb, rs_[half]],
                              accum_out=st[:, 3 * half:3 * half + 1])
                          rstd(st, 3 * half, 512, rs_[half])
                          TS("vector", mixb[:, hs], atok[:, hs], st[:, 3 * half + 2:3 * half + 3], None, ALU.mult, None,
                             [r_atok, rs_[half]], [r_mixb])
                      yield
                      for c in range(NCH):
                          TR(ptr3_bf[:, c * 128:(c + 1) * 128], mixb[:, c * 128:(c + 1) * 128], identb[:],
                             [r_mixb, r_identb], [r_PTR])
                      CP("scalar", mixT[:], ptr3_bf[:, 0:1024].rearrange("p (c n) -> p c n", c=NCH),
                         [r_PTR], [r_mixT])
                      yield
                      for hh in range(2):
                          for c in range(NCH):
                              MM(PY[:, hh * 512:(hh + 1) * 512], mixT[:, c, :], wo[:, c, hh * 512:(hh + 1) * 512],
                                 c == 0, c == NCH - 1, [r_mixT, r_wo[c]], [r_PY])
                      TT("vector", X[:], X[:], PY[:, :], ALU.add, [rX, r_PY], [rX])
                      ACT(h2b[:], X[:], AF.Square, [rX], [r_h2b, rs_[2]], accum_out=st[:, 16:17])
                      rstd(st, 16, D, rs_[2])
                      TS("vector", h2b[:], X[:], st[:, 18:19], None, ALU.mult, None, [rX, rs_[2]], [r_h2b])
                      yield
                      for c in range(NCH):
                          TR(ptr3_bf[:, c * 128:(c + 1) * 128], h2b[:, c * 128:(c + 1) * 128], identb[:],
                             [r_h2b, r_identb], [r_PTR])
                      CP("scalar", h2T[slot][:, :, t * 128:(t + 1) * 128],
                         ptr3_bf[:, 0:1024].rearrange("p (c n) -> p c n", c=NCH), [r_PTR], [r_h2T[slot][t]])
                      yield

              def ffn_step(slot, f):
                  pi = f % 2
                  for c in range(NCH):
                      MM(PG[pi][:, 0:NTOK], wg[:, c, f * 128:(f + 1) * 128], h2T[slot][:, c, :], c == 0, c == NCH - 1,
                         r_h2T[slot] + [r_wg[c]], [r_PG[pi]])
                  for c in range(NCH):
                      MM(PU[pi][:, 0:NTOK], wu[:, c, f * 128:(f + 1) * 128], h2T[slot][:, c, :], c == 0, c == NCH - 1,
                         r_h2T[slot] + [r_wu[c]], [r_PU[pi]])
                  ACT(sg[pi][:], PG[pi][:, 0:NTOK], AF.Exp, [r_PG[pi]], [r_sg[pi]], scale=-1.0)
                  ACT(sg[pi][:], sg[pi][:], AF.Ln, [r_sg[pi], r_one], [r_sg[pi]], bias=onec[:, 0:1])
                  ACT(sg[pi][:], sg[pi][:], AF.Exp, [r_sg[pi]], [r_sg[pi]], scale=-1.0)
                  TT("vector", sg[pi][:], sg[pi][:], PG[pi][:, 0:NTOK], ALU.mult, [r_sg[pi], r_PG[pi]], [r_sg[pi]])
                  TT("vector", actT[:, f, :], sg[pi][:], PU[pi][:, 0:NTOK], ALU.mult,
                     [r_sg[pi], r_PU[pi]], [r_actT[f]])

              def down_epi(u, g, slot, t):
                  tok0 = g * NTOK + t * 128
                  X, rX = x1[slot][t], r_x1[slot][t]
                  st = st3[slot * 2 + t]
                  rr = r_st3[slot * 2 + t][3]
                  for hh in range(2):
                      for f in range(NFF):
                          MM(PY[:, hh * 512:(hh + 1) * 512], actT[:, f, t * 128:(t + 1) * 128],
                             wd[:, f, hh * 512:(hh + 1) * 512], f == 0, f == NFF - 1,
                             [r_actT[f], r_wd[f]], [r_PY])
                  TT("vector", X[:], X[:], PY[:, :], ALU.add, [rX, r_PY], [rX])
                  ACT(h2b[:], X[:], AF.Square, [rX], [r_h2b, rr], accum_out=st[:, 24:25])
                  rstd(st, 24, D, rr)
                  STT("vector", X[:], X[:], st[:, 26:27], gfin[:], ALU.mult, ALU.mult, [rX, rr, r_gfin], [rX])
                  DMA("gpsimd", u["yout"][tok0:tok0 + 128, :], X[:], [rX], ())

              groups = [(u, g) for u in units for g in range(u["NQ"] // NTOK)]
              cur = prologue(groups[0][0], groups[0][1], 0)
              for _ in cur:
                  pass
              for gi_, (u, g) in enumerate(groups):
                  slot = gi_ % 2
                  nxt = None
                  if gi_ + 1 < len(groups):
                      nxt = prologue(groups[gi_ + 1][0], groups[gi_ + 1][1], 1 - slot)
                  for f in range(NFF):
                      ffn_step(slot, f)
                      if nxt is not None:
                          next(nxt, None)
                  for t in range(TG):
                      down_epi(u, g, slot, t)
                      if nxt is not None:
                          next(nxt, None)
                  if nxt is not None:
                      for _ in nxt:
                          pass
        S.emit(nc, block, csem, dsem)
    return nc


def _rope_table(pos):
    inv = (10000.0 ** (-np.arange(0, RD, 2, dtype=np.float32) / np.float32(RD))).astype(np.float32)
    ang = pos.astype(np.float32)[:, None] * inv[None, :]
    tab = np.concatenate([np.cos(ang), np.sin(ang)], axis=1).astype(np.float32)
    nt = pos.shape[0] // 128
    return np.ascontiguousarray(tab.reshape(nt, 128, 32).transpose(1, 0, 2))


def _col(g):
    g = np.asarray(g, np.float32).reshape(-1, 128)
    return g.T


def make_in_maps(inputs, cfg):
    NP, SP, SC, NQ, NNA = cfg.NP, cfg.SP, cfg.SC, cfg.NQ, cfg.NNA
    xpr = np.asarray(inputs["x_prompt"], np.float32)
    xsm = np.asarray(inputs["x_sample"], np.float32)
    nquart = SC // NQ
    ncores = xsm.shape[0] * nquart
    assert xpr.shape[0] == NP * ncores
    Rq = NQ // 64
    Rtot = SC // 64
    f = lambda k: np.ascontiguousarray(np.asarray(inputs[k], np.float32)[0])
    gcol = np.zeros((128, 32), np.float32)
    gcol[:, G_ATTN:G_ATTN + 8] = _col(inputs["attn_norm_g"][0])
    gcol[:, G_Q:G_Q + 3] = _col(inputs["q_norm_g"][0])
    gcol[:, G_KV:G_KV + 2] = _col(inputs["kv_norm_g"][0])
    gcol[:, G_MIX:G_MIX + 4] = _col(inputs["mla_out_g"][0])
    gcol[:, G_MIX + 4:G_MIX + 8] = _col(inputs["na_out_g"][0])
    gcol[:, G_FFN:G_FFN + 8] = _col(inputs["ffn_norm_g"][0])
    rpb = np.asarray(inputs["na_rpb"], np.float32)[0]
    kc = np.arange(64)[:, None]
    qc = np.arange(64)[None, :]
    dj = np.clip(kc - qc, -15, 15) + 15
    cs = np.clip(qc - 8, 0, 48)
    m01 = ((kc >= cs) & (kc < cs + 16)).astype(np.float32)
    namask = np.zeros((128, 2, 64), np.float32)
    namask[:, 0, :] = np.tile(m01, (2, 1))
    namask[:, 1, :] = np.tile((1.0 - m01) * NEG, (2, 1))

    def blockT(h, di):
        return rpb[h, di + 7][dj]

    natab = np.zeros((128, 4, 15, 64), np.float32)
    for h in range(H):
        pb, pr = (h % 2) * 64, h // 2
        for j in range(15):
            natab[pb:pb + 64, pr, j, :] = blockT(h, 7 - j)
    natab = natab.reshape(128, 4, 15 * 64)
    shared = dict(w_in=f("w_in"), w_uq=f("w_uq"), w_ukv=f("w_ukv"), w_o=f("w_o"), w_gate=f("w_gate"),
                  w_up=f("w_up"), w_down=f("w_down"), gcol=gcol,
                  gfin=np.asarray(inputs["final_norm_g"], np.float32).reshape(1, D),
                  ident=np.eye(128, dtype=np.float32), natab=natab, namask=namask,
                  ropeP=_rope_table(np.arange(SP)), ropeC=_rope_table(np.arange(SC)))
    maps = []
    for c in range(ncores):
        sq, qt = c // nquart, c % nquart
        m = dict(shared)
        m["xp"] = np.ascontiguousarray(xpr[c * NP:(c + 1) * NP].reshape(NP * SP, D))
        seq = xsm[sq]
        m["xc"] = np.ascontiguousarray(seq)
        r0 = qt * Rq
        rows_b = [4, 5, 6, 7] if qt == 0 else [r0 - 4, r0 - 3, r0 - 2, r0 - 1]
        last = (qt == nquart - 1)
        rows_a = [Rtot - 8, Rtot - 7, Rtot - 6] if last else [r0 + Rq, r0 + Rq + 1, r0 + Rq + 2]
        rows = rows_b + list(range(r0, r0 + Rq)) + rows_a
        xn = np.zeros((NNA, D), np.float32)
        posn = np.zeros((NNA,), np.float32)
        for i, r in enumerate(rows):
            xn[i * 64:(i + 1) * 64] = seq[r * 64:(r + 1) * 64]
            posn[i * 64:(i + 1) * 64] = np.arange(r * 64, (r + 1) * 64)
        m["xn"] = xn
        m["ropeN"] = _rope_table(posn)
        hal = np.zeros((128, 4, 16, 64), np.float32)
        for h in range(H):
            pb, pr = (h % 2) * 64, h // 2
            for ks in range(4):
                for i in range(ks + 1):
                    di = (rows_b[ks] - r0) - i
                    hal[pb:pb + 64, pr, [0, 1, 3, 6][ks] + i, :] = blockT(h, di)
            for j in range(3):
                for n_, i in enumerate(range(Rq - 3 + j, Rq)):
                    di = (rows_a[j] - r0) - i
                    hal[pb:pb + 64, pr, 10 + [0, 3, 5][j] + n_, :] = blockT(h, di)
        m["nahalo"] = hal.reshape(128, 4, 16 * 64)
        maps.append(m)
    return maps


def assemble(results, cfg, nsample):
    NP, SP, SC, NQ = cfg.NP, cfg.SP, cfg.SC, cfg.NQ
    nquart = SC // NQ
    ncores = nsample * nquart
    yp = np.concatenate([np.asarray(results[c]["yp"], np.float32).reshape(NP, SP, D) for c in range(ncores)], axis=0)
    ys = np.zeros((nsample, SC, D), np.float32)
    for c in range(ncores):
        sq, qt = c // nquart, c % nquart
        ys[sq, qt * NQ:(qt + 1) * NQ] = np.asarray(results[c]["yn"], np.float32)
    return yp, ys


_NC_CACHE = {}


def kernel(**inputs):
    cfg = Cfg(NP=2, SP=2048, SC=8192, NQ=2048)
    maps = make_in_maps(inputs, cfg)
    key = (cfg.NP, cfg.SP, cfg.SC, cfg.NQ)
    if key not in _NC_CACHE:
        _NC_CACHE[key] = build_program(cfg)
    nc = _NC_CACHE[key]
    res = run_bass_kernel_spmd(nc, maps, core_ids=list(range(len(maps))))
    return assemble(res.results, cfg, np.asarray(inputs["x_sample"]).shape[0])
```

```python
import numpy as np
import concourse.bass as bass
import concourse.mybir as mybir
from concourse.bass_utils import run_bass_kernel_spmd

F32 = mybir.dt.float32
BF16 = mybir.dt.bfloat16
AF = mybir.ActivationFunctionType
ALU = mybir.AluOpType

QUEUES = ("sync", "scalar", "vector", "gpsimd", "tensor")
NDSEM = {"sync": 16, "gpsimd": 12, "scalar": 4}


class Res:
    __slots__ = ("name", "w", "rc", "rd", "psum")

    def __init__(self, name="", psum=False):
        self.name = name
        self.psum = psum
        self.w = None
        self.rc = {}
        self.rd = []


class Op:
    __slots__ = ("q", "dma", "fn", "deps", "signal", "cidx", "didx", "pos")


class Sched:
    def __init__(self):
        self.ops = {q: [] for q in QUEUES}
        self.ncomp = {q: 0 for q in QUEUES}
        self.ndma = {q: 0 for q in QUEUES}
        self.dma_ops = {q: [] for q in QUEUES}
        self.all_dma = []

    def op(self, q, fn, reads=(), writes=(), dma=False):
        o = Op()
        o.q, o.dma, o.fn, o.signal = q, dma, fn, dma
        deps = set()
        for r in reads:
            if r.w is not None:
                deps.add(r.w)
            if r.psum:
                deps.update(o2 for q2, o2 in r.rc.items() if q2 != q)
        for r in writes:
            if r.w is not None:
                deps.add(r.w)
            deps.update(r.rc.values())
            deps.update(r.rd)
        if dma:
            o.didx = self.ndma[q]
            self.ndma[q] += 1
            R = NDSEM[q]
            if o.didx >= R:
                deps.add(self.dma_ops[q][o.didx - R])
            self.dma_ops[q].append(o)
            self.all_dma.append(o)
            o.cidx = None
        else:
            o.cidx = self.ncomp[q]
            self.ncomp[q] += 1
            o.didx = None
        deps.discard(o)
        if q == "tensor" and not dma:
            deps = {d for d in deps if not (d.q == "tensor" and not d.dma)}
        o.deps = deps
        for d in deps:
            d.signal = True
        for r in reads:
            if dma:
                r.rd.append(o)
            else:
                r.rc[q] = o
        for r in writes:
            r.w = o
            r.rc = {}
            r.rd = []
        o.pos = len(self.ops[q])
        self.ops[q].append(o)
        return o

    def emit(self, nc, block, csem, dsem):
        sched = self
        val = {}
        for q in QUEUES:
            n = 0
            for o in self.ops[q]:
                if o.dma:
                    val[o] = (dsem[q][o.didx % NDSEM[q]], 16 * (o.didx // NDSEM[q] + 1))
                elif o.signal:
                    n += 1
                    val[o] = (csem[q], n)

        def run(q, eng):
            waited = {}
            last_dma = []
            for o in sched.ops[q]:
                need = {}
                for d in o.deps:
                    s, v = val[d]
                    k = id(s)
                    if v > need.get(k, (None, 0))[1]:
                        need[k] = (s, v)
                for k, (s, v) in need.items():
                    if waited.get(k, 0) < v:
                        eng.wait_ge(s, v)
                        waited[k] = v
                ins = o.fn(eng)
                if o.dma:
                    s, v = val[o]
                    ins.then_inc(s, 16)
                elif o.signal:
                    ins.then_inc(csem[q], 1)
            for o in sched.dma_ops[q][-NDSEM.get(q, 0):] if sched.dma_ops[q] else []:
                s, v = val[o]
                if waited.get(id(s), 0) < v:
                    eng.wait_ge(s, v)
                    waited[id(s)] = v

        @block.sync
        def _(e):
            run("sync", e)

        @block.scalar
        def _(e):
            run("scalar", e)

        @block.vector
        def _(e):
            run("vector", e)

        @block.gpsimd
        def _(e):
            run("gpsimd", e)

        @block.tensor
        def _(e):
            run("tensor", e)


D = 1024
NCH = 8
QL, KVL, RD = 384, 256, 32
H = 8
QKD = 96
INC = 2208
C_CQ, C_CKV, C_KR, C_NQ, C_NK, C_NV = 0, 384, 640, 672, 1184, 1696
DFF = 2816
NFF = 22
EPS = 1e-6
NEG = -30000.0
GRID_W = 64
G_ATTN, G_Q, G_KV, G_MIX, G_FFN = 0, 8, 11, 13, 21


class Cfg:
    def __init__(self, NP, SP, SC, NQ):
        self.NP, self.SP, self.SC, self.NQ = NP, SP, SC, NQ
        self.NNA = NQ + 512
        self.QOFF = 256


def na_plan(kind, nrows_q):
    plan = []
    if kind == "P":
        R = nrows_q
        kh = min(8, R)

        def start(r):
            return min(max(r - kh // 2, 0), R - kh)
        for b in range(R // 8):
            items = []
            for kr in range(R):
                rows = [i for i in range(8 * b, 8 * b + 8) if start(i) <= kr <= start(i) + kh - 1]
                if not rows:
                    continue
                i0, i1 = rows[0], rows[-1]
                assert rows == list(range(i0, i1 + 1))
                items.append((kr, i0, i1, "T", 7 - (kr - i0)))
            plan.append(items)
    else:
        R = nrows_q
        for b in range(R // 8):
            items = []
            for ks in range(R + 7):
                rows = [i for i in range(8 * b, 8 * b + 8) if i <= ks <= i + 7]
                if not rows:
                    continue
                i0, i1 = rows[0], rows[-1]
                assert rows == list(range(i0, i1 + 1))
                if ks < 4:
                    hb0 = [0, 1, 3, 6][ks]
                    assert i0 == 0
                    items.append((ks, i0, i1, "H", hb0))
                elif ks >= R + 4:
                    j = ks - (R + 4)
                    hb0 = 10 + [0, 3, 5][j]
                    assert i0 == R - 3 + j and i1 == R - 1
                    items.append((ks, i0, i1, "H", hb0))
                else:
                    kr = ks - 4
                    items.append((ks, i0, i1, "T", 7 - (kr - i0)))
            plan.append(items)
    return plan


def build_program(cfg):
    import contextlib
    nc = bass.Bass("TRN2", target_bir_lowering=False)
    NP, SP, SC, NQ, NNA, QOFF = cfg.NP, cfg.SP, cfg.SC, cfg.NQ, cfg.NNA, cfg.QOFF
    PHASES = getattr(cfg, 'phases', (1, 2, 3))

    def din(name, shape, dt=F32):
        return nc.dram_tensor(name, list(shape), dt, kind="ExternalInput").ap()

    def dscr(name, shape, dt):
        return nc.dram_tensor(name, list(shape), dt, kind="Internal").ap()

    xp = din("xp", [NP * SP, D])
    xc = din("xc", [SC, D])
    xn = din("xn", [NNA, D])
    w_in = din("w_in", [D, INC])
    w_uq = din("w_uq", [QL, H * QKD])
    w_ukv = din("w_ukv", [KVL, H * 128])
    w_o = din("w_o", [D, D])
    w_gate = din("w_gate", [D, DFF])
    w_up = din("w_up", [D, DFF])
    w_down = din("w_down", [DFF, D])
    gcol_d = din("gcol", [128, 32])
    gfin_d = din("gfin", [1, D])
    ident_d = din("ident", [128, 128])
    ropeP_d = din("ropeP", [128, SP // 128, 32])
    ropeC_d = din("ropeC", [128, SC // 128, 32])
    ropeN_d = din("ropeN", [128, NNA // 128, 32])
    natab_d = din("natab", [128, 4, 15 * 64])
    nahalo_d = din("nahalo", [128, 4, 16 * 64])
    namask_d = din("namask", [128, 2, 64])
    yp = nc.dram_tensor("yp", [NP * SP, D], F32, kind="ExternalOutput").ap()
    yn = nc.dram_tensor("yn", [NQ, D], F32, kind="ExternalOutput").ap()

    units = []
    for i in range(NP):
        units.append(dict(kind="P", S=SP, NT=SP, NQ=SP, q0=0, name=f"p{i}",
                          xres=xp[i * SP:(i + 1) * SP, :], yout=yp[i * SP:(i + 1) * SP, :]))
    units.append(dict(kind="S", S=SC, NT=NNA, NQ=NQ, q0=QOFF, name="s",
                      xres=xn[QOFF:QOFF + NQ, :], yout=yn))
    for u in units:
        n = u["name"]
        u["kt"] = dscr("s_kt_" + n, [H, QKD, u["S"]], BF16)
        u["v"] = dscr("s_v_" + n, [H, 128, u["S"] // 128, 65], BF16)
        u["qt"] = dscr("s_qt_" + n, [H, QKD, u["NT"]], BF16)
        u["naq"] = dscr("s_naq_" + n, [128, 4, u["NT"]], BF16)
        u["nak"] = dscr("s_nak_" + n, [128, 4, u["NT"]], BF16)
        u["nav"] = dscr("s_nav_" + n, [64, u["NT"] // 128, 2, 8 * 65], BF16)
        u["ot"] = dscr("s_ot_" + n, [16, 65, u["NQ"]], F32)

    S = Sched()
    top = contextlib.ExitStack()
    with top:
        csem = {q: top.enter_context(nc.semaphore("c_" + q)) for q in QUEUES}
        dsem = {q: [top.enter_context(nc.semaphore(f"d_{q}{i}")) for i in range(n)]
                for q, n in NDSEM.items()}
        block = top.enter_context(nc.Block())

        pending = {q: set() for q in QUEUES}

        def OP(q, fn, reads=(), writes=(), dma=False):
            o = S.op(q, fn, reads, writes, dma)
            if pending[q]:
                extra = {d for d in pending[q] if d is not o}
                if q == "tensor" and not dma:
                    extra = {d for d in extra if not (d.q == "tensor" and not d.dma)}
                for d in extra:
                    d.signal = True
                o.deps |= extra
                pending[q] = set()
            return o

        def barrier():
            B = set()
            for q in QUEUES:
                comp = [o for o in S.ops[q] if not o.dma]
                if comp:
                    B.add(comp[-1])
                if S.dma_ops[q]:
                    B.update(S.dma_ops[q][-NDSEM[q]:])
            for q in QUEUES:
                pending[q] |= B

        def DMA(q, out, in_, reads=(), writes=()):
            return OP(q, lambda e, o=out, i=in_: e.dma_start(out=o, in_=i), reads, writes, dma=True)

        def ACT(out, in_, func, reads=(), writes=(), **kw):
            return OP("scalar", lambda e, o=out, i=in_, f=func, kw=kw: e.activation(out=o, in_=i, func=f, **kw),
                      reads, writes)

        def TS(q, out, in0, s1, s2, op0, op1, reads=(), writes=()):
            if s2 is None:
                return OP(q, lambda e, o=out, i=in0, a=s1, p0=op0:
                          e.tensor_scalar(out=o, in0=i, scalar1=a, scalar2=0.0, op0=p0, op1=ALU.add), reads, writes)
            return OP(q, lambda e, o=out, i=in0, a=s1, b=s2, p0=op0, p1=op1:
                      e.tensor_scalar(out=o, in0=i, scalar1=a, scalar2=b, op0=p0, op1=p1), reads, writes)

        def TT(q, out, in0, in1, op, reads=(), writes=()):
            return OP(q, lambda e, o=out, a=in0, b=in1, p=op: e.tensor_tensor(out=o, in0=a, in1=b, op=p),
                      reads, writes)

        def STT(q, out, in0, scalar, in1, op0, op1, reads=(), writes=()):
            return OP(q, lambda e, o=out, a=in0, s=scalar, b=in1, p0=op0, p1=op1:
                      e.scalar_tensor_tensor(out=o, in0=a, scalar=s, in1=b, op0=p0, op1=p1), reads, writes)

        def CP(q, out, in_, reads=(), writes=()):
            if q == "scalar":
                return ACT(out, in_, AF.Copy, reads, writes)
            return OP(q, lambda e, o=out, i=in_: e.tensor_copy(out=o, in_=i), reads, writes)

        def MM(out, lhsT, rhs, start, stop, reads=(), writes=(), **kw):
            return OP("tensor", lambda e, o=out, l=lhsT, r=rhs, s=start, t=stop, kw=kw:
                      e.matmul(out=o, lhsT=l, rhs=r, start=s, stop=t, **kw), reads, writes)

        def TR(out, in_, idn, reads=(), writes=()):
            return OP("tensor", lambda e, o=out, i=in_, d=idn: e.transpose(out=o, in_=i, identity=d),
                      reads, writes)

        def MEMSET(q, ap, val, writes=()):
            return OP(q, lambda e, a=ap, v=val: e.memset(a, v), (), writes)

        def sb(stack, name, shape, dt):
            return stack.enter_context(nc.sbuf_tensor("sb_" + name, list(shape), dt))

        def psum(stack, name, shape, dt=F32):
            return stack.enter_context(nc.psum_tensor("ps_" + name, list(shape), dt))

        identf = sb(top, "identf", [128, 128], F32)
        identb = sb(top, "identb", [128, 128], BF16)
        gcol = sb(top, "gcol", [128, 32], F32)
        epsc = sb(top, "epsc", [128, 1], F32)
        r_identf, r_identb, r_gcol, r_eps = Res(), Res(), Res(), Res()
        DMA("sync", identf[:], ident_d[:, :], (), [r_identf])
        DMA("sync", gcol[:], gcol_d[:, :], (), [r_gcol])
        CP("vector", identb[:], identf[:], [r_identf], [r_identb])
        MEMSET("vector", epsc[:], EPS, [r_eps])

        def rstd(st, c0, n, R):
            ACT(st[:, c0 + 1:c0 + 2], st[:, c0:c0 + 1], AF.Ln, [R, r_eps], [R], bias=epsc[:, 0:1], scale=1.0 / n)
            ACT(st[:, c0 + 2:c0 + 3], st[:, c0 + 1:c0 + 2], AF.Exp, [R], [R], scale=-0.5)

        def load_weight(wst, r_wst, WST, rot, dst, r_dst, src, nchunk, ncols, gbase, col_scales):
            for c in range(nchunk):
                for c0 in range(0, ncols, WST):
                    c1 = min(ncols, c0 + WST)
                    k = rot[0] % len(wst)
                    rot[0] += 1
                    DMA("sync" if rot[0] % 2 else "gpsimd", wst[k][:, 0:c1 - c0], src[c * 128:(c + 1) * 128, c0:c1],
                        (), [r_wst[k]])
                    for (a, b, const) in col_scales:
                        lo, hi = max(a, c0), min(b, c1)
                        if lo >= hi:
                            continue
                        eng = "gpsimd" if (rot[1] % 2) else "vector"
                        rot[1] += 1
                        if gbase is None:
                            CP(eng, dst[:, c, lo:hi], wst[k][:, lo - c0:hi - c0], [r_wst[k]], [r_dst[c]])
                        else:
                            TS(eng, dst[:, c, lo:hi], wst[k][:, lo - c0:hi - c0], gcol[:, gbase + c:gbase + c + 1],
                               const, ALU.mult, ALU.mult, [r_wst[k], r_gcol], [r_dst[c]])

        WST = 1408
        ph1 = contextlib.ExitStack()
        with ph1 if 1 in PHASES else contextlib.nullcontext():
          if 1 in PHASES:
              wst = [sb(ph1, f"wst{i}", [128, WST], F32) for i in range(2)]
              r_wst = [Res(), Res()]
              rot = [0, 0]
              win = sb(ph1, "win", [128, NCH, INC], BF16)
              wuq = sb(ph1, "wuq", [128, 3, H * QKD], BF16)
              wukv = sb(ph1, "wukv", [128, 2, H * 128], BF16)
              r_win = [Res() for _ in range(NCH)]
              r_wuq = [Res() for _ in range(3)]
              r_wukv = [Res() for _ in range(2)]
              load_weight(wst, r_wst, WST, rot, win, r_win, w_in, NCH, INC, G_ATTN,
                          [(0, C_NQ, 1.0), (C_NQ, C_NK, 0.125), (C_NK, INC, 1.0)])
              load_weight(wst, r_wst, WST, rot, wuq, r_wuq, w_uq, 3, H * QKD, G_Q, [(0, H * QKD, QKD ** -0.5)])
              load_weight(wst, r_wst, WST, rot, wukv, r_wukv, w_ukv, 2, H * 128, G_KV, [(0, H * 128, 1.0)])

              ropeP = sb(ph1, "ropeP", [128, SP // 128, 32], F32)
              ropeC = sb(ph1, "ropeC", [128, SC // 128, 32], F32)
              ropeN = sb(ph1, "ropeN", [128, NNA // 128, 32], F32)
              r_rope = Res()
              DMA("sync", ropeP[:], ropeP_d[:, :, :], (), [r_rope])
              DMA("sync", ropeC[:], ropeC_d[:, :, :], (), [r_rope])
              DMA("sync", ropeN[:], ropeN_d[:, :, :], (), [r_rope])

              NX = 3
              xt = [sb(ph1, f"xt{i}", [128, D], F32) for i in range(NX)]
              r_xt = [Res() for _ in range(NX)]
              junk = [sb(ph1, f"junk{i}", [128, D], BF16) for i in range(2)]
              r_junk = [Res(), Res()]
              jctr = [0]
              hb = [sb(ph1, f"hb{i}", [128, D], BF16) for i in range(3)]
              r_hb = [Res(), Res(), Res()]
              hT = [sb(ph1, f"hT{i}", [128, NCH, 512], BF16) for i in range(2)]
              r_hT = [[Res() for _ in range(4)] for _ in range(2)]
              NST = 4
              stt = [sb(ph1, f"stt{i}", [128, 16], F32) for i in range(NST)]
              r_stx = [Res() for _ in range(NST)]
              r_stq = [Res() for _ in range(NST)]
              r_stkv = [Res() for _ in range(NST)]
              cqn = [sb(ph1, f"cqn{i}", [128, QL], BF16) for i in range(3)]
              ckvn = [sb(ph1, f"ckvn{i}", [128, KVL], BF16) for i in range(3)]
              r_cqn = [Res(), Res(), Res()]
              r_ckvn = [Res(), Res(), Res()]
              cqnT = [sb(ph1, f"cqnT{i}", [128, 3, 128], BF16) for i in range(3)]
              ckvnT = [sb(ph1, f"ckvnT{i}", [128, 2, 128], BF16) for i in range(3)]
              r_cqnT = [Res(), Res(), Res()]
              r_ckvnT = [Res(), Res(), Res()]
              rtmp = [sb(ph1, f"rtmp{i}", [128, 96 + 512], F32) for i in range(3)]
              r_tc = [Res(), Res(), Res()]
              r_ts = [Res(), Res(), Res()]
              r_kro = [Res(), Res(), Res()]
              r_tcq = [[Res(), Res()] for _ in range(3)]
              r_tsq = [[Res(), Res()] for _ in range(3)]
              Ktok = [sb(ph1, f"Ktok{i}", [128, H, QKD], BF16) for i in range(3)]
              Qtok = [sb(ph1, f"Qtok{i}", [128, H, QKD], BF16) for i in range(3)]
              r_Ktok = [Res(), Res(), Res()]
              r_Qtok = [Res(), Res(), Res()]
              KTst = [sb(ph1, f"KTst{i}", [128, H, 512], BF16) for i in range(2)]
              QTst = [sb(ph1, f"QTst{i}", [128, H, 512], BF16) for i in range(2)]
              Vst = [sb(ph1, f"Vst{i}", [128, H, 4, 65], BF16) for i in range(2)]
              naqst = [sb(ph1, f"naqst{i}", [128, 4, 512], BF16) for i in range(2)]
              nakst = [sb(ph1, f"nakst{i}", [128, 4, 512], BF16) for i in range(2)]
              navst = [sb(ph1, f"navst{i}", [128, 4, H, 65], BF16) for i in range(2)]
              r_KTst = [[Res() for _ in range(4)] for _ in range(2)]
              r_QTst = [[Res() for _ in range(4)] for _ in range(2)]
              r_Vst = [[Res() for _ in range(4)] for _ in range(2)]
              r_naqst = [[Res() for _ in range(4)] for _ in range(2)]
              r_nakst = [[Res() for _ in range(4)] for _ in range(2)]
              r_navst = [[Res() for _ in range(4)] for _ in range(2)]
              for i in range(2):
                  for t in range(4):
                      MEMSET("gpsimd", Vst[i][:, :, t, 64:65], 1.0, [r_Vst[i][t]])
                      MEMSET("gpsimd", navst[i][:, t, :, 64:65], 1.0, [r_navst[i][t]])

              PB = [psum(ph1, f"pb{i}", [128, 512], F32) for i in range(4)]
              PKVU = psum(ph1, "pkvu", [128, 1024], F32)
              PQU = psum(ph1, "pqu", [128, 1024], F32)
              r_PB = [Res(psum=True) for _ in range(4)]
              r_PKVU, r_PQU = Res(psum=True), Res(psum=True)
              ptr_bf = PB[0][:].bitcast(BF16)
              pkt_bf = PB[3][:].bitcast(BF16)
              pqt_bf = PB[0][:].bitcast(BF16)

              tilectr = [0]
              grpctr = [0]
              if getattr(cfg, "verbose", False):
                  print("ph1 sbuf remaining", nc.sbuf_bytes_remaining)

              def square_acc(in_ap, n, acc, reads, writes):
                  j = jctr[0] % 2
                  jctr[0] += 1
                  ACT(junk[j][:, 0:n], in_ap, AF.Square, reads, [r_junk[j]] + writes, accum_out=acc)

              def tile_gen(u, src, rope_sb, do_q, do_kv, do_na, g, gi, t, tc_):
                      if True:
                          tg = g * 4 + t
                          xi = tc_ % NX
                          k2 = tc_ % 3
                          si = tc_ % NST
                          st = stt[si]
                          tsl = slice(t * 128, (t + 1) * 128)
                          DMA("sync", xt[xi][:], src[tg * 128:(tg + 1) * 128, :], (), [r_xt[xi]])
                          square_acc(xt[xi][:], D, st[:, 0:1], [r_xt[xi]], [r_stx[si]])
                          rstd(st, 0, D, r_stx[si])
                          TS("vector", hb[k2][:], xt[xi][:], st[:, 2:3], None, ALU.mult, None,
                             [r_xt[xi], r_stx[si]], [r_hb[k2]])
                          for c in range(NCH):
                              TR(ptr_bf[:, c * 128:(c + 1) * 128], hb[k2][:, c * 128:(c + 1) * 128], identb[:],
                                 [r_hb[k2], r_identb], [r_PB[0]])
                          CP("scalar", hT[gi][:, :, tsl],
                             ptr_bf[:, 0:1024].rearrange("p (c n) -> p c n", c=NCH), [r_PB[0]], [r_hT[gi][t]])
                          yield
                          if do_q:
                              for c in range(NCH):
                                  MM(PB[1][:, 0:QL], hT[gi][:, c, tsl], win[:, c, C_CQ:C_CQ + QL],
                                     c == 0, c == NCH - 1, [r_hT[gi][t], r_win[c]], [r_PB[1]])
                          if do_kv:
                              for c in range(NCH):
                                  MM(PB[2][:, 0:KVL + RD], hT[gi][:, c, tsl], win[:, c, C_CKV:C_CKV + KVL + RD],
                                     c == 0, c == NCH - 1, [r_hT[gi][t], r_win[c]], [r_PB[2]])
                          if do_na:
                              for c in range(NCH):
                                  MM(PB[3][:, :], hT[gi][:, c, tsl], win[:, c, C_NV:C_NV + 512],
                                     c == 0, c == NCH - 1, [r_hT[gi][t], r_win[c]], [r_PB[3]])
                              CP("vector", navst[gi][:, t, :, 0:64], PB[3][:, :].rearrange("p (h d) -> p h d", h=H),
                                 [r_PB[3]], [r_navst[gi][t]])
                          yield
                          if do_q:
                              square_acc(PB[1][:, 0:QL], QL, st[:, 3:4], [r_PB[1]], [r_stq[si]])
                              rstd(st, 3, QL, r_stq[si])
                              TS("vector", cqn[k2][:], PB[1][:, 0:QL], st[:, 5:6], None, ALU.mult, None,
                                 [r_PB[1], r_stq[si]], [r_cqn[k2]])
                          if do_kv:
                              square_acc(PB[2][:, 0:KVL], KVL, st[:, 6:7], [r_PB[2]], [r_stkv[si]])
                              rstd(st, 6, KVL, r_stkv[si])
                              TS("vector", ckvn[k2][:], PB[2][:, 0:KVL], st[:, 8:9], None, ALU.mult, None,
                                 [r_PB[2], r_stkv[si]], [r_ckvn[k2]])
                              rt = rtmp[k2]
                              xk = PB[2][:, KVL:KVL + RD].rearrange("p (a j) -> p a j", a=2)
                              cosb = rope_sb[:, tg, 0:16].unsqueeze(1).broadcast_to([128, 2, 16])
                              sinb = rope_sb[:, tg, 16:32].unsqueeze(1).broadcast_to([128, 2, 16])
                              tcv = rt[:, 0:32].rearrange("p (a j) -> p a j", a=2)
                              tsv = rt[:, 32:64].rearrange("p (a j) -> p a j", a=2)
                              TT("vector", tcv, xk, cosb, ALU.mult, [r_PB[2], r_rope, r_stkv[si]], [r_tc[k2]])
                              TT("vector", tsv, xk, sinb, ALU.mult, [r_PB[2], r_rope, r_stkv[si]], [r_ts[k2]])
                              kro = rt[:, 64:96]
                              TT("vector", kro[:, 0:16], rt[:, 0:16], rt[:, 48:64], ALU.subtract,
                                 [r_tc[k2], r_ts[k2]], [r_kro[k2]])
                              TT("vector", kro[:, 16:32], rt[:, 32:48], rt[:, 16:32], ALU.add,
                                 [r_tc[k2], r_ts[k2]], [r_kro[k2]])
                          yield
                          if do_q:
                              for c in range(3):
                                  TR(ptr_bf[:, c * 128:(c + 1) * 128], cqn[k2][:, c * 128:(c + 1) * 128], identb[:],
                                     [r_cqn[k2], r_identb], [r_PB[0]])
                              CP("scalar", cqnT[k2][:], ptr_bf[:, 0:384].rearrange("p (c n) -> p c n", c=3),
                                 [r_PB[0]], [r_cqnT[k2]])
                          if do_kv:
                              for c in range(2):
                                  TR(ptr_bf[:, 384 + c * 128:384 + (c + 1) * 128], ckvn[k2][:, c * 128:(c + 1) * 128],
                                     identb[:], [r_ckvn[k2], r_identb], [r_PB[0]])
                              CP("scalar", ckvnT[k2][:], ptr_bf[:, 384:640].rearrange("p (c n) -> p c n", c=2),
                                 [r_PB[0]], [r_ckvnT[k2]])
                          yield
                          if do_kv:
                              for hh in range(2):
                                  for c in range(2):
                                      MM(PKVU[:, hh * 512:(hh + 1) * 512], ckvnT[k2][:, c, :],
                                         wukv[:, c, hh * 512:(hh + 1) * 512], c == 0, c == 1,
                                         [r_ckvnT[k2], r_wukv[c]], [r_PKVU])
                              kvv = PKVU[:, :].rearrange("p (h d) -> p h d", h=H)
                              CP("vector", Ktok[k2][:, :, 0:64], kvv[:, :, 0:64], [r_PKVU], [r_Ktok[k2]])
                              CP("vector", Ktok[k2][:, :, 64:96], kro.unsqueeze(1).broadcast_to([128, H, RD]),
                                 [r_kro[k2]], [r_Ktok[k2]])
                              CP("scalar", Vst[gi][:, :, t, 0:64], kvv[:, :, 64:128], [r_PKVU], [r_Vst[gi][t]])
                              yield
                              for h in range(H):
                                  TR(pkt_bf[0:QKD, h * 128:(h + 1) * 128], Ktok[k2][:, h, :], identb[:],
                                     [r_Ktok[k2], r_identb], [r_PB[3]])
                              CP("scalar", KTst[gi][0:QKD, :, tsl],
                                 pkt_bf[0:QKD, :].rearrange("p (h n) -> p h n", h=H), [r_PB[3]], [r_KTst[gi][t]])
                          if do_q:
                              for hh in range(2):
                                  for c in range(3):
                                      MM(PQU[:, hh * 512:hh * 512 + 384], cqnT[k2][:, c, :],
                                         wuq[:, c, hh * 384:(hh + 1) * 384], c == 0, c == 2,
                                         [r_cqnT[k2], r_wuq[c]], [r_PQU])
                              rt = rtmp[k2]
                              for hh in range(2):
                                  qv = PQU[:, hh * 512:hh * 512 + 384].rearrange("p (h d) -> p h d", h=4)
                                  CP("vector", Qtok[k2][:, hh * 4:(hh + 1) * 4, 0:64], qv[:, :, 0:64],
                                     [r_PQU], [r_Qtok[k2]])
                                  xq = qv[:, :, 64:96].rearrange("p h (a j) -> p h a j", a=2)
                                  cosb = rope_sb[:, tg, 0:16].unsqueeze(1).unsqueeze(1).broadcast_to([128, 4, 2, 16])
                                  sinb = rope_sb[:, tg, 16:32].unsqueeze(1).unsqueeze(1).broadcast_to([128, 4, 2, 16])
                                  o0 = 96 + hh * 256
                                  tcq = rt[:, o0:o0 + 128].rearrange("p (h a j) -> p h a j", h=4, a=2)
                                  tsq = rt[:, o0 + 128:o0 + 256].rearrange("p (h a j) -> p h a j", h=4, a=2)
                                  TT("vector", tcq, xq, cosb, ALU.mult, [r_PQU, r_rope], [r_tcq[k2][hh]])
                                  TT("vector", tsq, xq, sinb, ALU.mult, [r_PQU, r_rope], [r_tsq[k2][hh]])
                                  TT("gpsimd", Qtok[k2][:, hh * 4:(hh + 1) * 4, 64:80], tcq[:, :, 0, :], tsq[:, :, 1, :],
                                     ALU.subtract, [r_tcq[k2][hh], r_tsq[k2][hh]], [r_Qtok[k2]])
                                  TT("gpsimd", Qtok[k2][:, hh * 4:(hh + 1) * 4, 80:96], tsq[:, :, 0, :], tcq[:, :, 1, :],
                                     ALU.add, [r_tcq[k2][hh], r_tsq[k2][hh]], [r_Qtok[k2]])
                              yield
                              for h in range(H):
                                  TR(pqt_bf[0:QKD, h * 128:(h + 1) * 128], Qtok[k2][:, h, :], identb[:],
                                     [r_Qtok[k2], r_identb], [r_PB[0]])
                              CP("vector", QTst[gi][0:QKD, :, tsl],
                                 pqt_bf[0:QKD, :].rearrange("p (h n) -> p h n", h=H), [r_PB[0]], [r_QTst[gi][t]])
              def group_na_gen(u, do_na, g, gi):
                  if do_na:
                      for blk in range(8):
                          col0 = C_NQ + blk * 128
                          for c in range(NCH):
                              MM(PB[3][:, :], win[:, c, col0:col0 + 128], hT[gi][:, c, :], c == 0, c == NCH - 1,
                                 r_hT[gi] + [r_win[c]], [r_PB[3]])
                          dst = naqst[gi] if blk < 4 else nakst[gi]
                          rr = r_naqst[gi] if blk < 4 else r_nakst[gi]
                          CP("scalar" if blk % 2 else "vector", dst[:, blk % 4, :], PB[3][:, :], [r_PB[3]],
                             [rr[blk % 4]])
                          yield

              def group_dma(u, do_q, do_kv, do_na, g, gi):
                  if do_na:
                      DMA("gpsimd", u["naq"][:, :, g * 512:(g + 1) * 512], naqst[gi][:], r_naqst[gi], ())
                      DMA("gpsimd", u["nak"][:, :, g * 512:(g + 1) * 512], nakst[gi][:], r_nakst[gi], ())
                      for half in range(2):
                          DMA("gpsimd", u["nav"][:, g * 4:(g + 1) * 4, half, :],
                              navst[gi][half * 64:(half + 1) * 64, :, :, :].rearrange("p t h d -> p t (h d)"),
                              r_navst[gi], ())
                  if do_kv:
                      DMA("gpsimd", u["kt"][:, :, g * 512:(g + 1) * 512].rearrange("h d n -> d h n"),
                          KTst[gi][0:QKD, :, :], r_KTst[gi], ())
                      DMA("gpsimd", u["v"][:, :, g * 4:(g + 1) * 4, :].rearrange("h p t d -> p h t d"),
                          Vst[gi][:], r_Vst[gi], ())
                  if do_q:
                      DMA("gpsimd", u["qt"][:, :, g * 512:(g + 1) * 512].rearrange("h d n -> d h n"),
                          QTst[gi][0:QKD, :, :], r_QTst[gi], ())

              def run_token_passes(passes):
                  dbg = getattr(cfg, 'dbg', (1, 1, 1))
                  WIN = 3
                  jobs = []
                  ginfos = []
                  for (u, src, ntile, rope_sb, do_q, do_kv, do_na) in passes:
                      do_q, do_kv, do_na = do_q and dbg[0], do_kv and dbg[1], do_na and dbg[2]
                      for g in range(ntile // 4):
                          gi = grpctr[0] % 2
                          grpctr[0] += 1
                          ginfo = dict(u=u, do_q=do_q, do_kv=do_kv, do_na=do_na, g=g, gi=gi, left=5 if do_na else 4,
                                       idx=len(ginfos))
                          ginfos.append(ginfo)
                          for t in range(4):
                              tc_ = tilectr[0]
                              tilectr[0] += 1
                              jobs.append((ginfo, tile_gen(u, src, rope_sb, do_q, do_kv, do_na, g, gi, t, tc_), t == 3))
                  active = []
                  ji = 0

                  def finish(ginfo):
                      ginfo["left"] -= 1
                      if ginfo["left"] == 0:
                          group_dma(ginfo["u"], ginfo["do_q"], ginfo["do_kv"], ginfo["do_na"], ginfo["g"], ginfo["gi"])

                  while active or ji < len(jobs):
                      if (ji < len(jobs) and sum(1 for a_ in active if a_[2] != "na") < WIN
                              and all(gq["left"] == 0 for gq in ginfos[:max(0, jobs[ji][0]["idx"] - 1)])):
                          ginfo, gen, lastt = jobs[ji]
                          ji += 1
                          active.append([ginfo, gen, "tile", lastt, False])
                      for a_ in list(active):
                          ginfo, gen = a_[0], a_[1]
                          try:
                              next(gen)
                              if a_[2] == "tile" and a_[3] and not a_[4] and ginfo["do_na"]:
                                  a_[4] = True
                                  active.append([ginfo, group_na_gen(ginfo["u"], True, ginfo["g"], ginfo["gi"]),
                                                 "na", False, False])
                          except StopIteration:
                              active.remove(a_)
                              finish(ginfo)

              plist = []
              for ui, u in enumerate(units):
                  if u["kind"] == "P":
                      plist.append((u, xp[ui * SP:(ui + 1) * SP, :], SP // 128, ropeP, True, True, True))
                  else:
                      plist.append((u, xn, NNA // 128, ropeN, True, False, True))
                      plist.append((u, xc, SC // 128, ropeC, False, True, False))
              run_token_passes(plist)
        barrier()

        ph2 = contextlib.ExitStack()
        with ph2 if 2 in PHASES else contextlib.nullcontext():
          if 2 in PHASES:
              SMAX = max(u["S"] for u in units)
              NQMAX = max(u["NQ"] for u in units)
              NTMAX = max(u["NT"] for u in units)
              KTb = [sb(ph2, f"KTb{i}", [128, SMAX], BF16) for i in range(2)]
              Vb = [sb(ph2, f"Vb{i}", [128, SMAX // 128, 65], BF16) for i in range(2)]
              QTb = [sb(ph2, f"QTb{i}", [128, NQMAX], BF16) for i in range(2)]
              r_KTb, r_Vb, r_QTb = [Res(), Res()], [Res(), Res()], [Res(), Res()]
              PT = [sb(ph2, f"PT{i}", [128, 1024], BF16) for i in range(3)]
              r_PT = [Res() for _ in range(3)]
              oTs = [sb(ph2, f"oTs{i}", [128, 512], F32) for i in range(2)]
              r_oTs = [Res(), Res()]
              naq = sb(ph2, "naq", [128, 4, NTMAX], BF16)
              nak = sb(ph2, "nak", [128, 4, NTMAX], BF16)
              nav = sb(ph2, "nav", [64, NTMAX // 64, 8 * 65], BF16)
              r_naq, r_nak, r_nav = Res(), Res(), Res()
              tabf = sb(ph2, "tabf", [128, 4, 15 * 64], F32)
              half_ = sb(ph2, "halof", [128, 4, 16 * 64], F32)
              maskf = sb(ph2, "maskf", [128, 2, 64], F32)
              tabb = sb(ph2, "tabb", [128, 4, 15 * 64], BF16)
              halb = sb(ph2, "halob", [128, 4, 16 * 64], BF16)
              r_tabf, r_half, r_maskf, r_tabb, r_halb = Res(), Res(), Res(), Res(), Res()
              NCHAIN = 4
              PTn = [sb(ph2, f"PTn{i}", [64, 512], BF16) for i in range(2 * NCHAIN)]
              r_PTn = [Res() for _ in range(2 * NCHAIN)]

              PS = [psum(ph2, f"ps{i}", [128, 1024], F32) for i in range(2)]
              PO = [psum(ph2, f"po{i}", [128, 512], F32) for i in range(4)]
              r_PSh = [[Res(psum=True), Res(psum=True)] for _ in range(2)]
              r_PO = [Res(psum=True) for _ in range(4)]
              PSn = [PS[0][:, 0:512], PS[0][:, 512:1024], PS[1][:, 0:512], PS[1][:, 512:1024]]
              r_PSn = [r_PSh[0][0], r_PSh[0][1], r_PSh[1][0], r_PSh[1][1]]
              PA = PO
              r_PA = r_PO

              DMA("sync", tabf[:], natab_d[:, :, :], (), [r_tabf])
              DMA("sync", half_[:], nahalo_d[:, :, :], (), [r_half])
              DMA("sync", maskf[:], namask_d[:, :, :], (), [r_maskf])
              for (srcf, dstb, nb, rs, rd) in ((tabf, tabb, 15, r_tabf, r_tabb), (half_, halb, 16, r_half, r_halb)):
                  v = srcf[:].rearrange("p a (b q) -> p (a b) q", q=64)
                  vb = dstb[:].rearrange("p a (b q) -> p (a b) q", q=64)
                  m01 = maskf[:, 0, :].unsqueeze(1).broadcast_to([128, 4 * nb, 64])
                  mng = maskf[:, 1, :].unsqueeze(1).broadcast_to([128, 4 * nb, 64])
                  TT("vector", v, v, m01, ALU.mult, [rs, r_maskf], [rs])
                  TT("vector", vb, v, mng, ALU.add, [rs, r_maskf], [rd])

              ctr = dict(hb=0, pt=0, ps=0, po=0, ot=0, ptn=0)
              if getattr(cfg, "verbose", False):
                  print("ph2 sbuf remaining", nc.sbuf_bytes_remaining)

              def mla_attention(u):
                  Sx, NQu, q0 = u["S"], u["NQ"], u["q0"]
                  nkt = Sx // 128
                  nk2 = nkt // 2
                  nqb = NQu // 512
                  steps = [(h, qb, k2) for h in range(H) for qb in range(nqb) for k2 in range(nk2)]
                  hbase = ctr["hb"]
                  ctr["hb"] += H
                  pobase = ctr["po"]
                  ctr["po"] += H * nqb

                  def load(h):
                      bi = (hbase + h) % 2
                      DMA("sync", KTb[bi][0:QKD, 0:Sx], u["kt"][h, :, :], (), [r_KTb[bi]])
                      DMA("sync", Vb[bi][:, 0:nkt, :], u["v"][h, :, :, :], (), [r_Vb[bi]])
                      DMA("sync", QTb[bi][0:QKD, 0:NQu], u["qt"][h, :, q0:q0 + NQu], (), [r_QTb[bi]])

                  def QK(i):
                      h, qb, k2 = steps[i]
                      bi = (hbase + h) % 2
                      si = i % 2
                      for j in range(2):
                          kt = 2 * k2 + j
                          MM(PS[si][:, j * 512:(j + 1) * 512], KTb[bi][0:QKD, kt * 128:(kt + 1) * 128],
                             QTb[bi][0:QKD, qb * 512:(qb + 1) * 512], True, True,
                             [r_KTb[bi], r_QTb[bi]], [r_PSh[si][j]])

                  def EXPPV(i):
                      h, qb, k2 = steps[i]
                      bi = (hbase + h) % 2
                      si = i % 2
                      pi = i % 3
                      oi = (pobase + h * nqb + qb) % 4
                      ACT(PT[pi][:, :], PS[si][:, :], AF.Exp, r_PSh[si], [r_PT[pi]])
                      for j in range(2):
                          kt = 2 * k2 + j
                          MM(PO[oi][0:65, :], Vb[bi][:, kt, :], PT[pi][:, j * 512:(j + 1) * 512],
                             kt == 0, kt == nkt - 1, [r_Vb[bi], r_PT[pi]], [r_PO[oi]])
                      if k2 == nk2 - 1:
                          ti = ctr["ot"] % 2
                          ctr["ot"] += 1
                          CP("vector", oTs[ti][0:65, :], PO[oi][0:65, :], [r_PO[oi]], [r_oTs[ti]])
                          DMA("gpsimd", u["ot"][h, :, qb * 512:(qb + 1) * 512], oTs[ti][0:65, :], [r_oTs[ti]], ())

                  load(0)
                  QK(0)
                  for i in range(len(steps)):
                      h, qb, k2 = steps[i]
                      if qb == 0 and k2 == 0 and h + 1 < H:
                          load(h + 1)
                      if i + 1 < len(steps):
                          QK(i + 1)
                      EXPPV(i)

              def na_attention(u):
                  NT, NQu = u["NT"], u["NQ"]
                  nrq = NQu // 64
                  plan = na_plan(u["kind"], nrq)
                  qrow0 = 0 if u["kind"] == "P" else 4
                  DMA("sync", naq[:, :, 0:NT], u["naq"][:, :, :], (), [r_naq])
                  DMA("sync", nak[:, :, 0:NT], u["nak"][:, :, :], (), [r_nak])
                  DMA("sync", nav[:, 0:NT // 64, :], u["nav"][:, :, :, :].rearrange("p t a d -> p (t a) d"), (), [r_nav])
                  chains = [(h, b) for b in range(len(plan)) for h in range(H)]
                  for c0_ in range(0, len(chains), NCHAIN):
                      grp = chains[c0_:c0_ + NCHAIN]
                      nst = max(len(plan[b]) for (h, b) in grp)

                      def QKB(j, s):
                          h, b = grp[j]
                          items = plan[b]
                          if s >= len(items):
                              return
                          ks, i0, i1, src, blk0 = items[s]
                          pr, pb = h // 2, (h % 2) * 64
                          N = 64 * (i1 - i0 + 1)
                          qtok0 = (qrow0 + i0) * 64
                          MM(PSn[j][0:64, 0:N], nak[pb:pb + 64, pr, ks * 64:(ks + 1) * 64],
                             naq[pb:pb + 64, pr, qtok0:qtok0 + N], True, False, [r_nak, r_naq], [r_PSn[j]])
                          tb = tabb if src == "T" else halb
                          rtb = r_tabb if src == "T" else r_halb
                          MM(PSn[j][0:64, 0:N], identb[pb:pb + 64, pb:pb + 64],
                             tb[pb:pb + 64, pr, blk0 * 64:blk0 * 64 + N], False, True, [rtb, r_identb], [r_PSn[j]])

                      def EXPN(j, s):
                          h, b = grp[j]
                          items = plan[b]
                          if s >= len(items):
                              return
                          ks, i0, i1, src, blk0 = items[s]
                          N = 64 * (i1 - i0 + 1)
                          pi = 2 * j + (s % 2)
                          ACT(PTn[pi][:, 0:N], PSn[j][0:64, 0:N], AF.Exp, [r_PSn[j]], [r_PTn[pi]])

                      def PVN(j, s):
                          h, b = grp[j]
                          items = plan[b]
                          if s >= len(items):
                              return
                          ks, i0, i1, src, blk0 = items[s]
                          N = 64 * (i1 - i0 + 1)
                          c0 = (i0 - 8 * b) * 64
                          pi = 2 * j + (s % 2)
                          MM(PA[j][0:65, c0:c0 + N], nav[:, ks, h * 65:(h + 1) * 65], PTn[pi][:, 0:N],
                             s == 0, s == len(items) - 1, [r_nav, r_PTn[pi]], [r_PA[j]], skip_group_check=True)
                          if s == len(items) - 1:
                              ti = ctr["ot"] % 2
                              ctr["ot"] += 1
                              CP("vector", oTs[ti][0:65, :], PA[j][0:65, :], [r_PA[j]], [r_oTs[ti]])
                              DMA("gpsimd", u["ot"][8 + h, :, b * 512:(b + 1) * 512], oTs[ti][0:65, :],
                                  [r_oTs[ti]], ())

                      for j in range(len(grp)):
                          QKB(j, 0)
                      for s in range(nst):
                          for j in range(len(grp)):
                              EXPN(j, s)
                          for j in range(len(grp)):
                              QKB(j, s + 1)
                          for j in range(len(grp)):
                              PVN(j, s)

              for u in units:
                  na_attention(u)
                  mla_attention(u)
        barrier()

        ph3 = contextlib.ExitStack()
        with ph3 if 3 in PHASES else contextlib.nullcontext():
          if 3 in PHASES:
              WST = 352
              NWST3 = 3
              wst = [sb(ph3, f"wst3_{i}", [128, WST], F32) for i in range(NWST3)]
              r_wst = [Res() for _ in range(NWST3)]
              rot = [0, 0]
              wo = sb(ph3, "wo", [128, NCH, D], BF16)
              wg = sb(ph3, "wg", [128, NCH, DFF], BF16)
              wu = sb(ph3, "wu", [128, NCH, DFF], BF16)
              wd = sb(ph3, "wd", [128, NFF, D], BF16)
              r_wo = [Res() for _ in range(NCH)]
              r_wg = [Res() for _ in range(NCH)]
              r_wu = [Res() for _ in range(NCH)]
              r_wd = [Res() for _ in range(NFF)]
              load_weight(wst, r_wst, WST, rot, wo, r_wo, w_o, NCH, D, G_MIX, [(0, D, 1.0)])
              load_weight(wst, r_wst, WST, rot, wg, r_wg, w_gate, NCH, DFF, G_FFN, [(0, DFF, 1.0)])
              load_weight(wst, r_wst, WST, rot, wu, r_wu, w_up, NCH, DFF, G_FFN, [(0, DFF, 1.0)])
              load_weight(wst, r_wst, WST, rot, wd, r_wd, w_down, NFF, D, None, [(0, D, 1.0)])
              gfin = sb(ph3, "gfin", [128, D], F32)
              r_gfin = Res()
              DMA("sync", gfin[:], gfin_d[0:1, :].partition_broadcast(128), (), [r_gfin])
              onec = sb(ph3, "onec", [128, 1], F32)
              r_one = Res()
              MEMSET("vector", onec[:], 1.0, [r_one])

              TG = 2
              NTOK = TG * 128
              oTin = sb(ph3, "oTin", [128, 4, 128], F32)
              r_oTin = Res()
              atok = sb(ph3, "atok", [128, D], F32)
              r_atok = Res()
              x1 = [[sb(ph3, f"x1_{s_}_{i}", [128, D], F32) for i in range(TG)] for s_ in range(2)]
              r_x1 = [[Res() for _ in range(TG)] for _ in range(2)]
              mixb = sb(ph3, "mixb", [128, D], BF16)
              r_mixb = Res()
              mixT = sb(ph3, "mixT", [128, NCH, 128], BF16)
              r_mixT = Res()
              h2b = sb(ph3, "h2b", [128, D], BF16)
              r_h2b = Res()
              h2T = [sb(ph3, f"h2T{s_}", [128, NCH, NTOK], BF16) for s_ in range(2)]
              r_h2T = [[Res() for _ in range(TG)] for _ in range(2)]
              actT = sb(ph3, "actT", [128, NFF, NTOK], BF16)
              r_actT = [Res() for _ in range(NFF)]
              sg = [sb(ph3, f"sg{i}", [128, NTOK], F32) for i in range(2)]
              r_sg = [Res(), Res()]
              st3 = [sb(ph3, f"st3_{i}", [128, 32], F32) for i in range(4)]
              r_st3 = [[Res() for _ in range(6)] for _ in range(4)]

              POh = psum(ph3, "poh", [128, 512], F32)
              r_POh = Res(psum=True)
              PTR = psum(ph3, "ptr3", [128, 512], F32)
              r_PTR = Res(psum=True)
              ptr3_bf = PTR[:].bitcast(BF16)
              PY = psum(ph3, "py", [128, 1024], F32)
              r_PY = Res(psum=True)
              PG = [psum(ph3, f"pg{i}", [128, 512], F32) for i in range(2)]
              PU = [psum(ph3, f"pu{i}", [128, 512], F32) for i in range(2)]
              r_PG, r_PU = [Res(psum=True), Res(psum=True)], [Res(psum=True), Res(psum=True)]
              if getattr(cfg, "verbose", False):
                  print("ph3 sbuf remaining", nc.sbuf_bytes_remaining)

              def prologue(u, g, slot):
                  for t in range(TG):
                      tok0 = g * NTOK + t * 128
                      X, rX = x1[slot][t], r_x1[slot][t]
                      st = st3[slot * 2 + t]
                      rs_ = r_st3[slot * 2 + t]
                      DMA("sync", X[:], u["xres"][tok0:tok0 + 128, :], (), [rX])
                      for hq in range(4):
                          DMA("sync", oTin[0:65, :, :],
                              u["ot"][hq * 4:(hq + 1) * 4, :, tok0:tok0 + 128].rearrange("h d n -> d h n"),
                              (), [r_oTin])
                          yield
                          for hh in range(4):
                              TR(POh[:, hh * 128:hh * 128 + 65], oTin[0:65, hh, :], identf[0:65, 0:65],
                                 [r_oTin, r_identf], [r_POh])
                          pov = POh[:, :].rearrange("p (h d) -> p h d", h=4)
                          rc = st[:, 8:12].unsqueeze(2)
                          OP("vector", lambda e, o=rc, i=pov[:, :, 64:65]: e.reciprocal(out=o, in_=i),
                             [r_POh], [rs_[4]])
                          TT("vector", atok[:, hq * 256:(hq + 1) * 256].rearrange("p (h d) -> p h d", h=4),
                             pov[:, :, 0:64], rc.broadcast_to([128, 4, 64]), ALU.mult,
                             [r_POh, rs_[4]], [r_atok])
                      for half in range(2):
                          hs = slice(half * 512, (half + 1) * 512)
                          ACT(mixb[:, hs], atok[:, hs], AF.Square, [r_atok], [r_mixb, rs_[half]],
                              accum_out=st[:, 3 * half:3 * half + 1])
                          rstd(st, 3 * half, 512, rs_[half])
                          TS("vector", mixb[:, hs], atok[:, hs], st[:, 3 * half + 2:3 * half + 3], None, ALU.mult, None,
                             [r_atok, rs_[half]], [r_mixb])
                      yield
                      for c in range(NCH):
                          TR(ptr3_bf[:, c * 128:(c + 1) * 128], mixb[:, c * 128:(c + 1) * 128], identb[:],
                             [r_mixb, r_identb], [r_PTR])
                      CP("scalar", mixT[:], ptr3_bf[:, 0:1024].rearrange("p (c n) -> p c n", c=NCH),
                         [r_PTR], [r_mixT])
                      yield
                      for hh in range(2):
                          for c in range(NCH):
                              MM(PY[:, hh * 512:(hh + 1) * 512], mixT[:, c, :], wo[:, c, hh * 512:(hh + 1) * 512],
                                 c == 0, c == NCH - 1, [r_mixT, r_wo[c]], [r_PY])
                      TT("vector", X[:], X[:], PY[:, :], ALU.add, [rX, r_PY], [rX])
                      ACT(h2b[:], X[:], AF.Square, [rX], [r_h2b, rs_[2]], accum_out=st[:, 16:17])
                      rstd(st, 16, D, rs_[2])
                      TS("vector", h2b[:], X[:], st[:, 18:19], None, ALU.mult, None, [rX, rs_[2]], [r_h2b])
                      yield
                      for c in range(NCH):
                          TR(ptr3_bf[:, c * 128:(c + 1) * 128], h2b[:, c * 128:(c + 1) * 128], identb[:],
                             [r_h2b, r_identb], [r_PTR])
                      CP("scalar", h2T[slot][:, :, t * 128:(t + 1) * 128],
                         ptr3_bf[:, 0:1024].rearrange("p (c n) -> p c n", c=NCH), [r_PTR], [r_h2T[slot][t]])
                      yield

              def ffn_step(slot, f):
                  pi = f % 2
                  for c in range(NCH):
                      MM(PG[pi][:, 0:NTOK], wg[:, c, f * 128:(f + 1) * 128], h2T[slot][:, c, :], c == 0, c == NCH - 1,
                         r_h2T[slot] + [r_wg[c]], [r_PG[pi]])
                  for c in range(NCH):
                      MM(PU[pi][:, 0:NTOK], wu[:, c, f * 128:(f + 1) * 128], h2T[slot][:, c, :], c == 0, c == NCH - 1,
                         r_h2T[slot] + [r_wu[c]], [r_PU[pi]])
                  ACT(sg[pi][:], PG[pi][:, 0:NTOK], AF.Exp, [r_PG[pi]], [r_sg[pi]], scale=-1.0)
                  ACT(sg[pi][:], sg[pi][:], AF.Ln, [r_sg[pi], r_one], [r_sg[pi]], bias=onec[:, 0:1])
                  ACT(sg[pi][:], sg[pi][:], AF.Exp, [r_sg[pi]], [r_sg[pi]], scale=-1.0)
                  TT("vector", sg[pi][:], sg[pi][:], PG[pi][:, 0:NTOK], ALU.mult, [r_sg[pi], r_PG[pi]], [r_sg[pi]])
                  TT("vector", actT[:, f, :], sg[pi][:], PU[pi][:, 0:NTOK], ALU.mult,
                     [r_sg[pi], r_PU[pi]], [r_actT[f]])

              def down_epi(u, g, slot, t):
                  tok0 = g * NTOK + t * 128
                  X, rX = x1[slot][t], r_x1[slot][t]
                  st = st3[slot * 2 + t]
                  rr = r_st3[slot * 2 + t][3]
                  for hh in range(2):
                      for f in range(NFF):
                          MM(PY[:, hh * 512:(hh + 1) * 512], actT[:, f, t * 128:(t + 1) * 128],
                             wd[:, f, hh * 512:(hh + 1) * 512], f == 0, f == NFF - 1,
                             [r_actT[f], r_wd[f]], [r_PY])
                  TT("vector", X[:], X[:], PY[:, :], ALU.add, [rX, r_PY], [rX])
                  ACT(h2b[:], X[:], AF.Square, [rX], [r_h2b, rr], accum_out=st[:, 24:25])
                  rstd(st, 24, D, rr)
                  STT("vector", X[:], X[:], st[:, 26:27], gfin[:], ALU.mult, ALU.mult, [rX, rr, r_gfin], [rX])
                  DMA("gpsimd", u["yout"][tok0:tok0 + 128, :], X[:], [rX], ())

              groups = [(u, g) for u in units for g in range(u["NQ"] // NTOK)]
              cur = prologue(groups[0][0], groups[0][1], 0)
              for _ in cur:
                  pass
              for gi_, (u, g) in enumerate(groups):
                  slot = gi_ % 2
                  nxt = None
                  if gi_ + 1 < len(groups):
                      nxt = prologue(groups[gi_ + 1][0], groups[gi_ + 1][1], 1 - slot)
                  for f in range(NFF):
                      ffn_step(slot, f)
                      if nxt is not None:
                          next(nxt, None)
                  if nxt is not None:
                      for _ in nxt:
                          pass
                  for t in range(TG):
                      down_epi(u, g, slot, t)
        S.emit(nc, block, csem, dsem)
    return nc


def _rope_table(pos):
    inv = (10000.0 ** (-np.arange(0, RD, 2, dtype=np.float32) / np.float32(RD))).astype(np.float32)
    ang = pos.astype(np.float32)[:, None] * inv[None, :]
    tab = np.concatenate([np.cos(ang), np.sin(ang)], axis=1).astype(np.float32)
    nt = pos.shape[0] // 128
    return np.ascontiguousarray(tab.reshape(nt, 128, 32).transpose(1, 0, 2))


def _col(g):
    g = np.asarray(g, np.float32).reshape(-1, 128)
    return g.T


def make_in_maps(inputs, cfg):
    NP, SP, SC, NQ, NNA = cfg.NP, cfg.SP, cfg.SC, cfg.NQ, cfg.NNA
    xpr = np.asarray(inputs["x_prompt"], np.float32)
    xsm = np.asarray(inputs["x_sample"], np.float32)
    nquart = SC // NQ
    ncores = xsm.shape[0] * nquart
    assert xpr.shape[0] == NP * ncores
    Rq = NQ // 64
    Rtot = SC // 64
    f = lambda k: np.ascontiguousarray(np.asarray(inputs[k], np.float32)[0])
    gcol = np.zeros((128, 32), np.float32)
    gcol[:, G_ATTN:G_ATTN + 8] = _col(inputs["attn_norm_g"][0])
    gcol[:, G_Q:G_Q + 3] = _col(inputs["q_norm_g"][0])
    gcol[:, G_KV:G_KV + 2] = _col(inputs["kv_norm_g"][0])
    gcol[:, G_MIX:G_MIX + 4] = _col(inputs["mla_out_g"][0])
    gcol[:, G_MIX + 4:G_MIX + 8] = _col(inputs["na_out_g"][0])
    gcol[:, G_FFN:G_FFN + 8] = _col(inputs["ffn_norm_g"][0])
    rpb = np.asarray(inputs["na_rpb"], np.float32)[0]
    kc = np.arange(64)[:, None]
    qc = np.arange(64)[None, :]
    dj = np.clip(kc - qc, -15, 15) + 15
    cs = np.clip(qc - 8, 0, 48)
    m01 = ((kc >= cs) & (kc < cs + 16)).astype(np.float32)
    namask = np.zeros((128, 2, 64), np.float32)
    namask[:, 0, :] = np.tile(m01, (2, 1))
    namask[:, 1, :] = np.tile((1.0 - m01) * NEG, (2, 1))

    def blockT(h, di):
        return rpb[h, di + 7][dj]

    natab = np.zeros((128, 4, 15, 64), np.float32)
    for h in range(H):
        pb, pr = (h % 2) * 64, h // 2
        for j in range(15):
            natab[pb:pb + 64, pr, j, :] = blockT(h, 7 - j)
    natab = natab.reshape(128, 4, 15 * 64)
    shared = dict(w_in=f("w_in"), w_uq=f("w_uq"), w_ukv=f("w_ukv"), w_o=f("w_o"), w_gate=f("w_gate"),
                  w_up=f("w_up"), w_down=f("w_down"), gcol=gcol,
                  gfin=np.asarray(inputs["final_norm_g"], np.float32).reshape(1, D),
                  ident=np.eye(128, dtype=np.float32), natab=natab, namask=namask,
                  ropeP=_rope_table(np.arange(SP)), ropeC=_rope_table(np.arange(SC)))
    maps = []
    for c in range(ncores):
        sq, qt = c // nquart, c % nquart
        m = dict(shared)
        m["xp"] = np.ascontiguousarray(xpr[c * NP:(c + 1) * NP].reshape(NP * SP, D))
        seq = xsm[sq]
        m["xc"] = np.ascontiguousarray(seq)
        r0 = qt * Rq
        rows_b = [4, 5, 6, 7] if qt == 0 else [r0 - 4, r0 - 3, r0 - 2, r0 - 1]
        last = (qt == nquart - 1)
        rows_a = [Rtot - 8, Rtot - 7, Rtot - 6] if last else [r0 + Rq, r0 + Rq + 1, r0 + Rq + 2]
        rows = rows_b + list(range(r0, r0 + Rq)) + rows_a
        xn = np.zeros((NNA, D), np.float32)
        posn = np.zeros((NNA,), np.float32)
        for i, r in enumerate(rows):
            xn[i * 64:(i + 1) * 64] = seq[r * 64:(r + 1) * 64]
            posn[i * 64:(i + 1) * 64] = np.arange(r * 64, (r + 1) * 64)
        m["xn"] = xn
        m["ropeN"] = _rope_table(posn)
        hal = np.zeros((128, 4, 16, 64), np.float32)
        for h in range(H):
            pb, pr = (h % 2) * 64, h // 2
            for ks in range(4):
                for i in range(ks + 1):
                    di = (rows_b[ks] - r0) - i
                    hal[pb:pb + 64, pr, [0, 1, 3, 6][ks] + i, :] = blockT(h, di)
            for j in range(3):
                for n_, i in enumerate(range(Rq - 3 + j, Rq)):
                    di = (rows_a[j] - r0) - i
                    hal[pb:pb + 64, pr, 10 + [0, 3, 5][j] + n_, :] = blockT(h, di)
        m["nahalo"] = hal.reshape(128, 4, 16 * 64)
        maps.append(m)
    return maps


def assemble(results, cfg, nsample):
    NP, SP, SC, NQ = cfg.NP, cfg.SP, cfg.SC, cfg.NQ
    nquart = SC // NQ
    ncores = nsample * nquart
    yp = np.concatenate([np.asarray(results[c]["yp"], np.float32).reshape(NP, SP, D) for c in range(ncores)], axis=0)
    ys = np.zeros((nsample, SC, D), np.float32)
    for c in range(ncores):
        sq, qt = c // nquart, c % nquart
        ys[sq, qt * NQ:(qt + 1) * NQ] = np.asarray(results[c]["yn"], np.float32)
    return yp, ys


_NC_CACHE = {}


def kernel(**inputs):
    cfg = Cfg(NP=2, SP=2048, SC=8192, NQ=2048)
    maps = make_in_maps(inputs, cfg)
    key = (cfg.NP, cfg.SP, cfg.SC, cfg.NQ)
    if key not in _NC_CACHE:
        _NC_CACHE[key] = build_program(cfg)
    nc = _NC_CACHE[key]
    res = run_bass_kernel_spmd(nc, maps, core_ids=list(range(len(maps))))
    return assemble(res.results, cfg, np.asarray(inputs["x_sample"]).shape[0])
```

```python
import numpy as np
import concourse.bass as bass
import concourse.mybir as mybir
from concourse.bass_utils import run_bass_kernel_spmd

F32 = mybir.dt.float32
BF16 = mybir.dt.bfloat16
AF = mybir.ActivationFunctionType
ALU = mybir.AluOpType

QUEUES = ("sync", "scalar", "vector", "gpsimd", "tensor")
NDSEM = {"sync": 16, "gpsimd": 12, "scalar": 4}


class Res:
    __slots__ = ("name", "w", "rc", "rd", "psum")

    def __init__(self, name="", psum=False):
        self.name = name
        self.psum = psum
        self.w = None
        self.rc = {}
        self.rd = []


class Op:
    __slots__ = ("q", "dma", "fn", "deps", "signal", "cidx", "didx", "pos")


class Sched:
    def __init__(self):
        self.ops = {q: [] for q in QUEUES}
        self.ncomp = {q: 0 for q in QUEUES}
        self.ndma = {q: 0 for q in QUEUES}
        self.dma_ops = {q: [] for q in QUEUES}
        self.all_dma = []

    def op(self, q, fn, reads=(), writes=(), dma=False):
        o = Op()
        o.q, o.dma, o.fn, o.signal = q, dma, fn, dma
        deps = set()
        for r in reads:
            if r.w is not None:
                deps.add(r.w)
            if r.psum:
                deps.update(o2 for q2, o2 in r.rc.items() if q2 != q)
        for r in writes:
            if r.w is not None:
                deps.add(r.w)
            deps.update(r.rc.values())
            deps.update(r.rd)
        if dma:
            o.didx = self.ndma[q]
            self.ndma[q] += 1
            R = NDSEM[q]
            if o.didx >= R:
                deps.add(self.dma_ops[q][o.didx - R])
            self.dma_ops[q].append(o)
            self.all_dma.append(o)
            o.cidx = None
        else:
            o.cidx = self.ncomp[q]
            self.ncomp[q] += 1
            o.didx = None
        deps.discard(o)
        if q == "tensor" and not dma:
            deps = {d for d in deps if not (d.q == "tensor" and not d.dma)}
        o.deps = deps
        for d in deps:
            d.signal = True
        for r in reads:
            if dma:
                r.rd.append(o)
            else:
                r.rc[q] = o
        for r in writes:
            r.w = o
            r.rc = {}
            r.rd = []
        o.pos = len(self.ops[q])
        self.ops[q].append(o)
        return o

    def emit(self, nc, block, csem, dsem):
        sched = self
        val = {}
        for q in QUEUES:
            n = 0
            for o in self.ops[q]:
                if o.dma:
                    val[o] = (dsem[q][o.didx % NDSEM[q]], 16 * (o.didx // NDSEM[q] + 1))
                elif o.signal:
                    n += 1
                    val[o] = (csem[q], n)

        def run(q, eng):
            waited = {}
            last_dma = []
            for o in sched.ops[q]:
                need = {}
                for d in o.deps:
                    s, v = val[d]
                    k = id(s)
                    if v > need.get(k, (None, 0))[1]:
                        need[k] = (s, v)
                for k, (s, v) in need.items():
                    if waited.get(k, 0) < v:
                        eng.wait_ge(s, v)
                        waited[k] = v
                ins = o.fn(eng)
                if o.dma:
                    s, v = val[o]
                    ins.then_inc(s, 16)
                elif o.signal:
                    ins.then_inc(csem[q], 1)
            for o in sched.dma_ops[q][-NDSEM.get(q, 0):] if sched.dma_ops[q] else []:
                s, v = val[o]
                if waited.get(id(s), 0) < v:
                    eng.wait_ge(s, v)
                    waited[id(s)] = v

        @block.sync
        def _(e):
            run("sync", e)

        @block.scalar
        def _(e):
            run("scalar", e)

        @block.vector
        def _(e):
            run("vector", e)

        @block.gpsimd
        def _(e):
            run("gpsimd", e)

        @block.tensor
        def _(e):
            run("tensor", e)


D = 1024
NCH = 8
QL, KVL, RD = 384, 256, 32
H = 8
QKD = 96
INC = 2208
C_CQ, C_CKV, C_KR, C_NQ, C_NK, C_NV = 0, 384, 640, 672, 1184, 1696
DFF = 2816
NFF = 22
EPS = 1e-6
NEG = -30000.0
GRID_W = 64
G_ATTN, G_Q, G_KV, G_MIX, G_FFN = 0, 8, 11, 13, 21


class Cfg:
    def __init__(self, NP, SP, SC, NQ):
        self.NP, self.SP, self.SC, self.NQ = NP, SP, SC, NQ
        self.NNA = NQ + 512
        self.QOFF = 256


def na_plan(kind, nrows_q):
    plan = []
    if kind == "P":
        R = nrows_q
        kh = min(8, R)

        def start(r):
            return min(max(r - kh // 2, 0), R - kh)
        for b in range(R // 8):
            items = []
            for kr in range(R):
                rows = [i for i in range(8 * b, 8 * b + 8) if start(i) <= kr <= start(i) + kh - 1]
                if not rows:
                    continue
                i0, i1 = rows[0], rows[-1]
                assert rows == list(range(i0, i1 + 1))
                items.append((kr, i0, i1, "T", 7 - (kr - i0)))
            plan.append(items)
    else:
        R = nrows_q
        for b in range(R // 8):
            items = []
            for ks in range(R + 7):
                rows = [i for i in range(8 * b, 8 * b + 8) if i <= ks <= i + 7]
                if not rows:
                    continue
                i0, i1 = rows[0], rows[-1]
                assert rows == list(range(i0, i1 + 1))
                if ks < 4:
                    hb0 = [0, 1, 3, 6][ks]
                    assert i0 == 0
                    items.append((ks, i0, i1, "H", hb0))
                elif ks >= R + 4:
                    j = ks - (R + 4)
                    hb0 = 10 + [0, 3, 5][j]
                    assert i0 == R - 3 + j and i1 == R - 1
                    items.append((ks, i0, i1, "H", hb0))
                else:
                    kr = ks - 4
                    items.append((ks, i0, i1, "T", 7 - (kr - i0)))
            plan.append(items)
    return plan


def build_program(cfg):
    import contextlib
    nc = bass.Bass("TRN2", target_bir_lowering=False)
    NP, SP, SC, NQ, NNA, QOFF = cfg.NP, cfg.SP, cfg.SC, cfg.NQ, cfg.NNA, cfg.QOFF
    PHASES = getattr(cfg, 'phases', (1, 2, 3))

    def din(name, shape, dt=F32):
        return nc.dram_tensor(name, list(shape), dt, kind="ExternalInput").ap()

    def dscr(name, shape, dt):
        return nc.dram_tensor(name, list(shape), dt, kind="Internal").ap()

    xp = din("xp", [NP * SP, D])
    xc = din("xc", [SC, D])
    xn = din("xn", [NNA, D])
    w_in = din("w_in", [D, INC])
    w_uq = din("w_uq", [QL, H * QKD])
    w_ukv = din("w_ukv", [KVL, H * 128])
    w_o = din("w_o", [D, D])
    w_gate = din("w_gate", [D, DFF])
    w_up = din("w_up", [D, DFF])
    w_down = din("w_down", [DFF, D])
    gcol_d = din("gcol", [128, 32])
    gfin_d = din("gfin", [1, D])
    ident_d = din("ident", [128, 128])
    ropeP_d = din("ropeP", [128, SP // 128, 32])
    ropeC_d = din("ropeC", [128, SC // 128, 32])
    ropeN_d = din("ropeN", [128, NNA // 128, 32])
    natab_d = din("natab", [128, 4, 15 * 64])
    nahalo_d = din("nahalo", [128, 4, 16 * 64])
    namask_d = din("namask", [128, 2, 64])
    yp = nc.dram_tensor("yp", [NP * SP, D], F32, kind="ExternalOutput").ap()
    yn = nc.dram_tensor("yn", [NQ, D], F32, kind="ExternalOutput").ap()

    units = []
    for i in range(NP):
        units.append(dict(kind="P", S=SP, NT=SP, NQ=SP, q0=0, name=f"p{i}",
                          xres=xp[i * SP:(i + 1) * SP, :], yout=yp[i * SP:(i + 1) * SP, :]))
    units.append(dict(kind="S", S=SC, NT=NNA, NQ=NQ, q0=QOFF, name="s",
                      xres=xn[QOFF:QOFF + NQ, :], yout=yn))
    for u in units:
        n = u["name"]
        u["kt"] = dscr("s_kt_" + n, [H, QKD, u["S"]], BF16)
        u["v"] = dscr("s_v_" + n, [H, 128, u["S"] // 128, 65], BF16)
        u["qt"] = dscr("s_qt_" + n, [H, QKD, u["NT"]], BF16)
        u["naq"] = dscr("s_naq_" + n, [128, 4, u["NT"]], BF16)
        u["nak"] = dscr("s_nak_" + n, [128, 4, u["NT"]], BF16)
        u["nav"] = dscr("s_nav_" + n, [64, u["NT"] // 128, 2, 8 * 65], BF16)
        u["ot"] = dscr("s_ot_" + n, [16, 65, u["NQ"]], F32)

    S = Sched()
    top = contextlib.ExitStack()
    with top:
        csem = {q: top.enter_context(nc.semaphore("c_" + q)) for q in QUEUES}
        dsem = {q: [top.enter_context(nc.semaphore(f"d_{q}{i}")) for i in range(n)]
                for q, n in NDSEM.items()}
        block = top.enter_context(nc.Block())

        pending = {q: set() for q in QUEUES}

        def OP(q, fn, reads=(), writes=(), dma=False):
            o = S.op(q, fn, reads, writes, dma)
            if pending[q]:
                extra = {d for d in pending[q] if d is not o}
                if q == "tensor" and not dma:
                    extra = {d for d in extra if not (d.q == "tensor" and not d.dma)}
                for d in extra:
                    d.signal = True
                o.deps |= extra
                pending[q] = set()
            return o

        def barrier():
            B = set()
            for q in QUEUES:
                comp = [o for o in S.ops[q] if not o.dma]
                if comp:
                    B.add(comp[-1])
                if S.dma_ops[q]:
                    B.update(S.dma_ops[q][-NDSEM[q]:])
            for q in QUEUES:
                pending[q] |= B

        def DMA(q, out, in_, reads=(), writes=()):
            return OP(q, lambda e, o=out, i=in_: e.dma_start(out=o, in_=i), reads, writes, dma=True)

        def ACT(out, in_, func, reads=(), writes=(), **kw):
            return OP("scalar", lambda e, o=out, i=in_, f=func, kw=kw: e.activation(out=o, in_=i, func=f, **kw),
                      reads, writes)

        def TS(q, out, in0, s1, s2, op0, op1, reads=(), writes=()):
            if s2 is None:
                return OP(q, lambda e, o=out, i=in0, a=s1, p0=op0:
                          e.tensor_scalar(out=o, in0=i, scalar1=a, scalar2=0.0, op0=p0, op1=ALU.add), reads, writes)
            return OP(q, lambda e, o=out, i=in0, a=s1, b=s2, p0=op0, p1=op1:
                      e.tensor_scalar(out=o, in0=i, scalar1=a, scalar2=b, op0=p0, op1=p1), reads, writes)

        def TT(q, out, in0, in1, op, reads=(), writes=()):
            return OP(q, lambda e, o=out, a=in0, b=in1, p=op: e.tensor_tensor(out=o, in0=a, in1=b, op=p),
                      reads, writes)

        def STT(q, out, in0, scalar, in1, op0, op1, reads=(), writes=()):
            return OP(q, lambda e, o=out, a=in0, s=scalar, b=in1, p0=op0, p1=op1:
                      e.scalar_tensor_tensor(out=o, in0=a, scalar=s, in1=b, op0=p0, op1=p1), reads, writes)

        def CP(q, out, in_, reads=(), writes=()):
            if q == "scalar":
                return ACT(out, in_, AF.Copy, reads, writes)
            return OP(q, lambda e, o=out, i=in_: e.tensor_copy(out=o, in_=i), reads, writes)

        def MM(out, lhsT, rhs, start, stop, reads=(), writes=(), **kw):
            return OP("tensor", lambda e, o=out, l=lhsT, r=rhs, s=start, t=stop, kw=kw:
                      e.matmul(out=o, lhsT=l, rhs=r, start=s, stop=t, **kw), reads, writes)

        def TR(out, in_, idn, reads=(), writes=()):
            return OP("tensor", lambda e, o=out, i=in_, d=idn: e.transpose(out=o, in_=i, identity=d),
                      reads, writes)

        def MEMSET(q, ap, val, writes=()):
            return OP(q, lambda e, a=ap, v=val: e.memset(a, v), (), writes)

        def sb(stack, name, shape, dt):
            return stack.enter_context(nc.sbuf_tensor("sb_" + name, list(shape), dt))

        def psum(stack, name, shape, dt=F32):
            return stack.enter_context(nc.psum_tensor("ps_" + name, list(shape), dt))

        identf = sb(top, "identf", [128, 128], F32)
        identb = sb(top, "identb", [128, 128], BF16)
        gcol = sb(top, "gcol", [128, 32], F32)
        epsc = sb(top, "epsc", [128, 1], F32)
        r_identf, r_identb, r_gcol, r_eps = Res(), Res(), Res(), Res()
        DMA("sync", identf[:], ident_d[:, :], (), [r_identf])
        DMA("sync", gcol[:], gcol_d[:, :], (), [r_gcol])
        CP("vector", identb[:], identf[:], [r_identf], [r_identb])
        MEMSET("vector", epsc[:], EPS, [r_eps])

        def rstd(st, c0, n, R):
            ACT(st[:, c0 + 1:c0 + 2], st[:, c0:c0 + 1], AF.Ln, [R, r_eps], [R], bias=epsc[:, 0:1], scale=1.0 / n)
            ACT(st[:, c0 + 2:c0 + 3], st[:, c0 + 1:c0 + 2], AF.Exp, [R], [R], scale=-0.5)

        def load_weight(wst, r_wst, WST, rot, dst, r_dst, src, nchunk, ncols, gbase, col_scales,
                        cv_engs=("vector", "gpsimd")):
            for c in range(nchunk):
                for c0 in range(0, ncols, WST):
                    c1 = min(ncols, c0 + WST)
                    k = rot[0] % len(wst)
                    rot[0] += 1
                    DMA("sync" if rot[0] % 2 else "gpsimd", wst[k][:, 0:c1 - c0], src[c * 128:(c + 1) * 128, c0:c1],
                        (), [r_wst[k]])
                    for (a, b, const) in col_scales:
                        lo, hi = max(a, c0), min(b, c1)
                        if lo >= hi:
                            continue
                        eng = cv_engs[rot[1] % len(cv_engs)]
                        rot[1] += 1
                        if eng == "scalar":
                            if gbase is None:
                                CP(eng, dst[:, c, lo:hi], wst[k][:, lo - c0:hi - c0], [r_wst[k]], [r_dst[c]])
                            elif const == 1.0:
                                ACT(dst[:, c, lo:hi], wst[k][:, lo - c0:hi - c0], AF.Copy, [r_wst[k], r_gcol], [r_dst[c]],
                                    scale=gcol[:, gbase + c:gbase + c + 1])
                            else:
                                TS("vector", dst[:, c, lo:hi], wst[k][:, lo - c0:hi - c0],
                                   gcol[:, gbase + c:gbase + c + 1], const, ALU.mult, ALU.mult,
                                   [r_wst[k], r_gcol], [r_dst[c]])
                        elif gbase is None:
                            CP(eng, dst[:, c, lo:hi], wst[k][:, lo - c0:hi - c0], [r_wst[k]], [r_dst[c]])
                        else:
                            TS(eng, dst[:, c, lo:hi], wst[k][:, lo - c0:hi - c0], gcol[:, gbase + c:gbase + c + 1],
                               const, ALU.mult, ALU.mult, [r_wst[k], r_gcol], [r_dst[c]])

        WST = 1408
        ph1 = contextlib.ExitStack()
        with ph1 if 1 in PHASES else contextlib.nullcontext():
          if 1 in PHASES:
              wst = [sb(ph1, f"wst{i}", [128, WST], F32) for i in range(2)]
              r_wst = [Res(), Res()]
              rot = [0, 0]
              win = sb(ph1, "win", [128, NCH, INC], BF16)
              wuq = sb(ph1, "wuq", [128, 3, H * QKD], BF16)
              wukv = sb(ph1, "wukv", [128, 2, H * 128], BF16)
              r_win = [Res() for _ in range(NCH)]
              r_wuq = [Res() for _ in range(3)]
              r_wukv = [Res() for _ in range(2)]
              load_weight(wst, r_wst, WST, rot, win, r_win, w_in, NCH, INC, G_ATTN,
                          [(0, C_NQ, 1.0), (C_NQ, C_NK, 0.125), (C_NK, INC, 1.0)])
              load_weight(wst, r_wst, WST, rot, wuq, r_wuq, w_uq, 3, H * QKD, G_Q, [(0, H * QKD, QKD ** -0.5)])
              load_weight(wst, r_wst, WST, rot, wukv, r_wukv, w_ukv, 2, H * 128, G_KV, [(0, H * 128, 1.0)])

              ropeP = sb(ph1, "ropeP", [128, SP // 128, 32], F32)
              ropeC = sb(ph1, "ropeC", [128, SC // 128, 32], F32)
              ropeN = sb(ph1, "ropeN", [128, NNA // 128, 32], F32)
              r_rope = Res()
              DMA("sync", ropeP[:], ropeP_d[:, :, :], (), [r_rope])
              DMA("sync", ropeC[:], ropeC_d[:, :, :], (), [r_rope])
              DMA("sync", ropeN[:], ropeN_d[:, :, :], (), [r_rope])

              NX = 3
              xt = [sb(ph1, f"xt{i}", [128, D], F32) for i in range(NX)]
              r_xt = [Res() for _ in range(NX)]
              junk = [sb(ph1, f"junk{i}", [128, D], BF16) for i in range(2)]
              r_junk = [Res(), Res()]
              jctr = [0]
              hb = [sb(ph1, f"hb{i}", [128, D], BF16) for i in range(3)]
              r_hb = [Res(), Res(), Res()]
              hT = [sb(ph1, f"hT{i}", [128, NCH, 512], BF16) for i in range(2)]
              r_hT = [[Res() for _ in range(4)] for _ in range(2)]
              NST = 4
              stt = [sb(ph1, f"stt{i}", [128, 16], F32) for i in range(NST)]
              r_stx = [Res() for _ in range(NST)]
              r_stq = [Res() for _ in range(NST)]
              r_stkv = [Res() for _ in range(NST)]
              cqn = [sb(ph1, f"cqn{i}", [128, QL], BF16) for i in range(3)]
              ckvn = [sb(ph1, f"ckvn{i}", [128, KVL], BF16) for i in range(3)]
              r_cqn = [Res(), Res(), Res()]
              r_ckvn = [Res(), Res(), Res()]
              cqnT = [sb(ph1, f"cqnT{i}", [128, 3, 128], BF16) for i in range(3)]
              ckvnT = [sb(ph1, f"ckvnT{i}", [128, 2, 128], BF16) for i in range(3)]
              r_cqnT = [Res(), Res(), Res()]
              r_ckvnT = [Res(), Res(), Res()]
              rtmp = [sb(ph1, f"rtmp{i}", [128, 96 + 512], F32) for i in range(3)]
              r_tc = [Res(), Res(), Res()]
              r_ts = [Res(), Res(), Res()]
              r_kro = [Res(), Res(), Res()]
              r_tcq = [[Res(), Res()] for _ in range(3)]
              r_tsq = [[Res(), Res()] for _ in range(3)]
              Ktok = [sb(ph1, f"Ktok{i}", [128, H, QKD], BF16) for i in range(3)]
              Qtok = [sb(ph1, f"Qtok{i}", [128, H, QKD], BF16) for i in range(3)]
              r_Ktok = [Res(), Res(), Res()]
              r_Qtok = [Res(), Res(), Res()]
              KTst = [sb(ph1, f"KTst{i}", [128, H, 512], BF16) for i in range(2)]
              QTst = [sb(ph1, f"QTst{i}", [128, H, 512], BF16) for i in range(2)]
              Vst = [sb(ph1, f"Vst{i}", [128, H, 4, 65], BF16) for i in range(2)]
              naqst = [sb(ph1, f"naqst{i}", [128, 4, 512], BF16) for i in range(2)]
              nakst = [sb(ph1, f"nakst{i}", [128, 4, 512], BF16) for i in range(2)]
              navst = [sb(ph1, f"navst{i}", [128, 4, H, 65], BF16) for i in range(2)]
              r_KTst = [[Res() for _ in range(4)] for _ in range(2)]
              r_QTst = [[Res() for _ in range(4)] for _ in range(2)]
              r_Vst = [[Res() for _ in range(4)] for _ in range(2)]
              r_naqst = [[Res() for _ in range(4)] for _ in range(2)]
              r_nakst = [[Res() for _ in range(4)] for _ in range(2)]
              r_navst = [[Res() for _ in range(4)] for _ in range(2)]
              for i in range(2):
                  for t in range(4):
                      MEMSET("gpsimd", Vst[i][:, :, t, 64:65], 1.0, [r_Vst[i][t]])
                      MEMSET("gpsimd", navst[i][:, t, :, 64:65], 1.0, [r_navst[i][t]])

              PB = [psum(ph1, f"pb{i}", [128, 512], F32) for i in range(4)]
              PKVU = psum(ph1, "pkvu", [128, 1024], F32)
              PQU = psum(ph1, "pqu", [128, 1024], F32)
              r_PB = [Res(psum=True) for _ in range(4)]
              r_PKVU, r_PQU = Res(psum=True), Res(psum=True)
              ptr_bf = PB[0][:].bitcast(BF16)
              pkt_bf = PB[3][:].bitcast(BF16)
              pqt_bf = PB[0][:].bitcast(BF16)

              tilectr = [0]
              grpctr = [0]
              if getattr(cfg, "verbose", False):
                  print("ph1 sbuf remaining", nc.sbuf_bytes_remaining)

              def square_acc(in_ap, n, acc, reads, writes):
                  j = jctr[0] % 2
                  jctr[0] += 1
                  ACT(junk[j][:, 0:n], in_ap, AF.Square, reads, [r_junk[j]] + writes, accum_out=acc)

              def tile_gen(u, src, rope_sb, do_q, do_kv, do_na, g, gi, t, tc_):
                      if True:
                          tg = g * 4 + t
                          xi = tc_ % NX
                          k2 = tc_ % 3
                          si = tc_ % NST
                          st = stt[si]
                          tsl = slice(t * 128, (t + 1) * 128)
                          DMA("sync", xt[xi][:], src[tg * 128:(tg + 1) * 128, :], (), [r_xt[xi]])
                          square_acc(xt[xi][:], D, st[:, 0:1], [r_xt[xi]], [r_stx[si]])
                          rstd(st, 0, D, r_stx[si])
                          TS("vector", hb[k2][:], xt[xi][:], st[:, 2:3], None, ALU.mult, None,
                             [r_xt[xi], r_stx[si]], [r_hb[k2]])
                          for c in range(NCH):
                              TR(ptr_bf[:, c * 128:(c + 1) * 128], hb[k2][:, c * 128:(c + 1) * 128], identb[:],
                                 [r_hb[k2], r_identb], [r_PB[0]])
                          CP("scalar", hT[gi][:, :, tsl],
                             ptr_bf[:, 0:1024].rearrange("p (c n) -> p c n", c=NCH), [r_PB[0]], [r_hT[gi][t]])
                          yield
                          if do_q:
                              for c in range(NCH):
                                  MM(PB[1][:, 0:QL], hT[gi][:, c, tsl], win[:, c, C_CQ:C_CQ + QL],
                                     c == 0, c == NCH - 1, [r_hT[gi][t], r_win[c]], [r_PB[1]])
                          if do_kv:
                              for c in range(NCH):
                                  MM(PB[2][:, 0:KVL + RD], hT[gi][:, c, tsl], win[:, c, C_CKV:C_CKV + KVL + RD],
                                     c == 0, c == NCH - 1, [r_hT[gi][t], r_win[c]], [r_PB[2]])
                          if do_na:
                              for c in range(NCH):
                                  MM(PB[3][:, :], hT[gi][:, c, tsl], win[:, c, C_NV:C_NV + 512],
                                     c == 0, c == NCH - 1, [r_hT[gi][t], r_win[c]], [r_PB[3]])
                              CP("vector", navst[gi][:, t, :, 0:64], PB[3][:, :].rearrange("p (h d) -> p h d", h=H),
                                 [r_PB[3]], [r_navst[gi][t]])
                          yield
                          if do_q:
                              square_acc(PB[1][:, 0:QL], QL, st[:, 3:4], [r_PB[1]], [r_stq[si]])
                              rstd(st, 3, QL, r_stq[si])
                              TS("vector", cqn[k2][:], PB[1][:, 0:QL], st[:, 5:6], None, ALU.mult, None,
                                 [r_PB[1], r_stq[si]], [r_cqn[k2]])
                          if do_kv:
                              square_acc(PB[2][:, 0:KVL], KVL, st[:, 6:7], [r_PB[2]], [r_stkv[si]])
                              rstd(st, 6, KVL, r_stkv[si])
                              TS("vector", ckvn[k2][:], PB[2][:, 0:KVL], st[:, 8:9], None, ALU.mult, None,
                                 [r_PB[2], r_stkv[si]], [r_ckvn[k2]])
                              rt = rtmp[k2]
                              xk = PB[2][:, KVL:KVL + RD].rearrange("p (a j) -> p a j", a=2)
                              cosb = rope_sb[:, tg, 0:16].unsqueeze(1).broadcast_to([128, 2, 16])
                              sinb = rope_sb[:, tg, 16:32].unsqueeze(1).broadcast_to([128, 2, 16])
                              tcv = rt[:, 0:32].rearrange("p (a j) -> p a j", a=2)
                              tsv = rt[:, 32:64].rearrange("p (a j) -> p a j", a=2)
                              TT("vector", tcv, xk, cosb, ALU.mult, [r_PB[2], r_rope, r_stkv[si]], [r_tc[k2]])
                              TT("vector", tsv, xk, sinb, ALU.mult, [r_PB[2], r_rope, r_stkv[si]], [r_ts[k2]])
                              kro = rt[:, 64:96]
                              TT("vector", kro[:, 0:16], rt[:, 0:16], rt[:, 48:64], ALU.subtract,
                                 [r_tc[k2], r_ts[k2]], [r_kro[k2]])
                              TT("vector", kro[:, 16:32], rt[:, 32:48], rt[:, 16:32], ALU.add,
                                 [r_tc[k2], r_ts[k2]], [r_kro[k2]])
                          yield
                          if do_q:
                              for c in range(3):
                                  TR(ptr_bf[:, c * 128:(c + 1) * 128], cqn[k2][:, c * 128:(c + 1) * 128], identb[:],
                                     [r_cqn[k2], r_identb], [r_PB[0]])
                              CP("scalar", cqnT[k2][:], ptr_bf[:, 0:384].rearrange("p (c n) -> p c n", c=3),
                                 [r_PB[0]], [r_cqnT[k2]])
                          if do_kv:
                              for c in range(2):
                                  TR(ptr_bf[:, 384 + c * 128:384 + (c + 1) * 128], ckvn[k2][:, c * 128:(c + 1) * 128],
                                     identb[:], [r_ckvn[k2], r_identb], [r_PB[0]])
                              CP("scalar", ckvnT[k2][:], ptr_bf[:, 384:640].rearrange("p (c n) -> p c n", c=2),
                                 [r_PB[0]], [r_ckvnT[k2]])
                          yield
                          if do_kv:
                              for hh in range(2):
                                  for c in range(2):
                                      MM(PKVU[:, hh * 512:(hh + 1) * 512], ckvnT[k2][:, c, :],
                                         wukv[:, c, hh * 512:(hh + 1) * 512], c == 0, c == 1,
                                         [r_ckvnT[k2], r_wukv[c]], [r_PKVU])
                              kvv = PKVU[:, :].rearrange("p (h d) -> p h d", h=H)
                              CP("vector", Ktok[k2][:, :, 0:64], kvv[:, :, 0:64], [r_PKVU], [r_Ktok[k2]])
                              CP("vector", Ktok[k2][:, :, 64:96], kro.unsqueeze(1).broadcast_to([128, H, RD]),
                                 [r_kro[k2]], [r_Ktok[k2]])
                              CP("scalar", Vst[gi][:, :, t, 0:64], kvv[:, :, 64:128], [r_PKVU], [r_Vst[gi][t]])
                              yield
                              for h in range(H):
                                  TR(pkt_bf[0:QKD, h * 128:(h + 1) * 128], Ktok[k2][:, h, :], identb[:],
                                     [r_Ktok[k2], r_identb], [r_PB[3]])
                              CP("scalar", KTst[gi][0:QKD, :, tsl],
                                 pkt_bf[0:QKD, :].rearrange("p (h n) -> p h n", h=H), [r_PB[3]], [r_KTst[gi][t]])
                          if do_q:
                              for hh in range(2):
                                  for c in range(3):
                                      MM(PQU[:, hh * 512:hh * 512 + 384], cqnT[k2][:, c, :],
                                         wuq[:, c, hh * 384:(hh + 1) * 384], c == 0, c == 2,
                                         [r_cqnT[k2], r_wuq[c]], [r_PQU])
                              rt = rtmp[k2]
                              for hh in range(2):
                                  qv = PQU[:, hh * 512:hh * 512 + 384].rearrange("p (h d) -> p h d", h=4)
                                  CP("vector", Qtok[k2][:, hh * 4:(hh + 1) * 4, 0:64], qv[:, :, 0:64],
                                     [r_PQU], [r_Qtok[k2]])
                                  xq = qv[:, :, 64:96].rearrange("p h (a j) -> p h a j", a=2)
                                  cosb = rope_sb[:, tg, 0:16].unsqueeze(1).unsqueeze(1).broadcast_to([128, 4, 2, 16])
                                  sinb = rope_sb[:, tg, 16:32].unsqueeze(1).unsqueeze(1).broadcast_to([128, 4, 2, 16])
                                  o0 = 96 + hh * 256
                                  tcq = rt[:, o0:o0 + 128].rearrange("p (h a j) -> p h a j", h=4, a=2)
                                  tsq = rt[:, o0 + 128:o0 + 256].rearrange("p (h a j) -> p h a j", h=4, a=2)
                                  TT("vector", tcq, xq, cosb, ALU.mult, [r_PQU, r_rope], [r_tcq[k2][hh]])
                                  TT("vector", tsq, xq, sinb, ALU.mult, [r_PQU, r_rope], [r_tsq[k2][hh]])
                                  TT("gpsimd", Qtok[k2][:, hh * 4:(hh + 1) * 4, 64:80], tcq[:, :, 0, :], tsq[:, :, 1, :],
                                     ALU.subtract, [r_tcq[k2][hh], r_tsq[k2][hh]], [r_Qtok[k2]])
                                  TT("gpsimd", Qtok[k2][:, hh * 4:(hh + 1) * 4, 80:96], tsq[:, :, 0, :], tcq[:, :, 1, :],
                                     ALU.add, [r_tcq[k2][hh], r_tsq[k2][hh]], [r_Qtok[k2]])
                              yield
                              for h in range(H):
                                  TR(pqt_bf[0:QKD, h * 128:(h + 1) * 128], Qtok[k2][:, h, :], identb[:],
                                     [r_Qtok[k2], r_identb], [r_PB[0]])
                              CP("vector", QTst[gi][0:QKD, :, tsl],
                                 pqt_bf[0:QKD, :].rearrange("p (h n) -> p h n", h=H), [r_PB[0]], [r_QTst[gi][t]])
              def group_na_gen(u, do_na, g, gi):
                  if do_na:
                      for blk in range(8):
                          col0 = C_NQ + blk * 128
                          for c in range(NCH):
                              MM(PB[3][:, :], win[:, c, col0:col0 + 128], hT[gi][:, c, :], c == 0, c == NCH - 1,
                                 r_hT[gi] + [r_win[c]], [r_PB[3]])
                          dst = naqst[gi] if blk < 4 else nakst[gi]
                          rr = r_naqst[gi] if blk < 4 else r_nakst[gi]
                          CP("scalar" if blk % 2 else "vector", dst[:, blk % 4, :], PB[3][:, :], [r_PB[3]],
                             [rr[blk % 4]])
                          yield

              def group_dma(u, do_q, do_kv, do_na, g, gi):
                  if do_na:
                      DMA("gpsimd", u["naq"][:, :, g * 512:(g + 1) * 512], naqst[gi][:], r_naqst[gi], ())
                      DMA("gpsimd", u["nak"][:, :, g * 512:(g + 1) * 512], nakst[gi][:], r_nakst[gi], ())
                      for half in range(2):
                          DMA("gpsimd", u["nav"][:, g * 4:(g + 1) * 4, half, :],
                              navst[gi][half * 64:(half + 1) * 64, :, :, :].rearrange("p t h d -> p t (h d)"),
                              r_navst[gi], ())
                  if do_kv:
                      DMA("gpsimd", u["kt"][:, :, g * 512:(g + 1) * 512].rearrange("h d n -> d h n"),
                          KTst[gi][0:QKD, :, :], r_KTst[gi], ())
                      DMA("gpsimd", u["v"][:, :, g * 4:(g + 1) * 4, :].rearrange("h p t d -> p h t d"),
                          Vst[gi][:], r_Vst[gi], ())
                  if do_q:
                      DMA("gpsimd", u["qt"][:, :, g * 512:(g + 1) * 512].rearrange("h d n -> d h n"),
                          QTst[gi][0:QKD, :, :], r_QTst[gi], ())

              def run_token_passes(passes):
                  dbg = getattr(cfg, 'dbg', (1, 1, 1))
                  WIN = 3
                  jobs = []
                  ginfos = []
                  for (u, src, ntile, rope_sb, do_q, do_kv, do_na) in passes:
                      do_q, do_kv, do_na = do_q and dbg[0], do_kv and dbg[1], do_na and dbg[2]
                      for g in range(ntile // 4):
                          gi = grpctr[0] % 2
                          grpctr[0] += 1
                          ginfo = dict(u=u, do_q=do_q, do_kv=do_kv, do_na=do_na, g=g, gi=gi, left=5 if do_na else 4,
                                       idx=len(ginfos))
                          ginfos.append(ginfo)
                          for t in range(4):
                              tc_ = tilectr[0]
                              tilectr[0] += 1
                              jobs.append((ginfo, tile_gen(u, src, rope_sb, do_q, do_kv, do_na, g, gi, t, tc_), t == 3))
                  active = []
                  ji = 0

                  def finish(ginfo):
                      ginfo["left"] -= 1
                      if ginfo["left"] == 0:
                          group_dma(ginfo["u"], ginfo["do_q"], ginfo["do_kv"], ginfo["do_na"], ginfo["g"], ginfo["gi"])

                  while active or ji < len(jobs):
                      if (ji < len(jobs) and sum(1 for a_ in active if a_[2] != "na") < WIN
                              and all(gq["left"] == 0 for gq in ginfos[:max(0, jobs[ji][0]["idx"] - 1)])):
                          ginfo, gen, lastt = jobs[ji]
                          ji += 1
                          active.append([ginfo, gen, "tile", lastt, False])
                      for a_ in list(active):
                          ginfo, gen = a_[0], a_[1]
                          try:
                              next(gen)
                              if a_[2] == "tile" and a_[3] and not a_[4] and ginfo["do_na"]:
                                  a_[4] = True
                                  active.append([ginfo, group_na_gen(ginfo["u"], True, ginfo["g"], ginfo["gi"]),
                                                 "na", False, False])
                          except StopIteration:
                              active.remove(a_)
                              finish(ginfo)

              plist = []
              for ui, u in enumerate(units):
                  if u["kind"] == "P":
                      plist.append((u, xp[ui * SP:(ui + 1) * SP, :], SP // 128, ropeP, True, True, True))
                  else:
                      plist.append((u, xn, NNA // 128, ropeN, True, False, True))
                      plist.append((u, xc, SC // 128, ropeC, False, True, False))
              run_token_passes(plist)
        barrier()

        ph2 = contextlib.ExitStack()
        with ph2 if 2 in PHASES else contextlib.nullcontext():
          if 2 in PHASES:
              SMAX = max(u["S"] for u in units)
              NQMAX = max(u["NQ"] for u in units)
              NTMAX = max(u["NT"] for u in units)
              KTb = [sb(ph2, f"KTb{i}", [128, SMAX], BF16) for i in range(2)]
              Vb = [sb(ph2, f"Vb{i}", [128, SMAX // 128, 65], BF16) for i in range(2)]
              QTb = [sb(ph2, f"QTb{i}", [128, NQMAX], BF16) for i in range(2)]
              r_KTb, r_Vb, r_QTb = [Res(), Res()], [Res(), Res()], [Res(), Res()]
              PT = [sb(ph2, f"PT{i}", [128, 1024], BF16) for i in range(3)]
              r_PT = [Res() for _ in range(3)]
              oTs = [sb(ph2, f"oTs{i}", [128, 512], F32) for i in range(2)]
              r_oTs = [Res(), Res()]
              naq = sb(ph2, "naq", [128, 4, NTMAX], BF16)
              nak = sb(ph2, "nak", [128, 4, NTMAX], BF16)
              nav = sb(ph2, "nav", [64, NTMAX // 64, 8 * 65], BF16)
              r_naq, r_nak, r_nav = Res(), Res(), Res()
              tabf = sb(ph2, "tabf", [128, 4, 15 * 64], F32)
              half_ = sb(ph2, "halof", [128, 4, 16 * 64], F32)
              maskf = sb(ph2, "maskf", [128, 2, 64], F32)
              tabb = sb(ph2, "tabb", [128, 4, 15 * 64], BF16)
              halb = sb(ph2, "halob", [128, 4, 16 * 64], BF16)
              r_tabf, r_half, r_maskf, r_tabb, r_halb = Res(), Res(), Res(), Res(), Res()
              NCHAIN = 4
              PTn = [sb(ph2, f"PTn{i}", [64, 512], BF16) for i in range(2 * NCHAIN)]
              r_PTn = [Res() for _ in range(2 * NCHAIN)]

              PS = [psum(ph2, f"ps{i}", [128, 1024], F32) for i in range(2)]
              PO = [psum(ph2, f"po{i}", [128, 512], F32) for i in range(4)]
              r_PSh = [[Res(psum=True), Res(psum=True)] for _ in range(2)]
              r_PO = [Res(psum=True) for _ in range(4)]
              PSn = [PS[0][:, 0:512], PS[0][:, 512:1024], PS[1][:, 0:512], PS[1][:, 512:1024]]
              r_PSn = [r_PSh[0][0], r_PSh[0][1], r_PSh[1][0], r_PSh[1][1]]
              PA = PO
              r_PA = r_PO

              DMA("sync", tabf[:], natab_d[:, :, :], (), [r_tabf])
              DMA("sync", half_[:], nahalo_d[:, :, :], (), [r_half])
              DMA("sync", maskf[:], namask_d[:, :, :], (), [r_maskf])
              for (srcf, dstb, nb, rs, rd) in ((tabf, tabb, 15, r_tabf, r_tabb), (half_, halb, 16, r_half, r_halb)):
                  v = srcf[:].rearrange("p a (b q) -> p (a b) q", q=64)
                  vb = dstb[:].rearrange("p a (b q) -> p (a b) q", q=64)
                  m01 = maskf[:, 0, :].unsqueeze(1).broadcast_to([128, 4 * nb, 64])
                  mng = maskf[:, 1, :].unsqueeze(1).broadcast_to([128, 4 * nb, 64])
                  TT("vector", v, v, m01, ALU.mult, [rs, r_maskf], [rs])
                  TT("vector", vb, v, mng, ALU.add, [rs, r_maskf], [rd])

              ctr = dict(hb=0, pt=0, ps=0, po=0, ot=0, ptn=0)
              if getattr(cfg, "verbose", False):
                  print("ph2 sbuf remaining", nc.sbuf_bytes_remaining)

              def mla_attention(u):
                  Sx, NQu, q0 = u["S"], u["NQ"], u["q0"]
                  nkt = Sx // 128
                  nk2 = nkt // 2
                  nqb = NQu // 512
                  steps = [(h, qb, k2) for h in range(H) for qb in range(nqb) for k2 in range(nk2)]
                  hbase = ctr["hb"]
                  ctr["hb"] += H
                  pobase = ctr["po"]
                  ctr["po"] += H * nqb

                  def load(h):
                      bi = (hbase + h) % 2
                      DMA("sync", KTb[bi][0:QKD, 0:Sx], u["kt"][h, :, :], (), [r_KTb[bi]])
                      DMA("sync", Vb[bi][:, 0:nkt, :], u["v"][h, :, :, :], (), [r_Vb[bi]])
                      DMA("sync", QTb[bi][0:QKD, 0:NQu], u["qt"][h, :, q0:q0 + NQu], (), [r_QTb[bi]])

                  def QK(i):
                      h, qb, k2 = steps[i]
                      bi = (hbase + h) % 2
                      si = i % 2
                      for j in range(2):
                          kt = 2 * k2 + j
                          MM(PS[si][:, j * 512:(j + 1) * 512], KTb[bi][0:QKD, kt * 128:(kt + 1) * 128],
                             QTb[bi][0:QKD, qb * 512:(qb + 1) * 512], True, True,
                             [r_KTb[bi], r_QTb[bi]], [r_PSh[si][j]])

                  def EXPPV(i):
                      h, qb, k2 = steps[i]
                      bi = (hbase + h) % 2
                      si = i % 2
                      pi = i % 3
                      oi = (pobase + h * nqb + qb) % 4
                      ACT(PT[pi][:, :], PS[si][:, :], AF.Exp, r_PSh[si], [r_PT[pi]])
                      for j in range(2):
                          kt = 2 * k2 + j
                          MM(PO[oi][0:65, :], Vb[bi][:, kt, :], PT[pi][:, j * 512:(j + 1) * 512],
                             kt == 0, kt == nkt - 1, [r_Vb[bi], r_PT[pi]], [r_PO[oi]])
                      if k2 == nk2 - 1:
                          ti = ctr["ot"] % 2
                          ctr["ot"] += 1
                          CP("vector", oTs[ti][0:65, :], PO[oi][0:65, :], [r_PO[oi]], [r_oTs[ti]])
                          DMA("gpsimd", u["ot"][h, :, qb * 512:(qb + 1) * 512], oTs[ti][0:65, :], [r_oTs[ti]], ())

                  load(0)
                  QK(0)
                  for i in range(len(steps)):
                      h, qb, k2 = steps[i]
                      if qb == 0 and k2 == 0 and h + 1 < H:
                          load(h + 1)
                      if i + 1 < len(steps):
                          QK(i + 1)
                      EXPPV(i)

              def na_attention(u):
                  NT, NQu = u["NT"], u["NQ"]
                  nrq = NQu // 64
                  plan = na_plan(u["kind"], nrq)
                  qrow0 = 0 if u["kind"] == "P" else 4
                  DMA("sync", naq[:, :, 0:NT], u["naq"][:, :, :], (), [r_naq])
                  DMA("sync", nak[:, :, 0:NT], u["nak"][:, :, :], (), [r_nak])
                  DMA("sync", nav[:, 0:NT // 64, :], u["nav"][:, :, :, :].rearrange("p t a d -> p (t a) d"), (), [r_nav])
                  chains = [(h, b) for b in range(len(plan)) for h in range(H)]
                  for c0_ in range(0, len(chains), NCHAIN):
                      grp = chains[c0_:c0_ + NCHAIN]
                      nst = max(len(plan[b]) for (h, b) in grp)

                      def QKB(j, s):
                          h, b = grp[j]
                          items = plan[b]
                          if s >= len(items):
                              return
                          ks, i0, i1, src, blk0 = items[s]
                          pr, pb = h // 2, (h % 2) * 64
                          N = 64 * (i1 - i0 + 1)
                          qtok0 = (qrow0 + i0) * 64
                          MM(PSn[j][0:64, 0:N], nak[pb:pb + 64, pr, ks * 64:(ks + 1) * 64],
                             naq[pb:pb + 64, pr, qtok0:qtok0 + N], True, False, [r_nak, r_naq], [r_PSn[j]])
                          tb = tabb if src == "T" else halb
                          rtb = r_tabb if src == "T" else r_halb
                          MM(PSn[j][0:64, 0:N], identb[pb:pb + 64, pb:pb + 64],
                             tb[pb:pb + 64, pr, blk0 * 64:blk0 * 64 + N], False, True, [rtb, r_identb], [r_PSn[j]])

                      def EXPN(j, s):
                          h, b = grp[j]
                          items = plan[b]
                          if s >= len(items):
                              return
                          ks, i0, i1, src, blk0 = items[s]
                          N = 64 * (i1 - i0 + 1)
                          pi = 2 * j + (s % 2)
                          ACT(PTn[pi][:, 0:N], PSn[j][0:64, 0:N], AF.Exp, [r_PSn[j]], [r_PTn[pi]])

                      def PVN(j, s):
                          h, b = grp[j]
                          items = plan[b]
                          if s >= len(items):
                              return
                          ks, i0, i1, src, blk0 = items[s]
                          N = 64 * (i1 - i0 + 1)
                          c0 = (i0 - 8 * b) * 64
                          pi = 2 * j + (s % 2)
                          MM(PA[j][0:65, c0:c0 + N], nav[:, ks, h * 65:(h + 1) * 65], PTn[pi][:, 0:N],
                             s == 0, s == len(items) - 1, [r_nav, r_PTn[pi]], [r_PA[j]], skip_group_check=True)
                          if s == len(items) - 1:
                              ti = ctr["ot"] % 2
                              ctr["ot"] += 1
                              CP("vector", oTs[ti][0:65, :], PA[j][0:65, :], [r_PA[j]], [r_oTs[ti]])
                              DMA("gpsimd", u["ot"][8 + h, :, b * 512:(b + 1) * 512], oTs[ti][0:65, :],
                                  [r_oTs[ti]], ())

                      for j in range(len(grp)):
                          QKB(j, 0)
                      for s in range(nst):
                          for j in range(len(grp)):
                              EXPN(j, s)
                          for j in range(len(grp)):
                              QKB(j, s + 1)
                          for j in range(len(grp)):
                              PVN(j, s)

              for u in units:
                  na_attention(u)
                  mla_attention(u)
        barrier()

        ph3 = contextlib.ExitStack()
        with ph3 if 3 in PHASES else contextlib.nullcontext():
          if 3 in PHASES:
              WST = 352
              NWST3 = 3
              wst = [sb(ph3, f"wst3_{i}", [128, WST], F32) for i in range(NWST3)]
              r_wst = [Res() for _ in range(NWST3)]
              rot = [0, 0]
              wo = sb(ph3, "wo", [128, NCH, D], BF16)
              wg = sb(ph3, "wg", [128, NCH, DFF], BF16)
              wu = sb(ph3, "wu", [128, NCH, DFF], BF16)
              wd = sb(ph3, "wd", [128, NFF, D], BF16)
              r_wo = [Res() for _ in range(NCH)]
              r_wg = [Res() for _ in range(NCH)]
              r_wu = [Res() for _ in range(NCH)]
              r_wd = [Res() for _ in range(NFF)]
              load_weight(wst, r_wst, WST, rot, wo, r_wo, w_o, NCH, D, G_MIX, [(0, D, 1.0)], cv_engs=("vector", "scalar"))
              load_weight(wst, r_wst, WST, rot, wg, r_wg, w_gate, NCH, DFF, G_FFN, [(0, DFF, 1.0)], cv_engs=("vector", "scalar"))
              load_weight(wst, r_wst, WST, rot, wu, r_wu, w_up, NCH, DFF, G_FFN, [(0, DFF, 1.0)], cv_engs=("vector", "scalar"))
              load_weight(wst, r_wst, WST, rot, wd, r_wd, w_down, NFF, D, None, [(0, D, 1.0)], cv_engs=("vector", "scalar"))
              gfin = sb(ph3, "gfin", [128, D], F32)
              r_gfin = Res()
              DMA("sync", gfin[:], gfin_d[0:1, :].partition_broadcast(128), (), [r_gfin])
              onec = sb(ph3, "onec", [128, 1], F32)
              r_one = Res()
              MEMSET("vector", onec[:], 1.0, [r_one])

              TG = 2
              NTOK = TG * 128
              oTin = sb(ph3, "oTin", [128, 4, 128], F32)
              r_oTin = Res()
              atok = sb(ph3, "atok", [128, D], F32)
              r_atok = Res()
              x1 = [[sb(ph3, f"x1_{s_}_{i}", [128, D], F32) for i in range(TG)] for s_ in range(2)]
              r_x1 = [[Res() for _ in range(TG)] for _ in range(2)]
              mixb = sb(ph3, "mixb", [128, D], BF16)
              r_mixb = Res()
              mixT = sb(ph3, "mixT", [128, NCH, 128], BF16)
              r_mixT = Res()
              h2b = sb(ph3, "h2b", [128, D], BF16)
              r_h2b = Res()
              h2T = [sb(ph3, f"h2T{s_}", [128, NCH, NTOK], BF16) for s_ in range(2)]
              r_h2T = [[Res() for _ in range(TG)] for _ in range(2)]
              actT = sb(ph3, "actT", [128, NFF, NTOK], BF16)
              r_actT = [Res() for _ in range(NFF)]
              sg = [sb(ph3, f"sg{i}", [128, NTOK], F32) for i in range(2)]
              r_sg = [Res(), Res()]
              st3 = [sb(ph3, f"st3_{i}", [128, 32], F32) for i in range(4)]
              r_st3 = [[Res() for _ in range(6)] for _ in range(4)]

              POh = psum(ph3, "poh", [128, 512], F32)
              r_POh = Res(psum=True)
              PTR = psum(ph3, "ptr3", [128, 512], F32)
              r_PTR = Res(psum=True)
              ptr3_bf = PTR[:].bitcast(BF16)
              PY = psum(ph3, "py", [128, 1024], F32)
              r_PY = Res(psum=True)
              PG = [psum(ph3, f"pg{i}", [128, 512], F32) for i in range(2)]
              PU = [psum(ph3, f"pu{i}", [128, 512], F32) for i in range(2)]
              r_PG, r_PU = [Res(psum=True), Res(psum=True)], [Res(psum=True), Res(psum=True)]
              if getattr(cfg, "verbose", False):
                  print("ph3 sbuf remaining", nc.sbuf_bytes_remaining)

              def prologue(u, g, slot):
                  for t in range(TG):
                      tok0 = g * NTOK + t * 128
                      X, rX = x1[slot][t], r_x1[slot][t]
                      st = st3[slot * 2 + t]
                      rs_ = r_st3[slot * 2 + t]
                      DMA("sync", X[:], u["xres"][tok0:tok0 + 128, :], (), [rX])
                      for hq in range(4):
                          DMA("sync", oTin[0:65, :, :],
                              u["ot"][hq * 4:(hq + 1) * 4, :, tok0:tok0 + 128].rearrange("h d n -> d h n"),
                              (), [r_oTin])
                          yield
                          for hh in range(4):
                              TR(POh[:, hh * 128:hh * 128 + 65], oTin[0:65, hh, :], identf[0:65, 0:65],
                                 [r_oTin, r_identf], [r_POh])
                          pov = POh[:, :].rearrange("p (h d) -> p h d", h=4)
                          rc = st[:, 8:12].unsqueeze(2)
                          OP("vector", lambda e, o=rc, i=pov[:, :, 64:65]: e.reciprocal(out=o, in_=i),
                             [r_POh], [rs_[4]])
                          TT("vector", atok[:, hq * 256:(hq + 1) * 256].rearrange("p (h d) -> p h d", h=4),
                             pov[:, :, 0:64], rc.broadcast_to([128, 4, 64]), ALU.mult,
                             [r_POh, rs_[4]], [r_atok])
                      for half in range(2):
                          hs = slice(half * 512, (half + 1) * 512)
                          ACT(mixb[:, hs], atok[:, hs], AF.Square, [r_atok], [r_mixb, rs_[half]],
                              accum_out=st[:, 3 * half:3 * half + 1])
                          rstd(st, 3 * half, 512, rs_[half])
                          TS("vector", mixb[:, hs], atok[:, hs], st[:, 3 * half + 2:3 * half + 3], None, ALU.mult, None,
                             [r_atok, rs_[half]], [r_mixb])
                      yield
                      for c in range(NCH):
                          TR(ptr3_bf[:, c * 128:(c + 1) * 128], mixb[:, c * 128:(c + 1) * 128], identb[:],
                             [r_mixb, r_identb], [r_PTR])
                      CP("scalar", mixT[:], ptr3_bf[:, 0:1024].rearrange("p (c n) -> p c n", c=NCH),
                         [r_PTR], [r_mixT])
                      yield
                      for hh in range(2):
                          for c in range(NCH):
                              MM(PY[:, hh * 512:(hh + 1) * 512], mixT[:, c, :], wo[:, c, hh * 512:(hh + 1) * 512],
                                 c == 0, c == NCH - 1, [r_mixT, r_wo[c]], [r_PY])
                      TT("vector", X[:], X[:], PY[:, :], ALU.add, [rX, r_PY], [rX])
                      ACT(h2b[:], X[:], AF.Square, [rX], [r_h2b, rs_[2]], accum_out=st[:, 16:17])
                      rstd(st, 16, D, rs_[2])
                      TS("vector", h2b[:], X[:], st[:, 18:19], None, ALU.mult, None, [rX, rs_[2]], [r_h2b])
                      yield
                      for c in range(NCH):
                          TR(ptr3_bf[:, c * 128:(c + 1) * 128], h2b[:, c * 128:(c + 1) * 128], identb[:],
                             [r_h2b, r_identb], [r_PTR])
                      CP("scalar", h2T[slot][:, :, t * 128:(t + 1) * 128],
                         ptr3_bf[:, 0:1024].rearrange("p (c n) -> p c n", c=NCH), [r_PTR], [r_h2T[slot][t]])
                      yield

              def ffn_step(slot, f):
                  pi = f % 2
                  for c in range(NCH):
                      MM(PG[pi][:, 0:NTOK], wg[:, c, f * 128:(f + 1) * 128], h2T[slot][:, c, :], c == 0, c == NCH - 1,
                         r_h2T[slot] + [r_wg[c]], [r_PG[pi]])
                  for c in range(NCH):
                      MM(PU[pi][:, 0:NTOK], wu[:, c, f * 128:(f + 1) * 128], h2T[slot][:, c, :], c == 0, c == NCH - 1,
                         r_h2T[slot] + [r_wu[c]], [r_PU[pi]])
                  ACT(sg[pi][:], PG[pi][:, 0:NTOK], AF.Exp, [r_PG[pi]], [r_sg[pi]], scale=-1.0)
                  ACT(sg[pi][:], sg[pi][:], AF.Ln, [r_sg[pi], r_one], [r_sg[pi]], bias=onec[:, 0:1])
                  ACT(sg[pi][:], sg[pi][:], AF.Exp, [r_sg[pi]], [r_sg[pi]], scale=-1.0)
                  TT("vector", sg[pi][:], sg[pi][:], PG[pi][:, 0:NTOK], ALU.mult, [r_sg[pi], r_PG[pi]], [r_sg[pi]])
                  TT("vector", actT[:, f, :], sg[pi][:], PU[pi][:, 0:NTOK], ALU.mult,
                     [r_sg[pi], r_PU[pi]], [r_actT[f]])

              def down_epi(u, g, slot, t):
                  tok0 = g * NTOK + t * 128
                  X, rX = x1[slot][t], r_x1[slot][t]
                  st = st3[slot * 2 + t]
                  rr = r_st3[slot * 2 + t][3]
                  for hh in range(2):
                      for f in range(NFF):
                          MM(PY[:, hh * 512:(hh + 1) * 512], actT[:, f, t * 128:(t + 1) * 128],
                             wd[:, f, hh * 512:(hh + 1) * 512], f == 0, f == NFF - 1,
                             [r_actT[f], r_wd[f]], [r_PY])
                  TT("vector", X[:], X[:], PY[:, :], ALU.add, [rX, r_PY], [rX])
                  ACT(h2b[:], X[:], AF.Square, [rX], [r_h2b, rr], accum_out=st[:, 24:25])
                  rstd(st, 24, D, rr)
                  STT("vector", X[:], X[:], st[:, 26:27], gfin[:], ALU.mult, ALU.mult, [rX, rr, r_gfin], [rX])
                  DMA("gpsimd", u["yout"][tok0:tok0 + 128, :], X[:], [rX], ())

              groups = [(u, g) for u in units for g in range(u["NQ"] // NTOK)]
              cur = prologue(groups[0][0], groups[0][1], 0)
              for _ in cur:
                  pass
              for gi_, (u, g) in enumerate(groups):
                  slot = gi_ % 2
                  nxt = None
                  if gi_ + 1 < len(groups):
                      nxt = prologue(groups[gi_ + 1][0], groups[gi_ + 1][1], 1 - slot)
                  for f in range(NFF):
                      ffn_step(slot, f)
                      if nxt is not None:
                          next(nxt, None)
                  if nxt is not None:
                      for _ in nxt:
                          pass
                  for t in range(TG):
                      down_epi(u, g, slot, t)
        S.emit(nc, block, csem, dsem)
    return nc


def _rope_table(pos):
    inv = (10000.0 ** (-np.arange(0, RD, 2, dtype=np.float32) / np.float32(RD))).astype(np.float32)
    ang = pos.astype(np.float32)[:, None] * inv[None, :]
    tab = np.concatenate([np.cos(ang), np.sin(ang)], axis=1).astype(np.float32)
    nt = pos.shape[0] // 128
    return np.ascontiguousarray(tab.reshape(nt, 128, 32).transpose(1, 0, 2))


def _col(g):
    g = np.asarray(g, np.float32).reshape(-1, 128)
    return g.T


def make_in_maps(inputs, cfg):
    NP, SP, SC, NQ, NNA = cfg.NP, cfg.SP, cfg.SC, cfg.NQ, cfg.NNA
    xpr = np.asarray(inputs["x_prompt"], np.float32)
    xsm = np.asarray(inputs["x_sample"], np.float32)
    nquart = SC // NQ
    ncores = xsm.shape[0] * nquart
    assert xpr.shape[0] == NP * ncores
    Rq = NQ // 64
    Rtot = SC // 64
    f = lambda k: np.ascontiguousarray(np.asarray(inputs[k], np.float32)[0])
    gcol = np.zeros((128, 32), np.float32)
    gcol[:, G_ATTN:G_ATTN + 8] = _col(inputs["attn_norm_g"][0])
    gcol[:, G_Q:G_Q + 3] = _col(inputs["q_norm_g"][0])
    gcol[:, G_KV:G_KV + 2] = _col(inputs["kv_norm_g"][0])
    gcol[:, G_MIX:G_MIX + 4] = _col(inputs["mla_out_g"][0])
    gcol[:, G_MIX + 4:G_MIX + 8] = _col(inputs["na_out_g"][0])
    gcol[:, G_FFN:G_FFN + 8] = _col(inputs["ffn_norm_g"][0])
    rpb = np.asarray(inputs["na_rpb"], np.float32)[0]
    kc = np.arange(64)[:, None]
    qc = np.arange(64)[None, :]
    dj = np.clip(kc - qc, -15, 15) + 15
    cs = np.clip(qc - 8, 0, 48)
    m01 = ((kc >= cs) & (kc < cs + 16)).astype(np.float32)
    namask = np.zeros((128, 2, 64), np.float32)
    namask[:, 0, :] = np.tile(m01, (2, 1))
    namask[:, 1, :] = np.tile((1.0 - m01) * NEG, (2, 1))

    def blockT(h, di):
        return rpb[h, di + 7][dj]

    natab = np.zeros((128, 4, 15, 64), np.float32)
    for h in range(H):
        pb, pr = (h % 2) * 64, h // 2
        for j in range(15):
            natab[pb:pb + 64, pr, j, :] = blockT(h, 7 - j)
    natab = natab.reshape(128, 4, 15 * 64)
    shared = dict(w_in=f("w_in"), w_uq=f("w_uq"), w_ukv=f("w_ukv"), w_o=f("w_o"), w_gate=f("w_gate"),
                  w_up=f("w_up"), w_down=f("w_down"), gcol=gcol,
                  gfin=np.asarray(inputs["final_norm_g"], np.float32).reshape(1, D),
                  ident=np.eye(128, dtype=np.float32), natab=natab, namask=namask,
                  ropeP=_rope_table(np.arange(SP)), ropeC=_rope_table(np.arange(SC)))
    maps = []
    for c in range(ncores):
        sq, qt = c // nquart, c % nquart
        m = dict(shared)
        m["xp"] = np.ascontiguousarray(xpr[c * NP:(c + 1) * NP].reshape(NP * SP, D))
        seq = xsm[sq]
        m["xc"] = np.ascontiguousarray(seq)
        r0 = qt * Rq
        rows_b = [4, 5, 6, 7] if qt == 0 else [r0 - 4, r0 - 3, r0 - 2, r0 - 1]
        last = (qt == nquart - 1)
        rows_a = [Rtot - 8, Rtot - 7, Rtot - 6] if last else [r0 + Rq, r0 + Rq + 1, r0 + Rq + 2]
        rows = rows_b + list(range(r0, r0 + Rq)) + rows_a
        xn = np.zeros((NNA, D), np.float32)
        posn = np.zeros((NNA,), np.float32)
        for i, r in enumerate(rows):
            xn[i * 64:(i + 1) * 64] = seq[r * 64:(r + 1) * 64]
            posn[i * 64:(i + 1) * 64] = np.arange(r * 64, (r + 1) * 64)
        m["xn"] = xn
        m["ropeN"] = _rope_table(posn)
        hal = np.zeros((128, 4, 16, 64), np.float32)
        for h in range(H):
            pb, pr = (h % 2) * 64, h // 2
            for ks in range(4):
                for i in range(ks + 1):
                    di = (rows_b[ks] - r0) - i
                    hal[pb:pb + 64, pr, [0, 1, 3, 6][ks] + i, :] = blockT(h, di)
            for j in range(3):
                for n_, i in enumerate(range(Rq - 3 + j, Rq)):
                    di = (rows_a[j] - r0) - i
                    hal[pb:pb + 64, pr, 10 + [0, 3, 5][j] + n_, :] = blockT(h, di)
        m["nahalo"] = hal.reshape(128, 4, 16 * 64)
        maps.append(m)
    return maps


def assemble(results, cfg, nsample):
    NP, SP, SC, NQ = cfg.NP, cfg.SP, cfg.SC, cfg.NQ
    nquart = SC // NQ
    ncores = nsample * nquart
    yp = np.concatenate([np.asarray(results[c]["yp"], np.float32).reshape(NP, SP, D) for c in range(ncores)], axis=0)
    ys = np.zeros((nsample, SC, D), np.float32)
    for c in range(ncores):
        sq, qt = c // nquart, c % nquart
        ys[sq, qt * NQ:(qt + 1) * NQ] = np.asarray(results[c]["yn"], np.float32)
    return yp, ys


_NC_CACHE = {}


def kernel(**inputs):
    cfg = Cfg(NP=2, SP=2048, SC=8192, NQ=2048)
    maps = make_in_maps(inputs, cfg)
    key = (cfg.NP, cfg.SP, cfg.SC, cfg.NQ)
    if key not in _NC_CACHE:
        _NC_CACHE[key] = build_program(cfg)
    nc = _NC_CACHE[key]
    res = run_bass_kernel_spmd(nc, maps, core_ids=list(range(len(maps))))
    return assemble(res.results, cfg, np.asarray(inputs["x_sample"]).shape[0])
```

```python
import numpy as np
import concourse.bass as bass
import concourse.mybir as mybir
from concourse.bass_utils import run_bass_kernel_spmd

F32 = mybir.dt.float32
BF16 = mybir.dt.bfloat16
AF = mybir.ActivationFunctionType
ALU = mybir.AluOpType

QUEUES = ("sync", "scalar", "vector", "gpsimd", "tensor")
NDSEM = {"sync": 16, "gpsimd": 12, "scalar": 4}


class Res:
    __slots__ = ("name", "w", "rc", "rd", "psum")

    def __init__(self, name="", psum=False):
        self.name = name
        self.psum = psum
        self.w = None
        self.rc = {}
        self.rd = []


class Op:
    __slots__ = ("q", "dma", "fn", "deps", "signal", "cidx", "didx", "pos")


class Sched:
    def __init__(self):
        self.ops = {q: [] for q in QUEUES}
        self.ncomp = {q: 0 for q in QUEUES}
        self.ndma = {q: 0 for q in QUEUES}
        self.dma_ops = {q: [] for q in QUEUES}
        self.all_dma = []

    def op(self, q, fn, reads=(), writes=(), dma=False):
        o = Op()
        o.q, o.dma, o.fn, o.signal = q, dma, fn, dma
        deps = set()
        for r in reads:
            if r.w is not None:
                deps.add(r.w)
            if r.psum:
                deps.update(o2 for q2, o2 in r.rc.items() if q2 != q)
        for r in writes:
            if r.w is not None:
                deps.add(r.w)
            deps.update(r.rc.values())
            deps.update(r.rd)
        if dma:
            o.didx = self.ndma[q]
            self.ndma[q] += 1
            R = NDSEM[q]
            if o.didx >= R:
                deps.add(self.dma_ops[q][o.didx - R])
            self.dma_ops[q].append(o)
            self.all_dma.append(o)
            o.cidx = None
        else:
            o.cidx = self.ncomp[q]
            self.ncomp[q] += 1
            o.didx = None
        deps.discard(o)
        if q == "tensor" and not dma:
            deps = {d for d in deps if not (d.q == "tensor" and not d.dma)}
        o.deps = deps
        for d in deps:
            d.signal = True
        for r in reads:
            if dma:
                r.rd.append(o)
            else:
                r.rc[q] = o
        for r in writes:
            r.w = o
            r.rc = {}
            r.rd = []
        o.pos = len(self.ops[q])
        self.ops[q].append(o)
        return o

    def emit(self, nc, block, csem, dsem):
        sched = self
        val = {}
        for q in QUEUES:
            n = 0
            for o in self.ops[q]:
                if o.dma:
                    val[o] = (dsem[q][o.didx % NDSEM[q]], 16 * (o.didx // NDSEM[q] + 1))
                elif o.signal:
                    n += 1
                    val[o] = (csem[q], n)

        def run(q, eng):
            waited = {}
            last_dma = []
            for o in sched.ops[q]:
                need = {}
                for d in o.deps:
                    s, v = val[d]
                    k = id(s)
                    if v > need.get(k, (None, 0))[1]:
                        need[k] = (s, v)
                for k, (s, v) in need.items():
                    if waited.get(k, 0) < v:
                        eng.wait_ge(s, v)
                        waited[k] = v
                ins = o.fn(eng)
                if o.dma:
                    s, v = val[o]
                    ins.then_inc(s, 16)
                elif o.signal:
                    ins.then_inc(csem[q], 1)
            for o in sched.dma_ops[q][-NDSEM.get(q, 0):] if sched.dma_ops[q] else []:
                s, v = val[o]
                if waited.get(id(s), 0) < v:
                    eng.wait_ge(s, v)
                    waited[id(s)] = v

        @block.sync
        def _(e):
            run("sync", e)

        @block.scalar
        def _(e):
            run("scalar", e)

        @block.vector
        def _(e):
            run("vector", e)

        @block.gpsimd
        def _(e):
            run("gpsimd", e)

        @block.tensor
        def _(e):
            run("tensor", e)


D = 1024
NCH = 8
QL, KVL, RD = 384, 256, 32
H = 8
QKD = 96
INC = 2208
C_CQ, C_CKV, C_KR, C_NQ, C_NK, C_NV = 0, 384, 640, 672, 1184, 1696
DFF = 2816
NFF = 22
EPS = 1e-6
NEG = -30000.0
GRID_W = 64
G_ATTN, G_Q, G_KV, G_MIX, G_FFN = 0, 8, 11, 13, 21


class Cfg:
    def __init__(self, NP, SP, SC, NQ):
        self.NP, self.SP, self.SC, self.NQ = NP, SP, SC, NQ
        self.NNA = NQ + 512
        self.QOFF = 256


def na_plan(kind, nrows_q):
    plan = []
    if kind == "P":
        R = nrows_q
        kh = min(8, R)

        def start(r):
            return min(max(r - kh // 2, 0), R - kh)
        for b in range(R // 8):
            items = []
            for kr in range(R):
                rows = [i for i in range(8 * b, 8 * b + 8) if start(i) <= kr <= start(i) + kh - 1]
                if not rows:
                    continue
                i0, i1 = rows[0], rows[-1]
                assert rows == list(range(i0, i1 + 1))
                items.append((kr, i0, i1, "T", 7 - (kr - i0)))
            plan.append(items)
    else:
        R = nrows_q
        for b in range(R // 8):
            items = []
            for ks in range(R + 7):
                rows = [i for i in range(8 * b, 8 * b + 8) if i <= ks <= i + 7]
                if not rows:
                    continue
                i0, i1 = rows[0], rows[-1]
                assert rows == list(range(i0, i1 + 1))
                if ks < 4:
                    hb0 = [0, 1, 3, 6][ks]
                    assert i0 == 0
                    items.append((ks, i0, i1, "H", hb0))
                elif ks >= R + 4:
                    j = ks - (R + 4)
                    hb0 = 10 + [0, 3, 5][j]
                    assert i0 == R - 3 + j and i1 == R - 1
                    items.append((ks, i0, i1, "H", hb0))
                else:
                    kr = ks - 4
                    items.append((ks, i0, i1, "T", 7 - (kr - i0)))
            plan.append(items)
    return plan


def build_program(cfg):
    import contextlib
    nc = bass.Bass("TRN2", target_bir_lowering=False)
    NP, SP, SC, NQ, NNA, QOFF = cfg.NP, cfg.SP, cfg.SC, cfg.NQ, cfg.NNA, cfg.QOFF
    PHASES = getattr(cfg, 'phases', (1, 2, 3))

    def din(name, shape, dt=F32):
        return nc.dram_tensor(name, list(shape), dt, kind="ExternalInput").ap()

    def dscr(name, shape, dt):
        return nc.dram_tensor(name, list(shape), dt, kind="Internal").ap()

    xp = din("xp", [NP * SP, D])
    xc = din("xc", [SC, D])
    xn = din("xn", [NNA, D])
    w_in = din("w_in", [D, INC])
    w_uq = din("w_uq", [QL, H * QKD])
    w_ukv = din("w_ukv", [KVL, H * 128])
    w_o = din("w_o", [D, D])
    w_gate = din("w_gate", [D, DFF])
    w_up = din("w_up", [D, DFF])
    w_down = din("w_down", [DFF, D])
    gcol_d = din("gcol", [128, 32])
    gfin_d = din("gfin", [1, D])
    ident_d = din("ident", [128, 128])
    ropeP_d = din("ropeP", [128, SP // 128, 32])
    ropeC_d = din("ropeC", [128, SC // 128, 32])
    ropeN_d = din("ropeN", [128, NNA // 128, 32])
    natab_d = din("natab", [128, 4, 15 * 64])
    nahalo_d = din("nahalo", [128, 4, 16 * 64])
    namask_d = din("namask", [128, 2, 64])
    yp = nc.dram_tensor("yp", [NP * SP, D], F32, kind="ExternalOutput").ap()
    yn = nc.dram_tensor("yn", [NQ, D], F32, kind="ExternalOutput").ap()

    units = []
    for i in range(NP):
        units.append(dict(kind="P", S=SP, NT=SP, NQ=SP, q0=0, name=f"p{i}",
                          xres=xp[i * SP:(i + 1) * SP, :], yout=yp[i * SP:(i + 1) * SP, :]))
    units.append(dict(kind="S", S=SC, NT=NNA, NQ=NQ, q0=QOFF, name="s",
                      xres=xn[QOFF:QOFF + NQ, :], yout=yn))
    for u in units:
        n = u["name"]
        u["kt"] = dscr("s_kt_" + n, [H, QKD, u["S"]], BF16)
        u["v"] = dscr("s_v_" + n, [H, 128, u["S"] // 128, 65], BF16)
        u["qt"] = dscr("s_qt_" + n, [H, QKD, u["NT"]], BF16)
        u["naq"] = dscr("s_naq_" + n, [128, 4, u["NT"]], BF16)
        u["nak"] = dscr("s_nak_" + n, [128, 4, u["NT"]], BF16)
        u["nav"] = dscr("s_nav_" + n, [64, u["NT"] // 128, 2, 8 * 65], BF16)
        u["ot"] = dscr("s_ot_" + n, [16, 65, u["NQ"]], F32)

    S = Sched()
    top = contextlib.ExitStack()
    with top:
        csem = {q: top.enter_context(nc.semaphore("c_" + q)) for q in QUEUES}
        dsem = {q: [top.enter_context(nc.semaphore(f"d_{q}{i}")) for i in range(n)]
                for q, n in NDSEM.items()}
        block = top.enter_context(nc.Block())

        pending = {q: set() for q in QUEUES}

        def OP(q, fn, reads=(), writes=(), dma=False):
            o = S.op(q, fn, reads, writes, dma)
            if pending[q]:
                extra = {d for d in pending[q] if d is not o}
                if q == "tensor" and not dma:
                    extra = {d for d in extra if not (d.q == "tensor" and not d.dma)}
                for d in extra:
                    d.signal = True
                o.deps |= extra
                pending[q] = set()
            return o

        def barrier():
            B = set()
            for q in QUEUES:
                comp = [o for o in S.ops[q] if not o.dma]
                if comp:
                    B.add(comp[-1])
                if S.dma_ops[q]:
                    B.update(S.dma_ops[q][-NDSEM[q]:])
            for q in QUEUES:
                pending[q] |= B

        def DMA(q, out, in_, reads=(), writes=()):
            return OP(q, lambda e, o=out, i=in_: e.dma_start(out=o, in_=i), reads, writes, dma=True)

        def ACT(out, in_, func, reads=(), writes=(), **kw):
            return OP("scalar", lambda e, o=out, i=in_, f=func, kw=kw: e.activation(out=o, in_=i, func=f, **kw),
                      reads, writes)

        def TS(q, out, in0, s1, s2, op0, op1, reads=(), writes=()):
            if s2 is None:
                return OP(q, lambda e, o=out, i=in0, a=s1, p0=op0:
                          e.tensor_scalar(out=o, in0=i, scalar1=a, scalar2=0.0, op0=p0, op1=ALU.add), reads, writes)
            return OP(q, lambda e, o=out, i=in0, a=s1, b=s2, p0=op0, p1=op1:
                      e.tensor_scalar(out=o, in0=i, scalar1=a, scalar2=b, op0=p0, op1=p1), reads, writes)

        def TT(q, out, in0, in1, op, reads=(), writes=()):
            return OP(q, lambda e, o=out, a=in0, b=in1, p=op: e.tensor_tensor(out=o, in0=a, in1=b, op=p),
                      reads, writes)

        def STT(q, out, in0, scalar, in1, op0, op1, reads=(), writes=()):
            return OP(q, lambda e, o=out, a=in0, s=scalar, b=in1, p0=op0, p1=op1:
                      e.scalar_tensor_tensor(out=o, in0=a, scalar=s, in1=b, op0=p0, op1=p1), reads, writes)

        def CP(q, out, in_, reads=(), writes=()):
            if q == "scalar":
                return ACT(out, in_, AF.Copy, reads, writes)
            return OP(q, lambda e, o=out, i=in_: e.tensor_copy(out=o, in_=i), reads, writes)

        def MM(out, lhsT, rhs, start, stop, reads=(), writes=(), **kw):
            return OP("tensor", lambda e, o=out, l=lhsT, r=rhs, s=start, t=stop, kw=kw:
                      e.matmul(out=o, lhsT=l, rhs=r, start=s, stop=t, **kw), reads, writes)

        def TR(out, in_, idn, reads=(), writes=()):
            return OP("tensor", lambda e, o=out, i=in_, d=idn: e.transpose(out=o, in_=i, identity=d),
                      reads, writes)

        def MEMSET(q, ap, val, writes=()):
            return OP(q, lambda e, a=ap, v=val: e.memset(a, v), (), writes)

        def sb(stack, name, shape, dt):
            return stack.enter_context(nc.sbuf_tensor("sb_" + name, list(shape), dt))

        def psum(stack, name, shape, dt=F32):
            return stack.enter_context(nc.psum_tensor("ps_" + name, list(shape), dt))

        identf = sb(top, "identf", [128, 128], F32)
        identb = sb(top, "identb", [128, 128], BF16)
        gcol = sb(top, "gcol", [128, 32], F32)
        epsc = sb(top, "epsc", [128, 1], F32)
        r_identf, r_identb, r_gcol, r_eps = Res(), Res(), Res(), Res()
        DMA("sync", identf[:], ident_d[:, :], (), [r_identf])
        DMA("sync", gcol[:], gcol_d[:, :], (), [r_gcol])
        CP("vector", identb[:], identf[:], [r_identf], [r_identb])
        MEMSET("vector", epsc[:], EPS, [r_eps])

        def rstd(st, c0, n, R):
            ACT(st[:, c0 + 1:c0 + 2], st[:, c0:c0 + 1], AF.Ln, [R, r_eps], [R], bias=epsc[:, 0:1], scale=1.0 / n)
            ACT(st[:, c0 + 2:c0 + 3], st[:, c0 + 1:c0 + 2], AF.Exp, [R], [R], scale=-0.5)

        def load_weight(wst, r_wst, WST, rot, dst, r_dst, src, nchunk, ncols, gbase, col_scales,
                        cv_engs=("vector", "gpsimd")):
            for c in range(nchunk):
                for c0 in range(0, ncols, WST):
                    c1 = min(ncols, c0 + WST)
                    k = rot[0] % len(wst)
                    rot[0] += 1
                    DMA("sync" if rot[0] % 2 else "gpsimd", wst[k][:, 0:c1 - c0], src[c * 128:(c + 1) * 128, c0:c1],
                        (), [r_wst[k]])
                    for (a, b, const) in col_scales:
                        lo, hi = max(a, c0), min(b, c1)
                        if lo >= hi:
                            continue
                        eng = cv_engs[rot[1] % len(cv_engs)]
                        rot[1] += 1
                        if eng == "scalar":
                            if gbase is None:
                                CP(eng, dst[:, c, lo:hi], wst[k][:, lo - c0:hi - c0], [r_wst[k]], [r_dst[c]])
                            elif const == 1.0:
                                ACT(dst[:, c, lo:hi], wst[k][:, lo - c0:hi - c0], AF.Copy, [r_wst[k], r_gcol], [r_dst[c]],
                                    scale=gcol[:, gbase + c:gbase + c + 1])
                            else:
                                TS("vector", dst[:, c, lo:hi], wst[k][:, lo - c0:hi - c0],
                                   gcol[:, gbase + c:gbase + c + 1], const, ALU.mult, ALU.mult,
                                   [r_wst[k], r_gcol], [r_dst[c]])
                        elif gbase is None:
                            CP(eng, dst[:, c, lo:hi], wst[k][:, lo - c0:hi - c0], [r_wst[k]], [r_dst[c]])
                        else:
                            TS(eng, dst[:, c, lo:hi], wst[k][:, lo - c0:hi - c0], gcol[:, gbase + c:gbase + c + 1],
                               const, ALU.mult, ALU.mult, [r_wst[k], r_gcol], [r_dst[c]])

        WST = 1408
        ph1 = contextlib.ExitStack()
        with ph1 if 1 in PHASES else contextlib.nullcontext():
          if 1 in PHASES:
              wst = [sb(ph1, f"wst{i}", [128, WST], F32) for i in range(2)]
              r_wst = [Res(), Res()]
              rot = [0, 0]
              win = sb(ph1, "win", [128, NCH, INC], BF16)
              wuq = sb(ph1, "wuq", [128, 3, H * QKD], BF16)
              wukv = sb(ph1, "wukv", [128, 2, H * 128], BF16)
              r_win = [Res() for _ in range(NCH)]
              r_wuq = [Res() for _ in range(3)]
              r_wukv = [Res() for _ in range(2)]
              load_weight(wst, r_wst, WST, rot, win, r_win, w_in, NCH, INC, G_ATTN,
                          [(0, C_NQ, 1.0), (C_NQ, C_NK, 0.125), (C_NK, INC, 1.0)])
              load_weight(wst, r_wst, WST, rot, wuq, r_wuq, w_uq, 3, H * QKD, G_Q, [(0, H * QKD, QKD ** -0.5)])
              load_weight(wst, r_wst, WST, rot, wukv, r_wukv, w_ukv, 2, H * 128, G_KV, [(0, H * 128, 1.0)])

              ropeP = sb(ph1, "ropeP", [128, SP // 128, 32], F32)
              ropeC = sb(ph1, "ropeC", [128, SC // 128, 32], F32)
              ropeN = sb(ph1, "ropeN", [128, NNA // 128, 32], F32)
              r_rope = Res()
              DMA("sync", ropeP[:], ropeP_d[:, :, :], (), [r_rope])
              DMA("sync", ropeC[:], ropeC_d[:, :, :], (), [r_rope])
              DMA("sync", ropeN[:], ropeN_d[:, :, :], (), [r_rope])

              NX = 4
              xt = [sb(ph1, f"xt{i}", [128, D], F32) for i in range(NX)]
              r_xt = [Res() for _ in range(NX)]
              junk = [sb(ph1, f"junk{i}", [128, D], BF16) for i in range(2)]
              r_junk = [Res(), Res()]
              jctr = [0]
              hb = [sb(ph1, f"hb{i}", [128, D], BF16) for i in range(4)]
              r_hb = [Res() for _ in range(4)]
              hT = [sb(ph1, f"hT{i}", [128, NCH, 512], BF16) for i in range(2)]
              r_hT = [[Res() for _ in range(4)] for _ in range(2)]
              NST = 6
              stt = [sb(ph1, f"stt{i}", [128, 16], F32) for i in range(NST)]
              r_stx = [Res() for _ in range(NST)]
              r_stq = [Res() for _ in range(NST)]
              r_stkv = [Res() for _ in range(NST)]
              cqn = [sb(ph1, f"cqn{i}", [128, QL], BF16) for i in range(4)]
              ckvn = [sb(ph1, f"ckvn{i}", [128, KVL], BF16) for i in range(4)]
              r_cqn = [Res() for _ in range(4)]
              r_ckvn = [Res() for _ in range(4)]
              cqnT = [sb(ph1, f"cqnT{i}", [128, 3, 128], BF16) for i in range(4)]
              ckvnT = [sb(ph1, f"ckvnT{i}", [128, 2, 128], BF16) for i in range(4)]
              r_cqnT = [Res() for _ in range(4)]
              r_ckvnT = [Res() for _ in range(4)]
              rtmp = [sb(ph1, f"rtmp{i}", [128, 96 + 512], F32) for i in range(4)]
              r_tc = [Res() for _ in range(4)]
              r_ts = [Res() for _ in range(4)]
              r_kro = [Res() for _ in range(4)]
              r_tcq = [[Res(), Res()] for _ in range(4)]
              r_tsq = [[Res(), Res()] for _ in range(4)]
              Ktok = [sb(ph1, f"Ktok{i}", [128, H, QKD], BF16) for i in range(4)]
              Qtok = [sb(ph1, f"Qtok{i}", [128, H, QKD], BF16) for i in range(4)]
              r_Ktok = [Res() for _ in range(4)]
              r_Qtok = [Res() for _ in range(4)]
              KTst = [sb(ph1, f"KTst{i}", [128, H, 512], BF16) for i in range(2)]
              QTst = [sb(ph1, f"QTst{i}", [128, H, 512], BF16) for i in range(2)]
              Vst = [sb(ph1, f"Vst{i}", [128, H, 4, 65], BF16) for i in range(2)]
              naqst = [sb(ph1, f"naqst{i}", [128, 4, 512], BF16) for i in range(2)]
              nakst = [sb(ph1, f"nakst{i}", [128, 4, 512], BF16) for i in range(2)]
              navst = [sb(ph1, f"navst{i}", [128, 4, H, 65], BF16) for i in range(2)]
              r_KTst = [[Res() for _ in range(4)] for _ in range(2)]
              r_QTst = [[Res() for _ in range(4)] for _ in range(2)]
              r_Vst = [[Res() for _ in range(4)] for _ in range(2)]
              r_naqst = [[Res() for _ in range(4)] for _ in range(2)]
              r_nakst = [[Res() for _ in range(4)] for _ in range(2)]
              r_navst = [[Res() for _ in range(4)] for _ in range(2)]
              for i in range(2):
                  for t in range(4):
                      MEMSET("gpsimd", Vst[i][:, :, t, 64:65], 1.0, [r_Vst[i][t]])
                      MEMSET("gpsimd", navst[i][:, t, :, 64:65], 1.0, [r_navst[i][t]])

              PB = [psum(ph1, f"pb{i}", [128, 512], F32) for i in range(4)]
              PKVU = psum(ph1, "pkvu", [128, 1024], F32)
              PQU = psum(ph1, "pqu", [128, 1024], F32)
              r_PB = [Res(psum=True) for _ in range(4)]
              r_PKVU, r_PQU = Res(psum=True), Res(psum=True)
              ptr_bf = PB[0][:].bitcast(BF16)
              pkt_bf = PB[3][:].bitcast(BF16)
              pqt_bf = PB[0][:].bitcast(BF16)

              tilectr = [0]
              grpctr = [0]
              if getattr(cfg, "verbose", False):
                  print("ph1 sbuf remaining", nc.sbuf_bytes_remaining)

              def square_acc(in_ap, n, acc, reads, writes):
                  j = jctr[0] % 2
                  jctr[0] += 1
                  ACT(junk[j][:, 0:n], in_ap, AF.Square, reads, [r_junk[j]] + writes, accum_out=acc)

              def tile_gen(u, src, rope_sb, do_q, do_kv, do_na, g, gi, t, tc_):
                      if True:
                          tg = g * 4 + t
                          xi = tc_ % NX
                          k2 = tc_ % 4
                          si = tc_ % NST
                          st = stt[si]
                          tsl = slice(t * 128, (t + 1) * 128)
                          DMA("sync", xt[xi][:], src[tg * 128:(tg + 1) * 128, :], (), [r_xt[xi]])
                          square_acc(xt[xi][:], D, st[:, 0:1], [r_xt[xi]], [r_stx[si]])
                          rstd(st, 0, D, r_stx[si])
                          TS("vector", hb[k2][:], xt[xi][:], st[:, 2:3], None, ALU.mult, None,
                             [r_xt[xi], r_stx[si]], [r_hb[k2]])
                          for c in range(NCH):
                              TR(ptr_bf[:, c * 128:(c + 1) * 128], hb[k2][:, c * 128:(c + 1) * 128], identb[:],
                                 [r_hb[k2], r_identb], [r_PB[0]])
                          CP("scalar" if (tc_ % 2 and do_q) else "vector", hT[gi][:, :, tsl],
                             ptr_bf[:, 0:1024].rearrange("p (c n) -> p c n", c=NCH), [r_PB[0]], [r_hT[gi][t]])
                          yield
                          if do_q:
                              for c in range(NCH):
                                  MM(PB[1][:, 0:QL], hT[gi][:, c, tsl], win[:, c, C_CQ:C_CQ + QL],
                                     c == 0, c == NCH - 1, [r_hT[gi][t], r_win[c]], [r_PB[1]])
                          if do_kv:
                              for c in range(NCH):
                                  MM(PB[2][:, 0:KVL + RD], hT[gi][:, c, tsl], win[:, c, C_CKV:C_CKV + KVL + RD],
                                     c == 0, c == NCH - 1, [r_hT[gi][t], r_win[c]], [r_PB[2]])
                          if do_na:
                              for c in range(NCH):
                                  MM(PB[3][:, :], hT[gi][:, c, tsl], win[:, c, C_NV:C_NV + 512],
                                     c == 0, c == NCH - 1, [r_hT[gi][t], r_win[c]], [r_PB[3]])
                              CP("vector", navst[gi][:, t, :, 0:64], PB[3][:, :].rearrange("p (h d) -> p h d", h=H),
                                 [r_PB[3]], [r_navst[gi][t]])
                          yield
                          if do_q:
                              square_acc(PB[1][:, 0:QL], QL, st[:, 3:4], [r_PB[1]], [r_stq[si]])
                              rstd(st, 3, QL, r_stq[si])
                              TS("vector", cqn[k2][:], PB[1][:, 0:QL], st[:, 5:6], None, ALU.mult, None,
                                 [r_PB[1], r_stq[si]], [r_cqn[k2]])
                          if do_kv:
                              square_acc(PB[2][:, 0:KVL], KVL, st[:, 6:7], [r_PB[2]], [r_stkv[si]])
                              rstd(st, 6, KVL, r_stkv[si])
                              TS("vector", ckvn[k2][:], PB[2][:, 0:KVL], st[:, 8:9], None, ALU.mult, None,
                                 [r_PB[2], r_stkv[si]], [r_ckvn[k2]])
                              rt = rtmp[k2]
                              xk = PB[2][:, KVL:KVL + RD].rearrange("p (a j) -> p a j", a=2)
                              cosb = rope_sb[:, tg, 0:16].unsqueeze(1).broadcast_to([128, 2, 16])
                              sinb = rope_sb[:, tg, 16:32].unsqueeze(1).broadcast_to([128, 2, 16])
                              tcv = rt[:, 0:32].rearrange("p (a j) -> p a j", a=2)
                              tsv = rt[:, 32:64].rearrange("p (a j) -> p a j", a=2)
                              TT("vector", tcv, xk, cosb, ALU.mult, [r_PB[2], r_rope, r_stkv[si]], [r_tc[k2]])
                              TT("vector", tsv, xk, sinb, ALU.mult, [r_PB[2], r_rope, r_stkv[si]], [r_ts[k2]])
                              kro = rt[:, 64:96]
                              TT("vector", kro[:, 0:16], rt[:, 0:16], rt[:, 48:64], ALU.subtract,
                                 [r_tc[k2], r_ts[k2]], [r_kro[k2]])
                              TT("vector", kro[:, 16:32], rt[:, 32:48], rt[:, 16:32], ALU.add,
                                 [r_tc[k2], r_ts[k2]], [r_kro[k2]])
                          yield
                          if do_q:
                              for c in range(3):
                                  TR(ptr_bf[:, c * 128:(c + 1) * 128], cqn[k2][:, c * 128:(c + 1) * 128], identb[:],
                                     [r_cqn[k2], r_identb], [r_PB[0]])
                              CP("scalar", cqnT[k2][:], ptr_bf[:, 0:384].rearrange("p (c n) -> p c n", c=3),
                                 [r_PB[0]], [r_cqnT[k2]])
                          if do_kv:
                              for c in range(2):
                                  TR(ptr_bf[:, 384 + c * 128:384 + (c + 1) * 128], ckvn[k2][:, c * 128:(c + 1) * 128],
                                     identb[:], [r_ckvn[k2], r_identb], [r_PB[0]])
                              CP("scalar", ckvnT[k2][:], ptr_bf[:, 384:640].rearrange("p (c n) -> p c n", c=2),
                                 [r_PB[0]], [r_ckvnT[k2]])
                          yield
                          if do_kv:
                              for hh in range(2):
                                  for c in range(2):
                                      MM(PKVU[:, hh * 512:(hh + 1) * 512], ckvnT[k2][:, c, :],
                                         wukv[:, c, hh * 512:(hh + 1) * 512], c == 0, c == 1,
                                         [r_ckvnT[k2], r_wukv[c]], [r_PKVU])
                              kvv = PKVU[:, :].rearrange("p (h d) -> p h d", h=H)
                              CP("vector", Ktok[k2][:, :, 0:64], kvv[:, :, 0:64], [r_PKVU], [r_Ktok[k2]])
                              CP("vector", Ktok[k2][:, :, 64:96], kro.unsqueeze(1).broadcast_to([128, H, RD]),
                                 [r_kro[k2]], [r_Ktok[k2]])
                              CP("scalar", Vst[gi][:, :, t, 0:64], kvv[:, :, 64:128], [r_PKVU], [r_Vst[gi][t]])
                              yield
                              for h in range(H):
                                  TR(pkt_bf[0:QKD, h * 128:(h + 1) * 128], Ktok[k2][:, h, :], identb[:],
                                     [r_Ktok[k2], r_identb], [r_PB[3]])
                              CP("vector" if not do_q else "scalar", KTst[gi][0:QKD, :, tsl],
                                 pkt_bf[0:QKD, :].rearrange("p (h n) -> p h n", h=H), [r_PB[3]], [r_KTst[gi][t]])
                          if do_q:
                              for hh in range(2):
                                  for c in range(3):
                                      MM(PQU[:, hh * 512:hh * 512 + 384], cqnT[k2][:, c, :],
                                         wuq[:, c, hh * 384:(hh + 1) * 384], c == 0, c == 2,
                                         [r_cqnT[k2], r_wuq[c]], [r_PQU])
                              rt = rtmp[k2]
                              for hh in range(2):
                                  qv = PQU[:, hh * 512:hh * 512 + 384].rearrange("p (h d) -> p h d", h=4)
                                  CP("vector", Qtok[k2][:, hh * 4:(hh + 1) * 4, 0:64], qv[:, :, 0:64],
                                     [r_PQU], [r_Qtok[k2]])
                                  xq = qv[:, :, 64:96].rearrange("p h (a j) -> p h a j", a=2)
                                  cosb = rope_sb[:, tg, 0:16].unsqueeze(1).unsqueeze(1).broadcast_to([128, 4, 2, 16])
                                  sinb = rope_sb[:, tg, 16:32].unsqueeze(1).unsqueeze(1).broadcast_to([128, 4, 2, 16])
                                  o0 = 96 + hh * 256
                                  tcq = rt[:, o0:o0 + 128].rearrange("p (h a j) -> p h a j", h=4, a=2)
                                  tsq = rt[:, o0 + 128:o0 + 256].rearrange("p (h a j) -> p h a j", h=4, a=2)
                                  TT("vector", tcq, xq, cosb, ALU.mult, [r_PQU, r_rope], [r_tcq[k2][hh]])
                                  TT("vector", tsq, xq, sinb, ALU.mult, [r_PQU, r_rope], [r_tsq[k2][hh]])
                                  TT("gpsimd", Qtok[k2][:, hh * 4:(hh + 1) * 4, 64:80], tcq[:, :, 0, :], tsq[:, :, 1, :],
                                     ALU.subtract, [r_tcq[k2][hh], r_tsq[k2][hh]], [r_Qtok[k2]])
                                  TT("gpsimd", Qtok[k2][:, hh * 4:(hh + 1) * 4, 80:96], tsq[:, :, 0, :], tcq[:, :, 1, :],
                                     ALU.add, [r_tcq[k2][hh], r_tsq[k2][hh]], [r_Qtok[k2]])
                              yield
                              for h in range(H):
                                  TR(pqt_bf[0:QKD, h * 128:(h + 1) * 128], Qtok[k2][:, h, :], identb[:],
                                     [r_Qtok[k2], r_identb], [r_PB[0]])
                              CP("vector", QTst[gi][0:QKD, :, tsl],
                                 pqt_bf[0:QKD, :].rearrange("p (h n) -> p h n", h=H), [r_PB[0]], [r_QTst[gi][t]])
              def group_na_gen(u, do_na, g, gi):
                  if do_na:
                      for blk in range(8):
                          col0 = C_NQ + blk * 128
                          for c in range(NCH):
                              MM(PB[3][:, :], win[:, c, col0:col0 + 128], hT[gi][:, c, :], c == 0, c == NCH - 1,
                                 r_hT[gi] + [r_win[c]], [r_PB[3]])
                          dst = naqst[gi] if blk < 4 else nakst[gi]
                          rr = r_naqst[gi] if blk < 4 else r_nakst[gi]
                          CP("scalar" if blk % 2 else "vector", dst[:, blk % 4, :], PB[3][:, :], [r_PB[3]],
                             [rr[blk % 4]])
                          yield

              def group_dma(u, do_q, do_kv, do_na, g, gi):
                  if do_na:
                      DMA("gpsimd", u["naq"][:, :, g * 512:(g + 1) * 512], naqst[gi][:], r_naqst[gi], ())
                      DMA("gpsimd", u["nak"][:, :, g * 512:(g + 1) * 512], nakst[gi][:], r_nakst[gi], ())
                      for half in range(2):
                          DMA("gpsimd", u["nav"][:, g * 4:(g + 1) * 4, half, :],
                              navst[gi][half * 64:(half + 1) * 64, :, :, :].rearrange("p t h d -> p t (h d)"),
                              r_navst[gi], ())
                  if do_kv:
                      DMA("gpsimd", u["kt"][:, :, g * 512:(g + 1) * 512].rearrange("h d n -> d h n"),
                          KTst[gi][0:QKD, :, :], r_KTst[gi], ())
                      DMA("gpsimd", u["v"][:, :, g * 4:(g + 1) * 4, :].rearrange("h p t d -> p h t d"),
                          Vst[gi][:], r_Vst[gi], ())
                  if do_q:
                      DMA("gpsimd", u["qt"][:, :, g * 512:(g + 1) * 512].rearrange("h d n -> d h n"),
                          QTst[gi][0:QKD, :, :], r_QTst[gi], ())

              def run_token_passes(passes):
                  dbg = getattr(cfg, 'dbg', (1, 1, 1))
                  WIN = 4
                  jobs = []
                  ginfos = []
                  for (u, src, ntile, rope_sb, do_q, do_kv, do_na) in passes:
                      do_q, do_kv, do_na = do_q and dbg[0], do_kv and dbg[1], do_na and dbg[2]
                      for g in range(ntile // 4):
                          gi = grpctr[0] % 2
                          grpctr[0] += 1
                          ginfo = dict(u=u, do_q=do_q, do_kv=do_kv, do_na=do_na, g=g, gi=gi, left=5 if do_na else 4,
                                       idx=len(ginfos))
                          ginfos.append(ginfo)
                          for t in range(4):
                              tc_ = tilectr[0]
                              tilectr[0] += 1
                              jobs.append((ginfo, tile_gen(u, src, rope_sb, do_q, do_kv, do_na, g, gi, t, tc_), t == 3))
                  active = []
                  ji = 0

                  def finish(ginfo):
                      ginfo["left"] -= 1
                      if ginfo["left"] == 0:
                          group_dma(ginfo["u"], ginfo["do_q"], ginfo["do_kv"], ginfo["do_na"], ginfo["g"], ginfo["gi"])

                  while active or ji < len(jobs):
                      if (ji < len(jobs) and sum(1 for a_ in active if a_[2] != "na") < WIN
                              and all(gq["left"] == 0 for gq in ginfos[:max(0, jobs[ji][0]["idx"] - 1)])):
                          ginfo, gen, lastt = jobs[ji]
                          ji += 1
                          active.append([ginfo, gen, "tile", lastt, False])
                      for a_ in list(active):
                          ginfo, gen = a_[0], a_[1]
                          try:
                              next(gen)
                              if a_[2] == "tile" and a_[3] and not a_[4] and ginfo["do_na"]:
                                  a_[4] = True
                                  active.append([ginfo, group_na_gen(ginfo["u"], True, ginfo["g"], ginfo["gi"]),
                                                 "na", False, False])
                          except StopIteration:
                              active.remove(a_)
                              finish(ginfo)

              plist = []
              for ui, u in enumerate(units):
                  if u["kind"] == "P":
                      plist.append((u, xp[ui * SP:(ui + 1) * SP, :], SP // 128, ropeP, True, True, True))
                  else:
                      plist.append((u, xn, NNA // 128, ropeN, True, False, True))
                      plist.append((u, xc, SC // 128, ropeC, False, True, False))
              run_token_passes(plist)
        barrier()

        ph2 = contextlib.ExitStack()
        with ph2 if 2 in PHASES else contextlib.nullcontext():
          if 2 in PHASES:
              SMAX = max(u["S"] for u in units)
              NQMAX = max(u["NQ"] for u in units)
              NTMAX = max(u["NT"] for u in units)
              KTb = [sb(ph2, f"KTb{i}", [128, SMAX], BF16) for i in range(2)]
              Vb = [sb(ph2, f"Vb{i}", [128, SMAX // 128, 65], BF16) for i in range(2)]
              QTb = [sb(ph2, f"QTb{i}", [128, NQMAX], BF16) for i in range(2)]
              r_KTb, r_Vb, r_QTb = [Res(), Res()], [Res(), Res()], [Res(), Res()]
              PT = [sb(ph2, f"PT{i}", [128, 1024], BF16) for i in range(3)]
              r_PT = [Res() for _ in range(3)]
              oTs = [sb(ph2, f"oTs{i}", [128, 512], F32) for i in range(2)]
              r_oTs = [Res(), Res()]
              naq = sb(ph2, "naq", [128, 4, NTMAX], BF16)
              nak = sb(ph2, "nak", [128, 4, NTMAX], BF16)
              nav = sb(ph2, "nav", [64, NTMAX // 64, 8 * 65], BF16)
              r_naq, r_nak, r_nav = Res(), Res(), Res()
              tabf = sb(ph2, "tabf", [128, 4, 15 * 64], F32)
              half_ = sb(ph2, "halof", [128, 4, 16 * 64], F32)
              maskf = sb(ph2, "maskf", [128, 2, 64], F32)
              tabb = sb(ph2, "tabb", [128, 4, 15 * 64], BF16)
              halb = sb(ph2, "halob", [128, 4, 16 * 64], BF16)
              r_tabf, r_half, r_maskf, r_tabb, r_halb = Res(), Res(), Res(), Res(), Res()
              NCHAIN = 4
              PTn = [sb(ph2, f"PTn{i}", [64, 512], BF16) for i in range(2 * NCHAIN)]
              r_PTn = [Res() for _ in range(2 * NCHAIN)]

              PS = [psum(ph2, f"ps{i}", [128, 1024], F32) for i in range(2)]
              PO = [psum(ph2, f"po{i}", [128, 512], F32) for i in range(4)]
              r_PSh = [[Res(psum=True), Res(psum=True)] for _ in range(2)]
              r_PO = [Res(psum=True) for _ in range(4)]
              PSn = [PS[0][:, 0:512], PS[0][:, 512:1024], PS[1][:, 0:512], PS[1][:, 512:1024]]
              r_PSn = [r_PSh[0][0], r_PSh[0][1], r_PSh[1][0], r_PSh[1][1]]
              PA = PO
              r_PA = r_PO

              DMA("sync", tabf[:], natab_d[:, :, :], (), [r_tabf])
              DMA("sync", half_[:], nahalo_d[:, :, :], (), [r_half])
              DMA("sync", maskf[:], namask_d[:, :, :], (), [r_maskf])
              for (srcf, dstb, nb, rs, rd) in ((tabf, tabb, 15, r_tabf, r_tabb), (half_, halb, 16, r_half, r_halb)):
                  v = srcf[:].rearrange("p a (b q) -> p (a b) q", q=64)
                  vb = dstb[:].rearrange("p a (b q) -> p (a b) q", q=64)
                  m01 = maskf[:, 0, :].unsqueeze(1).broadcast_to([128, 4 * nb, 64])
                  mng = maskf[:, 1, :].unsqueeze(1).broadcast_to([128, 4 * nb, 64])
                  TT("vector", v, v, m01, ALU.mult, [rs, r_maskf], [rs])
                  TT("vector", vb, v, mng, ALU.add, [rs, r_maskf], [rd])

              ctr = dict(hb=0, pt=0, ps=0, po=0, ot=0, ptn=0)
              if getattr(cfg, "verbose", False):
                  print("ph2 sbuf remaining", nc.sbuf_bytes_remaining)

              def mla_attention(u):
                  Sx, NQu, q0 = u["S"], u["NQ"], u["q0"]
                  nkt = Sx // 128
                  nk2 = nkt // 2
                  nqb = NQu // 512
                  steps = [(h, qb, k2) for h in range(H) for qb in range(nqb) for k2 in range(nk2)]
                  hbase = ctr["hb"]
                  ctr["hb"] += H
                  pobase = ctr["po"]
                  ctr["po"] += H * nqb

                  def load(h):
                      bi = (hbase + h) % 2
                      DMA("sync", KTb[bi][0:QKD, 0:Sx], u["kt"][h, :, :], (), [r_KTb[bi]])
                      DMA("sync", Vb[bi][:, 0:nkt, :], u["v"][h, :, :, :], (), [r_Vb[bi]])
                      DMA("sync", QTb[bi][0:QKD, 0:NQu], u["qt"][h, :, q0:q0 + NQu], (), [r_QTb[bi]])

                  def QK(i):
                      h, qb, k2 = steps[i]
                      bi = (hbase + h) % 2
                      si = i % 2
                      for j in range(2):
                          kt = 2 * k2 + j
                          MM(PS[si][:, j * 512:(j + 1) * 512], KTb[bi][0:QKD, kt * 128:(kt + 1) * 128],
                             QTb[bi][0:QKD, qb * 512:(qb + 1) * 512], True, True,
                             [r_KTb[bi], r_QTb[bi]], [r_PSh[si][j]])

                  def EXPPV(i):
                      h, qb, k2 = steps[i]
                      bi = (hbase + h) % 2
                      si = i % 2
                      pi = i % 3
                      oi = (pobase + h * nqb + qb) % 4
                      ACT(PT[pi][:, :], PS[si][:, :], AF.Exp, r_PSh[si], [r_PT[pi]])
                      for j in range(2):
                          kt = 2 * k2 + j
                          MM(PO[oi][0:65, :], Vb[bi][:, kt, :], PT[pi][:, j * 512:(j + 1) * 512],
                             kt == 0, kt == nkt - 1, [r_Vb[bi], r_PT[pi]], [r_PO[oi]])
                      if k2 == nk2 - 1:
                          ti = ctr["ot"] % 2
                          ctr["ot"] += 1
                          CP("vector", oTs[ti][0:65, :], PO[oi][0:65, :], [r_PO[oi]], [r_oTs[ti]])
                          DMA("gpsimd", u["ot"][h, :, qb * 512:(qb + 1) * 512], oTs[ti][0:65, :], [r_oTs[ti]], ())

                  load(0)
                  QK(0)
                  for i in range(len(steps)):
                      h, qb, k2 = steps[i]
                      if qb == 0 and k2 == 0 and h + 1 < H:
                          load(h + 1)
                      if i + 1 < len(steps):
                          QK(i + 1)
                      EXPPV(i)

              def na_attention(u):
                  NT, NQu = u["NT"], u["NQ"]
                  nrq = NQu // 64
                  plan = na_plan(u["kind"], nrq)
                  qrow0 = 0 if u["kind"] == "P" else 4
                  DMA("sync", naq[:, :, 0:NT], u["naq"][:, :, :], (), [r_naq])
                  DMA("sync", nak[:, :, 0:NT], u["nak"][:, :, :], (), [r_nak])
                  DMA("sync", nav[:, 0:NT // 64, :], u["nav"][:, :, :, :].rearrange("p t a d -> p (t a) d"), (), [r_nav])
                  chains = [(h, b) for b in range(len(plan)) for h in range(H)]
                  for c0_ in range(0, len(chains), NCHAIN):
                      grp = chains[c0_:c0_ + NCHAIN]
                      nst = max(len(plan[b]) for (h, b) in grp)

                      def QKB(j, s):
                          h, b = grp[j]
                          items = plan[b]
                          if s >= len(items):
                              return
                          ks, i0, i1, src, blk0 = items[s]
                          pr, pb = h // 2, (h % 2) * 64
                          N = 64 * (i1 - i0 + 1)
                          qtok0 = (qrow0 + i0) * 64
                          MM(PSn[j][0:64, 0:N], nak[pb:pb + 64, pr, ks * 64:(ks + 1) * 64],
                             naq[pb:pb + 64, pr, qtok0:qtok0 + N], True, False, [r_nak, r_naq], [r_PSn[j]])
                          tb = tabb if src == "T" else halb
                          rtb = r_tabb if src == "T" else r_halb
                          MM(PSn[j][0:64, 0:N], identb[pb:pb + 64, pb:pb + 64],
                             tb[pb:pb + 64, pr, blk0 * 64:blk0 * 64 + N], False, True, [rtb, r_identb], [r_PSn[j]])

                      def EXPN(j, s):
                          h, b = grp[j]
                          items = plan[b]
                          if s >= len(items):
                              return
                          ks, i0, i1, src, blk0 = items[s]
                          N = 64 * (i1 - i0 + 1)
                          pi = 2 * j + (s % 2)
                          ACT(PTn[pi][:, 0:N], PSn[j][0:64, 0:N], AF.Exp, [r_PSn[j]], [r_PTn[pi]])

                      def PVN(j, s):
                          h, b = grp[j]
                          items = plan[b]
                          if s >= len(items):
                              return
                          ks, i0, i1, src, blk0 = items[s]
                          N = 64 * (i1 - i0 + 1)
                          c0 = (i0 - 8 * b) * 64
                          pi = 2 * j + (s % 2)
                          MM(PA[j][0:65, c0:c0 + N], nav[:, ks, h * 65:(h + 1) * 65], PTn[pi][:, 0:N],
                             s == 0, s == len(items) - 1, [r_nav, r_PTn[pi]], [r_PA[j]], skip_group_check=True)
                          if s == len(items) - 1:
                              ti = ctr["ot"] % 2
                              ctr["ot"] += 1
                              CP("vector", oTs[ti][0:65, :], PA[j][0:65, :], [r_PA[j]], [r_oTs[ti]])
                              DMA("gpsimd", u["ot"][8 + h, :, b * 512:(b + 1) * 512], oTs[ti][0:65, :],
                                  [r_oTs[ti]], ())

                      for j in range(len(grp)):
                          QKB(j, 0)
                      for s in range(nst):
                          for j in range(len(grp)):
                              EXPN(j, s)
                          for j in range(len(grp)):
                              QKB(j, s + 1)
                          for j in range(len(grp)):
                              PVN(j, s)

              for u in units:
                  na_attention(u)
                  mla_attention(u)
        barrier()

        ph3 = contextlib.ExitStack()
        with ph3 if 3 in PHASES else contextlib.nullcontext():
          if 3 in PHASES:
              WST = 352
              NWST3 = 3
              wst = [sb(ph3, f"wst3_{i}", [128, WST], F32) for i in range(NWST3)]
              r_wst = [Res() for _ in range(NWST3)]
              rot = [0, 0]
              wo = sb(ph3, "wo", [128, NCH, D], BF16)
              wg = sb(ph3, "wg", [128, NCH, DFF], BF16)
              wu = sb(ph3, "wu", [128, NCH, DFF], BF16)
              wd = sb(ph3, "wd", [128, NFF, D], BF16)
              r_wo = [Res() for _ in range(NCH)]
              r_wg = [Res() for _ in range(NCH)]
              r_wu = [Res() for _ in range(NCH)]
              r_wd = [Res() for _ in range(NFF)]
              load_weight(wst, r_wst, WST, rot, wo, r_wo, w_o, NCH, D, G_MIX, [(0, D, 1.0)], cv_engs=("vector", "scalar"))
              load_weight(wst, r_wst, WST, rot, wg, r_wg, w_gate, NCH, DFF, G_FFN, [(0, DFF, 1.0)], cv_engs=("vector", "scalar"))
              load_weight(wst, r_wst, WST, rot, wu, r_wu, w_up, NCH, DFF, G_FFN, [(0, DFF, 1.0)], cv_engs=("vector", "scalar"))
              load_weight(wst, r_wst, WST, rot, wd, r_wd, w_down, NFF, D, None, [(0, D, 1.0)], cv_engs=("vector", "scalar"))
              gfin = sb(ph3, "gfin", [128, D], F32)
              r_gfin = Res()
              DMA("sync", gfin[:], gfin_d[0:1, :].partition_broadcast(128), (), [r_gfin])
              onec = sb(ph3, "onec", [128, 1], F32)
              r_one = Res()
              MEMSET("vector", onec[:], 1.0, [r_one])

              TG = 2
              NTOK = TG * 128
              oTin = sb(ph3, "oTin", [128, 4, 128], F32)
              r_oTin = Res()
              atok = sb(ph3, "atok", [128, D], F32)
              r_atok = Res()
              x1 = [[sb(ph3, f"x1_{s_}_{i}", [128, D], F32) for i in range(TG)] for s_ in range(2)]
              r_x1 = [[Res() for _ in range(TG)] for _ in range(2)]
              mixb = sb(ph3, "mixb", [128, D], BF16)
              r_mixb = Res()
              mixT = sb(ph3, "mixT", [128, NCH, 128], BF16)
              r_mixT = Res()
              h2b = sb(ph3, "h2b", [128, D], BF16)
              r_h2b = Res()
              h2T = [sb(ph3, f"h2T{s_}", [128, NCH, NTOK], BF16) for s_ in range(2)]
              r_h2T = [[Res() for _ in range(TG)] for _ in range(2)]
              actT = sb(ph3, "actT", [128, NFF, NTOK], BF16)
              r_actT = [Res() for _ in range(NFF)]
              sg = [sb(ph3, f"sg{i}", [128, NTOK], F32) for i in range(2)]
              r_sg = [Res(), Res()]
              st3 = [sb(ph3, f"st3_{i}", [128, 32], F32) for i in range(4)]
              r_st3 = [[Res() for _ in range(6)] for _ in range(4)]

              POh = psum(ph3, "poh", [128, 512], F32)
              r_POh = Res(psum=True)
              PTR = psum(ph3, "ptr3", [128, 512], F32)
              r_PTR = Res(psum=True)
              ptr3_bf = PTR[:].bitcast(BF16)
              PY = psum(ph3, "py", [128, 1024], F32)
              r_PY = Res(psum=True)
              PG = [psum(ph3, f"pg{i}", [128, 512], F32) for i in range(2)]
              PU = [psum(ph3, f"pu{i}", [128, 512], F32) for i in range(2)]
              r_PG, r_PU = [Res(psum=True), Res(psum=True)], [Res(psum=True), Res(psum=True)]
              if getattr(cfg, "verbose", False):
                  print("ph3 sbuf remaining", nc.sbuf_bytes_remaining)

              def prologue(u, g, slot):
                  for t in range(TG):
                      tok0 = g * NTOK + t * 128
                      X, rX = x1[slot][t], r_x1[slot][t]
                      st = st3[slot * 2 + t]
                      rs_ = r_st3[slot * 2 + t]
                      DMA("sync", X[:], u["xres"][tok0:tok0 + 128, :], (), [rX])
                      for hq in range(4):
                          DMA("sync", oTin[0:65, :, :],
                              u["ot"][hq * 4:(hq + 1) * 4, :, tok0:tok0 + 128].rearrange("h d n -> d h n"),
                              (), [r_oTin])
                          yield
                          for hh in range(4):
                              TR(POh[:, hh * 128:hh * 128 + 65], oTin[0:65, hh, :], identf[0:65, 0:65],
                                 [r_oTin, r_identf], [r_POh])
                          pov = POh[:, :].rearrange("p (h d) -> p h d", h=4)
                          rc = st[:, 8:12].unsqueeze(2)
                          OP("vector", lambda e, o=rc, i=pov[:, :, 64:65]: e.reciprocal(out=o, in_=i),
                             [r_POh], [rs_[4]])
                          TT("vector", atok[:, hq * 256:(hq + 1) * 256].rearrange("p (h d) -> p h d", h=4),
                             pov[:, :, 0:64], rc.broadcast_to([128, 4, 64]), ALU.mult,
                             [r_POh, rs_[4]], [r_atok])
                      for half in range(2):
                          hs = slice(half * 512, (half + 1) * 512)
                          ACT(mixb[:, hs], atok[:, hs], AF.Square, [r_atok], [r_mixb, rs_[half]],
                              accum_out=st[:, 3 * half:3 * half + 1])
                          rstd(st, 3 * half, 512, rs_[half])
                          TS("vector", mixb[:, hs], atok[:, hs], st[:, 3 * half + 2:3 * half + 3], None, ALU.mult, None,
                             [r_atok, rs_[half]], [r_mixb])
                      yield
                      for c in range(NCH):
                          TR(ptr3_bf[:, c * 128:(c + 1) * 128], mixb[:, c * 128:(c + 1) * 128], identb[:],
                             [r_mixb, r_identb], [r_PTR])
                      CP("scalar", mixT[:], ptr3_bf[:, 0:1024].rearrange("p (c n) -> p c n", c=NCH),
                         [r_PTR], [r_mixT])
                      yield
                      for hh in range(2):
                          for c in range(NCH):
                              MM(PY[:, hh * 512:(hh + 1) * 512], mixT[:, c, :], wo[:, c, hh * 512:(hh + 1) * 512],
                                 c == 0, c == NCH - 1, [r_mixT, r_wo[c]], [r_PY])
                      TT("vector", X[:], X[:], PY[:, :], ALU.add, [rX, r_PY], [rX])
                      ACT(h2b[:], X[:], AF.Square, [rX], [r_h2b, rs_[2]], accum_out=st[:, 16:17])
                      rstd(st, 16, D, rs_[2])
                      TS("vector", h2b[:], X[:], st[:, 18:19], None, ALU.mult, None, [rX, rs_[2]], [r_h2b])
                      yield
                      for c in range(NCH):
                          TR(ptr3_bf[:, c * 128:(c + 1) * 128], h2b[:, c * 128:(c + 1) * 128], identb[:],
                             [r_h2b, r_identb], [r_PTR])
                      CP("scalar", h2T[slot][:, :, t * 128:(t + 1) * 128],
                         ptr3_bf[:, 0:1024].rearrange("p (c n) -> p c n", c=NCH), [r_PTR], [r_h2T[slot][t]])
                      yield

              def ffn_step(slot, f):
                  pi = f % 2
                  for c in range(NCH):
                      MM(PG[pi][:, 0:NTOK], wg[:, c, f * 128:(f + 1) * 128], h2T[slot][:, c, :], c == 0, c == NCH - 1,
                         r_h2T[slot] + [r_wg[c]], [r_PG[pi]])
                  for c in range(NCH):
                      MM(PU[pi][:, 0:NTOK], wu[:, c, f * 128:(f + 1) * 128], h2T[slot][:, c, :], c == 0, c == NCH - 1,
                         r_h2T[slot] + [r_wu[c]], [r_PU[pi]])
                  ACT(sg[pi][:], PG[pi][:, 0:NTOK], AF.Exp, [r_PG[pi]], [r_sg[pi]], scale=-1.0)
                  ACT(sg[pi][:], sg[pi][:], AF.Ln, [r_sg[pi], r_one], [r_sg[pi]], bias=onec[:, 0:1])
                  ACT(sg[pi][:], sg[pi][:], AF.Exp, [r_sg[pi]], [r_sg[pi]], scale=-1.0)
                  TT("vector", sg[pi][:], sg[pi][:], PG[pi][:, 0:NTOK], ALU.mult, [r_sg[pi], r_PG[pi]], [r_sg[pi]])
                  TT("vector", actT[:, f, :], sg[pi][:], PU[pi][:, 0:NTOK], ALU.mult,
                     [r_sg[pi], r_PU[pi]], [r_actT[f]])

              def down_epi(u, g, slot, t):
                  tok0 = g * NTOK + t * 128
                  X, rX = x1[slot][t], r_x1[slot][t]
                  st = st3[slot * 2 + t]
                  rr = r_st3[slot * 2 + t][3]
                  for hh in range(2):
                      for f in range(NFF):
                          MM(PY[:, hh * 512:(hh + 1) * 512], actT[:, f, t * 128:(t + 1) * 128],
                             wd[:, f, hh * 512:(hh + 1) * 512], f == 0, f == NFF - 1,
                             [r_actT[f], r_wd[f]], [r_PY])
                  TT("vector", X[:], X[:], PY[:, :], ALU.add, [rX, r_PY], [rX])
                  ACT(h2b[:], X[:], AF.Square, [rX], [r_h2b, rr], accum_out=st[:, 24:25])
                  rstd(st, 24, D, rr)
                  STT("vector", X[:], X[:], st[:, 26:27], gfin[:], ALU.mult, ALU.mult, [rX, rr, r_gfin], [rX])
                  DMA("gpsimd", u["yout"][tok0:tok0 + 128, :], X[:], [rX], ())

              groups = [(u, g) for u in units for g in range(u["NQ"] // NTOK)]
              cur = prologue(groups[0][0], groups[0][1], 0)
              for _ in cur:
                  pass
              for gi_, (u, g) in enumerate(groups):
                  slot = gi_ % 2
                  nxt = None
                  if gi_ + 1 < len(groups):
                      nxt = prologue(groups[gi_ + 1][0], groups[gi_ + 1][1], 1 - slot)
                  for f in range(NFF):
                      ffn_step(slot, f)
                      if nxt is not None:
                          next(nxt, None)
                  if nxt is not None:
                      for _ in nxt:
                          pass
                  for t in range(TG):
                      down_epi(u, g, slot, t)
        S.emit(nc, block, csem, dsem)
    return nc


def _rope_table(pos):
    inv = (10000.0 ** (-np.arange(0, RD, 2, dtype=np.float32) / np.float32(RD))).astype(np.float32)
    ang = pos.astype(np.float32)[:, None] * inv[None, :]
    tab = np.concatenate([np.cos(ang), np.sin(ang)], axis=1).astype(np.float32)
    nt = pos.shape[0] // 128
    return np.ascontiguousarray(tab.reshape(nt, 128, 32).transpose(1, 0, 2))


def _col(g):
    g = np.asarray(g, np.float32).reshape(-1, 128)
    return g.T


def make_in_maps(inputs, cfg):
    NP, SP, SC, NQ, NNA = cfg.NP, cfg.SP, cfg.SC, cfg.NQ, cfg.NNA
    xpr = np.asarray(inputs["x_prompt"], np.float32)
    xsm = np.asarray(inputs["x_sample"], np.float32)
    nquart = SC // NQ
    ncores = xsm.shape[0] * nquart
    assert xpr.shape[0] == NP * ncores
    Rq = NQ // 64
    Rtot = SC // 64
    f = lambda k: np.ascontiguousarray(np.asarray(inputs[k], np.float32)[0])
    gcol = np.zeros((128, 32), np.float32)
    gcol[:, G_ATTN:G_ATTN + 8] = _col(inputs["attn_norm_g"][0])
    gcol[:, G_Q:G_Q + 3] = _col(inputs["q_norm_g"][0])
    gcol[:, G_KV:G_KV + 2] = _col(inputs["kv_norm_g"][0])
    gcol[:, G_MIX:G_MIX + 4] = _col(inputs["mla_out_g"][0])
    gcol[:, G_MIX + 4:G_MIX + 8] = _col(inputs["na_out_g"][0])
    gcol[:, G_FFN:G_FFN + 8] = _col(inputs["ffn_norm_g"][0])
    rpb = np.asarray(inputs["na_rpb"], np.float32)[0]
    kc = np.arange(64)[:, None]
    qc = np.arange(64)[None, :]
    dj = np.clip(kc - qc, -15, 15) + 15
    cs = np.clip(qc - 8, 0, 48)
    m01 = ((kc >= cs) & (kc < cs + 16)).astype(np.float32)
    namask = np.zeros((128, 2, 64), np.float32)
    namask[:, 0, :] = np.tile(m01, (2, 1))
    namask[:, 1, :] = np.tile((1.0 - m01) * NEG, (2, 1))

    def blockT(h, di):
        return rpb[h, di + 7][dj]

    natab = np.zeros((128, 4, 15, 64), np.float32)
    for h in range(H):
        pb, pr = (h % 2) * 64, h // 2
        for j in range(15):
            natab[pb:pb + 64, pr, j, :] = blockT(h, 7 - j)
    natab = natab.reshape(128, 4, 15 * 64)
    shared = dict(w_in=f("w_in"), w_uq=f("w_uq"), w_ukv=f("w_ukv"), w_o=f("w_o"), w_gate=f("w_gate"),
                  w_up=f("w_up"), w_down=f("w_down"), gcol=gcol,
                  gfin=np.asarray(inputs["final_norm_g"], np.float32).reshape(1, D),
                  ident=np.eye(128, dtype=np.float32), natab=natab, namask=namask,
                  ropeP=_rope_table(np.arange(SP)), ropeC=_rope_table(np.arange(SC)))
    maps = []
    for c in range(ncores):
        sq, qt = c // nquart, c % nquart
        m = dict(shared)
        m["xp"] = np.ascontiguousarray(xpr[c * NP:(c + 1) * NP].reshape(NP * SP, D))
        seq = xsm[sq]
        m["xc"] = np.ascontiguousarray(seq)
        r0 = qt * Rq
        rows_b = [4, 5, 6, 7] if qt == 0 else [r0 - 4, r0 - 3, r0 - 2, r0 - 1]
        last = (qt == nquart - 1)
        rows_a = [Rtot - 8, Rtot - 7, Rtot - 6] if last else [r0 + Rq, r0 + Rq + 1, r0 + Rq + 2]
        rows = rows_b + list(range(r0, r0 + Rq)) + rows_a
        xn = np.zeros((NNA, D), np.float32)
        posn = np.zeros((NNA,), np.float32)
        for i, r in enumerate(rows):
            xn[i * 64:(i + 1) * 64] = seq[r * 64:(r + 1) * 64]
            posn[i * 64:(i + 1) * 64] = np.arange(r * 64, (r + 1) * 64)
        m["xn"] = xn
        m["ropeN"] = _rope_table(posn)
        hal = np.zeros((128, 4, 16, 64), np.float32)
        for h in range(H):
            pb, pr = (h % 2) * 64, h // 2
            for ks in range(4):
                for i in range(ks + 1):
                    di = (rows_b[ks] - r0) - i
                    hal[pb:pb + 64, pr, [0, 1, 3, 6][ks] + i, :] = blockT(h, di)
            for j in range(3):
                for n_, i in enumerate(range(Rq - 3 + j, Rq)):
                    di = (rows_a[j] - r0) - i
                    hal[pb:pb + 64, pr, 10 + [0, 3, 5][j] + n_, :] = blockT(h, di)
        m["nahalo"] = hal.reshape(128, 4, 16 * 64)
        maps.append(m)
    return maps


def assemble(results, cfg, nsample):
    NP, SP, SC, NQ = cfg.NP, cfg.SP, cfg.SC, cfg.NQ
    nquart = SC // NQ
    ncores = nsample * nquart
    yp = np.concatenate([np.asarray(results[c]["yp"], np.float32).reshape(NP, SP, D) for c in range(ncores)], axis=0)
    ys = np.zeros((nsample, SC, D), np.float32)
    for c in range(ncores):
        sq, qt = c // nquart, c % nquart
        ys[sq, qt * NQ:(qt + 1) * NQ] = np.asarray(results[c]["yn"], np.float32)
    return yp, ys


_NC_CACHE = {}


def kernel(**inputs):
    cfg = Cfg(NP=2, SP=2048, SC=8192, NQ=2048)
    maps = make_in_maps(inputs, cfg)
    key = (cfg.NP, cfg.SP, cfg.SC, cfg.NQ)
    if key not in _NC_CACHE:
        _NC_CACHE[key] = build_program(cfg)
    nc = _NC_CACHE[key]
    res = run_bass_kernel_spmd(nc, maps, core_ids=list(range(len(maps))))
    return assemble(res.results, cfg, np.asarray(inputs["x_sample"]).shape[0])
```

```python
import numpy as np
import concourse.bass as bass
import concourse.mybir as mybir
from concourse.bass_utils import run_bass_kernel_spmd

F32 = mybir.dt.float32
BF16 = mybir.dt.bfloat16
AF = mybir.ActivationFunctionType
ALU = mybir.AluOpType

QUEUES = ("sync", "scalar", "vector", "gpsimd", "tensor")
NDSEM = {"sync": 16, "gpsimd": 12, "scalar": 4}


class Res:
    __slots__ = ("name", "w", "rc", "rd", "psum")

    def __init__(self, name="", psum=False):
        self.name = name
        self.psum = psum
        self.w = None
        self.rc = {}
        self.rd = []


class Op:
    __slots__ = ("q", "dma", "fn", "deps", "signal", "cidx", "didx", "pos")


class Sched:
    def __init__(self):
        self.ops = {q: [] for q in QUEUES}
        self.ncomp = {q: 0 for q in QUEUES}
        self.ndma = {q: 0 for q in QUEUES}
        self.dma_ops = {q: [] for q in QUEUES}
        self.all_dma = []

    def op(self, q, fn, reads=(), writes=(), dma=False):
        o = Op()
        o.q, o.dma, o.fn, o.signal = q, dma, fn, dma
        deps = set()
        for r in reads:
            if r.w is not None:
                deps.add(r.w)
            if r.psum:
                deps.update(o2 for q2, o2 in r.rc.items() if q2 != q)
        for r in writes:
            if r.w is not None:
                deps.add(r.w)
            deps.update(r.rc.values())
            deps.update(r.rd)
        if dma:
            o.didx = self.ndma[q]
            self.ndma[q] += 1
            R = NDSEM[q]
            if o.didx >= R:
                deps.add(self.dma_ops[q][o.didx - R])
            self.dma_ops[q].append(o)
            self.all_dma.append(o)
            o.cidx = None
        else:
            o.cidx = self.ncomp[q]
            self.ncomp[q] += 1
            o.didx = None
        deps.discard(o)
        if q == "tensor" and not dma:
            deps = {d for d in deps if not (d.q == "tensor" and not d.dma)}
        o.deps = deps
        for d in deps:
            d.signal = True
        for r in reads:
            if dma:
                r.rd.append(o)
            else:
                r.rc[q] = o
        for r in writes:
            r.w = o
            r.rc = {}
            r.rd = []
        o.pos = len(self.ops[q])
        self.ops[q].append(o)
        return o

    def emit(self, nc, block, csem, dsem):
        sched = self
        val = {}
        for q in QUEUES:
            n = 0
            for o in self.ops[q]:
                if o.dma:
                    val[o] = (dsem[q][o.didx % NDSEM[q]], 16 * (o.didx // NDSEM[q] + 1))
                elif o.signal:
                    n += 1
                    val[o] = (csem[q], n)

        def run(q, eng):
            waited = {}
            last_dma = []
            for o in sched.ops[q]:
                need = {}
                for d in o.deps:
                    s, v = val[d]
                    k = id(s)
                    if v > need.get(k, (None, 0))[1]:
                        need[k] = (s, v)
                for k, (s, v) in need.items():
                    if waited.get(k, 0) < v:
                        eng.wait_ge(s, v)
                        waited[k] = v
                ins = o.fn(eng)
                if o.dma:
                    s, v = val[o]
                    ins.then_inc(s, 16)
                elif o.signal:
                    ins.then_inc(csem[q], 1)
            for o in sched.dma_ops[q][-NDSEM.get(q, 0):] if sched.dma_ops[q] else []:
                s, v = val[o]
                if waited.get(id(s), 0) < v:
                    eng.wait_ge(s, v)
                    waited[id(s)] = v

        @block.sync
        def _(e):
            run("sync", e)

        @block.scalar
        def _(e):
            run("scalar", e)

        @block.vector
        def _(e):
            run("vector", e)

        @block.gpsimd
        def _(e):
            run("gpsimd", e)

        @block.tensor
        def _(e):
            run("tensor", e)


D = 1024
NCH = 8
QL, KVL, RD = 384, 256, 32
H = 8
QKD = 96
INC = 2208
C_CQ, C_CKV, C_KR, C_NQ, C_NK, C_NV = 0, 384, 640, 672, 1184, 1696
DFF = 2816
NFF = 22
EPS = 1e-6
NEG = -30000.0
GRID_W = 64
G_ATTN, G_Q, G_KV, G_MIX, G_FFN = 0, 8, 11, 13, 21


class Cfg:
    def __init__(self, NP, SP, SC, NQ):
        self.NP, self.SP, self.SC, self.NQ = NP, SP, SC, NQ
        self.NNA = NQ + 512
        self.QOFF = 256


def na_plan(kind, nrows_q):
    plan = []
    if kind == "P":
        R = nrows_q
        kh = min(8, R)

        def start(r):
            return min(max(r - kh // 2, 0), R - kh)
        for b in range(R // 8):
            items = []
            for kr in range(R):
                rows = [i for i in range(8 * b, 8 * b + 8) if start(i) <= kr <= start(i) + kh - 1]
                if not rows:
                    continue
                i0, i1 = rows[0], rows[-1]
                assert rows == list(range(i0, i1 + 1))
                items.append((kr, i0, i1, "T", 7 - (kr - i0)))
            plan.append(items)
    else:
        R = nrows_q
        for b in range(R // 8):
            items = []
            for ks in range(R + 7):
                rows = [i for i in range(8 * b, 8 * b + 8) if i <= ks <= i + 7]
                if not rows:
                    continue
                i0, i1 = rows[0], rows[-1]
                assert rows == list(range(i0, i1 + 1))
                if ks < 4:
                    hb0 = [0, 1, 3, 6][ks]
                    assert i0 == 0
                    items.append((ks, i0, i1, "H", hb0))
                elif ks >= R + 4:
                    j = ks - (R + 4)
                    hb0 = 10 + [0, 3, 5][j]
                    assert i0 == R - 3 + j and i1 == R - 1
                    items.append((ks, i0, i1, "H", hb0))
                else:
                    kr = ks - 4
                    items.append((ks, i0, i1, "T", 7 - (kr - i0)))
            plan.append(items)
    return plan


def build_program(cfg):
    import contextlib
    nc = bass.Bass("TRN2", target_bir_lowering=False)
    NP, SP, SC, NQ, NNA, QOFF = cfg.NP, cfg.SP, cfg.SC, cfg.NQ, cfg.NNA, cfg.QOFF
    PHASES = getattr(cfg, 'phases', (1, 2, 3))

    def din(name, shape, dt=F32):
        return nc.dram_tensor(name, list(shape), dt, kind="ExternalInput").ap()

    def dscr(name, shape, dt):
        return nc.dram_tensor(name, list(shape), dt, kind="Internal").ap()

    xp = din("xp", [NP * SP, D])
    xc = din("xc", [SC, D])
    xn = din("xn", [NNA, D])
    w_in = din("w_in", [D, INC])
    w_uq = din("w_uq", [QL, H * QKD])
    w_ukv = din("w_ukv", [KVL, H * 128])
    w_o = din("w_o", [D, D])
    w_gate = din("w_gate", [D, DFF])
    w_up = din("w_up", [D, DFF])
    w_down = din("w_down", [DFF, D])
    gcol_d = din("gcol", [128, 32])
    gfin_d = din("gfin", [1, D])
    ident_d = din("ident", [128, 128])
    ropeP_d = din("ropeP", [128, SP // 128, 32])
    ropeC_d = din("ropeC", [128, SC // 128, 32])
    ropeN_d = din("ropeN", [128, NNA // 128, 32])
    natab_d = din("natab", [128, 4, 15 * 64])
    nahalo_d = din("nahalo", [128, 4, 16 * 64])
    namask_d = din("namask", [128, 2, 64])
    yp = nc.dram_tensor("yp", [NP * SP, D], F32, kind="ExternalOutput").ap()
    yn = nc.dram_tensor("yn", [NQ, D], F32, kind="ExternalOutput").ap()

    units = []
    for i in range(NP):
        units.append(dict(kind="P", S=SP, NT=SP, NQ=SP, q0=0, name=f"p{i}",
                          xres=xp[i * SP:(i + 1) * SP, :], yout=yp[i * SP:(i + 1) * SP, :]))
    units.append(dict(kind="S", S=SC, NT=NNA, NQ=NQ, q0=QOFF, name="s",
                      xres=xn[QOFF:QOFF + NQ, :], yout=yn))
    for u in units:
        n = u["name"]
        u["kt"] = dscr("s_kt_" + n, [H, QKD, u["S"]], BF16)
        u["v"] = dscr("s_v_" + n, [H, 128, u["S"] // 128, 65], BF16)
        u["qt"] = dscr("s_qt_" + n, [H, QKD, u["NT"]], BF16)
        u["naq"] = dscr("s_naq_" + n, [128, 4, u["NT"]], BF16)
        u["nak"] = dscr("s_nak_" + n, [128, 4, u["NT"]], BF16)
        u["nav"] = dscr("s_nav_" + n, [64, u["NT"] // 128, 2, 8 * 65], BF16)
        u["ot"] = dscr("s_ot_" + n, [16, 65, u["NQ"]], F32)

    S = Sched()
    top = contextlib.ExitStack()
    with top:
        csem = {q: top.enter_context(nc.semaphore("c_" + q)) for q in QUEUES}
        dsem = {q: [top.enter_context(nc.semaphore(f"d_{q}{i}")) for i in range(n)]
                for q, n in NDSEM.items()}
        block = top.enter_context(nc.Block())

        pending = {q: set() for q in QUEUES}

        def OP(q, fn, reads=(), writes=(), dma=False):
            o = S.op(q, fn, reads, writes, dma)
            if pending[q]:
                extra = {d for d in pending[q] if d is not o}
                if q == "tensor" and not dma:
                    extra = {d for d in extra if not (d.q == "tensor" and not d.dma)}
                for d in extra:
                    d.signal = True
                o.deps |= extra
                pending[q] = set()
            return o

        def barrier():
            B = set()
            for q in QUEUES:
                comp = [o for o in S.ops[q] if not o.dma]
                if comp:
                    B.add(comp[-1])
                if S.dma_ops[q]:
                    B.update(S.dma_ops[q][-NDSEM[q]:])
            for q in QUEUES:
                pending[q] |= B

        def DMA(q, out, in_, reads=(), writes=()):
            return OP(q, lambda e, o=out, i=in_: e.dma_start(out=o, in_=i), reads, writes, dma=True)

        def ACT(out, in_, func, reads=(), writes=(), **kw):
            return OP("scalar", lambda e, o=out, i=in_, f=func, kw=kw: e.activation(out=o, in_=i, func=f, **kw),
                      reads, writes)

        def TS(q, out, in0, s1, s2, op0, op1, reads=(), writes=()):
            if s2 is None:
                return OP(q, lambda e, o=out, i=in0, a=s1, p0=op0:
                          e.tensor_scalar(out=o, in0=i, scalar1=a, scalar2=0.0, op0=p0, op1=ALU.add), reads, writes)
            return OP(q, lambda e, o=out, i=in0, a=s1, b=s2, p0=op0, p1=op1:
                      e.tensor_scalar(out=o, in0=i, scalar1=a, scalar2=b, op0=p0, op1=p1), reads, writes)

        def TT(q, out, in0, in1, op, reads=(), writes=()):
            return OP(q, lambda e, o=out, a=in0, b=in1, p=op: e.tensor_tensor(out=o, in0=a, in1=b, op=p),
                      reads, writes)

        def STT(q, out, in0, scalar, in1, op0, op1, reads=(), writes=()):
            return OP(q, lambda e, o=out, a=in0, s=scalar, b=in1, p0=op0, p1=op1:
                      e.scalar_tensor_tensor(out=o, in0=a, scalar=s, in1=b, op0=p0, op1=p1), reads, writes)

        def CP(q, out, in_, reads=(), writes=()):
            if q == "scalar":
                return ACT(out, in_, AF.Copy, reads, writes)
            return OP(q, lambda e, o=out, i=in_: e.tensor_copy(out=o, in_=i), reads, writes)

        def MM(out, lhsT, rhs, start, stop, reads=(), writes=(), **kw):
            return OP("tensor", lambda e, o=out, l=lhsT, r=rhs, s=start, t=stop, kw=kw:
                      e.matmul(out=o, lhsT=l, rhs=r, start=s, stop=t, **kw), reads, writes)

        def TR(out, in_, idn, reads=(), writes=()):
            return OP("tensor", lambda e, o=out, i=in_, d=idn: e.transpose(out=o, in_=i, identity=d),
                      reads, writes)

        def MEMSET(q, ap, val, writes=()):
            return OP(q, lambda e, a=ap, v=val: e.memset(a, v), (), writes)

        def sb(stack, name, shape, dt):
            return stack.enter_context(nc.sbuf_tensor("sb_" + name, list(shape), dt))

        def psum(stack, name, shape, dt=F32):
            return stack.enter_context(nc.psum_tensor("ps_" + name, list(shape), dt))

        identf = sb(top, "identf", [128, 128], F32)
        identb = sb(top, "identb", [128, 128], BF16)
        gcol = sb(top, "gcol", [128, 32], F32)
        epsc = sb(top, "epsc", [128, 1], F32)
        r_identf, r_identb, r_gcol, r_eps = Res(), Res(), Res(), Res()
        DMA("sync", identf[:], ident_d[:, :], (), [r_identf])
        DMA("sync", gcol[:], gcol_d[:, :], (), [r_gcol])
        CP("vector", identb[:], identf[:], [r_identf], [r_identb])
        MEMSET("vector", epsc[:], EPS, [r_eps])

        def rstd(st, c0, n, R):
            ACT(st[:, c0 + 1:c0 + 2], st[:, c0:c0 + 1], AF.Ln, [R, r_eps], [R], bias=epsc[:, 0:1], scale=1.0 / n)
            ACT(st[:, c0 + 2:c0 + 3], st[:, c0 + 1:c0 + 2], AF.Exp, [R], [R], scale=-0.5)

        def load_weight(wst, r_wst, WST, rot, dst, r_dst, src, nchunk, ncols, gbase, col_scales,
                        cv_engs=("vector", "gpsimd")):
            for c in range(nchunk):
                for c0 in range(0, ncols, WST):
                    c1 = min(ncols, c0 + WST)
                    k = rot[0] % len(wst)
                    rot[0] += 1
                    DMA("sync" if rot[0] % 2 else "gpsimd", wst[k][:, 0:c1 - c0], src[c * 128:(c + 1) * 128, c0:c1],
                        (), [r_wst[k]])
                    for (a, b, const) in col_scales:
                        lo, hi = max(a, c0), min(b, c1)
                        if lo >= hi:
                            continue
                        eng = cv_engs[rot[1] % len(cv_engs)]
                        rot[1] += 1
                        if eng == "scalar":
                            if gbase is None:
                                CP(eng, dst[:, c, lo:hi], wst[k][:, lo - c0:hi - c0], [r_wst[k]], [r_dst[c]])
                            elif const == 1.0:
                                ACT(dst[:, c, lo:hi], wst[k][:, lo - c0:hi - c0], AF.Copy, [r_wst[k], r_gcol], [r_dst[c]],
                                    scale=gcol[:, gbase + c:gbase + c + 1])
                            else:
                                TS("vector", dst[:, c, lo:hi], wst[k][:, lo - c0:hi - c0],
                                   gcol[:, gbase + c:gbase + c + 1], const, ALU.mult, ALU.mult,
                                   [r_wst[k], r_gcol], [r_dst[c]])
                        elif gbase is None:
                            CP(eng, dst[:, c, lo:hi], wst[k][:, lo - c0:hi - c0], [r_wst[k]], [r_dst[c]])
                        else:
                            TS(eng, dst[:, c, lo:hi], wst[k][:, lo - c0:hi - c0], gcol[:, gbase + c:gbase + c + 1],
                               const, ALU.mult, ALU.mult, [r_wst[k], r_gcol], [r_dst[c]])

        WST = 1408
        ph1 = contextlib.ExitStack()
        with ph1 if 1 in PHASES else contextlib.nullcontext():
          if 1 in PHASES:
              wst = [sb(ph1, f"wst{i}", [128, WST], F32) for i in range(2)]
              r_wst = [Res(), Res()]
              rot = [0, 0]
              win = sb(ph1, "win", [128, NCH, INC], BF16)
              wuq = sb(ph1, "wuq", [128, 3, H * QKD], BF16)
              wukv = sb(ph1, "wukv", [128, 2, H * 128], BF16)
              r_win = [Res() for _ in range(NCH)]
              r_wuq = [Res() for _ in range(3)]
              r_wukv = [Res() for _ in range(2)]
              load_weight(wst, r_wst, WST, rot, win, r_win, w_in, NCH, INC, G_ATTN,
                          [(0, C_NQ, 1.0), (C_NQ, C_NK, 0.125), (C_NK, INC, 1.0)])
              load_weight(wst, r_wst, WST, rot, wuq, r_wuq, w_uq, 3, H * QKD, G_Q, [(0, H * QKD, QKD ** -0.5)])
              load_weight(wst, r_wst, WST, rot, wukv, r_wukv, w_ukv, 2, H * 128, G_KV, [(0, H * 128, 1.0)])

              ropeP = sb(ph1, "ropeP", [128, SP // 128, 32], F32)
              ropeC = sb(ph1, "ropeC", [128, SC // 128, 32], F32)
              ropeN = sb(ph1, "ropeN", [128, NNA // 128, 32], F32)
              r_rope = Res()
              DMA("sync", ropeP[:], ropeP_d[:, :, :], (), [r_rope])
              DMA("sync", ropeC[:], ropeC_d[:, :, :], (), [r_rope])
              DMA("sync", ropeN[:], ropeN_d[:, :, :], (), [r_rope])

              NX = 4
              xt = [sb(ph1, f"xt{i}", [128, D], F32) for i in range(NX)]
              r_xt = [Res() for _ in range(NX)]
              junk = [sb(ph1, f"junk{i}", [128, D], BF16) for i in range(2)]
              r_junk = [Res(), Res()]
              jctr = [0]
              hb = [sb(ph1, f"hb{i}", [128, D], BF16) for i in range(4)]
              r_hb = [Res() for _ in range(4)]
              hT = [sb(ph1, f"hT{i}", [128, NCH, 512], BF16) for i in range(2)]
              r_hT = [[Res() for _ in range(4)] for _ in range(2)]
              NST = 6
              stt = [sb(ph1, f"stt{i}", [128, 16], F32) for i in range(NST)]
              r_stx = [Res() for _ in range(NST)]
              r_stq = [Res() for _ in range(NST)]
              r_stkv = [Res() for _ in range(NST)]
              cqn = [sb(ph1, f"cqn{i}", [128, QL], BF16) for i in range(4)]
              ckvn = [sb(ph1, f"ckvn{i}", [128, KVL], BF16) for i in range(4)]
              r_cqn = [Res() for _ in range(4)]
              r_ckvn = [Res() for _ in range(4)]
              cqnT = [sb(ph1, f"cqnT{i}", [128, 3, 128], BF16) for i in range(4)]
              ckvnT = [sb(ph1, f"ckvnT{i}", [128, 2, 128], BF16) for i in range(4)]
              r_cqnT = [Res() for _ in range(4)]
              r_ckvnT = [Res() for _ in range(4)]
              rtmp = [sb(ph1, f"rtmp{i}", [128, 96 + 512], F32) for i in range(4)]
              r_tc = [Res() for _ in range(4)]
              r_ts = [Res() for _ in range(4)]
              r_kro = [Res() for _ in range(4)]
              r_tcq = [[Res(), Res()] for _ in range(4)]
              r_tsq = [[Res(), Res()] for _ in range(4)]
              Ktok = [sb(ph1, f"Ktok{i}", [128, H, QKD], BF16) for i in range(4)]
              Qtok = [sb(ph1, f"Qtok{i}", [128, H, QKD], BF16) for i in range(4)]
              r_Ktok = [Res() for _ in range(4)]
              r_Qtok = [Res() for _ in range(4)]
              KTst = [sb(ph1, f"KTst{i}", [128, H, 512], BF16) for i in range(2)]
              QTst = [sb(ph1, f"QTst{i}", [128, H, 512], BF16) for i in range(2)]
              Vst = [sb(ph1, f"Vst{i}", [128, H, 4, 65], BF16) for i in range(2)]
              naqst = [sb(ph1, f"naqst{i}", [128, 4, 512], BF16) for i in range(2)]
              nakst = [sb(ph1, f"nakst{i}", [128, 4, 512], BF16) for i in range(2)]
              navst = [sb(ph1, f"navst{i}", [128, 4, H, 65], BF16) for i in range(2)]
              r_KTst = [[Res() for _ in range(4)] for _ in range(2)]
              r_QTst = [[Res() for _ in range(4)] for _ in range(2)]
              r_Vst = [[Res() for _ in range(4)] for _ in range(2)]
              r_naqst = [[Res() for _ in range(4)] for _ in range(2)]
              r_nakst = [[Res() for _ in range(4)] for _ in range(2)]
              r_navst = [[Res() for _ in range(4)] for _ in range(2)]
              for i in range(2):
                  for t in range(4):
                      MEMSET("gpsimd", Vst[i][:, :, t, 64:65], 1.0, [r_Vst[i][t]])
                      MEMSET("gpsimd", navst[i][:, t, :, 64:65], 1.0, [r_navst[i][t]])

              PB = [psum(ph1, f"pb{i}", [128, 512], F32) for i in range(4)]
              PKVU = psum(ph1, "pkvu", [128, 1024], F32)
              PQU = psum(ph1, "pqu", [128, 1024], F32)
              r_PB = [Res(psum=True) for _ in range(4)]
              r_PKVU, r_PQU = Res(psum=True), Res(psum=True)
              ptr_bf = PB[0][:].bitcast(BF16)
              pkt_bf = PB[3][:].bitcast(BF16)
              pqt_bf = PB[0][:].bitcast(BF16)

              tilectr = [0]
              grpctr = [0]
              if getattr(cfg, "verbose", False):
                  print("ph1 sbuf remaining", nc.sbuf_bytes_remaining)

              def square_acc(in_ap, n, acc, reads, writes):
                  j = jctr[0] % 2
                  jctr[0] += 1
                  ACT(junk[j][:, 0:n], in_ap, AF.Square, reads, [r_junk[j]] + writes, accum_out=acc)

              def tile_gen(u, src, rope_sb, do_q, do_kv, do_na, g, gi, t, tc_):
                      if True:
                          tg = g * 4 + t
                          xi = tc_ % NX
                          k2 = tc_ % 4
                          si = tc_ % NST
                          st = stt[si]
                          tsl = slice(t * 128, (t + 1) * 128)
                          DMA("sync", xt[xi][:], src[tg * 128:(tg + 1) * 128, :], (), [r_xt[xi]])
                          square_acc(xt[xi][:], D, st[:, 0:1], [r_xt[xi]], [r_stx[si]])
                          rstd(st, 0, D, r_stx[si])
                          TS("vector", hb[k2][:], xt[xi][:], st[:, 2:3], None, ALU.mult, None,
                             [r_xt[xi], r_stx[si]], [r_hb[k2]])
                          yield
                          for c in range(NCH):
                              TR(ptr_bf[:, c * 128:(c + 1) * 128], hb[k2][:, c * 128:(c + 1) * 128], identb[:],
                                 [r_hb[k2], r_identb], [r_PB[0]])
                          CP("scalar" if (tc_ % 2 and do_q) else "vector", hT[gi][:, :, tsl],
                             ptr_bf[:, 0:1024].rearrange("p (c n) -> p c n", c=NCH), [r_PB[0]], [r_hT[gi][t]])
                          yield
                          if do_q:
                              for c in range(NCH):
                                  MM(PB[1][:, 0:QL], hT[gi][:, c, tsl], win[:, c, C_CQ:C_CQ + QL],
                                     c == 0, c == NCH - 1, [r_hT[gi][t], r_win[c]], [r_PB[1]])
                          if do_kv:
                              for c in range(NCH):
                                  MM(PB[2][:, 0:KVL + RD], hT[gi][:, c, tsl], win[:, c, C_CKV:C_CKV + KVL + RD],
                                     c == 0, c == NCH - 1, [r_hT[gi][t], r_win[c]], [r_PB[2]])
                          if do_na:
                              for c in range(NCH):
                                  MM(PB[3][:, :], hT[gi][:, c, tsl], win[:, c, C_NV:C_NV + 512],
                                     c == 0, c == NCH - 1, [r_hT[gi][t], r_win[c]], [r_PB[3]])
                              CP("vector", navst[gi][:, t, :, 0:64], PB[3][:, :].rearrange("p (h d) -> p h d", h=H),
                                 [r_PB[3]], [r_navst[gi][t]])
                          yield
                          if do_q:
                              square_acc(PB[1][:, 0:QL], QL, st[:, 3:4], [r_PB[1]], [r_stq[si]])
                              rstd(st, 3, QL, r_stq[si])
                              TS("vector", cqn[k2][:], PB[1][:, 0:QL], st[:, 5:6], None, ALU.mult, None,
                                 [r_PB[1], r_stq[si]], [r_cqn[k2]])
                          if do_kv:
                              square_acc(PB[2][:, 0:KVL], KVL, st[:, 6:7], [r_PB[2]], [r_stkv[si]])
                              rstd(st, 6, KVL, r_stkv[si])
                              TS("vector", ckvn[k2][:], PB[2][:, 0:KVL], st[:, 8:9], None, ALU.mult, None,
                                 [r_PB[2], r_stkv[si]], [r_ckvn[k2]])
                              rt = rtmp[k2]
                              xk = PB[2][:, KVL:KVL + RD].rearrange("p (a j) -> p a j", a=2)
                              cosb = rope_sb[:, tg, 0:16].unsqueeze(1).broadcast_to([128, 2, 16])
                              sinb = rope_sb[:, tg, 16:32].unsqueeze(1).broadcast_to([128, 2, 16])
                              tcv = rt[:, 0:32].rearrange("p (a j) -> p a j", a=2)
                              tsv = rt[:, 32:64].rearrange("p (a j) -> p a j", a=2)
                              TT("vector", tcv, xk, cosb, ALU.mult, [r_PB[2], r_rope, r_stkv[si]], [r_tc[k2]])
                              TT("vector", tsv, xk, sinb, ALU.mult, [r_PB[2], r_rope, r_stkv[si]], [r_ts[k2]])
                              kro = rt[:, 64:96]
                              TT("vector", kro[:, 0:16], rt[:, 0:16], rt[:, 48:64], ALU.subtract,
                                 [r_tc[k2], r_ts[k2]], [r_kro[k2]])
                              TT("vector", kro[:, 16:32], rt[:, 32:48], rt[:, 16:32], ALU.add,
                                 [r_tc[k2], r_ts[k2]], [r_kro[k2]])
                          yield
                          if do_q:
                              for c in range(3):
                                  TR(ptr_bf[:, c * 128:(c + 1) * 128], cqn[k2][:, c * 128:(c + 1) * 128], identb[:],
                                     [r_cqn[k2], r_identb], [r_PB[0]])
                              CP("scalar", cqnT[k2][:], ptr_bf[:, 0:384].rearrange("p (c n) -> p c n", c=3),
                                 [r_PB[0]], [r_cqnT[k2]])
                          if do_kv:
                              for c in range(2):
                                  TR(ptr_bf[:, 384 + c * 128:384 + (c + 1) * 128], ckvn[k2][:, c * 128:(c + 1) * 128],
                                     identb[:], [r_ckvn[k2], r_identb], [r_PB[0]])
                              CP("scalar", ckvnT[k2][:], ptr_bf[:, 384:640].rearrange("p (c n) -> p c n", c=2),
                                 [r_PB[0]], [r_ckvnT[k2]])
                          yield
                          if do_kv:
                              for hh in range(2):
                                  for c in range(2):
                                      MM(PKVU[:, hh * 512:(hh + 1) * 512], ckvnT[k2][:, c, :],
                                         wukv[:, c, hh * 512:(hh + 1) * 512], c == 0, c == 1,
                                         [r_ckvnT[k2], r_wukv[c]], [r_PKVU])
                              kvv = PKVU[:, :].rearrange("p (h d) -> p h d", h=H)
                              CP("vector", Ktok[k2][:, :, 0:64], kvv[:, :, 0:64], [r_PKVU], [r_Ktok[k2]])
                              CP("vector", Ktok[k2][:, :, 64:96], kro.unsqueeze(1).broadcast_to([128, H, RD]),
                                 [r_kro[k2]], [r_Ktok[k2]])
                              CP("scalar", Vst[gi][:, :, t, 0:64], kvv[:, :, 64:128], [r_PKVU], [r_Vst[gi][t]])
                              yield
                              for h in range(H):
                                  TR(pkt_bf[0:QKD, h * 128:(h + 1) * 128], Ktok[k2][:, h, :], identb[:],
                                     [r_Ktok[k2], r_identb], [r_PB[3]])
                              CP("vector" if not do_q else "scalar", KTst[gi][0:QKD, :, tsl],
                                 pkt_bf[0:QKD, :].rearrange("p (h n) -> p h n", h=H), [r_PB[3]], [r_KTst[gi][t]])
                          if do_q:
                              for hh in range(2):
                                  for c in range(3):
                                      MM(PQU[:, hh * 512:hh * 512 + 384], cqnT[k2][:, c, :],
                                         wuq[:, c, hh * 384:(hh + 1) * 384], c == 0, c == 2,
                                         [r_cqnT[k2], r_wuq[c]], [r_PQU])
                              rt = rtmp[k2]
                              for hh in range(2):
                                  qv = PQU[:, hh * 512:hh * 512 + 384].rearrange("p (h d) -> p h d", h=4)
                                  CP("vector", Qtok[k2][:, hh * 4:(hh + 1) * 4, 0:64], qv[:, :, 0:64],
                                     [r_PQU], [r_Qtok[k2]])
                                  xq = qv[:, :, 64:96].rearrange("p h (a j) -> p h a j", a=2)
                                  cosb = rope_sb[:, tg, 0:16].unsqueeze(1).unsqueeze(1).broadcast_to([128, 4, 2, 16])
                                  sinb = rope_sb[:, tg, 16:32].unsqueeze(1).unsqueeze(1).broadcast_to([128, 4, 2, 16])
                                  o0 = 96 + hh * 256
                                  tcq = rt[:, o0:o0 + 128].rearrange("p (h a j) -> p h a j", h=4, a=2)
                                  tsq = rt[:, o0 + 128:o0 + 256].rearrange("p (h a j) -> p h a j", h=4, a=2)
                                  TT("vector", tcq, xq, cosb, ALU.mult, [r_PQU, r_rope], [r_tcq[k2][hh]])
                                  TT("vector", tsq, xq, sinb, ALU.mult, [r_PQU, r_rope], [r_tsq[k2][hh]])
                                  TT("gpsimd", Qtok[k2][:, hh * 4:(hh + 1) * 4, 64:80], tcq[:, :, 0, :], tsq[:, :, 1, :],
                                     ALU.subtract, [r_tcq[k2][hh], r_tsq[k2][hh]], [r_Qtok[k2]])
                                  TT("gpsimd", Qtok[k2][:, hh * 4:(hh + 1) * 4, 80:96], tsq[:, :, 0, :], tcq[:, :, 1, :],
                                     ALU.add, [r_tcq[k2][hh], r_tsq[k2][hh]], [r_Qtok[k2]])
                              yield
                              for h in range(H):
                                  TR(pqt_bf[0:QKD, h * 128:(h + 1) * 128], Qtok[k2][:, h, :], identb[:],
                                     [r_Qtok[k2], r_identb], [r_PB[0]])
                              CP("vector", QTst[gi][0:QKD, :, tsl],
                                 pqt_bf[0:QKD, :].rearrange("p (h n) -> p h n", h=H), [r_PB[0]], [r_QTst[gi][t]])
              def group_na_gen(u, do_na, g, gi):
                  if do_na:
                      for blk in range(8):
                          col0 = C_NQ + blk * 128
                          for c in range(NCH):
                              MM(PB[3][:, :], win[:, c, col0:col0 + 128], hT[gi][:, c, :], c == 0, c == NCH - 1,
                                 r_hT[gi] + [r_win[c]], [r_PB[3]])
                          dst = naqst[gi] if blk < 4 else nakst[gi]
                          rr = r_naqst[gi] if blk < 4 else r_nakst[gi]
                          CP("scalar" if blk % 2 else "vector", dst[:, blk % 4, :], PB[3][:, :], [r_PB[3]],
                             [rr[blk % 4]])
                          yield

              def group_dma(u, do_q, do_kv, do_na, g, gi):
                  if do_na:
                      DMA("gpsimd", u["naq"][:, :, g * 512:(g + 1) * 512], naqst[gi][:], r_naqst[gi], ())
                      DMA("gpsimd", u["nak"][:, :, g * 512:(g + 1) * 512], nakst[gi][:], r_nakst[gi], ())
                      for half in range(2):
                          DMA("gpsimd", u["nav"][:, g * 4:(g + 1) * 4, half, :],
                              navst[gi][half * 64:(half + 1) * 64, :, :, :].rearrange("p t h d -> p t (h d)"),
                              r_navst[gi], ())
                  if do_kv:
                      DMA("gpsimd", u["kt"][:, :, g * 512:(g + 1) * 512].rearrange("h d n -> d h n"),
                          KTst[gi][0:QKD, :, :], r_KTst[gi], ())
                      DMA("gpsimd", u["v"][:, :, g * 4:(g + 1) * 4, :].rearrange("h p t d -> p h t d"),
                          Vst[gi][:], r_Vst[gi], ())
                  if do_q:
                      DMA("gpsimd", u["qt"][:, :, g * 512:(g + 1) * 512].rearrange("h d n -> d h n"),
                          QTst[gi][0:QKD, :, :], r_QTst[gi], ())

              def run_token_passes(passes):
                  dbg = getattr(cfg, 'dbg', (1, 1, 1))
                  WIN = 4
                  jobs = []
                  ginfos = []
                  for (u, src, ntile, rope_sb, do_q, do_kv, do_na) in passes:
                      do_q, do_kv, do_na = do_q and dbg[0], do_kv and dbg[1], do_na and dbg[2]
                      for g in range(ntile // 4):
                          gi = grpctr[0] % 2
                          grpctr[0] += 1
                          ginfo = dict(u=u, do_q=do_q, do_kv=do_kv, do_na=do_na, g=g, gi=gi, left=5 if do_na else 4,
                                       idx=len(ginfos))
                          ginfos.append(ginfo)
                          for t in range(4):
                              tc_ = tilectr[0]
                              tilectr[0] += 1
                              jobs.append((ginfo, tile_gen(u, src, rope_sb, do_q, do_kv, do_na, g, gi, t, tc_), t == 3))
                  active = []
                  ji = 0

                  def finish(ginfo):
                      ginfo["left"] -= 1
                      if ginfo["left"] == 0:
                          group_dma(ginfo["u"], ginfo["do_q"], ginfo["do_kv"], ginfo["do_na"], ginfo["g"], ginfo["gi"])

                  while active or ji < len(jobs):
                      if (ji < len(jobs) and sum(1 for a_ in active if a_[2] != "na") < WIN
                              and all(gq["left"] == 0 for gq in ginfos[:max(0, jobs[ji][0]["idx"] - 1)])):
                          ginfo, gen, lastt = jobs[ji]
                          ji += 1
                          active.append([ginfo, gen, "tile", lastt, False])
                      for a_ in list(active):
                          ginfo, gen = a_[0], a_[1]
                          try:
                              next(gen)
                              a_.append(None)
                              if a_[2] == "tile" and a_[3] and not a_[4] and ginfo["do_na"] and len(a_) - 5 >= 2:
                                  a_[4] = True
                                  active.append([ginfo, group_na_gen(ginfo["u"], True, ginfo["g"], ginfo["gi"]),
                                                 "na", False, False])
                          except StopIteration:
                              active.remove(a_)
                              finish(ginfo)

              plist = []
              for ui, u in enumerate(units):
                  if u["kind"] == "P":
                      plist.append((u, xp[ui * SP:(ui + 1) * SP, :], SP // 128, ropeP, True, True, True))
                  else:
                      plist.append((u, xn, NNA // 128, ropeN, True, False, True))
                      plist.append((u, xc, SC // 128, ropeC, False, True, False))
              run_token_passes(plist)
        barrier()

        ph2 = contextlib.ExitStack()
        with ph2 if 2 in PHASES else contextlib.nullcontext():
          if 2 in PHASES:
              SMAX = max(u["S"] for u in units)
              NQMAX = max(u["NQ"] for u in units)
              NTMAX = max(u["NT"] for u in units)
              KTb = [sb(ph2, f"KTb{i}", [128, SMAX], BF16) for i in range(2)]
              Vb = [sb(ph2, f"Vb{i}", [128, SMAX // 128, 65], BF16) for i in range(2)]
              QTb = [sb(ph2, f"QTb{i}", [128, NQMAX], BF16) for i in range(2)]
              r_KTb, r_Vb, r_QTb = [Res(), Res()], [Res(), Res()], [Res(), Res()]
              PT = [sb(ph2, f"PT{i}", [128, 1024], BF16) for i in range(3)]
              r_PT = [Res() for _ in range(3)]
              oTs = [sb(ph2, f"oTs{i}", [128, 512], F32) for i in range(2)]
              r_oTs = [Res(), Res()]
              naq = sb(ph2, "naq", [128, 4, NTMAX], BF16)
              nak = sb(ph2, "nak", [128, 4, NTMAX], BF16)
              nav = sb(ph2, "nav", [64, NTMAX // 64, 8 * 65], BF16)
              r_naq, r_nak, r_nav = Res(), Res(), Res()
              tabf = sb(ph2, "tabf", [128, 4, 15 * 64], F32)
              half_ = sb(ph2, "halof", [128, 4, 16 * 64], F32)
              maskf = sb(ph2, "maskf", [128, 2, 64], F32)
              tabb = sb(ph2, "tabb", [128, 4, 15 * 64], BF16)
              halb = sb(ph2, "halob", [128, 4, 16 * 64], BF16)
              r_tabf, r_half, r_maskf, r_tabb, r_halb = Res(), Res(), Res(), Res(), Res()
              NCHAIN = 4
              PTn = [sb(ph2, f"PTn{i}", [64, 512], BF16) for i in range(2 * NCHAIN)]
              r_PTn = [Res() for _ in range(2 * NCHAIN)]

              PS = [psum(ph2, f"ps{i}", [128, 1024], F32) for i in range(2)]
              PO = [psum(ph2, f"po{i}", [128, 512], F32) for i in range(4)]
              r_PSh = [[Res(psum=True), Res(psum=True)] for _ in range(2)]
              r_PO = [Res(psum=True) for _ in range(4)]
              PSn = [PS[0][:, 0:512], PS[0][:, 512:1024], PS[1][:, 0:512], PS[1][:, 512:1024]]
              r_PSn = [r_PSh[0][0], r_PSh[0][1], r_PSh[1][0], r_PSh[1][1]]
              PA = PO
              r_PA = r_PO

              DMA("sync", tabf[:], natab_d[:, :, :], (), [r_tabf])
              DMA("sync", half_[:], nahalo_d[:, :, :], (), [r_half])
              DMA("sync", maskf[:], namask_d[:, :, :], (), [r_maskf])
              for (srcf, dstb, nb, rs, rd) in ((tabf, tabb, 15, r_tabf, r_tabb), (half_, halb, 16, r_half, r_halb)):
                  v = srcf[:].rearrange("p a (b q) -> p (a b) q", q=64)
                  vb = dstb[:].rearrange("p a (b q) -> p (a b) q", q=64)
                  m01 = maskf[:, 0, :].unsqueeze(1).broadcast_to([128, 4 * nb, 64])
                  mng = maskf[:, 1, :].unsqueeze(1).broadcast_to([128, 4 * nb, 64])
                  TT("vector", v, v, m01, ALU.mult, [rs, r_maskf], [rs])
                  TT("vector", vb, v, mng, ALU.add, [rs, r_maskf], [rd])

              ctr = dict(hb=0, pt=0, ps=0, po=0, ot=0, ptn=0)
              if getattr(cfg, "verbose", False):
                  print("ph2 sbuf remaining", nc.sbuf_bytes_remaining)

              def mla_attention(u):
                  Sx, NQu, q0 = u["S"], u["NQ"], u["q0"]
                  nkt = Sx // 128
                  nk2 = nkt // 2
                  nqb = NQu // 512
                  steps = [(h, qb, k2) for h in range(H) for qb in range(nqb) for k2 in range(nk2)]
                  hbase = ctr["hb"]
                  ctr["hb"] += H
                  pobase = ctr["po"]
                  ctr["po"] += H * nqb

                  def load(h):
                      bi = (hbase + h) % 2
                      DMA("sync", KTb[bi][0:QKD, 0:Sx], u["kt"][h, :, :], (), [r_KTb[bi]])
                      DMA("sync", Vb[bi][:, 0:nkt, :], u["v"][h, :, :, :], (), [r_Vb[bi]])
                      DMA("sync", QTb[bi][0:QKD, 0:NQu], u["qt"][h, :, q0:q0 + NQu], (), [r_QTb[bi]])

                  def QK(i):
                      h, qb, k2 = steps[i]
                      bi = (hbase + h) % 2
                      si = i % 2
                      for j in range(2):
                          kt = 2 * k2 + j
                          MM(PS[si][:, j * 512:(j + 1) * 512], KTb[bi][0:QKD, kt * 128:(kt + 1) * 128],
                             QTb[bi][0:QKD, qb * 512:(qb + 1) * 512], True, True,
                             [r_KTb[bi], r_QTb[bi]], [r_PSh[si][j]])

                  def EXPPV(i):
                      h, qb, k2 = steps[i]
                      bi = (hbase + h) % 2
                      si = i % 2
                      pi = i % 3
                      oi = (pobase + h * nqb + qb) % 4
                      ACT(PT[pi][:, :], PS[si][:, :], AF.Exp, r_PSh[si], [r_PT[pi]])
                      for j in range(2):
                          kt = 2 * k2 + j
                          MM(PO[oi][0:65, :], Vb[bi][:, kt, :], PT[pi][:, j * 512:(j + 1) * 512],
                             kt == 0, kt == nkt - 1, [r_Vb[bi], r_PT[pi]], [r_PO[oi]])
                      if k2 == nk2 - 1:
                          ti = ctr["ot"] % 2
                          ctr["ot"] += 1
                          CP("vector", oTs[ti][0:65, :], PO[oi][0:65, :], [r_PO[oi]], [r_oTs[ti]])
                          DMA("gpsimd", u["ot"][h, :, qb * 512:(qb + 1) * 512], oTs[ti][0:65, :], [r_oTs[ti]], ())

                  load(0)
                  QK(0)
                  for i in range(len(steps)):
                      h, qb, k2 = steps[i]
                      if qb == 0 and k2 == 0 and h + 1 < H:
                          load(h + 1)
                      if i + 1 < len(steps):
                          QK(i + 1)
                      EXPPV(i)

              def na_attention(u):
                  NT, NQu = u["NT"], u["NQ"]
                  nrq = NQu // 64
                  plan = na_plan(u["kind"], nrq)
                  qrow0 = 0 if u["kind"] == "P" else 4
                  DMA("sync", naq[:, :, 0:NT], u["naq"][:, :, :], (), [r_naq])
                  DMA("sync", nak[:, :, 0:NT], u["nak"][:, :, :], (), [r_nak])
                  DMA("sync", nav[:, 0:NT // 64, :], u["nav"][:, :, :, :].rearrange("p t a d -> p (t a) d"), (), [r_nav])
                  chains = [(h, b) for b in range(len(plan)) for h in range(H)]
                  for c0_ in range(0, len(chains), NCHAIN):
                      grp = chains[c0_:c0_ + NCHAIN]
                      nst = max(len(plan[b]) for (h, b) in grp)

                      def QKB(j, s):
                          h, b = grp[j]
                          items = plan[b]
                          if s >= len(items):
                              return
                          ks, i0, i1, src, blk0 = items[s]
                          pr, pb = h // 2, (h % 2) * 64
                          N = 64 * (i1 - i0 + 1)
                          qtok0 = (qrow0 + i0) * 64
                          MM(PSn[j][0:64, 0:N], nak[pb:pb + 64, pr, ks * 64:(ks + 1) * 64],
                             naq[pb:pb + 64, pr, qtok0:qtok0 + N], True, False, [r_nak, r_naq], [r_PSn[j]])
                          tb = tabb if src == "T" else halb
                          rtb = r_tabb if src == "T" else r_halb
                          MM(PSn[j][0:64, 0:N], identb[pb:pb + 64, pb:pb + 64],
                             tb[pb:pb + 64, pr, blk0 * 64:blk0 * 64 + N], False, True, [rtb, r_identb], [r_PSn[j]])

                      def EXPN(j, s):
                          h, b = grp[j]
                          items = plan[b]
                          if s >= len(items):
                              return
                          ks, i0, i1, src, blk0 = items[s]
                          N = 64 * (i1 - i0 + 1)
                          pi = 2 * j + (s % 2)
                          ACT(PTn[pi][:, 0:N], PSn[j][0:64, 0:N], AF.Exp, [r_PSn[j]], [r_PTn[pi]])

                      def PVN(j, s):
                          h, b = grp[j]
                          items = plan[b]
                          if s >= len(items):
                              return
                          ks, i0, i1, src, blk0 = items[s]
                          N = 64 * (i1 - i0 + 1)
                          c0 = (i0 - 8 * b) * 64
                          pi = 2 * j + (s % 2)
                          MM(PA[j][0:65, c0:c0 + N], nav[:, ks, h * 65:(h + 1) * 65], PTn[pi][:, 0:N],
                             s == 0, s == len(items) - 1, [r_nav, r_PTn[pi]], [r_PA[j]], skip_group_check=True)
                          if s == len(items) - 1:
                              ti = ctr["ot"] % 2
                              ctr["ot"] += 1
                              CP("vector", oTs[ti][0:65, :], PA[j][0:65, :], [r_PA[j]], [r_oTs[ti]])
                              DMA("gpsimd", u["ot"][8 + h, :, b * 512:(b + 1) * 512], oTs[ti][0:65, :],
                                  [r_oTs[ti]], ())

                      for j in range(len(grp)):
                          QKB(j, 0)
                      for s in range(nst):
                          for j in range(len(grp)):
                              EXPN(j, s)
                          for j in range(len(grp)):
                              QKB(j, s + 1)
                          for j in range(len(grp)):
                              PVN(j, s)

              for u in units:
                  na_attention(u)
                  mla_attention(u)
        barrier()

        ph3 = contextlib.ExitStack()
        with ph3 if 3 in PHASES else contextlib.nullcontext():
          if 3 in PHASES:
              WST = 352
              NWST3 = 3
              wst = [sb(ph3, f"wst3_{i}", [128, WST], F32) for i in range(NWST3)]
              r_wst = [Res() for _ in range(NWST3)]
              rot = [0, 0]
              wo = sb(ph3, "wo", [128, NCH, D], BF16)
              wg = sb(ph3, "wg", [128, NCH, DFF], BF16)
              wu = sb(ph3, "wu", [128, NCH, DFF], BF16)
              wd = sb(ph3, "wd", [128, NFF, D], BF16)
              r_wo = [Res() for _ in range(NCH)]
              r_wg = [Res() for _ in range(NCH)]
              r_wu = [Res() for _ in range(NCH)]
              r_wd = [Res() for _ in range(NFF)]
              load_weight(wst, r_wst, WST, rot, wo, r_wo, w_o, NCH, D, G_MIX, [(0, D, 1.0)], cv_engs=("vector", "scalar"))
              load_weight(wst, r_wst, WST, rot, wg, r_wg, w_gate, NCH, DFF, G_FFN, [(0, DFF, 1.0)], cv_engs=("vector", "scalar"))
              load_weight(wst, r_wst, WST, rot, wu, r_wu, w_up, NCH, DFF, G_FFN, [(0, DFF, 1.0)], cv_engs=("vector", "scalar"))
              load_weight(wst, r_wst, WST, rot, wd, r_wd, w_down, NFF, D, None, [(0, D, 1.0)], cv_engs=("vector", "scalar"))
              gfin = sb(ph3, "gfin", [128, D], F32)
              r_gfin = Res()
              DMA("sync", gfin[:], gfin_d[0:1, :].partition_broadcast(128), (), [r_gfin])
              onec = sb(ph3, "onec", [128, 1], F32)
              r_one = Res()
              MEMSET("vector", onec[:], 1.0, [r_one])

              TG = 2
              NTOK = TG * 128
              oTin = sb(ph3, "oTin", [128, 4, 128], F32)
              r_oTin = Res()
              atok = sb(ph3, "atok", [128, D], F32)
              r_atok = Res()
              x1 = [[sb(ph3, f"x1_{s_}_{i}", [128, D], F32) for i in range(TG)] for s_ in range(2)]
              r_x1 = [[Res() for _ in range(TG)] for _ in range(2)]
              mixb = sb(ph3, "mixb", [128, D], BF16)
              r_mixb = Res()
              mixT = sb(ph3, "mixT", [128, NCH, 128], BF16)
              r_mixT = Res()
              h2b = sb(ph3, "h2b", [128, D], BF16)
              r_h2b = Res()
              h2T = [sb(ph3, f"h2T{s_}", [128, NCH, NTOK], BF16) for s_ in range(2)]
              r_h2T = [[Res() for _ in range(TG)] for _ in range(2)]
              actT = sb(ph3, "actT", [128, NFF, NTOK], BF16)
              r_actT = [Res() for _ in range(NFF)]
              sg = [sb(ph3, f"sg{i}", [128, NTOK], F32) for i in range(2)]
              r_sg = [Res(), Res()]
              st3 = [sb(ph3, f"st3_{i}", [128, 32], F32) for i in range(4)]
              r_st3 = [[Res() for _ in range(6)] for _ in range(4)]

              POh = psum(ph3, "poh", [128, 512], F32)
              r_POh = Res(psum=True)
              PTR = psum(ph3, "ptr3", [128, 512], F32)
              r_PTR = Res(psum=True)
              ptr3_bf = PTR[:].bitcast(BF16)
              PY = psum(ph3, "py", [128, 1024], F32)
              r_PY = Res(psum=True)
              PG = [psum(ph3, f"pg{i}", [128, 512], F32) for i in range(2)]
              PU = [psum(ph3, f"pu{i}", [128, 512], F32) for i in range(2)]
              r_PG, r_PU = [Res(psum=True), Res(psum=True)], [Res(psum=True), Res(psum=True)]
              if getattr(cfg, "verbose", False):
                  print("ph3 sbuf remaining", nc.sbuf_bytes_remaining)

              def prologue(u, g, slot):
                  for t in range(TG):
                      tok0 = g * NTOK + t * 128
                      X, rX = x1[slot][t], r_x1[slot][t]
                      st = st3[slot * 2 + t]
                      rs_ = r_st3[slot * 2 + t]
                      DMA("sync", X[:], u["xres"][tok0:tok0 + 128, :], (), [rX])
                      for hq in range(4):
                          DMA("sync", oTin[0:65, :, :],
                              u["ot"][hq * 4:(hq + 1) * 4, :, tok0:tok0 + 128].rearrange("h d n -> d h n"),
                              (), [r_oTin])
                          yield
                          for hh in range(4):
                              TR(POh[:, hh * 128:hh * 128 + 65], oTin[0:65, hh, :], identf[0:65, 0:65],
                                 [r_oTin, r_identf], [r_POh])
                          pov = POh[:, :].rearrange("p (h d) -> p h d", h=4)
                          rc = st[:, 8:12].unsqueeze(2)
                          OP("vector", lambda e, o=rc, i=pov[:, :, 64:65]: e.reciprocal(out=o, in_=i),
                             [r_POh], [rs_[4]])
                          TT("vector", atok[:, hq * 256:(hq + 1) * 256].rearrange("p (h d) -> p h d", h=4),
                             pov[:, :, 0:64], rc.broadcast_to([128, 4, 64]), ALU.mult,
                             [r_POh, rs_[4]], [r_atok])
                      for half in range(2):
                          hs = slice(half * 512, (half + 1) * 512)
                          ACT(mixb[:, hs], atok[:, hs], AF.Square, [r_atok], [r_mixb, rs_[half]],
                              accum_out=st[:, 3 * half:3 * half + 1])
                          rstd(st, 3 * half, 512, rs_[half])
                          TS("vector", mixb[:, hs], atok[:, hs], st[:, 3 * half + 2:3 * half + 3], None, ALU.mult, None,
                             [r_atok, rs_[half]], [r_mixb])
                      yield
                      yield
                      for c in range(NCH):
                          TR(ptr3_bf[:, c * 128:(c + 1) * 128], mixb[:, c * 128:(c + 1) * 128], identb[:],
                             [r_mixb, r_identb], [r_PTR])
                      CP("scalar", mixT[:], ptr3_bf[:, 0:1024].rearrange("p (c n) -> p c n", c=NCH),
                         [r_PTR], [r_mixT])
                      yield
                      for hh in range(2):
                          for c in range(NCH):
                              MM(PY[:, hh * 512:(hh + 1) * 512], mixT[:, c, :], wo[:, c, hh * 512:(hh + 1) * 512],
                                 c == 0, c == NCH - 1, [r_mixT, r_wo[c]], [r_PY])
                      TT("vector", X[:], X[:], PY[:, :], ALU.add, [rX, r_PY], [rX])
                      ACT(h2b[:], X[:], AF.Square, [rX], [r_h2b, rs_[2]], accum_out=st[:, 16:17])
                      rstd(st, 16, D, rs_[2])
                      TS("vector", h2b[:], X[:], st[:, 18:19], None, ALU.mult, None, [rX, rs_[2]], [r_h2b])
                      yield
                      yield
                      for c in range(NCH):
                          TR(ptr3_bf[:, c * 128:(c + 1) * 128], h2b[:, c * 128:(c + 1) * 128], identb[:],
                             [r_h2b, r_identb], [r_PTR])
                      CP("scalar", h2T[slot][:, :, t * 128:(t + 1) * 128],
                         ptr3_bf[:, 0:1024].rearrange("p (c n) -> p c n", c=NCH), [r_PTR], [r_h2T[slot][t]])
                      yield

              def ffn_step(slot, f):
                  pi = f % 2
                  for c in range(NCH):
                      MM(PG[pi][:, 0:NTOK], wg[:, c, f * 128:(f + 1) * 128], h2T[slot][:, c, :], c == 0, c == NCH - 1,
                         r_h2T[slot] + [r_wg[c]], [r_PG[pi]])
                  for c in range(NCH):
                      MM(PU[pi][:, 0:NTOK], wu[:, c, f * 128:(f + 1) * 128], h2T[slot][:, c, :], c == 0, c == NCH - 1,
                         r_h2T[slot] + [r_wu[c]], [r_PU[pi]])
                  ACT(sg[pi][:], PG[pi][:, 0:NTOK], AF.Exp, [r_PG[pi]], [r_sg[pi]], scale=-1.0)
                  ACT(sg[pi][:], sg[pi][:], AF.Ln, [r_sg[pi], r_one], [r_sg[pi]], bias=onec[:, 0:1])
                  ACT(sg[pi][:], sg[pi][:], AF.Exp, [r_sg[pi]], [r_sg[pi]], scale=-1.0)
                  TT("vector", sg[pi][:], sg[pi][:], PG[pi][:, 0:NTOK], ALU.mult, [r_sg[pi], r_PG[pi]], [r_sg[pi]])
                  TT("vector", actT[:, f, :], sg[pi][:], PU[pi][:, 0:NTOK], ALU.mult,
                     [r_sg[pi], r_PU[pi]], [r_actT[f]])

              def down_epi(u, g, slot, t):
                  tok0 = g * NTOK + t * 128
                  X, rX = x1[slot][t], r_x1[slot][t]
                  st = st3[slot * 2 + t]
                  rr = r_st3[slot * 2 + t][3]
                  for hh in range(2):
                      for f in range(NFF):
                          MM(PY[:, hh * 512:(hh + 1) * 512], actT[:, f, t * 128:(t + 1) * 128],
                             wd[:, f, hh * 512:(hh + 1) * 512], f == 0, f == NFF - 1,
                             [r_actT[f], r_wd[f]], [r_PY])
                  TT("vector", X[:], X[:], PY[:, :], ALU.add, [rX, r_PY], [rX])
                  ACT(h2b[:], X[:], AF.Square, [rX], [r_h2b, rr], accum_out=st[:, 24:25])
                  rstd(st, 24, D, rr)
                  STT("vector", X[:], X[:], st[:, 26:27], gfin[:], ALU.mult, ALU.mult, [rX, rr, r_gfin], [rX])
                  DMA("gpsimd", u["yout"][tok0:tok0 + 128, :], X[:], [rX], ())

              groups = [(u, g) for u in units for g in range(u["NQ"] // NTOK)]
              cur = prologue(groups[0][0], groups[0][1], 0)
              for _ in cur:
                  pass
              for gi_, (u, g) in enumerate(groups):
                  slot = gi_ % 2
                  nxt = None
                  if gi_ + 1 < len(groups):
                      nxt = prologue(groups[gi_ + 1][0], groups[gi_ + 1][1], 1 - slot)
                  for f in range(NFF):
                      ffn_step(slot, f)
                      if nxt is not None:
                          next(nxt, None)
                  if nxt is not None:
                      for _ in nxt:
                          pass
                  for t in range(TG):
                      down_epi(u, g, slot, t)
        S.emit(nc, block, csem, dsem)
    return nc


def _rope_table(pos):
    inv = (10000.0 ** (-np.arange(0, RD, 2, dtype=np.float32) / np.float32(RD))).astype(np.float32)
    ang = pos.astype(np.float32)[:, None] * inv[None, :]
    tab = np.concatenate([np.cos(ang), np.sin(ang)], axis=1).astype(np.float32)
    nt = pos.shape[0] // 128
    return np.ascontiguousarray(tab.reshape(nt, 128, 32).transpose(1, 0, 2))


def _col(g):
    g = np.asarray(g, np.float32).reshape(-1, 128)
    return g.T


def make_in_maps(inputs, cfg):
    NP, SP, SC, NQ, NNA = cfg.NP, cfg.SP, cfg.SC, cfg.NQ, cfg.NNA
    xpr = np.asarray(inputs["x_prompt"], np.float32)
    xsm = np.asarray(inputs["x_sample"], np.float32)
    nquart = SC // NQ
    ncores = xsm.shape[0] * nquart
    assert xpr.shape[0] == NP * ncores
    Rq = NQ // 64
    Rtot = SC // 64
    f = lambda k: np.ascontiguousarray(np.asarray(inputs[k], np.float32)[0])
    gcol = np.zeros((128, 32), np.float32)
    gcol[:, G_ATTN:G_ATTN + 8] = _col(inputs["attn_norm_g"][0])
    gcol[:, G_Q:G_Q + 3] = _col(inputs["q_norm_g"][0])
    gcol[:, G_KV:G_KV + 2] = _col(inputs["kv_norm_g"][0])
    gcol[:, G_MIX:G_MIX + 4] = _col(inputs["mla_out_g"][0])
    gcol[:, G_MIX + 4:G_MIX + 8] = _col(inputs["na_out_g"][0])
    gcol[:, G_FFN:G_FFN + 8] = _col(inputs["ffn_norm_g"][0])
    rpb = np.asarray(inputs["na_rpb"], np.float32)[0]
    kc = np.arange(64)[:, None]
    qc = np.arange(64)[None, :]
    dj = np.clip(kc - qc, -15, 15) + 15
    cs = np.clip(qc - 8, 0, 48)
    m01 = ((kc >= cs) & (kc < cs + 16)).astype(np.float32)
    namask = np.zeros((128, 2, 64), np.float32)
    namask[:, 0, :] = np.tile(m01, (2, 1))
    namask[:, 1, :] = np.tile((1.0 - m01) * NEG, (2, 1))

    def blockT(h, di):
        return rpb[h, di + 7][dj]

    natab = np.zeros((128, 4, 15, 64), np.float32)
    for h in range(H):
        pb, pr = (h % 2) * 64, h // 2
        for j in range(15):
            natab[pb:pb + 64, pr, j, :] = blockT(h, 7 - j)
    natab = natab.reshape(128, 4, 15 * 64)
    shared = dict(w_in=f("w_in"), w_uq=f("w_uq"), w_ukv=f("w_ukv"), w_o=f("w_o"), w_gate=f("w_gate"),
                  w_up=f("w_up"), w_down=f("w_down"), gcol=gcol,
                  gfin=np.asarray(inputs["final_norm_g"], np.float32).reshape(1, D),
                  ident=np.eye(128, dtype=np.float32), natab=natab, namask=namask,
                  ropeP=_rope_table(np.arange(SP)), ropeC=_rope_table(np.arange(SC)))
    maps = []
    for c in range(ncores):
        sq, qt = c // nquart, c % nquart
        m = dict(shared)
        m["xp"] = np.ascontiguousarray(xpr[c * NP:(c + 1) * NP].reshape(NP * SP, D))
        seq = xsm[sq]
        m["xc"] = np.ascontiguousarray(seq)
        r0 = qt * Rq
        rows_b = [4, 5, 6, 7] if qt == 0 else [r0 - 4, r0 - 3, r0 - 2, r0 - 1]
        last = (qt == nquart - 1)
        rows_a = [Rtot - 8, Rtot - 7, Rtot - 6] if last else [r0 + Rq, r0 + Rq + 1, r0 + Rq + 2]
        rows = rows_b + list(range(r0, r0 + Rq)) + rows_a
        xn = np.zeros((NNA, D), np.float32)
        posn = np.zeros((NNA,), np.float32)
        for i, r in enumerate(rows):
            xn[i * 64:(i + 1) * 64] = seq[r * 64:(r + 1) * 64]
            posn[i * 64:(i + 1) * 64] = np.arange(r * 64, (r + 1) * 64)
        m["xn"] = xn
        m["ropeN"] = _rope_table(posn)
        hal = np.zeros((128, 4, 16, 64), np.float32)
        for h in range(H):
            pb, pr = (h % 2) * 64, h // 2
            for ks in range(4):
                for i in range(ks + 1):
                    di = (rows_b[ks] - r0) - i
                    hal[pb:pb + 64, pr, [0, 1, 3, 6][ks] + i, :] = blockT(h, di)
            for j in range(3):
                for n_, i in enumerate(range(Rq - 3 + j, Rq)):
                    di = (rows_a[j] - r0) - i
                    hal[pb:pb + 64, pr, 10 + [0, 3, 5][j] + n_, :] = blockT(h, di)
        m["nahalo"] = hal.reshape(128, 4, 16 * 64)
        maps.append(m)
    return maps


def assemble(results, cfg, nsample):
    NP, SP, SC, NQ = cfg.NP, cfg.SP, cfg.SC, cfg.NQ
    nquart = SC // NQ
    ncores = nsample * nquart
    yp = np.concatenate([np.asarray(results[c]["yp"], np.float32).reshape(NP, SP, D) for c in range(ncores)], axis=0)
    ys = np.zeros((nsample, SC, D), np.float32)
    for c in range(ncores):
        sq, qt = c // nquart, c % nquart
        ys[sq, qt * NQ:(qt + 1) * NQ] = np.asarray(results[c]["yn"], np.float32)
    return yp, ys


_NC_CACHE = {}


def kernel(**inputs):
    cfg = Cfg(NP=2, SP=2048, SC=8192, NQ=2048)
    maps = make_in_maps(inputs, cfg)
    key = (cfg.NP, cfg.SP, cfg.SC, cfg.NQ)
    if key not in _NC_CACHE:
        _NC_CACHE[key] = build_program(cfg)
    nc = _NC_CACHE[key]
    res = run_bass_kernel_spmd(nc, maps, core_ids=list(range(len(maps))))
    return assemble(res.results, cfg, np.asarray(inputs["x_sample"]).shape[0])
```

```python
import numpy as np
import concourse.bass as bass
import concourse.mybir as mybir
from concourse.bass_utils import run_bass_kernel_spmd

F32 = mybir.dt.float32
BF16 = mybir.dt.bfloat16
AF = mybir.ActivationFunctionType
ALU = mybir.AluOpType

QUEUES = ("sync", "scalar", "vector", "gpsimd", "tensor")
NDSEM = {"sync": 16, "gpsimd": 12, "scalar": 4}


class Res:
    __slots__ = ("name", "w", "rc", "rd", "psum")

    def __init__(self, name="", psum=False):
        self.name = name
        self.psum = psum
        self.w = None
        self.rc = {}
        self.rd = []


class Op:
    __slots__ = ("q", "dma", "fn", "deps", "signal", "cidx", "didx", "pos")


class Sched:
    def __init__(self):
        self.ops = {q: [] for q in QUEUES}
        self.ncomp = {q: 0 for q in QUEUES}
        self.ndma = {q: 0 for q in QUEUES}
        self.dma_ops = {q: [] for q in QUEUES}
        self.all_dma = []

    def op(self, q, fn, reads=(), writes=(), dma=False):
        o = Op()
        o.q, o.dma, o.fn, o.signal = q, dma, fn, dma
        deps = set()
        for r in reads:
            if r.w is not None:
                deps.add(r.w)
            if r.psum:
                deps.update(o2 for q2, o2 in r.rc.items() if q2 != q)
        for r in writes:
            if r.w is not None:
                deps.add(r.w)
            deps.update(r.rc.values())
            deps.update(r.rd)
        if dma:
            o.didx = self.ndma[q]
            self.ndma[q] += 1
            R = NDSEM[q]
            if o.didx >= R:
                deps.add(self.dma_ops[q][o.didx - R])
            self.dma_ops[q].append(o)
            self.all_dma.append(o)
            o.cidx = None
        else:
            o.cidx = self.ncomp[q]
            self.ncomp[q] += 1
            o.didx = None
        deps.discard(o)
        if q == "tensor" and not dma:
            deps = {d for d in deps if not (d.q == "tensor" and not d.dma)}
        o.deps = deps
        for d in deps:
            d.signal = True
        for r in reads:
            if dma:
                r.rd.append(o)
            else:
                r.rc[q] = o
        for r in writes:
            r.w = o
            r.rc = {}
            r.rd = []
        o.pos = len(self.ops[q])
        self.ops[q].append(o)
        return o

    def emit(self, nc, block, csem, dsem):
        sched = self
        val = {}
        for q in QUEUES:
            n = 0
            for o in self.ops[q]:
                if o.dma:
                    val[o] = (dsem[q][o.didx % NDSEM[q]], 16 * (o.didx // NDSEM[q] + 1))
                elif o.signal:
                    n += 1
                    val[o] = (csem[q], n)

        def run(q, eng):
            waited = {}
            last_dma = []
            for o in sched.ops[q]:
                need = {}
                for d in o.deps:
                    s, v = val[d]
                    k = id(s)
                    if v > need.get(k, (None, 0))[1]:
                        need[k] = (s, v)
                for k, (s, v) in need.items():
                    if waited.get(k, 0) < v:
                        eng.wait_ge(s, v)
                        waited[k] = v
                ins = o.fn(eng)
                if o.dma:
                    s, v = val[o]
                    ins.then_inc(s, 16)
                elif o.signal:
                    ins.then_inc(csem[q], 1)
            for o in sched.dma_ops[q][-NDSEM.get(q, 0):] if sched.dma_ops[q] else []:
                s, v = val[o]
                if waited.get(id(s), 0) < v:
                    eng.wait_ge(s, v)
                    waited[id(s)] = v

        @block.sync
        def _(e):
            run("sync", e)

        @block.scalar
        def _(e):
            run("scalar", e)

        @block.vector
        def _(e):
            run("vector", e)

        @block.gpsimd
        def _(e):
            run("gpsimd", e)

        @block.tensor
        def _(e):
            run("tensor", e)


D = 1024
NCH = 8
QL, KVL, RD = 384, 256, 32
H = 8
QKD = 96
INC = 2208
C_CQ, C_CKV, C_KR, C_NQ, C_NK, C_NV = 0, 384, 640, 672, 1184, 1696
DFF = 2816
NFF = 22
EPS = 1e-6
NEG = -30000.0
GRID_W = 64
G_ATTN, G_Q, G_KV, G_MIX, G_FFN = 0, 8, 11, 13, 21


class Cfg:
    def __init__(self, NP, SP, SC, NQ):
        self.NP, self.SP, self.SC, self.NQ = NP, SP, SC, NQ
        self.NNA = NQ + 512
        self.QOFF = 256


def na_plan(kind, nrows_q):
    plan = []
    if kind == "P":
        R = nrows_q
        kh = min(8, R)

        def start(r):
            return min(max(r - kh // 2, 0), R - kh)
        for b in range(R // 8):
            items = []
            for kr in range(R):
                rows = [i for i in range(8 * b, 8 * b + 8) if start(i) <= kr <= start(i) + kh - 1]
                if not rows:
                    continue
                i0, i1 = rows[0], rows[-1]
                assert rows == list(range(i0, i1 + 1))
                items.append((kr, i0, i1, "T", 7 - (kr - i0)))
            plan.append(items)
    else:
        R = nrows_q
        for b in range(R // 8):
            items = []
            for ks in range(R + 7):
                rows = [i for i in range(8 * b, 8 * b + 8) if i <= ks <= i + 7]
                if not rows:
                    continue
                i0, i1 = rows[0], rows[-1]
                assert rows == list(range(i0, i1 + 1))
                if ks < 4:
                    hb0 = [0, 1, 3, 6][ks]
                    assert i0 == 0
                    items.append((ks, i0, i1, "H", hb0))
                elif ks >= R + 4:
                    j = ks - (R + 4)
                    hb0 = 10 + [0, 3, 5][j]
                    assert i0 == R - 3 + j and i1 == R - 1
                    items.append((ks, i0, i1, "H", hb0))
                else:
                    kr = ks - 4
                    items.append((ks, i0, i1, "T", 7 - (kr - i0)))
            plan.append(items)
    return plan


def build_program(cfg):
    import contextlib
    nc = bass.Bass("TRN2", target_bir_lowering=False)
    NP, SP, SC, NQ, NNA, QOFF = cfg.NP, cfg.SP, cfg.SC, cfg.NQ, cfg.NNA, cfg.QOFF
    PHASES = getattr(cfg, 'phases', (1, 2, 3))

    def din(name, shape, dt=F32):
        return nc.dram_tensor(name, list(shape), dt, kind="ExternalInput").ap()

    def dscr(name, shape, dt):
        return nc.dram_tensor(name, list(shape), dt, kind="Internal").ap()

    xp = din("xp", [NP * SP, D])
    xc = din("xc", [SC, D])
    xn = din("xn", [NNA, D])
    w_in = din("w_in", [D, INC])
    w_uq = din("w_uq", [QL, H * QKD])
    w_ukv = din("w_ukv", [KVL, H * 128])
    w_o = din("w_o", [D, D])
    w_gate = din("w_gate", [D, DFF])
    w_up = din("w_up", [D, DFF])
    w_down = din("w_down", [DFF, D])
    gcol_d = din("gcol", [128, 32])
    gfin_d = din("gfin", [1, D])
    ident_d = din("ident", [128, 128])
    ropeP_d = din("ropeP", [128, SP // 128, 32])
    ropeC_d = din("ropeC", [128, SC // 128, 32])
    ropeN_d = din("ropeN", [128, NNA // 128, 32])
    natab_d = din("natab", [128, 4, 15 * 64])
    nahalo_d = din("nahalo", [128, 4, 16 * 64])
    namask_d = din("namask", [128, 2, 64])
    yp = nc.dram_tensor("yp", [NP * SP, D], F32, kind="ExternalOutput").ap()
    yn = nc.dram_tensor("yn", [NQ, D], F32, kind="ExternalOutput").ap()

    units = []
    for i in range(NP):
        units.append(dict(kind="P", S=SP, NT=SP, NQ=SP, q0=0, name=f"p{i}",
                          xres=xp[i * SP:(i + 1) * SP, :], yout=yp[i * SP:(i + 1) * SP, :]))
    units.append(dict(kind="S", S=SC, NT=NNA, NQ=NQ, q0=QOFF, name="s",
                      xres=xn[QOFF:QOFF + NQ, :], yout=yn))
    for u in units:
        n = u["name"]
        u["kt"] = dscr("s_kt_" + n, [H, QKD, u["S"]], BF16)
        u["v"] = dscr("s_v_" + n, [H, 128, u["S"] // 128, 65], BF16)
        u["qt"] = dscr("s_qt_" + n, [H, QKD, u["NT"]], BF16)
        u["naq"] = dscr("s_naq_" + n, [128, 4, u["NT"]], BF16)
        u["nak"] = dscr("s_nak_" + n, [128, 4, u["NT"]], BF16)
        u["nav"] = dscr("s_nav_" + n, [64, u["NT"] // 128, 2, 8 * 65], BF16)
        u["ot"] = dscr("s_ot_" + n, [16, 65, u["NQ"]], F32)

    S = Sched()
    top = contextlib.ExitStack()
    with top:
        csem = {q: top.enter_context(nc.semaphore("c_" + q)) for q in QUEUES}
        dsem = {q: [top.enter_context(nc.semaphore(f"d_{q}{i}")) for i in range(n)]
                for q, n in NDSEM.items()}
        block = top.enter_context(nc.Block())

        pending = {q: set() for q in QUEUES}

        def OP(q, fn, reads=(), writes=(), dma=False):
            o = S.op(q, fn, reads, writes, dma)
            if pending[q]:
                extra = {d for d in pending[q] if d is not o}
                if q == "tensor" and not dma:
                    extra = {d for d in extra if not (d.q == "tensor" and not d.dma)}
                for d in extra:
                    d.signal = True
                o.deps |= extra
                pending[q] = set()
            return o

        def barrier():
            B = set()
            for q in QUEUES:
                comp = [o for o in S.ops[q] if not o.dma]
                if comp:
                    B.add(comp[-1])
                if S.dma_ops[q]:
                    B.update(S.dma_ops[q][-NDSEM[q]:])
            for q in QUEUES:
                pending[q] |= B

        def DMA(q, out, in_, reads=(), writes=()):
            return OP(q, lambda e, o=out, i=in_: e.dma_start(out=o, in_=i), reads, writes, dma=True)

        def ACT(out, in_, func, reads=(), writes=(), **kw):
            return OP("scalar", lambda e, o=out, i=in_, f=func, kw=kw: e.activation(out=o, in_=i, func=f, **kw),
                      reads, writes)

        def TS(q, out, in0, s1, s2, op0, op1, reads=(), writes=()):
            if s2 is None:
                return OP(q, lambda e, o=out, i=in0, a=s1, p0=op0:
                          e.tensor_scalar(out=o, in0=i, scalar1=a, scalar2=0.0, op0=p0, op1=ALU.add), reads, writes)
            return OP(q, lambda e, o=out, i=in0, a=s1, b=s2, p0=op0, p1=op1:
                      e.tensor_scalar(out=o, in0=i, scalar1=a, scalar2=b, op0=p0, op1=p1), reads, writes)

        def TT(q, out, in0, in1, op, reads=(), writes=()):
            return OP(q, lambda e, o=out, a=in0, b=in1, p=op: e.tensor_tensor(out=o, in0=a, in1=b, op=p),
                      reads, writes)

        def STT(q, out, in0, scalar, in1, op0, op1, reads=(), writes=()):
            return OP(q, lambda e, o=out, a=in0, s=scalar, b=in1, p0=op0, p1=op1:
                      e.scalar_tensor_tensor(out=o, in0=a, scalar=s, in1=b, op0=p0, op1=p1), reads, writes)

        def CP(q, out, in_, reads=(), writes=()):
            if q == "scalar":
                return ACT(out, in_, AF.Copy, reads, writes)
            return OP(q, lambda e, o=out, i=in_: e.tensor_copy(out=o, in_=i), reads, writes)

        def MM(out, lhsT, rhs, start, stop, reads=(), writes=(), **kw):
            return OP("tensor", lambda e, o=out, l=lhsT, r=rhs, s=start, t=stop, kw=kw:
                      e.matmul(out=o, lhsT=l, rhs=r, start=s, stop=t, **kw), reads, writes)

        def TR(out, in_, idn, reads=(), writes=()):
            return OP("tensor", lambda e, o=out, i=in_, d=idn: e.transpose(out=o, in_=i, identity=d),
                      reads, writes)

        def MEMSET(q, ap, val, writes=()):
            return OP(q, lambda e, a=ap, v=val: e.memset(a, v), (), writes)

        class Region:
            def __init__(self, arena, lo, hi):
                self.arena, self.lo, self.hi, self.p = arena, lo, hi, lo

            def reset(self):
                self.p = self.lo

            def alloc(self, name, shape, dt):
                esz = 2 if dt == BF16 else 4
                n = 1
                for d_ in shape[1:]:
                    n *= d_
                nbytes = (n * esz + 31) // 32 * 32
                assert self.p + nbytes <= self.hi, f"arena region overflow allocating {name}: {self.p}+{nbytes}>{self.hi}"
                v = self.arena[:, self.p // 4:(self.p + nbytes) // 4]
                self.p += nbytes
                if dt == BF16:
                    v = v.bitcast(BF16)
                v = v[:, 0:n]
                if len(shape) == 3:
                    v = v.rearrange("p (a b) -> p a b", a=shape[1])
                elif len(shape) == 4:
                    v = v.rearrange("p (a b c) -> p a b c", a=shape[1], b=shape[2])
                if shape[0] < 128:
                    v = v[0:shape[0]]
                return v

        def sb(stack, name, shape, dt):
            if isinstance(stack, Region):
                return stack.alloc(name, list(shape), dt)
            return stack.enter_context(nc.sbuf_tensor("sb_" + name, list(shape), dt))

        def psum(stack, name, shape, dt=F32):
            return stack.enter_context(nc.psum_tensor("ps_" + name, list(shape), dt))

        identf = sb(top, "identf", [128, 128], F32)
        identb = sb(top, "identb", [128, 128], BF16)
        gcol = sb(top, "gcol", [128, 32], F32)
        epsc = sb(top, "epsc", [128, 1], F32)
        r_identf, r_identb, r_gcol, r_eps = Res(), Res(), Res(), Res()
        DMA("sync", identf[:], ident_d[:, :], (), [r_identf])
        DMA("sync", gcol[:], gcol_d[:, :], (), [r_gcol])
        CP("vector", identb[:], identf[:], [r_identf], [r_identb])
        MEMSET("vector", epsc[:], EPS, [r_eps])

        def rstd(st, c0, n, R):
            ACT(st[:, c0 + 1:c0 + 2], st[:, c0:c0 + 1], AF.Ln, [R, r_eps], [R], bias=epsc[:, 0:1], scale=1.0 / n)
            ACT(st[:, c0 + 2:c0 + 3], st[:, c0 + 1:c0 + 2], AF.Exp, [R], [R], scale=-0.5)

        def load_weight(*a, **kw):
            for _ in load_weight_gen(*a, **kw):
                pass

        def load_weight_gen(wst, r_wst, WST, rot, dst, r_dst, src, nchunk, ncols, gbase, col_scales,
                            cv_engs=("vector", "gpsimd")):
            for c in range(nchunk):
                for c0 in range(0, ncols, WST):
                    c1 = min(ncols, c0 + WST)
                    k = rot[0] % len(wst)
                    rot[0] += 1
                    DMA("sync" if rot[0] % 2 else "gpsimd", wst[k][:, 0:c1 - c0], src[c * 128:(c + 1) * 128, c0:c1],
                        (), [r_wst[k]])
                    for (a, b, const) in col_scales:
                        lo, hi = max(a, c0), min(b, c1)
                        if lo >= hi:
                            continue
                        eng = cv_engs[rot[1] % len(cv_engs)]
                        rot[1] += 1
                        if eng == "scalar":
                            if gbase is None:
                                CP(eng, dst[:, c, lo:hi], wst[k][:, lo - c0:hi - c0], [r_wst[k]], [r_dst[c]])
                            elif const == 1.0:
                                ACT(dst[:, c, lo:hi], wst[k][:, lo - c0:hi - c0], AF.Copy, [r_wst[k], r_gcol], [r_dst[c]],
                                    scale=gcol[:, gbase + c:gbase + c + 1])
                            else:
                                TS("vector", dst[:, c, lo:hi], wst[k][:, lo - c0:hi - c0],
                                   gcol[:, gbase + c:gbase + c + 1], const, ALU.mult, ALU.mult,
                                   [r_wst[k], r_gcol], [r_dst[c]])
                        elif gbase is None:
                            CP(eng, dst[:, c, lo:hi], wst[k][:, lo - c0:hi - c0], [r_wst[k]], [r_dst[c]])
                        else:
                            TS(eng, dst[:, c, lo:hi], wst[k][:, lo - c0:hi - c0], gcol[:, gbase + c:gbase + c + 1],
                               const, ALU.mult, ALU.mult, [r_wst[k], r_gcol], [r_dst[c]])
                    yield

        WST = 1408
        ph1 = contextlib.ExitStack()
        with ph1 if 1 in PHASES else contextlib.nullcontext():
          if 1 in PHASES:
              wst = [sb(ph1, f"wst{i}", [128, WST], F32) for i in range(2)]
              r_wst = [Res(), Res()]
              rot = [0, 0]
              win = sb(ph1, "win", [128, NCH, INC], BF16)
              wuq = sb(ph1, "wuq", [128, 3, H * QKD], BF16)
              wukv = sb(ph1, "wukv", [128, 2, H * 128], BF16)
              r_win = [Res() for _ in range(NCH)]
              r_wuq = [Res() for _ in range(3)]
              r_wukv = [Res() for _ in range(2)]
              load_weight(wst, r_wst, WST, rot, win, r_win, w_in, NCH, INC, G_ATTN,
                          [(0, C_NQ, 1.0), (C_NQ, C_NK, 0.125), (C_NK, INC, 1.0)])
              load_weight(wst, r_wst, WST, rot, wuq, r_wuq, w_uq, 3, H * QKD, G_Q, [(0, H * QKD, QKD ** -0.5)])
              load_weight(wst, r_wst, WST, rot, wukv, r_wukv, w_ukv, 2, H * 128, G_KV, [(0, H * 128, 1.0)])

              ropeP = sb(ph1, "ropeP", [128, SP // 128, 32], F32)
              ropeC = sb(ph1, "ropeC", [128, SC // 128, 32], F32)
              ropeN = sb(ph1, "ropeN", [128, NNA // 128, 32], F32)
              r_rope = Res()
              DMA("sync", ropeP[:], ropeP_d[:, :, :], (), [r_rope])
              DMA("sync", ropeC[:], ropeC_d[:, :, :], (), [r_rope])
              DMA("sync", ropeN[:], ropeN_d[:, :, :], (), [r_rope])

              NX = 4
              xt = [sb(ph1, f"xt{i}", [128, D], F32) for i in range(NX)]
              r_xt = [Res() for _ in range(NX)]
              junk = [sb(ph1, f"junk{i}", [128, D], BF16) for i in range(2)]
              r_junk = [Res(), Res()]
              jctr = [0]
              hb = [sb(ph1, f"hb{i}", [128, D], BF16) for i in range(4)]
              r_hb = [Res() for _ in range(4)]
              hT = [sb(ph1, f"hT{i}", [128, NCH, 512], BF16) for i in range(2)]
              r_hT = [[Res() for _ in range(4)] for _ in range(2)]
              NST = 6
              stt = [sb(ph1, f"stt{i}", [128, 16], F32) for i in range(NST)]
              r_stx = [Res() for _ in range(NST)]
              r_stq = [Res() for _ in range(NST)]
              r_stkv = [Res() for _ in range(NST)]
              cqn = [sb(ph1, f"cqn{i}", [128, QL], BF16) for i in range(4)]
              ckvn = [sb(ph1, f"ckvn{i}", [128, KVL], BF16) for i in range(4)]
              r_cqn = [Res() for _ in range(4)]
              r_ckvn = [Res() for _ in range(4)]
              cqnT = [sb(ph1, f"cqnT{i}", [128, 3, 128], BF16) for i in range(4)]
              ckvnT = [sb(ph1, f"ckvnT{i}", [128, 2, 128], BF16) for i in range(4)]
              r_cqnT = [Res() for _ in range(4)]
              r_ckvnT = [Res() for _ in range(4)]
              rtmp = [sb(ph1, f"rtmp{i}", [128, 96 + 512], F32) for i in range(4)]
              r_tc = [Res() for _ in range(4)]
              r_ts = [Res() for _ in range(4)]
              r_kro = [Res() for _ in range(4)]
              r_tcq = [[Res(), Res()] for _ in range(4)]
              r_tsq = [[Res(), Res()] for _ in range(4)]
              Ktok = [sb(ph1, f"Ktok{i}", [128, H, QKD], BF16) for i in range(4)]
              Qtok = [sb(ph1, f"Qtok{i}", [128, H, QKD], BF16) for i in range(4)]
              r_Ktok = [Res() for _ in range(4)]
              r_Qtok = [Res() for _ in range(4)]
              KTst = [sb(ph1, f"KTst{i}", [128, H, 512], BF16) for i in range(2)]
              QTst = [sb(ph1, f"QTst{i}", [128, H, 512], BF16) for i in range(2)]
              Vst = [sb(ph1, f"Vst{i}", [128, H, 4, 65], BF16) for i in range(2)]
              naqst = [sb(ph1, f"naqst{i}", [128, 4, 512], BF16) for i in range(2)]
              nakst = [sb(ph1, f"nakst{i}", [128, 4, 512], BF16) for i in range(2)]
              navst = [sb(ph1, f"navst{i}", [128, 4, H, 65], BF16) for i in range(2)]
              r_KTst = [[Res() for _ in range(4)] for _ in range(2)]
              r_QTst = [[Res() for _ in range(4)] for _ in range(2)]
              r_Vst = [[Res() for _ in range(4)] for _ in range(2)]
              r_naqst = [[Res() for _ in range(4)] for _ in range(2)]
              r_nakst = [[Res() for _ in range(4)] for _ in range(2)]
              r_navst = [[Res() for _ in range(4)] for _ in range(2)]
              for i in range(2):
                  for t in range(4):
                      MEMSET("gpsimd", Vst[i][:, :, t, 64:65], 1.0, [r_Vst[i][t]])
                      MEMSET("gpsimd", navst[i][:, t, :, 64:65], 1.0, [r_navst[i][t]])

              PB = [psum(ph1, f"pb{i}", [128, 512], F32) for i in range(4)]
              PKVU = psum(ph1, "pkvu", [128, 1024], F32)
              PQU = psum(ph1, "pqu", [128, 1024], F32)
              r_PB = [Res(psum=True) for _ in range(4)]
              r_PKVU, r_PQU = Res(psum=True), Res(psum=True)
              ptr_bf = PB[0][:].bitcast(BF16)
              pkt_bf = PB[3][:].bitcast(BF16)
              pqt_bf = PB[0][:].bitcast(BF16)

              tilectr = [0]
              grpctr = [0]
              if getattr(cfg, "verbose", False):
                  print("ph1 sbuf remaining", nc.sbuf_bytes_remaining)

              def square_acc(in_ap, n, acc, reads, writes):
                  j = jctr[0] % 2
                  jctr[0] += 1
                  ACT(junk[j][:, 0:n], in_ap, AF.Square, reads, [r_junk[j]] + writes, accum_out=acc)

              def tile_gen(u, src, rope_sb, do_q, do_kv, do_na, g, gi, t, tc_):
                      if True:
                          tg = g * 4 + t
                          xi = tc_ % NX
                          k2 = tc_ % 4
                          si = tc_ % NST
                          st = stt[si]
                          tsl = slice(t * 128, (t + 1) * 128)
                          DMA("sync", xt[xi][:], src[tg * 128:(tg + 1) * 128, :], (), [r_xt[xi]])
                          square_acc(xt[xi][:], D, st[:, 0:1], [r_xt[xi]], [r_stx[si]])
                          rstd(st, 0, D, r_stx[si])
                          TS("vector", hb[k2][:], xt[xi][:], st[:, 2:3], None, ALU.mult, None,
                             [r_xt[xi], r_stx[si]], [r_hb[k2]])
                          yield
                          for c in range(NCH):
                              TR(ptr_bf[:, c * 128:(c + 1) * 128], hb[k2][:, c * 128:(c + 1) * 128], identb[:],
                                 [r_hb[k2], r_identb], [r_PB[0]])
                          CP("scalar" if (tc_ % 2 and do_q) else "vector", hT[gi][:, :, tsl],
                             ptr_bf[:, 0:1024].rearrange("p (c n) -> p c n", c=NCH), [r_PB[0]], [r_hT[gi][t]])
                          yield
                          if do_q:
                              for c in range(NCH):
                                  MM(PB[1][:, 0:QL], hT[gi][:, c, tsl], win[:, c, C_CQ:C_CQ + QL],
                                     c == 0, c == NCH - 1, [r_hT[gi][t], r_win[c]], [r_PB[1]])
                          if do_kv:
                              for c in range(NCH):
                                  MM(PB[2][:, 0:KVL + RD], hT[gi][:, c, tsl], win[:, c, C_CKV:C_CKV + KVL + RD],
                                     c == 0, c == NCH - 1, [r_hT[gi][t], r_win[c]], [r_PB[2]])
                          if do_na:
                              for c in range(NCH):
                                  MM(PB[3][:, :], hT[gi][:, c, tsl], win[:, c, C_NV:C_NV + 512],
                                     c == 0, c == NCH - 1, [r_hT[gi][t], r_win[c]], [r_PB[3]])
                              CP("vector", navst[gi][:, t, :, 0:64], PB[3][:, :].rearrange("p (h d) -> p h d", h=H),
                                 [r_PB[3]], [r_navst[gi][t]])
                          yield
                          if do_q:
                              square_acc(PB[1][:, 0:QL], QL, st[:, 3:4], [r_PB[1]], [r_stq[si]])
                              rstd(st, 3, QL, r_stq[si])
                              TS("vector", cqn[k2][:], PB[1][:, 0:QL], st[:, 5:6], None, ALU.mult, None,
                                 [r_PB[1], r_stq[si]], [r_cqn[k2]])
                          if do_kv:
                              square_acc(PB[2][:, 0:KVL], KVL, st[:, 6:7], [r_PB[2]], [r_stkv[si]])
                              rstd(st, 6, KVL, r_stkv[si])
                              TS("vector", ckvn[k2][:], PB[2][:, 0:KVL], st[:, 8:9], None, ALU.mult, None,
                                 [r_PB[2], r_stkv[si]], [r_ckvn[k2]])
                              rt = rtmp[k2]
                              xk = PB[2][:, KVL:KVL + RD].rearrange("p (a j) -> p a j", a=2)
                              cosb = rope_sb[:, tg, 0:16].unsqueeze(1).broadcast_to([128, 2, 16])
                              sinb = rope_sb[:, tg, 16:32].unsqueeze(1).broadcast_to([128, 2, 16])
                              tcv = rt[:, 0:32].rearrange("p (a j) -> p a j", a=2)
                              tsv = rt[:, 32:64].rearrange("p (a j) -> p a j", a=2)
                              TT("vector", tcv, xk, cosb, ALU.mult, [r_PB[2], r_rope, r_stkv[si]], [r_tc[k2]])
                              TT("vector", tsv, xk, sinb, ALU.mult, [r_PB[2], r_rope, r_stkv[si]], [r_ts[k2]])
                              kro = rt[:, 64:96]
                              TT("vector", kro[:, 0:16], rt[:, 0:16], rt[:, 48:64], ALU.subtract,
                                 [r_tc[k2], r_ts[k2]], [r_kro[k2]])
                              TT("vector", kro[:, 16:32], rt[:, 32:48], rt[:, 16:32], ALU.add,
                                 [r_tc[k2], r_ts[k2]], [r_kro[k2]])
                          yield
                          if do_q:
                              for c in range(3):
                                  TR(ptr_bf[:, c * 128:(c + 1) * 128], cqn[k2][:, c * 128:(c + 1) * 128], identb[:],
                                     [r_cqn[k2], r_identb], [r_PB[0]])
                              CP("scalar", cqnT[k2][:], ptr_bf[:, 0:384].rearrange("p (c n) -> p c n", c=3),
                                 [r_PB[0]], [r_cqnT[k2]])
                          if do_kv:
                              for c in range(2):
                                  TR(ptr_bf[:, 384 + c * 128:384 + (c + 1) * 128], ckvn[k2][:, c * 128:(c + 1) * 128],
                                     identb[:], [r_ckvn[k2], r_identb], [r_PB[0]])
                              CP("scalar", ckvnT[k2][:], ptr_bf[:, 384:640].rearrange("p (c n) -> p c n", c=2),
                                 [r_PB[0]], [r_ckvnT[k2]])
                          yield
                          if do_kv:
                              for hh in range(2):
                                  for c in range(2):
                                      MM(PKVU[:, hh * 512:(hh + 1) * 512], ckvnT[k2][:, c, :],
                                         wukv[:, c, hh * 512:(hh + 1) * 512], c == 0, c == 1,
                                         [r_ckvnT[k2], r_wukv[c]], [r_PKVU])
                              kvv = PKVU[:, :].rearrange("p (h d) -> p h d", h=H)
                              CP("vector", Ktok[k2][:, :, 0:64], kvv[:, :, 0:64], [r_PKVU], [r_Ktok[k2]])
                              CP("vector", Ktok[k2][:, :, 64:96], kro.unsqueeze(1).broadcast_to([128, H, RD]),
                                 [r_kro[k2]], [r_Ktok[k2]])
                              CP("scalar", Vst[gi][:, :, t, 0:64], kvv[:, :, 64:128], [r_PKVU], [r_Vst[gi][t]])
                              yield
                              for h in range(H):
                                  TR(pkt_bf[0:QKD, h * 128:(h + 1) * 128], Ktok[k2][:, h, :], identb[:],
                                     [r_Ktok[k2], r_identb], [r_PB[3]])
                              CP("vector" if not do_q else "scalar", KTst[gi][0:QKD, :, tsl],
                                 pkt_bf[0:QKD, :].rearrange("p (h n) -> p h n", h=H), [r_PB[3]], [r_KTst[gi][t]])
                          if do_q:
                              for hh in range(2):
                                  for c in range(3):
                                      MM(PQU[:, hh * 512:hh * 512 + 384], cqnT[k2][:, c, :],
                                         wuq[:, c, hh * 384:(hh + 1) * 384], c == 0, c == 2,
                                         [r_cqnT[k2], r_wuq[c]], [r_PQU])
                              rt = rtmp[k2]
                              for hh in range(2):
                                  qv = PQU[:, hh * 512:hh * 512 + 384].rearrange("p (h d) -> p h d", h=4)
                                  CP("vector", Qtok[k2][:, hh * 4:(hh + 1) * 4, 0:64], qv[:, :, 0:64],
                                     [r_PQU], [r_Qtok[k2]])
                                  xq = qv[:, :, 64:96].rearrange("p h (a j) -> p h a j", a=2)
                                  cosb = rope_sb[:, tg, 0:16].unsqueeze(1).unsqueeze(1).broadcast_to([128, 4, 2, 16])
                                  sinb = rope_sb[:, tg, 16:32].unsqueeze(1).unsqueeze(1).broadcast_to([128, 4, 2, 16])
                                  o0 = 96 + hh * 256
                                  tcq = rt[:, o0:o0 + 128].rearrange("p (h a j) -> p h a j", h=4, a=2)
                                  tsq = rt[:, o0 + 128:o0 + 256].rearrange("p (h a j) -> p h a j", h=4, a=2)
                                  TT("vector", tcq, xq, cosb, ALU.mult, [r_PQU, r_rope], [r_tcq[k2][hh]])
                                  TT("vector", tsq, xq, sinb, ALU.mult, [r_PQU, r_rope], [r_tsq[k2][hh]])
                                  TT("gpsimd", Qtok[k2][:, hh * 4:(hh + 1) * 4, 64:80], tcq[:, :, 0, :], tsq[:, :, 1, :],
                                     ALU.subtract, [r_tcq[k2][hh], r_tsq[k2][hh]], [r_Qtok[k2]])
                                  TT("gpsimd", Qtok[k2][:, hh * 4:(hh + 1) * 4, 80:96], tsq[:, :, 0, :], tcq[:, :, 1, :],
                                     ALU.add, [r_tcq[k2][hh], r_tsq[k2][hh]], [r_Qtok[k2]])
                              yield
                              for h in range(H):
                                  TR(pqt_bf[0:QKD, h * 128:(h + 1) * 128], Qtok[k2][:, h, :], identb[:],
                                     [r_Qtok[k2], r_identb], [r_PB[0]])
                              CP("vector", QTst[gi][0:QKD, :, tsl],
                                 pqt_bf[0:QKD, :].rearrange("p (h n) -> p h n", h=H), [r_PB[0]], [r_QTst[gi][t]])
              def group_na_gen(u, do_na, g, gi):
                  if do_na:
                      for blk in range(8):
                          col0 = C_NQ + blk * 128
                          for c in range(NCH):
                              MM(PB[3][:, :], win[:, c, col0:col0 + 128], hT[gi][:, c, :], c == 0, c == NCH - 1,
                                 r_hT[gi] + [r_win[c]], [r_PB[3]])
                          dst = naqst[gi] if blk < 4 else nakst[gi]
                          rr = r_naqst[gi] if blk < 4 else r_nakst[gi]
                          CP("scalar" if blk % 2 else "vector", dst[:, blk % 4, :], PB[3][:, :], [r_PB[3]],
                             [rr[blk % 4]])
                          yield

              def group_dma(u, do_q, do_kv, do_na, g, gi):
                  if do_na:
                      DMA("gpsimd", u["naq"][:, :, g * 512:(g + 1) * 512], naqst[gi][:], r_naqst[gi], ())
                      DMA("gpsimd", u["nak"][:, :, g * 512:(g + 1) * 512], nakst[gi][:], r_nakst[gi], ())
                      for half in range(2):
                          DMA("gpsimd", u["nav"][:, g * 4:(g + 1) * 4, half, :],
                              navst[gi][half * 64:(half + 1) * 64, :, :, :].rearrange("p t h d -> p t (h d)"),
                              r_navst[gi], ())
                  if do_kv:
                      DMA("gpsimd", u["kt"][:, :, g * 512:(g + 1) * 512].rearrange("h d n -> d h n"),
                          KTst[gi][0:QKD, :, :], r_KTst[gi], ())
                      DMA("gpsimd", u["v"][:, :, g * 4:(g + 1) * 4, :].rearrange("h p t d -> p h t d"),
                          Vst[gi][:], r_Vst[gi], ())
                  if do_q:
                      DMA("gpsimd", u["qt"][:, :, g * 512:(g + 1) * 512].rearrange("h d n -> d h n"),
                          QTst[gi][0:QKD, :, :], r_QTst[gi], ())

              def run_token_passes(passes):
                  dbg = getattr(cfg, 'dbg', (1, 1, 1))
                  WIN = 4
                  jobs = []
                  ginfos = []
                  for (u, src, ntile, rope_sb, do_q, do_kv, do_na) in passes:
                      do_q, do_kv, do_na = do_q and dbg[0], do_kv and dbg[1], do_na and dbg[2]
                      for g in range(ntile // 4):
                          gi = grpctr[0] % 2
                          grpctr[0] += 1
                          ginfo = dict(u=u, do_q=do_q, do_kv=do_kv, do_na=do_na, g=g, gi=gi, left=5 if do_na else 4,
                                       idx=len(ginfos))
                          ginfos.append(ginfo)
                          for t in range(4):
                              tc_ = tilectr[0]
                              tilectr[0] += 1
                              jobs.append((ginfo, tile_gen(u, src, rope_sb, do_q, do_kv, do_na, g, gi, t, tc_), t == 3))
                  active = []
                  ji = 0

                  def finish(ginfo):
                      ginfo["left"] -= 1
                      if ginfo["left"] == 0:
                          group_dma(ginfo["u"], ginfo["do_q"], ginfo["do_kv"], ginfo["do_na"], ginfo["g"], ginfo["gi"])

                  while active or ji < len(jobs):
                      if (ji < len(jobs) and sum(1 for a_ in active if a_[2] != "na") < WIN
                              and all(gq["left"] == 0 for gq in ginfos[:max(0, jobs[ji][0]["idx"] - 1)])):
                          ginfo, gen, lastt = jobs[ji]
                          ji += 1
                          active.append([ginfo, gen, "tile", lastt, False])
                      for a_ in list(active):
                          ginfo, gen = a_[0], a_[1]
                          try:
                              next(gen)
                              a_.append(None)
                              if a_[2] == "tile" and a_[3] and not a_[4] and ginfo["do_na"] and len(a_) - 5 >= 2:
                                  a_[4] = True
                                  active.append([ginfo, group_na_gen(ginfo["u"], True, ginfo["g"], ginfo["gi"]),
                                                 "na", False, False])
                          except StopIteration:
                              active.remove(a_)
                              finish(ginfo)

              plist = []
              for ui, u in enumerate(units):
                  if u["kind"] == "P":
                      plist.append((u, xp[ui * SP:(ui + 1) * SP, :], SP // 128, ropeP, True, True, True))
                  else:
                      plist.append((u, xn, NNA // 128, ropeN, True, False, True))
                      plist.append((u, xc, SC // 128, ropeC, False, True, False))
              run_token_passes(plist)
        barrier()

        ph2 = contextlib.ExitStack()
        with ph2 if 2 in PHASES else contextlib.nullcontext():
          if 2 in PHASES:
              SMAX = max(u["S"] for u in units)
              NQMAX = max(u["NQ"] for u in units)
              NTMAX = max(u["NT"] for u in units)
              ARENA_BYTES = (nc.sbuf_bytes_remaining - 64) // 32 * 32
              arena = top.enter_context(nc.sbuf_tensor("sb_arena", [128, ARENA_BYTES // 4], F32))
              WBYTES = 106752
              RW = Region(arena, 0, WBYTES)
              RM = Region(arena, WBYTES, ARENA_BYTES)
              NCHAIN = 4
              oTs = [sb(RM, f"oTs{i}", [128, 512], F32) for i in range(2)]
              r_oTs = [Res(), Res()]
              m_mark = RM.p
              tabf = sb(RM, "tabf", [128, 4, 15 * 64], F32)
              half_ = sb(RM, "halof", [128, 4, 16 * 64], F32)
              maskf = sb(RM, "maskf", [128, 2, 64], F32)
              naq = sb(RW, "naq", [128, 4, NTMAX], BF16)
              nak = sb(RW, "nak", [128, 4, NTMAX], BF16)
              nav = sb(RW, "nav", [64, NTMAX // 64, 8 * 65], BF16)
              r_naq, r_nak, r_nav = Res(), Res(), Res()
              tabb = sb(RW, "tabb", [128, 4, 15 * 64], BF16)
              halb = sb(RW, "halob", [128, 4, 16 * 64], BF16)
              r_tabf, r_half, r_maskf, r_tabb, r_halb = Res(), Res(), Res(), Res(), Res()
              PTn = [sb(RW, f"PTn{i}", [64, 512], BF16) for i in range(2 * NCHAIN)]
              r_PTn = [Res() for _ in range(2 * NCHAIN)]
              if getattr(cfg, "verbose", False):
                  print("arena", ARENA_BYTES, "W used", RW.p - RW.lo, "of", WBYTES, "M used (2a)", RM.p - RM.lo)

              PS = [psum(ph2, f"ps{i}", [128, 1024], F32) for i in range(2)]
              PO = [psum(ph2, f"po{i}", [128, 512], F32) for i in range(4)]
              r_PSh = [[Res(psum=True), Res(psum=True)] for _ in range(2)]
              r_PO = [Res(psum=True) for _ in range(4)]
              PSn = [PS[0][:, 0:512], PS[0][:, 512:1024], PS[1][:, 0:512], PS[1][:, 512:1024]]
              r_PSn = [r_PSh[0][0], r_PSh[0][1], r_PSh[1][0], r_PSh[1][1]]
              PA = PO
              r_PA = r_PO

              DMA("sync", tabf[:], natab_d[:, :, :], (), [r_tabf])
              DMA("sync", half_[:], nahalo_d[:, :, :], (), [r_half])
              DMA("sync", maskf[:], namask_d[:, :, :], (), [r_maskf])
              for (srcf, dstb, nb, rs, rd) in ((tabf, tabb, 15, r_tabf, r_tabb), (half_, halb, 16, r_half, r_halb)):
                  v = srcf[:].rearrange("p a (b q) -> p (a b) q", q=64)
                  vb = dstb[:].rearrange("p a (b q) -> p (a b) q", q=64)
                  m01 = maskf[:, 0, :].unsqueeze(1).broadcast_to([128, 4 * nb, 64])
                  mng = maskf[:, 1, :].unsqueeze(1).broadcast_to([128, 4 * nb, 64])
                  TT("vector", v, v, m01, ALU.mult, [rs, r_maskf], [rs])
                  TT("vector", vb, v, mng, ALU.add, [rs, r_maskf], [rd])

              ctr = dict(hb=0, pt=0, ps=0, po=0, ot=0, ptn=0)
              if getattr(cfg, "verbose", False):
                  print("ph2 sbuf remaining", nc.sbuf_bytes_remaining)

              def mla_attention(u):
                  Sx, NQu, q0 = u["S"], u["NQ"], u["q0"]
                  nkt = Sx // 128
                  nk2 = nkt // 2
                  nqb = NQu // 512
                  steps = [(h, qb, k2) for h in range(H) for qb in range(nqb) for k2 in range(nk2)]
                  hbase = ctr["hb"]
                  ctr["hb"] += H
                  pobase = ctr["po"]
                  ctr["po"] += H * nqb

                  def load(h):
                      bi = (hbase + h) % 2
                      DMA("sync", KTb[bi][0:QKD, 0:Sx], u["kt"][h, :, :], (), [r_KTb[bi]])
                      DMA("sync", Vb[bi][:, 0:nkt, :], u["v"][h, :, :, :], (), [r_Vb[bi]])
                      DMA("sync", QTb[bi][0:QKD, 0:NQu], u["qt"][h, :, q0:q0 + NQu], (), [r_QTb[bi]])

                  def QK(i):
                      h, qb, k2 = steps[i]
                      bi = (hbase + h) % 2
                      si = i % 2
                      for j in range(2):
                          kt = 2 * k2 + j
                          MM(PS[si][:, j * 512:(j + 1) * 512], KTb[bi][0:QKD, kt * 128:(kt + 1) * 128],
                             QTb[bi][0:QKD, qb * 512:(qb + 1) * 512], True, True,
                             [r_KTb[bi], r_QTb[bi]], [r_PSh[si][j]])

                  def EXPPV(i):
                      h, qb, k2 = steps[i]
                      bi = (hbase + h) % 2
                      si = i % 2
                      pi = i % 3
                      oi = (pobase + h * nqb + qb) % 4
                      ACT(PT[pi][:, :], PS[si][:, :], AF.Exp, r_PSh[si], [r_PT[pi]])
                      for j in range(2):
                          kt = 2 * k2 + j
                          MM(PO[oi][0:65, :], Vb[bi][:, kt, :], PT[pi][:, j * 512:(j + 1) * 512],
                             kt == 0, kt == nkt - 1, [r_Vb[bi], r_PT[pi]], [r_PO[oi]])
                      if k2 == nk2 - 1:
                          ti = ctr["ot"] % 2
                          ctr["ot"] += 1
                          CP("vector", oTs[ti][0:65, :], PO[oi][0:65, :], [r_PO[oi]], [r_oTs[ti]])
                          DMA("gpsimd", u["ot"][h, :, qb * 512:(qb + 1) * 512], oTs[ti][0:65, :], [r_oTs[ti]], ())

                  load(0)
                  QK(0)
                  for i in range(len(steps)):
                      h, qb, k2 = steps[i]
                      if qb == 0 and k2 == 0 and h + 1 < H:
                          load(h + 1)
                      if i + 1 < len(steps):
                          QK(i + 1)
                      EXPPV(i)
                      if i % 4 == 3:
                          next(wgen, None)

              def na_attention(u):
                  NT, NQu = u["NT"], u["NQ"]
                  nrq = NQu // 64
                  plan = na_plan(u["kind"], nrq)
                  qrow0 = 0 if u["kind"] == "P" else 4
                  DMA("sync", naq[:, :, 0:NT], u["naq"][:, :, :], (), [r_naq])
                  DMA("sync", nak[:, :, 0:NT], u["nak"][:, :, :], (), [r_nak])
                  DMA("sync", nav[:, 0:NT // 64, :], u["nav"][:, :, :, :].rearrange("p t a d -> p (t a) d"), (), [r_nav])
                  chains = [(h, b) for b in range(len(plan)) for h in range(H)]
                  for c0_ in range(0, len(chains), NCHAIN):
                      grp = chains[c0_:c0_ + NCHAIN]
                      nst = max(len(plan[b]) for (h, b) in grp)

                      def QKB(j, s):
                          h, b = grp[j]
                          items = plan[b]
                          if s >= len(items):
                              return
                          ks, i0, i1, src, blk0 = items[s]
                          pr, pb = h // 2, (h % 2) * 64
                          N = 64 * (i1 - i0 + 1)
                          qtok0 = (qrow0 + i0) * 64
                          MM(PSn[j][0:64, 0:N], nak[pb:pb + 64, pr, ks * 64:(ks + 1) * 64],
                             naq[pb:pb + 64, pr, qtok0:qtok0 + N], True, False, [r_nak, r_naq], [r_PSn[j]])
                          tb = tabb if src == "T" else halb
                          rtb = r_tabb if src == "T" else r_halb
                          MM(PSn[j][0:64, 0:N], identb[pb:pb + 64, pb:pb + 64],
                             tb[pb:pb + 64, pr, blk0 * 64:blk0 * 64 + N], False, True, [rtb, r_identb], [r_PSn[j]])

                      def EXPN(j, s):
                          h, b = grp[j]
                          items = plan[b]
                          if s >= len(items):
                              return
                          ks, i0, i1, src, blk0 = items[s]
                          N = 64 * (i1 - i0 + 1)
                          pi = 2 * j + (s % 2)
                          ACT(PTn[pi][:, 0:N], PSn[j][0:64, 0:N], AF.Exp, [r_PSn[j]], [r_PTn[pi]])

                      def PVN(j, s):
                          h, b = grp[j]
                          items = plan[b]
                          if s >= len(items):
                              return
                          ks, i0, i1, src, blk0 = items[s]
                          N = 64 * (i1 - i0 + 1)
                          c0 = (i0 - 8 * b) * 64
                          pi = 2 * j + (s % 2)
                          MM(PA[j][0:65, c0:c0 + N], nav[:, ks, h * 65:(h + 1) * 65], PTn[pi][:, 0:N],
                             s == 0, s == len(items) - 1, [r_nav, r_PTn[pi]], [r_PA[j]], skip_group_check=True)
                          if s == len(items) - 1:
                              ti = ctr["ot"] % 2
                              ctr["ot"] += 1
                              CP("vector", oTs[ti][0:65, :], PA[j][0:65, :], [r_PA[j]], [r_oTs[ti]])
                              DMA("gpsimd", u["ot"][8 + h, :, b * 512:(b + 1) * 512], oTs[ti][0:65, :],
                                  [r_oTs[ti]], ())

                      for j in range(len(grp)):
                          QKB(j, 0)
                      for s in range(nst):
                          for j in range(len(grp)):
                              EXPN(j, s)
                          for j in range(len(grp)):
                              QKB(j, s + 1)
                          for j in range(len(grp)):
                              PVN(j, s)

              for u in units:
                  na_attention(u)
              barrier()
              RW.reset()
              RM.p = m_mark
              wo = sb(RW, "wo", [128, NCH, D], BF16)
              wg = sb(RW, "wg", [128, NCH, DFF], BF16)
              wu = sb(RW, "wu", [128, NCH, DFF], BF16)
              r_wo = [Res() for _ in range(NCH)]
              r_wg = [Res() for _ in range(NCH)]
              r_wu = [Res() for _ in range(NCH)]
              KTb = [sb(RM, f"KTb{i}", [128, SMAX], BF16) for i in range(2)]
              Vb = [sb(RM, f"Vb{i}", [128, SMAX // 128, 65], BF16) for i in range(2)]
              QTb = [sb(RM, f"QTb{i}", [128, NQMAX], BF16) for i in range(2)]
              r_KTb, r_Vb, r_QTb = [Res(), Res()], [Res(), Res()], [Res(), Res()]
              PT = [sb(RM, f"PT{i}", [128, 1024], BF16) for i in range(3)]
              r_PT = [Res() for _ in range(3)]
              WST2 = 1408
              wst2 = [sb(RM, f"wst2_{i}", [128, WST2], F32) for i in range(4)]
              r_wst2 = [Res() for _ in range(4)]
              if getattr(cfg, "verbose", False):
                  print("W used (2b)", RW.p - RW.lo, "M used (2b)", RM.p - RM.lo, "of", RM.hi - RM.lo)
              rot2 = [0, 0]

              def wchain():
                  yield from load_weight_gen(wst2, r_wst2, WST2, rot2, wo, r_wo, w_o, NCH, D, G_MIX, [(0, D, 1.0)])
                  yield from load_weight_gen(wst2, r_wst2, WST2, rot2, wg, r_wg, w_gate, NCH, DFF, G_FFN,
                                             [(0, DFF, 1.0)])
                  yield from load_weight_gen(wst2, r_wst2, WST2, rot2, wu, r_wu, w_up, NCH, DFF, G_FFN,
                                             [(0, DFF, 1.0)])
              wgen = wchain()
              for u in units:
                  mla_attention(u)
              for _ in wgen:
                  pass
        barrier()

        ph3 = contextlib.ExitStack()
        with ph3 if 3 in PHASES else contextlib.nullcontext():
          if 3 in PHASES:
              RM.reset()
              wd = sb(RM, "wd", [128, NFF, D], BF16)
              r_wd = [Res() for _ in range(NFF)]
              wst3 = [sb(RM, "wst3_0", [128, D], F32)]
              r_wst3 = [Res()]
              rot3 = [0, 0]
              wdgen = load_weight_gen(wst3, r_wst3, D, rot3, wd, r_wd, w_down, NFF, D, None, [(0, D, 1.0)],
                                      cv_engs=("vector", "scalar"))
              gfin = sb(RM, "gfin", [128, D], F32)
              r_gfin = Res()
              DMA("sync", gfin[:], gfin_d[0:1, :].partition_broadcast(128), (), [r_gfin])
              onec = sb(RM, "onec", [128, 1], F32)
              r_one = Res()
              MEMSET("vector", onec[:], 1.0, [r_one])

              TG = 2
              NTOK = TG * 128
              oTin = sb(RM, "oTin", [128, 4, 128], F32)
              r_oTin = Res()
              atok = sb(RM, "atok", [128, D], F32)
              r_atok = Res()
              x1 = [[sb(RM, f"x1_{s_}_{i}", [128, D], F32) for i in range(TG)] for s_ in range(2)]
              r_x1 = [[Res() for _ in range(TG)] for _ in range(2)]
              mixb = sb(RM, "mixb", [128, D], BF16)
              r_mixb = Res()
              mixT = sb(RM, "mixT", [128, NCH, 128], BF16)
              r_mixT = Res()
              h2b = sb(RM, "h2b", [128, D], BF16)
              r_h2b = Res()
              h2T = [sb(RM, f"h2T{s_}", [128, NCH, NTOK], BF16) for s_ in range(2)]
              r_h2T = [[Res() for _ in range(TG)] for _ in range(2)]
              actT = sb(RM, "actT", [128, NFF, NTOK], BF16)
              r_actT = [Res() for _ in range(NFF)]
              sg = [sb(RM, f"sg{i}", [128, NTOK], F32) for i in range(2)]
              r_sg = [Res(), Res()]
              st3 = [sb(RM, f"st3_{i}", [128, 32], F32) for i in range(4)]
              r_st3 = [[Res() for _ in range(6)] for _ in range(4)]

              POh = psum(ph3, "poh", [128, 512], F32)
              r_POh = Res(psum=True)
              PTR = psum(ph3, "ptr3", [128, 512], F32)
              r_PTR = Res(psum=True)
              ptr3_bf = PTR[:].bitcast(BF16)
              PY = psum(ph3, "py", [128, 1024], F32)
              r_PY = Res(psum=True)
              PG = [psum(ph3, f"pg{i}", [128, 512], F32) for i in range(2)]
              PU = [psum(ph3, f"pu{i}", [128, 512], F32) for i in range(2)]
              r_PG, r_PU = [Res(psum=True), Res(psum=True)], [Res(psum=True), Res(psum=True)]
              if getattr(cfg, "verbose", False):
                  print("M used (3)", RM.p - RM.lo, "of", RM.hi - RM.lo)

              def prologue(u, g, slot):
                  for t in range(TG):
                      tok0 = g * NTOK + t * 128
                      X, rX = x1[slot][t], r_x1[slot][t]
                      st = st3[slot * 2 + t]
                      rs_ = r_st3[slot * 2 + t]
                      DMA("sync", X[:], u["xres"][tok0:tok0 + 128, :], (), [rX])
                      for hq in range(4):
                          DMA("sync", oTin[0:65, :, :],
                              u["ot"][hq * 4:(hq + 1) * 4, :, tok0:tok0 + 128].rearrange("h d n -> d h n"),
                              (), [r_oTin])
                          yield
                          for hh in range(4):
                              TR(POh[:, hh * 128:hh * 128 + 65], oTin[0:65, hh, :], identf[0:65, 0:65],
                                 [r_oTin, r_identf], [r_POh])
                          pov = POh[:, :].rearrange("p (h d) -> p h d", h=4)
                          rc = st[:, 8:12].unsqueeze(2)
                          OP("vector", lambda e, o=rc, i=pov[:, :, 64:65]: e.reciprocal(out=o, in_=i),
                             [r_POh], [rs_[4]])
                          TT("vector", atok[:, hq * 256:(hq + 1) * 256].rearrange("p (h d) -> p h d", h=4),
                             pov[:, :, 0:64], rc.broadcast_to([128, 4, 64]), ALU.mult,
                             [r_POh, rs_[4]], [r_atok])
                      for half in range(2):
                          hs = slice(half * 512, (half + 1) * 512)
                          ACT(mixb[:, hs], atok[:, hs], AF.Square, [r_atok], [r_mixb, rs_[half]],
                              accum_out=st[:, 3 * half:3 * half + 1])
                          rstd(st, 3 * half, 512, rs_[half])
                          TS("vector", mixb[:, hs], atok[:, hs], st[:, 3 * half + 2:3 * half + 3], None, ALU.mult, None,
                             [r_atok, rs_[half]], [r_mixb])
                      yield
                      yield
                      for c in range(NCH):
                          TR(ptr3_bf[:, c * 128:(c + 1) * 128], mixb[:, c * 128:(c + 1) * 128], identb[:],
                             [r_mixb, r_identb], [r_PTR])
                      CP("scalar", mixT[:], ptr3_bf[:, 0:1024].rearrange("p (c n) -> p c n", c=NCH),
                         [r_PTR], [r_mixT])
                      yield
                      for hh in range(2):
                          for c in range(NCH):
                              MM(PY[:, hh * 512:(hh + 1) * 512], mixT[:, c, :], wo[:, c, hh * 512:(hh + 1) * 512],
                                 c == 0, c == NCH - 1, [r_mixT, r_wo[c]], [r_PY])
                      TT("vector", X[:], X[:], PY[:, :], ALU.add, [rX, r_PY], [rX])
                      ACT(h2b[:], X[:], AF.Square, [rX], [r_h2b, rs_[2]], accum_out=st[:, 16:17])
                      rstd(st, 16, D, rs_[2])
                      TS("vector", h2b[:], X[:], st[:, 18:19], None, ALU.mult, None, [rX, rs_[2]], [r_h2b])
                      yield
                      yield
                      for c in range(NCH):
                          TR(ptr3_bf[:, c * 128:(c + 1) * 128], h2b[:, c * 128:(c + 1) * 128], identb[:],
                             [r_h2b, r_identb], [r_PTR])
                      CP("scalar", h2T[slot][:, :, t * 128:(t + 1) * 128],
                         ptr3_bf[:, 0:1024].rearrange("p (c n) -> p c n", c=NCH), [r_PTR], [r_h2T[slot][t]])
                      yield

              def ffn_step(slot, f):
                  pi = f % 2
                  for c in range(NCH):
                      MM(PG[pi][:, 0:NTOK], wg[:, c, f * 128:(f + 1) * 128], h2T[slot][:, c, :], c == 0, c == NCH - 1,
                         r_h2T[slot] + [r_wg[c]], [r_PG[pi]])
                  for c in range(NCH):
                      MM(PU[pi][:, 0:NTOK], wu[:, c, f * 128:(f + 1) * 128], h2T[slot][:, c, :], c == 0, c == NCH - 1,
                         r_h2T[slot] + [r_wu[c]], [r_PU[pi]])
                  ACT(sg[pi][:], PG[pi][:, 0:NTOK], AF.Exp, [r_PG[pi]], [r_sg[pi]], scale=-1.0)
                  ACT(sg[pi][:], sg[pi][:], AF.Ln, [r_sg[pi], r_one], [r_sg[pi]], bias=onec[:, 0:1])
                  ACT(sg[pi][:], sg[pi][:], AF.Exp, [r_sg[pi]], [r_sg[pi]], scale=-1.0)
                  TT("vector", sg[pi][:], sg[pi][:], PG[pi][:, 0:NTOK], ALU.mult, [r_sg[pi], r_PG[pi]], [r_sg[pi]])
                  TT("vector", actT[:, f, :], sg[pi][:], PU[pi][:, 0:NTOK], ALU.mult,
                     [r_sg[pi], r_PU[pi]], [r_actT[f]])

              def down_epi(u, g, slot, t):
                  tok0 = g * NTOK + t * 128
                  X, rX = x1[slot][t], r_x1[slot][t]
                  st = st3[slot * 2 + t]
                  rr = r_st3[slot * 2 + t][3]
                  for hh in range(2):
                      for f in range(NFF):
                          MM(PY[:, hh * 512:(hh + 1) * 512], actT[:, f, t * 128:(t + 1) * 128],
                             wd[:, f, hh * 512:(hh + 1) * 512], f == 0, f == NFF - 1,
                             [r_actT[f], r_wd[f]], [r_PY])
                  TT("vector", X[:], X[:], PY[:, :], ALU.add, [rX, r_PY], [rX])
                  ACT(h2b[:], X[:], AF.Square, [rX], [r_h2b, rr], accum_out=st[:, 24:25])
                  rstd(st, 24, D, rr)
                  STT("vector", X[:], X[:], st[:, 26:27], gfin[:], ALU.mult, ALU.mult, [rX, rr, r_gfin], [rX])
                  DMA("gpsimd", u["yout"][tok0:tok0 + 128, :], X[:], [rX], ())

              groups = [(u, g) for u in units for g in range(u["NQ"] // NTOK)]
              cur = prologue(groups[0][0], groups[0][1], 0)
              for _ in cur:
                  pass
              for gi_, (u, g) in enumerate(groups):
                  slot = gi_ % 2
                  nxt = None
                  if gi_ + 1 < len(groups):
                      nxt = prologue(groups[gi_ + 1][0], groups[gi_ + 1][1], 1 - slot)
                  for f in range(NFF):
                      ffn_step(slot, f)
                      if nxt is not None:
                          next(nxt, None)
                      next(wdgen, None)
                  if nxt is not None:
                      for _ in nxt:
                          pass
                  for t in range(TG):
                      down_epi(u, g, slot, t)
        S.emit(nc, block, csem, dsem)
    return nc


def _rope_table(pos):
    inv = (10000.0 ** (-np.arange(0, RD, 2, dtype=np.float32) / np.float32(RD))).astype(np.float32)
    ang = pos.astype(np.float32)[:, None] * inv[None, :]
    tab = np.concatenate([np.cos(ang), np.sin(ang)], axis=1).astype(np.float32)
    nt = pos.shape[0] // 128
    return np.ascontiguousarray(tab.reshape(nt, 128, 32).transpose(1, 0, 2))


def _col(g):
    g = np.asarray(g, np.float32).reshape(-1, 128)
    return g.T


def make_in_maps(inputs, cfg):
    NP, SP, SC, NQ, NNA = cfg.NP, cfg.SP, cfg.SC, cfg.NQ, cfg.NNA
    xpr = np.asarray(inputs["x_prompt"], np.float32)
    xsm = np.asarray(inputs["x_sample"], np.float32)
    nquart = SC // NQ
    ncores = xsm.shape[0] * nquart
    assert xpr.shape[0] == NP * ncores
    Rq = NQ // 64
    Rtot = SC // 64
    f = lambda k: np.ascontiguousarray(np.asarray(inputs[k], np.float32)[0])
    gcol = np.zeros((128, 32), np.float32)
    gcol[:, G_ATTN:G_ATTN + 8] = _col(inputs["attn_norm_g"][0])
    gcol[:, G_Q:G_Q + 3] = _col(inputs["q_norm_g"][0])
    gcol[:, G_KV:G_KV + 2] = _col(inputs["kv_norm_g"][0])
    gcol[:, G_MIX:G_MIX + 4] = _col(inputs["mla_out_g"][0])
    gcol[:, G_MIX + 4:G_MIX + 8] = _col(inputs["na_out_g"][0])
    gcol[:, G_FFN:G_FFN + 8] = _col(inputs["ffn_norm_g"][0])
    rpb = np.asarray(inputs["na_rpb"], np.float32)[0]
    kc = np.arange(64)[:, None]
    qc = np.arange(64)[None, :]
    dj = np.clip(kc - qc, -15, 15) + 15
    cs = np.clip(qc - 8, 0, 48)
    m01 = ((kc >= cs) & (kc < cs + 16)).astype(np.float32)
    namask = np.zeros((128, 2, 64), np.float32)
    namask[:, 0, :] = np.tile(m01, (2, 1))
    namask[:, 1, :] = np.tile((1.0 - m01) * NEG, (2, 1))

    def blockT(h, di):
        return rpb[h, di + 7][dj]

    natab = np.zeros((128, 4, 15, 64), np.float32)
    for h in range(H):
        pb, pr = (h % 2) * 64, h // 2
        for j in range(15):
            natab[pb:pb + 64, pr, j, :] = blockT(h, 7 - j)
    natab = natab.reshape(128, 4, 15 * 64)
    shared = dict(w_in=f("w_in"), w_uq=f("w_uq"), w_ukv=f("w_ukv"), w_o=f("w_o"), w_gate=f("w_gate"),
                  w_up=f("w_up"), w_down=f("w_down"), gcol=gcol,
                  gfin=np.asarray(inputs["final_norm_g"], np.float32).reshape(1, D),
                  ident=np.eye(128, dtype=np.float32), natab=natab, namask=namask,
                  ropeP=_rope_table(np.arange(SP)), ropeC=_rope_table(np.arange(SC)))
    maps = []
    for c in range(ncores):
        sq, qt = c // nquart, c % nquart
        m = dict(shared)
        m["xp"] = np.ascontiguousarray(xpr[c * NP:(c + 1) * NP].reshape(NP * SP, D))
        seq = xsm[sq]
        m["xc"] = np.ascontiguousarray(seq)
        r0 = qt * Rq
        rows_b = [4, 5, 6, 7] if qt == 0 else [r0 - 4, r0 - 3, r0 - 2, r0 - 1]
        last = (qt == nquart - 1)
        rows_a = [Rtot - 8, Rtot - 7, Rtot - 6] if last else [r0 + Rq, r0 + Rq + 1, r0 + Rq + 2]
        rows = rows_b + list(range(r0, r0 + Rq)) + rows_a
        xn = np.zeros((NNA, D), np.float32)
        posn = np.zeros((NNA,), np.float32)
        for i, r in enumerate(rows):
            xn[i * 64:(i + 1) * 64] = seq[r * 64:(r + 1) * 64]
            posn[i * 64:(i + 1) * 64] = np.arange(r * 64, (r + 1) * 64)
        m["xn"] = xn
        m["ropeN"] = _rope_table(posn)
        hal = np.zeros((128, 4, 16, 64), np.float32)
        for h in range(H):
            pb, pr = (h % 2) * 64, h // 2
            for ks in range(4):
                for i in range(ks + 1):
                    di = (rows_b[ks] - r0) - i
                    hal[pb:pb + 64, pr, [0, 1, 3, 6][ks] + i, :] = blockT(h, di)
            for j in range(3):
                for n_, i in enumerate(range(Rq - 3 + j, Rq)):
                    di = (rows_a[j] - r0) - i
                    hal[pb:pb + 64, pr, 10 + [0, 3, 5][j] + n_, :] = blockT(h, di)
        m["nahalo"] = hal.reshape(128, 4, 16 * 64)
        maps.append(m)
    return maps


def assemble(results, cfg, nsample):
    NP, SP, SC, NQ = cfg.NP, cfg.SP, cfg.SC, cfg.NQ
    nquart = SC // NQ
    ncores = nsample * nquart
    yp = np.concatenate([np.asarray(results[c]["yp"], np.float32).reshape(NP, SP, D) for c in range(ncores)], axis=0)
    ys = np.zeros((nsample, SC, D), np.float32)
    for c in range(ncores):
        sq, qt = c // nquart, c % nquart
        ys[sq, qt * NQ:(qt + 1) * NQ] = np.asarray(results[c]["yn"], np.float32)
    return yp, ys


_NC_CACHE = {}


def kernel(**inputs):
    cfg = Cfg(NP=2, SP=2048, SC=8192, NQ=2048)
    maps = make_in_maps(inputs, cfg)
    key = (cfg.NP, cfg.SP, cfg.SC, cfg.NQ)
    if key not in _NC_CACHE:
        _NC_CACHE[key] = build_program(cfg)
    nc = _NC_CACHE[key]
    res = run_bass_kernel_spmd(nc, maps, core_ids=list(range(len(maps))))
    return assemble(res.results, cfg, np.asarray(inputs["x_sample"]).shape[0])
```
